# Optimizing a Trainium2 kernel written in Bass

```python
import jax, jax.numpy as jnp
from jax import lax
import numpy as np

D_MODEL = 1024
BATCH = 8
SEQ = 2048
DEPTH = 2

MEM_LEN = 256
POOL_GROUPS = 4
POOL_GROUP_DIM = D_MODEL // 16
POOL_WIDTH = POOL_GROUPS * POOL_GROUP_DIM
POOL_WINDOWS = (2, 4, 8, 16)
FOX_HEADS = 8
FOX_HEAD_DIM = 64
FOX_WIDTH = FOX_HEADS * FOX_HEAD_DIM
Q_BLOCK = 128
SGU_GROUPS = 4
SGU_GROUP_DIM = D_MODEL // 16
SGU_WIDTH = SGU_GROUPS * SGU_GROUP_DIM
SGU_CHUNK = 128
N_BRANCH = 3
OFF_A = 0
OFF_Q = OFF_A + POOL_WIDTH
OFF_K = OFF_Q + FOX_WIDTH
OFF_V = OFF_K + FOX_WIDTH
OFF_F = OFF_V + FOX_WIDTH
OFF_C = OFF_F + FOX_HEADS
OFF_G = OFF_C + 2 * SGU_WIDTH
N_IN = OFF_G + N_BRANCH * D_MODEL
XATTN_HEADS = 4
XATTN_HEAD_DIM = D_MODEL // XATTN_HEADS
D_FF = 4 * D_MODEL
EPS = 1e-6
NEG = -1e30

kernel_name = "hybrid_pool_fox_sgu_gated_block"


def rmsnorm(x, g):
    xf = x.astype(jnp.float32)
    y = xf * lax.rsqrt(jnp.mean(xf * xf, axis=-1, keepdims=True) + EPS)
    return (y * g.astype(jnp.float32)).astype(x.dtype)


def pool_mixer(a, w, scale):
    B, S, _ = a.shape
    af = a.astype(jnp.float32)
    c = jnp.pad(jnp.cumsum(af, axis=1), ((0, 0), (1, 0), (0, 0)))
    t = jnp.arange(S)
    outs = []
    for gi, win in enumerate(POOL_WINDOWS):
        sl = slice(gi * POOL_GROUP_DIM, (gi + 1) * POOL_GROUP_DIM)
        cg = c[..., sl]
        lo = jnp.take(cg, jnp.maximum(t + 1 - win, 0), axis=1)
        cnt = jnp.minimum(t + 1, win).astype(jnp.float32)[None, :, None]
        outs.append((cg[:, 1:] - lo) / cnt - af[..., sl])
    d = jnp.stack(outs, axis=2).astype(a.dtype)
    y = jnp.einsum('bsgc,gcd->bsgd', d, w).reshape(B, S, POOL_WIDTH)
    return y * scale


def forgetting_attention(q, k, v, logf):
    S = q.shape[1]
    F = jnp.cumsum(logf, axis=1).transpose(0, 2, 1)
    scale = FOX_HEAD_DIM ** -0.5
    outs = []
    for i in range(S // Q_BLOCK):
        q0 = i * Q_BLOCK
        kend = q0 + Q_BLOCK
        s = jnp.einsum('bqhd,bkhd->bhqk', q[:, q0:kend], k[:, :kend]).astype(jnp.float32) * scale
        s = s + F[:, :, q0:kend, None] - F[:, :, None, :kend]
        mask = (q0 + jnp.arange(Q_BLOCK))[:, None] >= jnp.arange(kend)[None, :]
        s = jnp.where(mask, s, NEG)
        p = jax.nn.softmax(s, axis=-1).astype(v.dtype)
        outs.append(jnp.einsum('bhqk,bkhd->bqhd', p, v[:, :kend]))
    return jnp.concatenate(outs, axis=1)


def spatial_gating(z, norm_g, ws, b):
    B, S, _ = z.shape
    u, v = z[..., :SGU_WIDTH], z[..., SGU_WIDTH:]
    v = rmsnorm(v, norm_g)
    vc = v.reshape(B, S // SGU_CHUNK, SGU_CHUNK, SGU_GROUPS, SGU_GROUP_DIM)
    causal = jnp.tril(jnp.ones((SGU_CHUNK, SGU_CHUNK), dtype=ws.dtype))
    w = ws * causal[None]
    mixed = jnp.einsum('gts,bcsgd->bctgd', w, vc) + b.T[None, None, :, :, None]
    return u * mixed.reshape(B, S, SGU_WIDTH)


def setup_inputs(seed: int = 0) -> dict:
    key = jax.random.key(seed)
    ks = jax.random.split(key, 24)
    L, D = DEPTH, D_MODEL
    nrm = lambda k, shape, fan_in: jax.random.normal(k, shape, jnp.float32) * (fan_in ** -0.5)
    gain = lambda k, shape: 1.0 + 0.05 * jax.random.normal(k, shape, jnp.float32)
    b_forget = jnp.linspace(1.0, 6.0, FOX_HEADS, dtype=jnp.float32)[None, :] + 0.1 * jax.random.normal(ks[3], (L, FOX_HEADS), jnp.float32)
    return {
        "x": jax.random.normal(ks[0], (BATCH, SEQ, D), jnp.float32),
        "mem": jax.random.normal(ks[1], (BATCH, MEM_LEN, D), jnp.float32),
        "norm_mix_g": gain(ks[2], (L, D)),
        "w_in": nrm(ks[4], (L, D, N_IN), D),
        "b_forget": b_forget,
        "pool_w": nrm(ks[5], (L, POOL_GROUPS, POOL_GROUP_DIM, POOL_GROUP_DIM), POOL_GROUP_DIM),
        "pool_scale": gain(ks[6], (L, POOL_WIDTH)),
        "sgu_norm_g": gain(ks[7], (L, SGU_WIDTH)),
        "sgu_w": nrm(ks[8], (L, SGU_GROUPS, SGU_CHUNK, SGU_CHUNK), SGU_CHUNK),
        "sgu_b": gain(ks[9], (L, SGU_GROUPS, SGU_CHUNK)),
        "w_branch_a": nrm(ks[10], (L, POOL_WIDTH, D), POOL_WIDTH),
        "w_branch_b": nrm(ks[11], (L, FOX_WIDTH, D), FOX_WIDTH),
        "w_branch_c": nrm(ks[12], (L, SGU_WIDTH, D), SGU_WIDTH),
        "b_gate": 0.01 * jax.random.normal(ks[13], (L, N_BRANCH * D), jnp.float32),
        "w_out": nrm(ks[14], (L, D, D), D),
        "norm_xattn_g": gain(ks[15], (L, D)),
        "norm_mem_g": gain(ks[16], (L, D)),
        "w_xq": nrm(ks[17], (L, D, D), D),
        "w_xkv": nrm(ks[18], (L, D, 2 * D), D),
        "w_xo": nrm(ks[19], (L, D, D), D),
        "norm_ffn_g": gain(ks[20], (L, D)),
        "w_ff1": nrm(ks[21], (L, D, D_FF), D),
        "w_ff2": nrm(ks[22], (L, D_FF, D), D_FF),
        "final_norm_g": gain(ks[23], (D,)),
    }


def reference(x, mem, norm_mix_g, w_in, b_forget, pool_w, pool_scale, sgu_norm_g, sgu_w, sgu_b,
              w_branch_a, w_branch_b, w_branch_c, b_gate, w_out, norm_xattn_g, norm_mem_g,
              w_xq, w_xkv, w_xo, norm_ffn_g, w_ff1, w_ff2, final_norm_g):
    B, S, D = x.shape
    M = mem.shape[1]
    for l in range(DEPTH):
        h = rmsnorm(x, norm_mix_g[l])
        proj = h @ w_in[l]
        a = proj[..., OFF_A:OFF_Q]
        q = proj[..., OFF_Q:OFF_K].reshape(B, S, FOX_HEADS, FOX_HEAD_DIM)
        k = proj[..., OFF_K:OFF_V].reshape(B, S, FOX_HEADS, FOX_HEAD_DIM)
        v = proj[..., OFF_V:OFF_F].reshape(B, S, FOX_HEADS, FOX_HEAD_DIM)
        logf = jax.nn.log_sigmoid(proj[..., OFF_F:OFF_C].astype(jnp.float32) + b_forget[l].astype(jnp.float32))
        zc = jax.nn.gelu(proj[..., OFF_C:OFF_G])
        gates = jax.nn.sigmoid(proj[..., OFF_G:] + b_gate[l])

        y_a = pool_mixer(a, pool_w[l], pool_scale[l]) @ w_branch_a[l]
        y_b = forgetting_attention(q, k, v, logf).reshape(B, S, FOX_WIDTH) @ w_branch_b[l]
        y_c = spatial_gating(zc, sgu_norm_g[l], sgu_w[l], sgu_b[l]) @ w_branch_c[l]
        merged = gates[..., :D] * y_a + gates[..., D:2 * D] * y_b + gates[..., 2 * D:] * y_c
        x = x + merged @ w_out[l]

        hx = rmsnorm(x, norm_xattn_g[l])
        hm = rmsnorm(mem, norm_mem_g[l])
        xq = (hx @ w_xq[l]).reshape(B, S, XATTN_HEADS, XATTN_HEAD_DIM)
        kv = hm @ w_xkv[l]
        xk = kv[..., :D].reshape(B, M, XATTN_HEADS, XATTN_HEAD_DIM)
        xv = kv[..., D:].reshape(B, M, XATTN_HEADS, XATTN_HEAD_DIM)
        s = jnp.einsum('bqhd,bkhd->bhqk', xq, xk).astype(jnp.float32) * (XATTN_HEAD_DIM ** -0.5)
        p = jax.nn.softmax(s, axis=-1).astype(xv.dtype)
        o = jnp.einsum('bhqk,bkhd->bqhd', p, xv).reshape(B, S, D)
        x = x + o @ w_xo[l]

        hf = rmsnorm(x, norm_ffn_g[l])
        x = x + jnp.square(jax.nn.relu(hf @ w_ff1[l])) @ w_ff2[l]
    return rmsnorm(x, final_norm_g)
```

```python
import numpy as np
from contextlib import ExitStack
import concourse.bass as bass
import concourse.mybir as mybir
from concourse.bass_utils import run_bass_kernel_spmd

F32 = mybir.dt.float32
BF16 = mybir.dt.bfloat16
AF = mybir.ActivationFunctionType
ALU = mybir.AluOpType

ENGS = ("pe", "act", "dve", "pool", "sp")
N_DMA_SEMS = 24

D = 1024
S = 2048
DEPTH = 2
MEM = 256
TT = 512
NTT = S // TT
OFF_A, OFF_Q, OFF_K, OFF_V, OFF_F, OFF_C, OFF_G = 0, 256, 768, 1280, 1792, 1800, 2312
N_IN = 5384
EPS = 1e-6
SLOTW = 384
NSLOT = 3
VPL = 61
NV = VPL * DEPTH + 8
C_ID, C_U, C_INV, C_SEL, NCST = 0, 128, 256, 288, 288 + 1024


class Op:
    __slots__ = ("idx", "eng", "fn", "deps", "dma", "need_sig", "sig", "prev_use", "group")

    def __init__(self, idx, eng, fn, deps, dma, group):
        self.idx = idx
        self.eng = eng
        self.fn = fn
        self.deps = deps
        self.dma = dma
        self.need_sig = dma
        self.sig = None
        self.prev_use = None
        self.group = group


class Prog:
    def __init__(self, same_engine_sync=True):
        self.ops = []
        self.last_w = {}
        self.readers = {}
        self.same_engine_sync = same_engine_sync
        self.barrier_deps = {}
        self.last_on = {}
        self.open_dma = []

    def add(self, eng, fn, reads=(), writes=(), dma=False, group=None, persistent=False):
        idx = len(self.ops)
        deps = set()
        for r in reads:
            w = self.last_w.get(r)
            if w is not None:
                deps.update(w[1])
        for r in writes:
            w = self.last_w.get(r)
            if w is not None:
                if not (group is not None and w[0] == group):
                    deps.update(w[1])
            for rd in self.readers.get(r, ()):
                deps.add(rd)
        for r in reads:
            self.readers.setdefault(r, []).append(idx)
        for r in writes:
            w = self.last_w.get(r)
            if group is not None and w is not None and w[0] == group:
                w[1].append(idx)
            else:
                self.last_w[r] = (group, [idx])
                self.readers[r] = []
        if eng in self.barrier_deps:
            deps.update(self.barrier_deps.pop(eng))
        deps.discard(idx)
        self.ops.append(Op(idx, eng, fn, deps, dma, group))
        self.last_on[eng] = idx
        if dma and not persistent:
            self.open_dma.append(idx)
        return idx

    def barrier(self):
        deps = set(self.open_dma)
        for e, i in self.last_on.items():
            if not self.ops[i].dma:
                deps.add(i)
        self.open_dma = []
        for e in ENGS:
            self.barrier_deps.setdefault(e, set()).update(deps)

    def emit(self, nc, final_wait_eng="sp"):
        ops = self.ops
        for op in ops:
            nd = set()
            for d in op.deps:
                dop = ops[d]
                if (not dop.dma) and (not op.dma) and dop.eng == op.eng:
                    if op.eng == "pe" or not self.same_engine_sync:
                        continue
                nd.add(d)
            best = {}
            keep = set()
            for d in nd:
                dop = ops[d]
                if dop.dma:
                    keep.add(d)
                elif best.get(dop.eng, -1) < d:
                    best[dop.eng] = d
            keep.update(best.values())
            op.deps = keep
            for d in keep:
                ops[d].need_sig = True
        cnt = {e: 0 for e in ENGS}
        dma_use = [0] * N_DMA_SEMS
        dma_rr = 0
        for op in ops:
            if op.dma:
                s = dma_rr
                dma_rr = (dma_rr + 1) % N_DMA_SEMS
                op.prev_use = dma_use[s]
                dma_use[s] += 1
                op.sig = (("dma", s), 16 * dma_use[s])
            elif op.need_sig:
                cnt[op.eng] += 1
                op.sig = (("eng", op.eng), cnt[op.eng])
        with ExitStack() as es:
            sems = {}
            for e in ENGS:
                sems[("eng", e)] = es.enter_context(nc.semaphore("sem_" + e))
            for i in range(N_DMA_SEMS):
                sems[("dma", i)] = es.enter_context(nc.semaphore("sem_dma%d" % i))
            block = es.enter_context(nc.Block())
            streams = {e: [op for op in ops if op.eng == e] for e in ENGS}
            final = {}
            for op in ops:
                if op.dma:
                    k, v = op.sig
                    final[k] = max(final.get(k, 0), v)

            def run_stream(eng_name, h):
                known = {}
                for op in streams[eng_name]:
                    waits = {}
                    for d in op.deps:
                        k, v = ops[d].sig
                        if waits.get(k, 0) < v:
                            waits[k] = v
                    if op.dma and op.prev_use:
                        k = op.sig[0]
                        v = 16 * op.prev_use
                        if waits.get(k, 0) < v:
                            waits[k] = v
                    for k, v in waits.items():
                        if known.get(k, 0) >= v:
                            continue
                        h.wait_ge(sems[k], v)
                        known[k] = v
                    ins = op.fn(h)
                    if op.sig is not None:
                        k, v = op.sig
                        ins.then_inc(sems[k], 16 if op.dma else 1)
                if eng_name == final_wait_eng:
                    for k, v in final.items():
                        if known.get(k, 0) < v:
                            h.wait_ge(sems[k], v)

            @block.tensor
            def _(h):
                run_stream("pe", h)

            @block.scalar
            def _(h):
                run_stream("act", h)

            @block.vector
            def _(h):
                run_stream("dve", h)

            @block.gpsimd
            def _(h):
                run_stream("pool", h)

            @block.sync
            def _(h):
                run_stream("sp", h)
        return cnt


def build(n_layers=DEPTH, taps=(), same_engine_sync=True):
    nc = bass.Bass("TRN2", target_bir_lowering=False)

    def din(name, shape):
        return nc.dram_tensor(name, list(shape), F32, kind="ExternalInput").ap()

    xT = din("xT", [8, 128, S])
    memT = din("memT", [8, 128, MEM])
    w_in = din("w_in", [DEPTH, D, N_IN])
    pool_wbd = din("pool_wbd", [DEPTH, 128, 2, 128])
    sgu_wT = din("sgu_wT", [DEPTH, 4, 128, 128])
    sgub4 = din("sgub4", [DEPTH, 2, 128, 512])
    w_ba = din("w_branch_a", [DEPTH, 256, D])
    w_bb = din("w_branch_b", [DEPTH, 512, D])
    w_bc = din("w_branch_c", [DEPTH, 256, D])
    w_out = din("w_out", [DEPTH, D, D])
    w_xq = din("w_xq", [DEPTH, D, D])
    w_xkv = din("w_xkv", [DEPTH, D, 2 * D])
    w_xo = din("w_xo", [DEPTH, D, D])
    w_ff1 = din("w_ff1", [DEPTH, D, 4 * D])
    w_ff2 = din("w_ff2", [DEPTH, 4 * D, D])
    vecs_d = din("vecs", [128, NV])
    cst_d = din("cst", [128, NCST])
    outT = nc.dram_tensor("outT", [8, 128, S], F32, kind="ExternalOutput").ap()
    tap_d = {}
    for tname in taps:
        tap_d[tname] = nc.dram_tensor("tap_" + tname, [8, 128, S], F32, kind="ExternalOutput").ap()

    P = Prog(same_engine_sync=same_engine_sync)

    with ExitStack() as outer:
        _uid = [0]

        def sb(es, name, shape, dt):
            _uid[0] += 1
            return es.enter_context(nc.sbuf_tensor("s%d_%s" % (_uid[0], name), list(shape), dt))

        ps = outer.enter_context(nc.psum_tensor("ps", [128, 8, 512], F32))
        xres = sb(outer, "xres", [128, 8, S], F32)
        A = sb(outer, "A", [128, 8, S], BF16)
        wsl = [sb(outer, "wsl%d" % i, [128, 8, SLOTW], BF16) for i in range(NSLOT)]
        vecs = sb(outer, "vecs", [128, NV], F32)
        cst = sb(outer, "cst", [128, NCST], F32)
        halfb = sb(outer, "halfb", [128, DEPTH * 24], F32)
        negbf = sb(outer, "negbf", [128, DEPTH], F32)
        ones_bf = sb(outer, "ones_bf", [128, 128], BF16)
        wf = sb(outer, "wf", [128, 8, 128], BF16)
        Tp = [sb(outer, "T%d" % i, [128, 512], F32) for i in range(3)]
        Ptp = [sb(outer, "Pt%d" % i, [128, 512], BF16) for i in range(4)]
        rstd = [sb(outer, "rstd%d" % i, [128, 512], F32) for i in range(2)]
        sqp = [sb(outer, "sq%d" % i, [128, 512], BF16) for i in range(2)]

        ident = cst[:, C_ID:C_ID + 128]
        Umat = cst[:, C_U:C_U + 128]

        def act(out, in_, func, reads, writes, **kw):
            P.add("act", lambda h: h.activation(out=out, in_=in_, func=func, **kw), reads, writes)

        def mm(out, lhsT, rhs, start, stop, reads, writes, **kw):
            P.add("pe", lambda h: h.matmul(out, lhsT=lhsT, rhs=rhs, start=start, stop=stop, **kw), reads, writes)

        def stt(out, in0, scalar, in1, op0, op1, reads, writes, **kw):
            P.add("dve", lambda h: h.scalar_tensor_tensor(out=out, in0=in0, scalar=scalar, in1=in1, op0=op0, op1=op1, **kw), reads, writes)

        def tt_(out, in0, in1, op, reads, writes, eng="dve"):
            P.add(eng, lambda h: h.tensor_tensor(out=out, in0=in0, in1=in1, op=op), reads, writes)

        def ts(out, in0, s1, s2, op0, op1, reads, writes, eng="dve"):
            P.add(eng, lambda h: h.tensor_scalar(out=out, in0=in0, scalar1=s1, scalar2=s2, op0=op0, op1=op1), reads, writes)

        def recip(out, in_, reads, writes):
            P.add("dve", lambda h: h.reciprocal(out=out, in_=in_), reads, writes)

        def memset(ap, val, writes, eng="dve"):
            P.add(eng, lambda h: h.memset(ap, val), (), writes)

        def dma(eng, out, in_, reads, writes, group=None, persistent=False):
            P.add(eng, lambda h: h.dma_start(out=out, in_=in_), reads, writes, dma=True, group=group, persistent=persistent)

        class RR:
            def __init__(self, items):
                self.items = list(items)
                self.i = 0

            def next(self):
                v = self.items[self.i % len(self.items)]
                self.i += 1
                return v

        T_rr = RR(range(3))
        Pt_rr = RR(range(4))
        sq_rr = RR(range(2))
        rstd_rr = RR(range(2))

        def wsrc(ap2d):
            return ap2d.rearrange("(kc p) n -> p kc n", p=128)

        wsched = []

        def sched_layer(l):
            wsched.append([(0, 8, 0, 256, w_in[l, :, OFF_A:OFF_A + 256])])
            wsched.append([(0, 8, 0, 256, w_in[l, :, OFF_C:OFF_C + 256])])
            wsched.append([(0, 8, 0, 256, w_in[l, :, OFF_C + 256:OFF_C + 512])])
            for hc in range(4):
                wsched.append([(0, 8, i * 128, (i + 1) * 128, w_in[l, :, off + hc * 128: off + (hc + 1) * 128])
                               for i, off in enumerate((OFF_Q, OFF_K, OFF_V))])
            for j in range(8):
                wsched.append([(0, 8, br * 128, (br + 1) * 128,
                                w_in[l, :, OFF_G + br * 1024 + j * 128: OFF_G + br * 1024 + (j + 1) * 128])
                               for br in range(3)])
                wsched.append([(0, 2, 0, 128, w_ba[l, :, j * 128:(j + 1) * 128]),
                               (2, 6, 0, 128, w_bb[l, :, j * 128:(j + 1) * 128]),
                               (6, 8, 0, 128, w_bc[l, :, j * 128:(j + 1) * 128])])
            for q in range(4):
                wsched.append([(0, 8, 0, 256, w_out[l, :, q * 256:(q + 1) * 256])])
            for q in range(8):
                wsched.append([(0, 8, 0, 256, w_xkv[l, :, q * 256:(q + 1) * 256])])
            for q in range(4):
                wsched.append([(0, 8, 0, 256, w_xq[l, :, q * 256:(q + 1) * 256])])
            for q in range(4):
                wsched.append([(0, 8, 0, 256, w_xo[l, :, q * 256:(q + 1) * 256])])
            for fg in range(4):
                for q in range(4):
                    wsched.append([(0, 8, 0, 256, w_ff1[l, :, fg * 1024 + q * 256: fg * 1024 + (q + 1) * 256])])
                for q in range(4):
                    wsched.append([(0, 8, 0, 256, w_ff2[l, fg * 1024:(fg + 1) * 1024, q * 256:(q + 1) * 256])])

        for l in range(n_layers):
            sched_layer(l)
        wstate = {"issued": 0, "next": 0}

        def w_issue_upto(n):
            while wstate["issued"] < min(n, len(wsched)):
                i = wstate["issued"]
                slot = i % NSLOT
                for (k0, k1, c0, c1, src) in wsched[i]:
                    dma("pool", wsl[slot][:, k0:k1, c0:c1], wsrc(src), [], [("ws", slot)], group=("w", i), persistent=True)
                wstate["issued"] += 1

        def w_acquire(held=0):
            i = wstate["next"]
            wstate["next"] += 1
            w_issue_upto(i - held + NSLOT)
            slot = i % NSLOT
            return wsl[slot], ("ws", slot)

        dma("sp", vecs[:], vecs_d, [], ["vecs"])
        dma("sp", cst[:], cst_d, [], ["cst"])
        for dc in range(8):
            for t in range(NTT):
                dma("sp", xres[:, dc, t * TT:(t + 1) * TT], xT[dc, :, t * TT:(t + 1) * TT], [], [("x", dc, t)])
        memset(ones_bf[:], 1.0, ["ones"])
        for l in range(DEPTH):
            ts(halfb[:, l * 24:(l + 1) * 24], vecs[:, l * VPL + 8:l * VPL + 32], 0.5, None, ALU.mult, ALU.bypass,
               ["vecs"], [("halfb", l)])
            ts(negbf[:, l:l + 1], vecs[:, l * VPL + 60:l * VPL + 61], -1.0, None, ALU.mult, ALU.bypass,
               ["vecs"], [("negbf", l)])
        w_issue_upto(NSLOT)

        def rmsnorm(src, src_res, gcol0, dst, dst_res, ntiles, tw):
            for t in range(ntiles):
                for dc in range(8):
                    qi = sq_rr.next()
                    act(sqp[qi][:, :tw], src(dc, t), AF.Square, [src_res(dc, t)], [("sq", qi)])
                    mm(ps[:, t, :tw], ones_bf[:], sqp[qi][:, :tw], dc == 0, dc == 7,
                       [("sq", qi), "ones"], [("ps", t)])
            act(ps[:, 0:ntiles, :tw], ps[:, 0:ntiles, :tw], AF.Sqrt,
                [("ps", t) for t in range(ntiles)], [("ps", t) for t in range(ntiles)],
                scale=1.0 / D, bias=EPS)
            for t in range(ntiles):
                ri = rstd_rr.next()
                recip(rstd[ri][:, :tw], ps[:, t, :tw], [("ps", t)], [("rstd", ri)])
                for dc in range(8):
                    stt(dst(dc, t), src(dc, t), vecs[:, gcol0 + dc:gcol0 + dc + 1], rstd[ri][:, :tw],
                        ALU.mult, ALU.mult, [src_res(dc, t), ("rstd", ri), "vecs"], [dst_res(dc, t)])

        def xsl(dc, t):
            return xres[:, dc, t * TT:(t + 1) * TT]

        def Asl(dc, t):
            return A[:, dc, t * TT:(t + 1) * TT]

        def xr(dc, t):
            return ("x", dc, t)

        def Ar(dc, t):
            return ("A", dc, t)

        bank_rr = RR(range(8))
        lo_rr = RR(range(4))

        def proj_fm(wt, wres, c0, rhs_fn, rhs_res_fn, nk, t, extra_reads=(), rr=None):
            b = (rr or bank_rr).next()
            for kc in range(nk):
                mm(ps[:, b, :], wt[:, kc, c0:c0 + 128], rhs_fn(kc, t), kc == 0, kc == nk - 1,
                   [wres, rhs_res_fn(kc, t)] + list(extra_reads), [("ps", b)])
            return b

        def gelu_from_psum(src_ap, src_res, dst_ap, dst_res, width):
            t1 = T_rr.next()
            t2 = T_rr.next()
            T1 = Tp[t1][:, :width]
            T2 = Tp[t2][:, :width]
            act(T1, src_ap, AF.Identity, [src_res], [("T", t1)], scale=0.5)
            act(T2, src_ap, AF.Square, [src_res], [("T", t2)])
            ts(T2, T2, 0.044715, 1.0, ALU.mult, ALU.add, [("T", t2)], [("T", t2)])
            tt_(T2, T2, T1, ALU.mult, [("T", t1), ("T", t2)], [("T", t2)])
            act(T2, T2, AF.Tanh, [("T", t2)], [("T", t2)], scale=1.5957691216057308)
            stt(dst_ap, T2, 1.0, T1, ALU.add, ALU.mult, [("T", t1), ("T", t2)], [dst_res])

        def tap(name):
            if name in tap_d:
                for dc in range(8):
                    dma("sp", tap_d[name][dc], xres[:, dc, :], [("x", dc, t) for t in range(NTT)], [])

        for l in range(n_layers):
            vb = l * VPL
            rmsnorm(xsl, xr, vb + 0, Asl, Ar, NTT, TT)

            with ExitStack() as mix:
                pa = sb(mix, "pa", [128, 2, S], BF16)
                ao = sb(mix, "ao", [128, 4, S], BF16)
                sc = sb(mix, "sc", [128, 2, S], BF16)

                with ExitStack() as sub:
                    aT = sb(sub, "aT", [128, 16 + S], F32)
                    sA = sb(sub, "sA", [128, 16 + S], F32)
                    sB = sb(sub, "sB", [128, 16 + S], F32)
                    dT = sb(sub, "dT", [128, S], BF16)
                    wbd = sb(sub, "wbd", [128, 2, 128], BF16)
                    t16 = sb(sub, "t16", [128, 16], F32)
                    dma("pool", wbd[:], pool_wbd[l], [], ["wbd"])
                    for bufn, buf in (("aT", aT), ("sA", sA), ("sB", sB)):
                        memset(buf[:, 0:16], 0.0, [(bufn, "pad")])
                    wa, wa_res = w_acquire()
                    for fc in range(2):
                        for t in range(NTT):
                            b = proj_fm(wa, wa_res, fc * 128, Asl, Ar, 8, t)
                            act(aT[:, 16 + t * TT:16 + (t + 1) * TT], ps[:, b, :], AF.Identity,
                                [("ps", b)], [("aT", t)])
                        allr = [("aT", t) for t in range(NTT)] + [("aT", "pad")]
                        a_ = aT[:, 16:16 + S]
                        tt_(sA[:, 16:], a_, aT[:, 15:15 + S], ALU.add, allr + [("sA", "pad")], ["sA"])
                        if fc == 0:
                            lo, lo_res, lo_w = sA, "sA", 2
                            tt_(sB[:, 16:], sA[:, 16:], sA[:, 14:14 + S], ALU.add, ["sA", ("sA", "pad"), ("sB", "pad")], ["sB"])
                            hi, hi_res, hi_w = sB, "sB", 4
                        else:
                            tt_(sB[:, 16:], sA[:, 16:], sA[:, 14:14 + S], ALU.add, ["sA", ("sA", "pad"), ("sB", "pad")], ["sB"])
                            tt_(sA[:, 16:], sB[:, 16:], sB[:, 12:12 + S], ALU.add, ["sB", ("sB", "pad"), ("sA", "pad")], ["sA"])
                            lo, lo_res, lo_w = sA, "sA", 8
                            tt_(sB[:, 16:], sA[:, 16:], sA[:, 8:8 + S], ALU.add, ["sA", ("sA", "pad"), ("sB", "pad")], ["sB"])
                            hi, hi_res, hi_w = sB, "sB", 16
                        for (p0, buf, bres, win, hidx) in ((0, lo, lo_res, lo_w, 0), (64, hi, hi_res, hi_w, 1)):
                            stt(dT[p0:p0 + 64, :], buf[p0:p0 + 64, 16:], 1.0 / win, aT[p0:p0 + 64, 16:],
                                ALU.mult, ALU.subtract, allr + [bres], [("dT", hidx)])
                            tt_(t16[p0:p0 + 64, :], buf[p0:p0 + 64, 16:32],
                                cst[p0:p0 + 64, C_INV + fc * 16:C_INV + (fc + 1) * 16], ALU.mult,
                                [bres, "cst"], [("t16", hidx)])
                            tt_(dT[p0:p0 + 64, 0:16], t16[p0:p0 + 64, :], aT[p0:p0 + 64, 16:32], ALU.subtract,
                                allr + [("t16", hidx), ("dT", hidx)], [("dT", hidx)])
                        for t in range(NTT):
                            b = bank_rr.next()
                            mm(ps[:, b, :], wbd[:, fc, :], dT[:, t * TT:(t + 1) * TT], True, True,
                               [("dT", 0), ("dT", 1), "wbd"], [("ps", b)])
                            act(pa[:, fc, t * TT:(t + 1) * TT], ps[:, b, :], AF.Identity, [("ps", b), "vecs"],
                                [("pa", fc, t)], scale=vecs[:, vb + 32 + fc:vb + 33 + fc])
                P.barrier()

                with ExitStack() as sub:
                    uT = sb(sub, "uT", [128, 2, S], BF16)
                    gvb = sb(sub, "gvb", [128, 16, 256], BF16)
                    ss = sb(sub, "ss", [128, 16], F32)
                    rs16 = sb(sub, "rs16", [128, 16], F32)
                    gv = sb(sub, "gv", [128, 256], F32)
                    junk = sb(sub, "junk", [128, 256], F32)
                    wTs = sb(sub, "wTs", [128, 4, 128], F32)
                    wcm = sb(sub, "wcm", [128, 4, 128], BF16)
                    bT4 = sb(sub, "bT4", [128, 2, 512], F32)
                    for g in range(4):
                        dma("sp", wTs[:, g, :], sgu_wT[l, g], [], [("wTs", g)])
                        tt_(wcm[:, g, :], wTs[:, g, :], Umat, ALU.mult, [("wTs", g), "cst"], [("wcm", g)])
                    for fc in range(2):
                        dma("sp", bT4[:, fc, :], sgub4[l, fc], [], [("bT4", fc)])
                    wcu, wcu_res = w_acquire()
                    for fc in range(2):
                        for t in range(NTT):
                            b = proj_fm(wcu, wcu_res, fc * 128, Asl, Ar, 8, t)
                            gelu_from_psum(ps[:, b, :], ("ps", b), uT[:, fc, t * TT:(t + 1) * TT], ("uT", fc, t), 512)
                    wcv, wcv_res = w_acquire()
                    for c in range(16):
                        b = bank_rr.next()
                        for dc in range(8):
                            mm(ps[:, b, 0:256], A[:, dc, c * 128:(c + 1) * 128], wcv[:, dc, 0:256], dc == 0, dc == 7,
                               [wcv_res, Ar(dc, c // 4)], [("ps", b)])
                        gelu_from_psum(ps[:, b, 0:256], ("ps", b), gv[:], "gv", 256)
                        stt(junk[:], gv[:], 1.0, gv[:], ALU.mult, ALU.mult, ["gv"], ["junk", ("ss", c)],
                            accum_out=ss[:, c:c + 1])
                        act(gvb[:, c, :], gv[:], AF.Identity, ["gv"], [("gvb", c)])
                    ssr = [("ss", c) for c in range(16)]
                    act(rs16[:], ss[:], AF.Sqrt, ssr, ["rs16"], scale=1.0 / 256, bias=EPS)
                    recip(rs16[:], rs16[:], ["rs16"], ["rs16"])
                    for c in range(16):
                        ts(gvb[:, c, :], gvb[:, c, :], rs16[:, c:c + 1], None, ALU.mult, ALU.bypass,
                           [("gvb", c), "rs16"], [("gvb", c)])
                    for fc in range(2):
                        for t in range(NTT):
                            b = bank_rr.next()
                            for cc in range(4):
                                c = t * 4 + cc
                                for gi in range(2):
                                    g = 2 * fc + gi
                                    mm(ps[gi * 64:(gi + 1) * 64, b, cc * 128:(cc + 1) * 128],
                                       gvb[:, c, g * 64:(g + 1) * 64], wcm[:, g, :], True, True,
                                       [("gvb", c), ("wcm", g)], [("ps", b)], tile_position=(0, gi * 64))
                            ti = T_rr.next()
                            stt(Tp[ti][:], ps[:, b, :], vecs[:, vb + 34 + fc:vb + 35 + fc], bT4[:, fc, :],
                                ALU.mult, ALU.add, [("ps", b), "vecs", ("bT4", fc)], [("T", ti)])
                            tt_(sc[:, fc, t * TT:(t + 1) * TT], Tp[ti][:], uT[:, fc, t * TT:(t + 1) * TT], ALU.mult,
                                [("T", ti), ("uT", fc, t)], [("sc", fc, t)])
                P.barrier()

                with ExitStack() as sub:
                    EF = sb(sub, "EF", [128, 4, 512], F32)
                    E2 = sb(sub, "E2", [8, S], F32)
                    nFt = sb(sub, "nFt", [128, 128], F32)
                    qT = sb(sub, "qT", [128, S], BF16)
                    kT = sb(sub, "kT", [128, S], BF16)
                    vtk = sb(sub, "vtk", [128, 16, 128], BF16)
                    dma("pool", wf[:], wsrc(w_in[l, :, OFF_F:OFF_F + 128]), [], ["wf"])
                    for t in range(NTT):
                        for dc in range(8):
                            mm(ps[0:8, t, :], wf[:, dc, 0:8], Asl(dc, t), dc == 0, dc == 7, ["wf", Ar(dc, t)], [("ps", t)])
                    psr4 = [("ps", t) for t in range(4)]
                    E1 = EF[0:8, :, :]
                    act(E1, ps[0:8, 0:4, :], AF.Exp, psr4 + [("negbf", l)], ["EF"], scale=-1.0, bias=negbf[0:8, l:l + 1])
                    act(E1, E1, AF.Ln, ["EF"], ["EF"], bias=1.0)
                    E1f = EF[0:8, :, :].rearrange("p a b -> p (a b)")
                    P.add("dve", lambda h, o=E2[:], d=E1f: h.tensor_tensor_scan(out=o, data0=d, data1=d, initial=0.0,
                                                                                 op0=ALU.add, op1=ALU.max), ["EF"], ["E2"])
                    bT = lo_rr.next()
                    for j in range(16):
                        P.add("pe", lambda h, o=ps[:, bT, j * 8:(j + 1) * 8], i_=E2[0:8, j * 128:(j + 1) * 128],
                              idn=cst[0:8, C_ID:C_ID + 8]: h.transpose(o, i_, idn), ["E2", "cst"], [("ps", bT)])
                    act(nFt[:], ps[:, bT, 0:128], AF.Identity, [("ps", bT)], ["nFt"])

                    for hc in range(4):
                        wq, wq_res = w_acquire()
                        for t in range(NTT):
                            b = proj_fm(wq, wq_res, 0, Asl, Ar, 8, t, rr=lo_rr)
                            act(qT[:, t * TT:(t + 1) * TT], ps[:, b, :], AF.Identity, [("ps", b)], [("qT", t)], scale=0.125)
                            b = proj_fm(wq, wq_res, 128, Asl, Ar, 8, t, rr=lo_rr)
                            act(kT[:, t * TT:(t + 1) * TT], ps[:, b, :], AF.Identity, [("ps", b)], [("kT", t)])
                        for t in range(NTT):
                            b = lo_rr.next()
                            for cc in range(4):
                                c = 4 * t + cc
                                for dc in range(8):
                                    mm(ps[:, b, cc * 128:(cc + 1) * 128], A[:, dc, c * 128:(c + 1) * 128],
                                       wq[:, dc, 256:384], dc == 0, dc == 7, [wq_res, Ar(dc, t)], [("ps", b)])
                            P.add("dve", lambda h, o=vtk[:, 4 * t:4 * t + 4, :],
                                  i_=ps[:, b, :].rearrange("p (a c) -> p a c", c=128): h.tensor_copy(o, i_),
                                  [("ps", b)], [("vtk", t)])
                        for hi in range(2):
                            hh = 2 * hc + hi
                            p0 = hi * 64
                            for t in range(NTT):
                                b = lo_rr.next()
                                mm(ps[:, b, :], cst[0:8, C_SEL + hh * 128:C_SEL + (hh + 1) * 128],
                                   E2[0:8, t * TT:(t + 1) * TT], True, True, ["E2", "cst"], [("ps", b)])
                                act(EF[:, t, :], ps[:, b, :], AF.Identity, [("ps", b)], [("Fq", t), "EF"])
                            for Qi in range(NTT):
                                nb, dbk = 4 + (Qi % 2) * 2, 5 + (Qi % 2) * 2
                                nk = 4 * Qi + 4
                                steps = list(range(nk))
                                LA = 2
                                pend = []
                                for i in range(nk + LA):
                                    if i < nk:
                                        kj = steps[i]
                                        n0 = max(0, kj - 4 * Qi) * 128
                                        N = TT - n0
                                        sbk = i % 4
                                        q0 = Qi * TT + n0
                                        mm(ps[:, sbk, 0:N], kT[p0:p0 + 64, kj * 128:(kj + 1) * 128],
                                           qT[p0:p0 + 64, q0:q0 + N], True, True,
                                           [("kT", kj // 4), ("qT", Qi)], [("ps", sbk)],
                                           **({"tile_position": (64, 0)} if hi == 1 else {}))
                                        ti = T_rr.next()
                                        tt_(Tp[ti][:, 0:N], ps[:, sbk, 0:N], EF[:, Qi, n0:TT], ALU.add,
                                            [("ps", sbk), ("Fq", Qi)], [("T", ti)])
                                        pi = Pt_rr.next()
                                        act(Ptp[pi][:, 0:N], Tp[ti][:, 0:N], AF.Exp, [("T", ti), "nFt"], [("Pt", pi)],
                                            bias=nFt[:, kj * 8 + hh:kj * 8 + hh + 1])
                                        if kj >= 4 * Qi:
                                            tt_(Ptp[pi][:, 0:128], Ptp[pi][:, 0:128], Umat, ALU.mult,
                                                [("Pt", pi), "cst"], [("Pt", pi)])
                                        pend.append((kj, n0, N, pi))
                                    if i >= LA:
                                        kj, n0, N, pi = pend[i - LA]
                                        first, last = (kj == 0), (kj == nk - 1)
                                        mm(ps[p0:p0 + 64, nb, n0:TT], vtk[:, kj, p0:p0 + 64], Ptp[pi][:, 0:N], first, last,
                                           [("vtk", kj // 4), ("Pt", pi)], [("ps", nb, hi)], tile_position=(0, p0))
                                        mm(ps[p0:p0 + 64, dbk, n0:TT], ones_bf[:, 0:64], Ptp[pi][:, 0:N], first, last,
                                           ["ones", ("Pt", pi)], [("ps", dbk, hi)], tile_position=(0, p0))
                                ri = rstd_rr.next()
                                recip(rstd[ri][p0:p0 + 64, :], ps[p0:p0 + 64, dbk, :], [("ps", dbk, hi)], [("rstd", ri)])
                                tt_(ao[p0:p0 + 64, hc, Qi * TT:(Qi + 1) * TT], ps[p0:p0 + 64, nb, :], rstd[ri][p0:p0 + 64, :],
                                    ALU.mult, [("ps", nb, hi), ("rstd", ri)], [("ao", hc, Qi, hi)])
                P.barrier()
                for nm, buf, nch in (("pa", pa, 2), ("ao", ao, 4), ("sc", sc, 2)):
                    if l == 0 and nm in tap_d:
                        for kc in range(nch):
                            dma("pool", tap_d[nm][kc], buf[:, kc, :], [], [])
                P.barrier()

                with ExitStack() as sub:
                    mg = sb(sub, "mg", [128, 8, S], BF16)
                    for j in range(8):
                        wg, wg_res = w_acquire()
                        wb, wb_res = w_acquire(held=1)
                        for t in range(NTT):
                            gb = []
                            for br in range(3):
                                b = proj_fm(wg, wg_res, br * 128, Asl, Ar, 8, t)
                                gb.append(b)
                            tl = []
                            for br in range(3):
                                ti = T_rr.next()
                                tl.append(ti)
                                act(Tp[ti][:], ps[:, gb[br], :], AF.Tanh, [("ps", gb[br]), ("halfb", l)], [("T", ti)],
                                    scale=0.5, bias=halfb[:, l * 24 + br * 8 + j:l * 24 + br * 8 + j + 1])
                            srcs = [(pa, 0, 2, lambda kc, t_: ("pa", kc, t_)),
                                    (ao, 2, 4, None),
                                    (sc, 6, 2, lambda kc, t_: ("sc", kc, t_))]
                            for br, (buf, k0, nk, rf) in enumerate(srcs):
                                b = bank_rr.next()
                                for kc in range(nk):
                                    rr_ = [("ao", kc, t, 0), ("ao", kc, t, 1)] if rf is None else [rf(kc, t)]
                                    mm(ps[:, b, :], wb[:, k0 + kc, 0:128], buf[:, kc, t * TT:(t + 1) * TT], kc == 0, kc == nk - 1,
                                       [wb_res] + rr_, [("ps", b)])
                                ti = tl[br]
                                stt(Tp[ti][:], Tp[ti][:], 1.0, ps[:, b, :], ALU.add, ALU.mult, [("T", ti), ("ps", b)], [("T", ti)])
                            tt_(Tp[tl[0]][:], Tp[tl[0]][:], Tp[tl[1]][:], ALU.add, [("T", tl[0]), ("T", tl[1])], [("T", tl[0])])
                            tt_(mg[:, j, t * TT:(t + 1) * TT], Tp[tl[0]][:], Tp[tl[2]][:], ALU.add,
                                [("T", tl[0]), ("T", tl[2])], [("mg", j, t)])
                    for q in range(4):
                        wo, wo_res = w_acquire()
                        for jj in range(2):
                            j = 2 * q + jj
                            for t in range(NTT):
                                b = proj_fm(wo, wo_res, jj * 128, lambda kc, t_: mg[:, kc, t_ * TT:(t_ + 1) * TT],
                                            lambda kc, t_: ("mg", kc, t_), 8, t)
                                stt(xsl(j, t), ps[:, b, :], 0.5, xsl(j, t), ALU.mult, ALU.add, [("ps", b), xr(j, t)], [xr(j, t)])
            P.barrier()
            if l == 0:
                tap("mix")

            rmsnorm(xsl, xr, vb + 36, Asl, Ar, NTT, TT)
            with ExitStack() as xa:
                xq = sb(xa, "xq", [128, 8, S], BF16)
                hmT = sb(xa, "hmT", [128, 8, MEM], BF16)
                xkT = sb(xa, "xkT", [128, 8, MEM], BF16)
                xv = sb(xa, "xv", [128, 2, D], BF16)
                with ExitStack() as sub:
                    mT = sb(sub, "mT", [128, 8, MEM], F32)
                    for dc in range(8):
                        dma("sp", mT[:, dc, :], memT[dc], [], [("mT", dc)])
                    rmsnorm(lambda dc, t: mT[:, dc, :], lambda dc, t: ("mT", dc), vb + 44,
                            lambda dc, t: hmT[:, dc, :], lambda dc, t: ("hmT", dc), 1, MEM)
                    P.barrier()
                for q in range(4):
                    wk, wk_res = w_acquire()
                    for jj in range(2):
                        j = 2 * q + jj
                        b = bank_rr.next()
                        for dc in range(8):
                            mm(ps[:, b, 0:MEM], wk[:, dc, jj * 128:(jj + 1) * 128], hmT[:, dc, :], dc == 0, dc == 7,
                               [wk_res, ("hmT", dc)], [("ps", b)])
                        act(xkT[:, j, :], ps[:, b, 0:MEM], AF.Identity, [("ps", b)], [("xkT", j)])
                for q in range(4):
                    wv_, wv_res = w_acquire()
                    for mc in range(2):
                        b = bank_rr.next()
                        for dc in range(8):
                            mm(ps[:, b, 0:256], hmT[:, dc, mc * 128:(mc + 1) * 128], wv_[:, dc, 0:256], dc == 0, dc == 7,
                               [wv_res, ("hmT", dc)], [("ps", b)])
                        act(xv[:, mc, q * 256:(q + 1) * 256], ps[:, b, 0:256], AF.Identity, [("ps", b)], [("xv", mc, q)])
                for q in range(4):
                    wq_, wq_res = w_acquire()
                    for jj in range(2):
                        j = 2 * q + jj
                        for t in range(NTT):
                            b = proj_fm(wq_, wq_res, jj * 128, Asl, Ar, 8, t)
                            act(xq[:, j, t * TT:(t + 1) * TT], ps[:, b, :], AF.Identity, [("ps", b)], [("xq", j, t)],
                                scale=1.0 / 16)
                for t in range(NTT):
                    for hh in range(4):
                        pts = []
                        for mc in range(2):
                            b = bank_rr.next()
                            for dk in range(2):
                                mm(ps[:, b, :], xkT[:, 2 * hh + dk, mc * 128:(mc + 1) * 128],
                                   xq[:, 2 * hh + dk, t * TT:(t + 1) * TT], dk == 0, dk == 1,
                                   [("xkT", 2 * hh + dk), ("xq", 2 * hh + dk, t)], [("ps", b)])
                            pi = Pt_rr.next()
                            act(Ptp[pi][:], ps[:, b, :], AF.Exp, [("ps", b)], [("Pt", pi)])
                            pts.append(pi)
                        bd = bank_rr.next()
                        for mc in range(2):
                            mm(ps[:, bd, :], ones_bf[:], Ptp[pts[mc]][:], mc == 0, mc == 1,
                               ["ones", ("Pt", pts[mc])], [("ps", bd)])
                        ri = rstd_rr.next()
                        recip(rstd[ri][:], ps[:, bd, :], [("ps", bd)], [("rstd", ri)])
                        for dch in range(2):
                            b = bank_rr.next()
                            for mc in range(2):
                                mm(ps[:, b, :], xv[:, mc, hh * 256 + dch * 128: hh * 256 + (dch + 1) * 128], Ptp[pts[mc]][:],
                                   mc == 0, mc == 1, [("xv", mc, hh), ("Pt", pts[mc])], [("ps", b)])
                            tt_(xq[:, 2 * hh + dch, t * TT:(t + 1) * TT], ps[:, b, :], rstd[ri][:], ALU.mult,
                                [("ps", b), ("rstd", ri)], [("xq", 2 * hh + dch, t)])
                for q in range(4):
                    wo, wo_res = w_acquire()
                    for jj in range(2):
                        j = 2 * q + jj
                        for t in range(NTT):
                            b = proj_fm(wo, wo_res, jj * 128, lambda kc, t_: xq[:, kc, t_ * TT:(t_ + 1) * TT],
                                        lambda kc, t_: ("xq", kc, t_), 8, t)
                            tt_(xsl(j, t), ps[:, b, :], xsl(j, t), ALU.add, [("ps", b), xr(j, t)], [xr(j, t)])
            P.barrier()
            if l == 0:
                tap("xat")

            rmsnorm(xsl, xr, vb + 52, Asl, Ar, NTT, TT)
            with ExitStack() as ff:
                hid = sb(ff, "hid", [128, 8, S], BF16)
                for fg in range(4):
                    for q in range(4):
                        w1, w1_res = w_acquire()
                        for jj in range(2):
                            fcl = 2 * q + jj
                            for t in range(NTT):
                                b = proj_fm(w1, w1_res, jj * 128, Asl, Ar, 8, t)
                                ti = T_rr.next()
                                act(Tp[ti][:], ps[:, b, :], AF.Relu, [("ps", b)], [("T", ti)])
                                tt_(hid[:, fcl, t * TT:(t + 1) * TT], Tp[ti][:], Tp[ti][:], ALU.mult,
                                    [("T", ti)], [("hid", fcl, t)])
                    for q in range(4):
                        w2, w2_res = w_acquire()
                        for jj in range(2):
                            j = 2 * q + jj
                            for t in range(NTT):
                                b = proj_fm(w2, w2_res, jj * 128, lambda kc, t_: hid[:, kc, t_ * TT:(t_ + 1) * TT],
                                            lambda kc, t_: ("hid", kc, t_), 8, t)
                                tt_(xsl(j, t), ps[:, b, :], xsl(j, t), ALU.add, [("ps", b), xr(j, t)], [xr(j, t)])
            P.barrier()
            if l == 0:
                tap("ffn")

        rmsnorm(xsl, xr, VPL * DEPTH, xsl, xr, NTT, TT)
        for dc in range(8):
            dma("sp", outT[dc], xres[:, dc, :], [xr(dc, t) for t in range(NTT)], [])
        cnt = P.emit(nc)
        nc._cnt = cnt
        nc._nops = len(P.ops)
    return nc


def host_consts():
    cst = np.zeros((128, NCST), np.float32)
    cst[:, C_ID:C_ID + 128] = np.eye(128, dtype=np.float32)
    k = np.arange(128)
    cst[:, C_U:C_U + 128] = (k[:, None] <= k[None, :]).astype(np.float32)
    wins = {(0, 0): 2, (0, 1): 4, (1, 0): 8, (1, 1): 16}
    t = np.arange(16)
    for fc in range(2):
        for half in range(2):
            w = wins[(fc, half)]
            cst[half * 64:(half + 1) * 64, C_INV + fc * 16:C_INV + (fc + 1) * 16] = \
                (1.0 / np.minimum(t + 1, w)).astype(np.float32)[None, :]
    for h in range(8):
        cst[h, C_SEL + h * 128:C_SEL + (h + 1) * 128] = -1.0
    return cst


def pack_vecs(inp):
    v = np.zeros((128, NV), np.float32)

    def fm(a):
        a = np.asarray(a, np.float32)
        return a.reshape(-1, 128).T

    for l in range(DEPTH):
        b = l * VPL
        v[:, b + 0:b + 8] = fm(inp["norm_mix_g"][l])
        v[:, b + 8:b + 32] = fm(inp["b_gate"][l])
        v[:, b + 32:b + 34] = fm(inp["pool_scale"][l])
        v[:, b + 34:b + 36] = fm(inp["sgu_norm_g"][l])
        v[:, b + 36:b + 44] = fm(inp["norm_xattn_g"][l])
        v[:, b + 44:b + 52] = fm(inp["norm_mem_g"][l])
        v[:, b + 52:b + 60] = fm(inp["norm_ffn_g"][l])
        v[0:8, b + 60] = np.asarray(inp["b_forget"][l], np.float32)
    v[:, VPL * DEPTH:VPL * DEPTH + 8] = fm(inp["final_norm_g"])
    return v


def shared_inputs(inp):
    f = lambda k: np.ascontiguousarray(np.asarray(inp[k], np.float32))
    sgu_b = np.asarray(inp["sgu_b"], np.float32)
    sgub4 = np.zeros((DEPTH, 2, 128, 512), np.float32)
    for fc in range(2):
        for gi in range(2):
            sgub4[:, fc, gi * 64:(gi + 1) * 64, :] = np.tile(sgu_b[:, 2 * fc + gi, :], (1, 4))[:, None, :]
    pw = np.asarray(inp["pool_w"], np.float32)
    pool_wbd = np.zeros((DEPTH, 128, 2, 128), np.float32)
    for g in range(4):
        gi = g % 2
        pool_wbd[:, gi * 64:(gi + 1) * 64, g // 2, gi * 64:(gi + 1) * 64] = pw[:, g]
    return {
        "w_in": f("w_in"), "pool_wbd": pool_wbd,
        "sgu_wT": np.ascontiguousarray(np.asarray(inp["sgu_w"], np.float32).transpose(0, 1, 3, 2)),
        "sgub4": sgub4,
        "w_branch_a": f("w_branch_a"), "w_branch_b": f("w_branch_b"), "w_branch_c": f("w_branch_c"),
        "w_out": f("w_out"), "w_xq": f("w_xq"), "w_xkv": f("w_xkv"), "w_xo": f("w_xo"),
        "w_ff1": f("w_ff1"), "w_ff2": f("w_ff2"),
        "vecs": pack_vecs(inp), "cst": host_consts(),
    }


def core_inputs(inp, b):
    x = np.asarray(inp["x"], np.float32)[b]
    m = np.asarray(inp["mem"], np.float32)[b]
    return {
        "xT": np.ascontiguousarray(x.T).reshape(8, 128, S),
        "memT": np.ascontiguousarray(m.T).reshape(8, 128, MEM),
    }


_NC_CACHE = {}


def kernel(**inputs):
    if "nc" not in _NC_CACHE:
        _NC_CACHE["nc"] = build()
    nc = _NC_CACHE["nc"]
    shared = shared_inputs(inputs)
    in_maps = []
    for b in range(8):
        m = dict(shared)
        m.update(core_inputs(inputs, b))
        in_maps.append(m)
    res = run_bass_kernel_spmd(nc, in_maps, core_ids=list(range(8)))
    out = np.empty((8, S, D), np.float32)
    for b in range(8):
        out[b] = res.results[b]["outT"].reshape(D, S).T
    return out
```

```python
import numpy as np
from contextlib import ExitStack
import concourse.bass as bass
import concourse.mybir as mybir
from concourse.bass_utils import run_bass_kernel_spmd

F32 = mybir.dt.float32
BF16 = mybir.dt.bfloat16
AF = mybir.ActivationFunctionType
ALU = mybir.AluOpType

ENGS = ("pe", "act", "dve", "pool", "sp")
N_DMA_SEMS = 24

D = 1024
S = 2048
DEPTH = 2
MEM = 256
TT = 512
NTT = S // TT
OFF_A, OFF_Q, OFF_K, OFF_V, OFF_F, OFF_C, OFF_G = 0, 256, 768, 1280, 1792, 1800, 2312
N_IN = 5384
EPS = 1e-6
SLOTW = 384
NSLOT = 3
VPL = 61
NV = VPL * DEPTH + 8
C_ID, C_U, C_INV, C_SEL, C_SWAP, C_NEG, NCST = 0, 128, 256, 288, 1312, 1440, 1568


class Op:
    __slots__ = ("idx", "eng", "fn", "deps", "dma", "need_sig", "sig", "prev_use", "group")

    def __init__(self, idx, eng, fn, deps, dma, group):
        self.idx = idx
        self.eng = eng
        self.fn = fn
        self.deps = deps
        self.dma = dma
        self.need_sig = dma
        self.sig = None
        self.prev_use = None
        self.group = group


class Prog:
    def __init__(self, same_engine_sync=True):
        self.ops = []
        self.last_w = {}
        self.readers = {}
        self.same_engine_sync = same_engine_sync
        self.barrier_deps = {}
        self.last_on = {}
        self.open_dma = []

    def add(self, eng, fn, reads=(), writes=(), dma=False, group=None, persistent=False):
        idx = len(self.ops)
        deps = set()
        for r in reads:
            w = self.last_w.get(r)
            if w is not None:
                deps.update(w[1])
        for r in writes:
            w = self.last_w.get(r)
            if w is not None:
                if not (group is not None and w[0] == group):
                    deps.update(w[1])
            for rd in self.readers.get(r, ()):
                deps.add(rd)
        for r in reads:
            self.readers.setdefault(r, []).append(idx)
        for r in writes:
            w = self.last_w.get(r)
            if group is not None and w is not None and w[0] == group:
                w[1].append(idx)
            else:
                self.last_w[r] = (group, [idx])
                self.readers[r] = []
        if eng in self.barrier_deps:
            deps.update(self.barrier_deps.pop(eng))
        deps.discard(idx)
        self.ops.append(Op(idx, eng, fn, deps, dma, group))
        self.last_on[eng] = idx
        if dma and not persistent:
            self.open_dma.append(idx)
        return idx

    def barrier(self):
        deps = set(self.open_dma)
        for e, i in self.last_on.items():
            if not self.ops[i].dma:
                deps.add(i)
        self.open_dma = []
        for e in ENGS:
            self.barrier_deps.setdefault(e, set()).update(deps)

    def emit(self, nc, final_wait_eng="sp"):
        ops = self.ops
        for op in ops:
            nd = set()
            for d in op.deps:
                dop = ops[d]
                if (not dop.dma) and (not op.dma) and dop.eng == op.eng:
                    if op.eng == "pe" or not self.same_engine_sync:
                        continue
                nd.add(d)
            best = {}
            keep = set()
            for d in nd:
                dop = ops[d]
                if dop.dma:
                    keep.add(d)
                elif best.get(dop.eng, -1) < d:
                    best[dop.eng] = d
            keep.update(best.values())
            op.deps = keep
            for d in keep:
                ops[d].need_sig = True
        cnt = {e: 0 for e in ENGS}
        dma_use = [0] * N_DMA_SEMS
        dma_rr = 0
        for op in ops:
            if op.dma:
                s = dma_rr
                dma_rr = (dma_rr + 1) % N_DMA_SEMS
                op.prev_use = dma_use[s]
                dma_use[s] += 1
                op.sig = (("dma", s), 16 * dma_use[s])
            elif op.need_sig:
                cnt[op.eng] += 1
                op.sig = (("eng", op.eng), cnt[op.eng])
        with ExitStack() as es:
            sems = {}
            for e in ENGS:
                sems[("eng", e)] = es.enter_context(nc.semaphore("sem_" + e))
            for i in range(N_DMA_SEMS):
                sems[("dma", i)] = es.enter_context(nc.semaphore("sem_dma%d" % i))
            block = es.enter_context(nc.Block())
            streams = {e: [op for op in ops if op.eng == e] for e in ENGS}
            final = {}
            for op in ops:
                if op.dma:
                    k, v = op.sig
                    final[k] = max(final.get(k, 0), v)

            def run_stream(eng_name, h):
                known = {}
                for op in streams[eng_name]:
                    waits = {}
                    for d in op.deps:
                        k, v = ops[d].sig
                        if waits.get(k, 0) < v:
                            waits[k] = v
                    if op.dma and op.prev_use:
                        k = op.sig[0]
                        v = 16 * op.prev_use
                        if waits.get(k, 0) < v:
                            waits[k] = v
                    for k, v in waits.items():
                        if known.get(k, 0) >= v:
                            continue
                        h.wait_ge(sems[k], v)
                        known[k] = v
                    ins = op.fn(h)
                    if op.sig is not None:
                        k, v = op.sig
                        ins.then_inc(sems[k], 16 if op.dma else 1)
                if eng_name == final_wait_eng:
                    for k, v in final.items():
                        if known.get(k, 0) < v:
                            h.wait_ge(sems[k], v)

            @block.tensor
            def _(h):
                run_stream("pe", h)

            @block.scalar
            def _(h):
                run_stream("act", h)

            @block.vector
            def _(h):
                run_stream("dve", h)

            @block.gpsimd
            def _(h):
                run_stream("pool", h)

            @block.sync
            def _(h):
                run_stream("sp", h)
        return cnt


def build(n_layers=DEPTH, taps=(), same_engine_sync=True):
    nc = bass.Bass("TRN2", target_bir_lowering=False)

    def din(name, shape):
        return nc.dram_tensor(name, list(shape), F32, kind="ExternalInput").ap()

    xT = din("xT", [8, 128, S])
    memT = din("memT", [8, 128, MEM])
    w_in = din("w_in", [DEPTH, D, N_IN])
    pool_wbd = din("pool_wbd", [DEPTH, 128, 2, 128])
    sgu_wT = din("sgu_wT", [DEPTH, 4, 128, 128])
    sgub4 = din("sgub4", [DEPTH, 2, 128, 512])
    w_ba = din("w_branch_a", [DEPTH, 256, D])
    w_bb = din("w_branch_b", [DEPTH, 512, D])
    w_bc = din("w_branch_c", [DEPTH, 256, D])
    w_out = din("w_out", [DEPTH, D, D])
    w_xq = din("w_xq", [DEPTH, D, D])
    w_xkv = din("w_xkv", [DEPTH, D, 2 * D])
    w_xo = din("w_xo", [DEPTH, D, D])
    w_ff1 = din("w_ff1", [DEPTH, D, 4 * D])
    w_ff2 = din("w_ff2", [DEPTH, 4 * D, D])
    vecs_d = din("vecs", [128, NV])
    cst_d = din("cst", [128, NCST])
    outT = nc.dram_tensor("outT", [8, 128, S], F32, kind="ExternalOutput").ap()
    tap_d = {}
    for tname in taps:
        tap_d[tname] = nc.dram_tensor("tap_" + tname, [8, 128, S], F32, kind="ExternalOutput").ap()

    P = Prog(same_engine_sync=same_engine_sync)

    with ExitStack() as outer:
        _uid = [0]

        def sb(es, name, shape, dt):
            _uid[0] += 1
            return es.enter_context(nc.sbuf_tensor("s%d_%s" % (_uid[0], name), list(shape), dt))

        ps = outer.enter_context(nc.psum_tensor("ps", [128, 8, 512], F32))
        xres = sb(outer, "xres", [128, 8, S], F32)
        A = sb(outer, "A", [128, 8, S], BF16)
        wsl = [sb(outer, "wsl%d" % i, [128, 8, SLOTW], BF16) for i in range(NSLOT)]
        vecs = sb(outer, "vecs", [128, NV], F32)
        cst = sb(outer, "cst", [128, NCST], F32)
        halfb = sb(outer, "halfb", [128, DEPTH * 24], F32)
        negbf = sb(outer, "negbf", [128, DEPTH], F32)
        ones_bf = sb(outer, "ones_bf", [128, 128], BF16)
        wf = sb(outer, "wf", [128, 8, 8], BF16)
        ident_bf = sb(outer, "ident_bf", [128, 128], BF16)
        negmask_bf = sb(outer, "negmask_bf", [128, 128], BF16)
        Tp = [sb(outer, "T%d" % i, [128, 512], F32) for i in range(3)]
        Ptp = [sb(outer, "Pt%d" % i, [128, 512], BF16) for i in range(4)]
        rstd = [sb(outer, "rstd%d" % i, [128, 512], F32) for i in range(2)]
        sqp = [sb(outer, "sq%d" % i, [128, 512], BF16) for i in range(2)]

        ident = cst[:, C_ID:C_ID + 128]
        Umat = cst[:, C_U:C_U + 128]

        def act(out, in_, func, reads, writes, **kw):
            P.add("act", lambda h: h.activation(out=out, in_=in_, func=func, **kw), reads, writes)

        def mm(out, lhsT, rhs, start, stop, reads, writes, **kw):
            P.add("pe", lambda h: h.matmul(out, lhsT=lhsT, rhs=rhs, start=start, stop=stop, **kw), reads, writes)

        def stt(out, in0, scalar, in1, op0, op1, reads, writes, **kw):
            P.add("dve", lambda h: h.scalar_tensor_tensor(out=out, in0=in0, scalar=scalar, in1=in1, op0=op0, op1=op1, **kw), reads, writes)

        def tt_(out, in0, in1, op, reads, writes, eng="dve"):
            P.add(eng, lambda h: h.tensor_tensor(out=out, in0=in0, in1=in1, op=op), reads, writes)

        def ts(out, in0, s1, s2, op0, op1, reads, writes, eng="dve"):
            P.add(eng, lambda h: h.tensor_scalar(out=out, in0=in0, scalar1=s1, scalar2=s2, op0=op0, op1=op1), reads, writes)

        def recip(out, in_, reads, writes):
            P.add("dve", lambda h: h.reciprocal(out=out, in_=in_), reads, writes)

        def memset(ap, val, writes, eng="dve"):
            P.add(eng, lambda h: h.memset(ap, val), (), writes)

        def dma(eng, out, in_, reads, writes, group=None, persistent=False):
            P.add(eng, lambda h: h.dma_start(out=out, in_=in_), reads, writes, dma=True, group=group, persistent=persistent)

        class RR:
            def __init__(self, items):
                self.items = list(items)
                self.i = 0

            def next(self):
                v = self.items[self.i % len(self.items)]
                self.i += 1
                return v

        T_rr = RR(range(3))
        Pt_rr = RR(range(4))
        sq_rr = RR(range(2))
        rstd_rr = RR(range(2))

        def wsrc(ap2d):
            return ap2d.rearrange("(kc p) n -> p kc n", p=128)

        wsched = []

        def sched_layer(l):
            wsched.append([(0, 8, 0, 256, w_in[l, :, OFF_A:OFF_A + 256])])
            wsched.append([(0, 8, 0, 256, w_in[l, :, OFF_C:OFF_C + 256])])
            wsched.append([(0, 8, 0, 256, w_in[l, :, OFF_C + 256:OFF_C + 512])])
            for hc in range(4):
                wsched.append([(0, 8, i * 128, (i + 1) * 128, w_in[l, :, off + hc * 128: off + (hc + 1) * 128])
                               for i, off in enumerate((OFF_Q, OFF_K, OFF_V))])
            for j in range(8):
                wsched.append([(0, 8, br * 128, (br + 1) * 128,
                                w_in[l, :, OFF_G + br * 1024 + j * 128: OFF_G + br * 1024 + (j + 1) * 128])
                               for br in range(3)])
                wsched.append([(0, 2, 0, 128, w_ba[l, :, j * 128:(j + 1) * 128]),
                               (2, 6, 0, 128, w_bb[l, :, j * 128:(j + 1) * 128]),
                               (6, 8, 0, 128, w_bc[l, :, j * 128:(j + 1) * 128])])
            for q in range(4):
                wsched.append([(0, 8, 0, 256, w_out[l, :, q * 256:(q + 1) * 256])])
            for q in range(8):
                wsched.append([(0, 8, 0, 256, w_xkv[l, :, q * 256:(q + 1) * 256])])
            for q in range(4):
                wsched.append([(0, 8, 0, 256, w_xq[l, :, q * 256:(q + 1) * 256])])
            for q in range(4):
                wsched.append([(0, 8, 0, 256, w_xo[l, :, q * 256:(q + 1) * 256])])
            for fg in range(4):
                for q in range(4):
                    wsched.append([(0, 8, 0, 256, w_ff1[l, :, fg * 1024 + q * 256: fg * 1024 + (q + 1) * 256])])
                for q in range(4):
                    wsched.append([(0, 8, 0, 256, w_ff2[l, fg * 1024:(fg + 1) * 1024, q * 256:(q + 1) * 256])])

        for l in range(n_layers):
            sched_layer(l)
        wstate = {"issued": 0, "next": 0}

        def w_issue_upto(n):
            while wstate["issued"] < min(n, len(wsched)):
                i = wstate["issued"]
                slot = i % NSLOT
                for (k0, k1, c0, c1, src) in wsched[i]:
                    dma("pool", wsl[slot][:, k0:k1, c0:c1], wsrc(src), [], [("ws", slot)], group=("w", i), persistent=True)
                wstate["issued"] += 1

        def w_acquire(held=0):
            i = wstate["next"]
            wstate["next"] += 1
            w_issue_upto(i - held + NSLOT)
            slot = i % NSLOT
            return wsl[slot], ("ws", slot)

        dma("sp", vecs[:], vecs_d, [], ["vecs"])
        dma("sp", cst[:], cst_d, [], ["cst"])
        for dc in range(8):
            for t in range(NTT):
                dma("sp", xres[:, dc, t * TT:(t + 1) * TT], xT[dc, :, t * TT:(t + 1) * TT], [], [("x", dc, t)])
        memset(ones_bf[:], 1.0, ["ones"])
        P.add("dve", lambda h: h.tensor_copy(ident_bf[:], cst[:, C_ID:C_ID + 128]), ["cst"], ["cbf"])
        P.add("dve", lambda h: h.tensor_copy(negmask_bf[:], cst[:, C_NEG:C_NEG + 128]), ["cst"], ["cbf"])
        for l in range(DEPTH):
            ts(halfb[:, l * 24:(l + 1) * 24], vecs[:, l * VPL + 8:l * VPL + 32], 0.5, None, ALU.mult, ALU.bypass,
               ["vecs"], [("halfb", l)])
            ts(negbf[:, l:l + 1], vecs[:, l * VPL + 60:l * VPL + 61], -1.0, None, ALU.mult, ALU.bypass,
               ["vecs"], [("negbf", l)])
        w_issue_upto(NSLOT)

        def rmsnorm(src, src_res, gcol0, dst, dst_res, ntiles, tw):
            for t in range(ntiles):
                for dc in range(8):
                    qi = sq_rr.next()
                    act(sqp[qi][:, :tw], src(dc, t), AF.Square, [src_res(dc, t)], [("sq", qi)])
                    mm(ps[:, t, :tw], ones_bf[:], sqp[qi][:, :tw], dc == 0, dc == 7,
                       [("sq", qi), "ones"], [("ps", t)])
            act(ps[:, 0:ntiles, :tw], ps[:, 0:ntiles, :tw], AF.Sqrt,
                [("ps", t) for t in range(ntiles)], [("ps", t) for t in range(ntiles)],
                scale=1.0 / D, bias=EPS)
            for t in range(ntiles):
                ri = rstd_rr.next()
                P.add("dve", lambda h, o=rstd[ri][:, :tw], i_=ps[:, t, :tw]: h.reciprocal(out=o, in_=i_),
                      [("ps", t)], [("rstd", ri)])
                for dc in range(8):
                    stt(dst(dc, t), src(dc, t), vecs[:, gcol0 + dc:gcol0 + dc + 1], rstd[ri][:, :tw],
                        ALU.mult, ALU.mult, [src_res(dc, t), ("rstd", ri), "vecs"], [dst_res(dc, t)])

        def xsl(dc, t):
            return xres[:, dc, t * TT:(t + 1) * TT]

        def Asl(dc, t):
            return A[:, dc, t * TT:(t + 1) * TT]

        def xr(dc, t):
            return ("x", dc, t)

        def Ar(dc, t):
            return ("A", dc, t)

        bank_rr = RR(range(8))
        lo_rr = RR(range(4))

        def proj_fm(wt, wres, c0, rhs_fn, rhs_res_fn, nk, t, extra_reads=(), rr=None):
            b = (rr or bank_rr).next()
            for kc in range(nk):
                mm(ps[:, b, :], wt[:, kc, c0:c0 + 128], rhs_fn(kc, t), kc == 0, kc == nk - 1,
                   [wres, rhs_res_fn(kc, t)] + list(extra_reads), [("ps", b)])
            return b

        def gelu_from_psum(src_ap, src_res, dst_ap, dst_res, width):
            t1 = T_rr.next()
            t2 = T_rr.next()
            T1 = Tp[t1][:, :width]
            T2 = Tp[t2][:, :width]
            act(T1, src_ap, AF.Identity, [src_res], [("T", t1)], scale=0.5)
            act(T2, src_ap, AF.Square, [src_res], [("T", t2)])
            ts(T2, T2, 0.044715, 1.0, ALU.mult, ALU.add, [("T", t2)], [("T", t2)])
            tt_(T2, T2, T1, ALU.mult, [("T", t1), ("T", t2)], [("T", t2)])
            act(T2, T2, AF.Tanh, [("T", t2)], [("T", t2)], scale=1.5957691216057308)
            stt(dst_ap, T2, 1.0, T1, ALU.add, ALU.mult, [("T", t1), ("T", t2)], [dst_res])

        def tap(name):
            if name in tap_d:
                for dc in range(8):
                    dma("sp", tap_d[name][dc], xres[:, dc, :], [("x", dc, t) for t in range(NTT)], [])

        for l in range(n_layers):
            vb = l * VPL
            rmsnorm(xsl, xr, vb + 0, Asl, Ar, NTT, TT)

            with ExitStack() as mix:
                pa = sb(mix, "pa", [128, 2, S], BF16)
                ao = sb(mix, "ao", [128, 4, S], BF16)
                sc = sb(mix, "sc", [128, 2, S], BF16)

                with ExitStack() as sub:
                    aT = sb(sub, "aT", [128, 16 + S], F32)
                    sA = sb(sub, "sA", [128, 16 + S], F32)
                    sB = sb(sub, "sB", [128, 16 + S], F32)
                    dT = sb(sub, "dT", [128, S], BF16)
                    wbd = sb(sub, "wbd", [128, 2, 128], BF16)
                    t16 = sb(sub, "t16", [128, 16], F32)
                    dma("pool", wbd[:], pool_wbd[l], [], ["wbd"])
                    for bufn, buf in (("aT", aT), ("sA", sA), ("sB", sB)):
                        memset(buf[:, 0:16], 0.0, [(bufn, "pad")])
                    wa, wa_res = w_acquire()
                    for fc in range(2):
                        for t in range(NTT):
                            b = proj_fm(wa, wa_res, fc * 128, Asl, Ar, 8, t)
                            act(aT[:, 16 + t * TT:16 + (t + 1) * TT], ps[:, b, :], AF.Identity,
                                [("ps", b)], [("aT", t)])
                        allr = [("aT", t) for t in range(NTT)] + [("aT", "pad")]
                        a_ = aT[:, 16:16 + S]
                        tt_(sA[:, 16:], a_, aT[:, 15:15 + S], ALU.add, allr + [("sA", "pad")], ["sA"])
                        if fc == 0:
                            lo, lo_res, lo_w = sA, "sA", 2
                            tt_(sB[:, 16:], sA[:, 16:], sA[:, 14:14 + S], ALU.add, ["sA", ("sA", "pad"), ("sB", "pad")], ["sB"])
                            hi, hi_res, hi_w = sB, "sB", 4
                        else:
                            tt_(sB[:, 16:], sA[:, 16:], sA[:, 14:14 + S], ALU.add, ["sA", ("sA", "pad"), ("sB", "pad")], ["sB"])
                            tt_(sA[:, 16:], sB[:, 16:], sB[:, 12:12 + S], ALU.add, ["sB", ("sB", "pad"), ("sA", "pad")], ["sA"])
                            lo, lo_res, lo_w = sA, "sA", 8
                            tt_(sB[:, 16:], sA[:, 16:], sA[:, 8:8 + S], ALU.add, ["sA", ("sA", "pad"), ("sB", "pad")], ["sB"])
                            hi, hi_res, hi_w = sB, "sB", 16
                        for (p0, buf, bres, win, hidx) in ((0, lo, lo_res, lo_w, 0), (64, hi, hi_res, hi_w, 1)):
                            stt(dT[p0:p0 + 64, :], buf[p0:p0 + 64, 16:], 1.0 / win, aT[p0:p0 + 64, 16:],
                                ALU.mult, ALU.subtract, allr + [bres], [("dT", hidx)])
                            tt_(t16[p0:p0 + 64, :], buf[p0:p0 + 64, 16:32],
                                cst[p0:p0 + 64, C_INV + fc * 16:C_INV + (fc + 1) * 16], ALU.mult,
                                [bres, "cst"], [("t16", hidx)])
                            tt_(dT[p0:p0 + 64, 0:16], t16[p0:p0 + 64, :], aT[p0:p0 + 64, 16:32], ALU.subtract,
                                allr + [("t16", hidx), ("dT", hidx)], [("dT", hidx)])
                        for t in range(NTT):
                            b = bank_rr.next()
                            mm(ps[:, b, :], wbd[:, fc, :], dT[:, t * TT:(t + 1) * TT], True, True,
                               [("dT", 0), ("dT", 1), "wbd"], [("ps", b)])
                            act(pa[:, fc, t * TT:(t + 1) * TT], ps[:, b, :], AF.Identity, [("ps", b), "vecs"],
                                [("pa", fc, t)], scale=vecs[:, vb + 32 + fc:vb + 33 + fc])
                P.barrier()

                with ExitStack() as sub:
                    uT = sb(sub, "uT", [128, 2, S], BF16)
                    gvb = sb(sub, "gvb", [128, 16, 256], BF16)
                    ss = sb(sub, "ss", [128, 16], F32)
                    rs16 = sb(sub, "rs16", [128, 16], F32)
                    gv = sb(sub, "gv", [128, 256], F32)
                    junk = sb(sub, "junk", [128, 256], F32)
                    wTs = sb(sub, "wTs", [128, 4, 128], F32)
                    wcm = sb(sub, "wcm", [128, 4, 128], BF16)
                    bT4 = sb(sub, "bT4", [128, 2, 512], F32)
                    for g in range(4):
                        dma("sp", wTs[:, g, :], sgu_wT[l, g], [], [("wTs", g)])
                        tt_(wcm[:, g, :], wTs[:, g, :], Umat, ALU.mult, [("wTs", g), "cst"], [("wcm", g)])
                    for fc in range(2):
                        dma("sp", bT4[:, fc, :], sgub4[l, fc], [], [("bT4", fc)])
                    wcu, wcu_res = w_acquire()
                    for fc in range(2):
                        for t in range(NTT):
                            b = proj_fm(wcu, wcu_res, fc * 128, Asl, Ar, 8, t)
                            gelu_from_psum(ps[:, b, :], ("ps", b), uT[:, fc, t * TT:(t + 1) * TT], ("uT", fc, t), 512)
                    wcv, wcv_res = w_acquire()
                    for c in range(16):
                        b = bank_rr.next()
                        for dc in range(8):
                            mm(ps[:, b, 0:256], A[:, dc, c * 128:(c + 1) * 128], wcv[:, dc, 0:256], dc == 0, dc == 7,
                               [wcv_res, Ar(dc, c // 4)], [("ps", b)])
                        gelu_from_psum(ps[:, b, 0:256], ("ps", b), gv[:], "gv", 256)
                        stt(junk[:], gv[:], 1.0, gv[:], ALU.mult, ALU.mult, ["gv"], ["junk", ("ss", c)],
                            accum_out=ss[:, c:c + 1])
                        act(gvb[:, c, :], gv[:], AF.Identity, ["gv"], [("gvb", c)])
                    ssr = [("ss", c) for c in range(16)]
                    act(rs16[:], ss[:], AF.Sqrt, ssr, ["rs16"], scale=1.0 / 256, bias=EPS)
                    recip(rs16[:], rs16[:], ["rs16"], ["rs16"])
                    for c in range(16):
                        ts(gvb[:, c, :], gvb[:, c, :], rs16[:, c:c + 1], None, ALU.mult, ALU.bypass,
                           [("gvb", c), "rs16"], [("gvb", c)])
                    for fc in range(2):
                        for t in range(NTT):
                            b = bank_rr.next()
                            for cc in range(4):
                                c = t * 4 + cc
                                for gi in range(2):
                                    g = 2 * fc + gi
                                    mm(ps[gi * 64:(gi + 1) * 64, b, cc * 128:(cc + 1) * 128],
                                       gvb[:, c, g * 64:(g + 1) * 64], wcm[:, g, :], True, True,
                                       [("gvb", c), ("wcm", g)], [("ps", b)], tile_position=(0, gi * 64))
                            ti = T_rr.next()
                            stt(Tp[ti][:], ps[:, b, :], vecs[:, vb + 34 + fc:vb + 35 + fc], bT4[:, fc, :],
                                ALU.mult, ALU.add, [("ps", b), "vecs", ("bT4", fc)], [("T", ti)])
                            tt_(sc[:, fc, t * TT:(t + 1) * TT], Tp[ti][:], uT[:, fc, t * TT:(t + 1) * TT], ALU.mult,
                                [("T", ti), ("uT", fc, t)], [("sc", fc, t)])
                P.barrier()

                with ExitStack() as sub:
                    FQ = sb(sub, "FQ", [128, 4, 512], F32)
                    E2 = sb(sub, "E2", [8, S], F32)
                    nFt = sb(sub, "nFt", [128, 128], F32)
                    qT = sb(sub, "qT", [128, S], BF16)
                    kTz = [sb(sub, "kTz%d" % i, [128, S], BF16) for i in range(2)]
                    vz = [sb(sub, "vz%d" % i, [128, 16, 128], BF16) for i in range(2)]
                    memset(kTz[0][64:128, :], 0.0, [("kTz0", "pad")])
                    memset(kTz[1][0:64, :], 0.0, [("kTz1", "pad")])
                    memset(vz[0][:, :, 64:128], 1.0, [("vz0", "pad")])
                    memset(vz[1][:, :, 0:64], 1.0, [("vz1", "pad")])
                    dma("pool", wf[:], wsrc(w_in[l, :, OFF_F:OFF_F + 8]), [], ["wf"])
                    for t in range(NTT):
                        for dc in range(8):
                            mm(ps[0:8, t, :], wf[:, dc, 0:8], Asl(dc, t), dc == 0, dc == 7, ["wf", Ar(dc, t)], [("ps", t)])
                    psr4 = [("ps", t) for t in range(4)]
                    E1 = FQ[0:8, :, :]
                    fqall = [("Fq", i) for i in range(4)]
                    act(E1, ps[0:8, 0:4, :], AF.Exp, psr4 + [("negbf", l)], fqall, scale=-1.0, bias=negbf[0:8, l:l + 1])
                    act(E1, E1, AF.Ln, fqall, fqall, bias=1.0)
                    E1f = FQ[0:8, :, :].rearrange("p a b -> p (a b)")
                    P.add("dve", lambda h, o=E2[:], d=E1f: h.tensor_tensor_scan(out=o, data0=d, data1=d, initial=0.0,
                                                                                 op0=ALU.add, op1=ALU.max), fqall, ["E2"])
                    bT = lo_rr.next()
                    for j in range(16):
                        P.add("pe", lambda h, o=ps[:, bT, j * 8:(j + 1) * 8], i_=E2[0:8, j * 128:(j + 1) * 128],
                              idn=cst[0:8, C_ID:C_ID + 8]: h.transpose(o, i_, idn), ["E2", "cst"], [("ps", bT)])
                    act(nFt[:], ps[:, bT, 0:128], AF.Identity, [("ps", bT)], ["nFt"])

                    for hc in range(4):
                        wq, wq_res = w_acquire()
                        for t in range(NTT):
                            b = proj_fm(wq, wq_res, 0, Asl, Ar, 8, t, rr=lo_rr)
                            act(qT[:, t * TT:(t + 1) * TT], ps[:, b, :], AF.Identity, [("ps", b)], [("qT", t)], scale=0.125)
                            b = proj_fm(wq, wq_res, 128, Asl, Ar, 8, t, rr=lo_rr)
                            act(kTz[0][0:64, t * TT:(t + 1) * TT], ps[0:64, b, :], AF.Identity, [("ps", b)], [("kT", 0, t)])
                            act(kTz[1][64:128, t * TT:(t + 1) * TT], ps[64:128, b, :], AF.Identity, [("ps", b)], [("kT", 1, t)])
                        for t in range(NTT):
                            b = lo_rr.next()
                            for cc in range(4):
                                c = 4 * t + cc
                                for dc in range(8):
                                    mm(ps[:, b, cc * 128:(cc + 1) * 128], A[:, dc, c * 128:(c + 1) * 128],
                                       wq[:, dc, 256:384], dc == 0, dc == 7, [wq_res, Ar(dc, t)], [("ps", b)])
                            psv = ps[:, b, :].rearrange("p (a c) -> p a c", c=128)
                            P.add("dve", lambda h, o=vz[0][:, 4 * t:4 * t + 4, 0:64], i_=psv[:, :, 0:64]: h.tensor_copy(o, i_),
                                  [("ps", b)], [("vz", 0, t)])
                            P.add("dve", lambda h, o=vz[1][:, 4 * t:4 * t + 4, 64:128], i_=psv[:, :, 64:128]: h.tensor_copy(o, i_),
                                  [("ps", b)], [("vz", 1, t)])
                        for Qi in range(NTT):
                            fslot = [(Qi % 2) * 2 + hi for hi in range(2)]
                            for hi in range(2):
                                hh = 2 * hc + hi
                                b = lo_rr.next()
                                mm(ps[:, b, :], cst[0:8, C_SEL + hh * 128:C_SEL + (hh + 1) * 128],
                                   E2[0:8, Qi * TT:(Qi + 1) * TT], True, True, ["E2", "cst"], [("ps", b)])
                                act(FQ[:, fslot[hi], :], ps[:, b, :], AF.Identity, [("ps", b)], [("Fq", fslot[hi])])
                            xb = [4 + (Qi % 2) * 2, 5 + (Qi % 2) * 2]
                            nk = 4 * Qi + 4
                            steps = [(kj, hi) for kj in range(nk) for hi in range(2)]
                            LA = 3
                            pend = []
                            for i in range(len(steps) + LA):
                                if i < len(steps):
                                    kj, hi = steps[i]
                                    hh = 2 * hc + hi
                                    n0 = max(0, kj - 4 * Qi) * 128
                                    N = TT - n0
                                    sbk = lo_rr.next()
                                    q0 = Qi * TT + n0
                                    diag = kj >= 4 * Qi
                                    mm(ps[:, sbk, 0:N], kTz[hi][:, kj * 128:(kj + 1) * 128], qT[:, q0:q0 + N], True, not diag,
                                       [("kT", hi, kj // 4), ("kTz%d" % hi, "pad"), ("qT", Qi)], [("ps", sbk)])
                                    if diag:
                                        mm(ps[:, sbk, 0:128], ident_bf[:], negmask_bf[:], False, True, ["cbf"], [("ps", sbk)])
                                    ti = T_rr.next()
                                    tt_(Tp[ti][:, 0:N], ps[:, sbk, 0:N], FQ[:, fslot[hi], n0:TT], ALU.add,
                                        [("ps", sbk), ("Fq", fslot[hi])], [("T", ti)])
                                    pi = Pt_rr.next()
                                    act(Ptp[pi][:, 0:N], Tp[ti][:, 0:N], AF.Exp, [("T", ti), "nFt"], [("Pt", pi)],
                                        bias=nFt[:, kj * 8 + hh:kj * 8 + hh + 1])
                                    pend.append((kj, hi, n0, N, pi))
                                if i >= LA:
                                    kj, hi, n0, N, pi = pend[i - LA]
                                    mm(ps[:, xb[hi], n0:TT], vz[hi][:, kj, :], Ptp[pi][:, 0:N], kj == 0, kj == nk - 1,
                                       [("vz", hi, kj // 4), ("vz%d" % hi, "pad"), ("Pt", pi)], [("ps", xb[hi])])
                            ri = rstd_rr.next()
                            P.add("dve", lambda h, o=rstd[ri][0:64, :], i_=ps[0:64, xb[1], :]: h.reciprocal(out=o, in_=i_),
                                  [("ps", xb[1])], [("rstd", ri)])
                            P.add("dve", lambda h, o=rstd[ri][64:128, :], i_=ps[64:128, xb[0], :]: h.reciprocal(out=o, in_=i_),
                                  [("ps", xb[0])], [("rstd", ri)])
                            bs = lo_rr.next()
                            mm(ps[:, bs, :], cst[:, C_SWAP:C_SWAP + 128], rstd[ri][:], True, True,
                               [("rstd", ri), "cst"], [("ps", bs)])
                            ti = T_rr.next()
                            act(Tp[ti][:], ps[:, bs, :], AF.Identity, [("ps", bs)], [("T", ti)])
                            tt_(ao[0:64, hc, Qi * TT:(Qi + 1) * TT], ps[0:64, xb[0], :], Tp[ti][0:64, :],
                                ALU.mult, [("ps", xb[0]), ("T", ti)], [("ao", hc, Qi, 0)])
                            tt_(ao[64:128, hc, Qi * TT:(Qi + 1) * TT], ps[64:128, xb[1], :], Tp[ti][64:128, :],
                                ALU.mult, [("ps", xb[1]), ("T", ti)], [("ao", hc, Qi, 1)])
                P.barrier()
                for nm, buf, nch in (("pa", pa, 2), ("ao", ao, 4), ("sc", sc, 2)):
                    if l == 0 and nm in tap_d:
                        for kc in range(nch):
                            dma("pool", tap_d[nm][kc], buf[:, kc, :], [], [])
                P.barrier()

                with ExitStack() as sub:
                    mg = sb(sub, "mg", [128, 8, S], BF16)
                    for j in range(8):
                        wg, wg_res = w_acquire()
                        wb, wb_res = w_acquire(held=1)
                        for t in range(NTT):
                            gb = []
                            for br in range(3):
                                b = proj_fm(wg, wg_res, br * 128, Asl, Ar, 8, t)
                                gb.append(b)
                            tl = []
                            for br in range(3):
                                ti = T_rr.next()
                                tl.append(ti)
                                act(Tp[ti][:], ps[:, gb[br], :], AF.Tanh, [("ps", gb[br]), ("halfb", l)], [("T", ti)],
                                    scale=0.5, bias=halfb[:, l * 24 + br * 8 + j:l * 24 + br * 8 + j + 1])
                            srcs = [(pa, 0, 2, lambda kc, t_: ("pa", kc, t_)),
                                    (ao, 2, 4, None),
                                    (sc, 6, 2, lambda kc, t_: ("sc", kc, t_))]
                            for br, (buf, k0, nk, rf) in enumerate(srcs):
                                b = bank_rr.next()
                                for kc in range(nk):
                                    rr_ = [("ao", kc, t, 0), ("ao", kc, t, 1)] if rf is None else [rf(kc, t)]
                                    mm(ps[:, b, :], wb[:, k0 + kc, 0:128], buf[:, kc, t * TT:(t + 1) * TT], kc == 0, kc == nk - 1,
                                       [wb_res] + rr_, [("ps", b)])
                                ti = tl[br]
                                stt(Tp[ti][:], Tp[ti][:], 1.0, ps[:, b, :], ALU.add, ALU.mult, [("T", ti), ("ps", b)], [("T", ti)])
                            tt_(Tp[tl[0]][:], Tp[tl[0]][:], Tp[tl[1]][:], ALU.add, [("T", tl[0]), ("T", tl[1])], [("T", tl[0])])
                            tt_(mg[:, j, t * TT:(t + 1) * TT], Tp[tl[0]][:], Tp[tl[2]][:], ALU.add,
                                [("T", tl[0]), ("T", tl[2])], [("mg", j, t)])
                    for q in range(4):
                        wo, wo_res = w_acquire()
                        for jj in range(2):
                            j = 2 * q + jj
                            for t in range(NTT):
                                b = proj_fm(wo, wo_res, jj * 128, lambda kc, t_: mg[:, kc, t_ * TT:(t_ + 1) * TT],
                                            lambda kc, t_: ("mg", kc, t_), 8, t)
                                stt(xsl(j, t), ps[:, b, :], 0.5, xsl(j, t), ALU.mult, ALU.add, [("ps", b), xr(j, t)], [xr(j, t)])
            P.barrier()
            if l == 0:
                tap("mix")

            rmsnorm(xsl, xr, vb + 36, Asl, Ar, NTT, TT)
            with ExitStack() as xa:
                xq = sb(xa, "xq", [128, 8, S], BF16)
                hmT = sb(xa, "hmT", [128, 8, MEM], BF16)
                xkT = sb(xa, "xkT", [128, 8, MEM], BF16)
                xv = sb(xa, "xv", [128, 2, D], BF16)
                with ExitStack() as sub:
                    mT = sb(sub, "mT", [128, 8, MEM], F32)
                    for dc in range(8):
                        dma("sp", mT[:, dc, :], memT[dc], [], [("mT", dc)])
                    rmsnorm(lambda dc, t: mT[:, dc, :], lambda dc, t: ("mT", dc), vb + 44,
                            lambda dc, t: hmT[:, dc, :], lambda dc, t: ("hmT", dc), 1, MEM)
                    P.barrier()
                for q in range(4):
                    wk, wk_res = w_acquire()
                    for jj in range(2):
                        j = 2 * q + jj
                        b = bank_rr.next()
                        for dc in range(8):
                            mm(ps[:, b, 0:MEM], wk[:, dc, jj * 128:(jj + 1) * 128], hmT[:, dc, :], dc == 0, dc == 7,
                               [wk_res, ("hmT", dc)], [("ps", b)])
                        act(xkT[:, j, :], ps[:, b, 0:MEM], AF.Identity, [("ps", b)], [("xkT", j)])
                for q in range(4):
                    wv_, wv_res = w_acquire()
                    for mc in range(2):
                        b = bank_rr.next()
                        for dc in range(8):
                            mm(ps[:, b, 0:256], hmT[:, dc, mc * 128:(mc + 1) * 128], wv_[:, dc, 0:256], dc == 0, dc == 7,
                               [wv_res, ("hmT", dc)], [("ps", b)])
                        act(xv[:, mc, q * 256:(q + 1) * 256], ps[:, b, 0:256], AF.Identity, [("ps", b)], [("xv", mc, q)])
                for q in range(4):
                    wq_, wq_res = w_acquire()
                    for jj in range(2):
                        j = 2 * q + jj
                        for t in range(NTT):
                            b = proj_fm(wq_, wq_res, jj * 128, Asl, Ar, 8, t)
                            act(xq[:, j, t * TT:(t + 1) * TT], ps[:, b, :], AF.Identity, [("ps", b)], [("xq", j, t)],
                                scale=1.0 / 16)
                hi_rr = RR(range(4, 8))
                items = [(t, hh) for t in range(NTT) for hh in range(4)]
                xpend = []

                def xa_scores(t, hh):
                    pts = []
                    for mc in range(2):
                        b = lo_rr.next()
                        for dk in range(2):
                            mm(ps[:, b, :], xkT[:, 2 * hh + dk, mc * 128:(mc + 1) * 128],
                               xq[:, 2 * hh + dk, t * TT:(t + 1) * TT], dk == 0, dk == 1,
                               [("xkT", 2 * hh + dk), ("xq", 2 * hh + dk, t)], [("ps", b)])
                        pi = Pt_rr.next()
                        act(Ptp[pi][:], ps[:, b, :], AF.Exp, [("ps", b)], [("Pt", pi)])
                        pts.append(pi)
                    return pts

                def xa_pv(t, hh, pts):
                    bd = hi_rr.next()
                    for mc in range(2):
                        mm(ps[:, bd, :], ones_bf[:], Ptp[pts[mc]][:], mc == 0, mc == 1,
                           ["ones", ("Pt", pts[mc])], [("ps", bd)])
                    ri = rstd_rr.next()
                    P.add("dve", lambda h, o=rstd[ri][:], i_=ps[:, bd, :]: h.reciprocal(out=o, in_=i_),
                          [("ps", bd)], [("rstd", ri)])
                    for dch in range(2):
                        b = hi_rr.next()
                        for mc in range(2):
                            mm(ps[:, b, :], xv[:, mc, hh * 256 + dch * 128: hh * 256 + (dch + 1) * 128], Ptp[pts[mc]][:],
                               mc == 0, mc == 1, [("xv", mc, hh), ("Pt", pts[mc])], [("ps", b)])
                        tt_(xq[:, 2 * hh + dch, t * TT:(t + 1) * TT], ps[:, b, :], rstd[ri][:], ALU.mult,
                            [("ps", b), ("rstd", ri)], [("xq", 2 * hh + dch, t)])

                for i in range(len(items) + 1):
                    if i < len(items):
                        xpend.append(xa_scores(*items[i]))
                    if i >= 1:
                        xa_pv(items[i - 1][0], items[i - 1][1], xpend[i - 1])
                for q in range(4):
                    wo, wo_res = w_acquire()
                    for jj in range(2):
                        j = 2 * q + jj
                        for t in range(NTT):
                            b = proj_fm(wo, wo_res, jj * 128, lambda kc, t_: xq[:, kc, t_ * TT:(t_ + 1) * TT],
                                        lambda kc, t_: ("xq", kc, t_), 8, t)
                            tt_(xsl(j, t), ps[:, b, :], xsl(j, t), ALU.add, [("ps", b), xr(j, t)], [xr(j, t)])
            P.barrier()
            if l == 0:
                tap("xat")

            rmsnorm(xsl, xr, vb + 52, Asl, Ar, NTT, TT)
            with ExitStack() as ff:
                hid = sb(ff, "hid", [128, 8, S], BF16)
                for fg in range(4):
                    for q in range(4):
                        w1, w1_res = w_acquire()
                        for jj in range(2):
                            fcl = 2 * q + jj
                            for t in range(NTT):
                                b = proj_fm(w1, w1_res, jj * 128, Asl, Ar, 8, t)
                                ti = T_rr.next()
                                act(Tp[ti][:], ps[:, b, :], AF.Relu, [("ps", b)], [("T", ti)])
                                tt_(hid[:, fcl, t * TT:(t + 1) * TT], Tp[ti][:], Tp[ti][:], ALU.mult,
                                    [("T", ti)], [("hid", fcl, t)])
                    for q in range(4):
                        w2, w2_res = w_acquire()
                        for jj in range(2):
                            j = 2 * q + jj
                            for t in range(NTT):
                                b = proj_fm(w2, w2_res, jj * 128, lambda kc, t_: hid[:, kc, t_ * TT:(t_ + 1) * TT],
                                            lambda kc, t_: ("hid", kc, t_), 8, t)
                                tt_(xsl(j, t), ps[:, b, :], xsl(j, t), ALU.add, [("ps", b), xr(j, t)], [xr(j, t)])
            P.barrier()
            if l == 0:
                tap("ffn")

        rmsnorm(xsl, xr, VPL * DEPTH, xsl, xr, NTT, TT)
        for dc in range(8):
            dma("sp", outT[dc], xres[:, dc, :], [xr(dc, t) for t in range(NTT)], [])
        cnt = P.emit(nc)
        nc._cnt = cnt
        nc._nops = len(P.ops)
    return nc


def host_consts():
    cst = np.zeros((128, NCST), np.float32)
    cst[:, C_ID:C_ID + 128] = np.eye(128, dtype=np.float32)
    k = np.arange(128)
    cst[:, C_U:C_U + 128] = (k[:, None] <= k[None, :]).astype(np.float32)
    wins = {(0, 0): 2, (0, 1): 4, (1, 0): 8, (1, 1): 16}
    t = np.arange(16)
    for fc in range(2):
        for half in range(2):
            w = wins[(fc, half)]
            cst[half * 64:(half + 1) * 64, C_INV + fc * 16:C_INV + (fc + 1) * 16] = \
                (1.0 / np.minimum(t + 1, w)).astype(np.float32)[None, :]
    for h in range(8):
        cst[h, C_SEL + h * 128:C_SEL + (h + 1) * 128] = -1.0
    for kk in range(128):
        cst[kk, C_SWAP + (kk + 64) % 128] = 1.0
    cst[:, C_NEG:C_NEG + 128] = np.where(k[:, None] > k[None, :], -30000.0, 0.0).astype(np.float32)
    return cst


def pack_vecs(inp):
    v = np.zeros((128, NV), np.float32)

    def fm(a):
        a = np.asarray(a, np.float32)
        return a.reshape(-1, 128).T

    for l in range(DEPTH):
        b = l * VPL
        v[:, b + 0:b + 8] = fm(inp["norm_mix_g"][l])
        v[:, b + 8:b + 32] = fm(inp["b_gate"][l])
        v[:, b + 32:b + 34] = fm(inp["pool_scale"][l])
        v[:, b + 34:b + 36] = fm(inp["sgu_norm_g"][l])
        v[:, b + 36:b + 44] = fm(inp["norm_xattn_g"][l])
        v[:, b + 44:b + 52] = fm(inp["norm_mem_g"][l])
        v[:, b + 52:b + 60] = fm(inp["norm_ffn_g"][l])
        v[0:8, b + 60] = np.asarray(inp["b_forget"][l], np.float32)
    v[:, VPL * DEPTH:VPL * DEPTH + 8] = fm(inp["final_norm_g"])
    return v


def shared_inputs(inp):
    f = lambda k: np.ascontiguousarray(np.asarray(inp[k], np.float32))
    sgu_b = np.asarray(inp["sgu_b"], np.float32)
    sgub4 = np.zeros((DEPTH, 2, 128, 512), np.float32)
    for fc in range(2):
        for gi in range(2):
            sgub4[:, fc, gi * 64:(gi + 1) * 64, :] = np.tile(sgu_b[:, 2 * fc + gi, :], (1, 4))[:, None, :]
    pw = np.asarray(inp["pool_w"], np.float32)
    pool_wbd = np.zeros((DEPTH, 128, 2, 128), np.float32)
    for g in range(4):
        gi = g % 2
        pool_wbd[:, gi * 64:(gi + 1) * 64, g // 2, gi * 64:(gi + 1) * 64] = pw[:, g]
    return {
        "w_in": f("w_in"), "pool_wbd": pool_wbd,
        "sgu_wT": np.ascontiguousarray(np.asarray(inp["sgu_w"], np.float32).transpose(0, 1, 3, 2)),
        "sgub4": sgub4,
        "w_branch_a": f("w_branch_a"), "w_branch_b": f("w_branch_b"), "w_branch_c": f("w_branch_c"),
        "w_out": f("w_out"), "w_xq": f("w_xq"), "w_xkv": f("w_xkv"), "w_xo": f("w_xo"),
        "w_ff1": f("w_ff1"), "w_ff2": f("w_ff2"),
        "vecs": pack_vecs(inp), "cst": host_consts(),
    }


def core_inputs(inp, b):
    x = np.asarray(inp["x"], np.float32)[b]
    m = np.asarray(inp["mem"], np.float32)[b]
    return {
        "xT": np.ascontiguousarray(x.T).reshape(8, 128, S),
        "memT": np.ascontiguousarray(m.T).reshape(8, 128, MEM),
    }


_NC_CACHE = {}


def kernel(**inputs):
    if "nc" not in _NC_CACHE:
        _NC_CACHE["nc"] = build()
    nc = _NC_CACHE["nc"]
    shared = shared_inputs(inputs)
    in_maps = []
    for b in range(8):
        m = dict(shared)
        m.update(core_inputs(inputs, b))
        in_maps.append(m)
    res = run_bass_kernel_spmd(nc, in_maps, core_ids=list(range(8)))
    out = np.empty((8, S, D), np.float32)
    for b in range(8):
        out[b] = res.results[b]["outT"].reshape(D, S).T
    return out
```

```python
import numpy as np
from contextlib import ExitStack
import concourse.bass as bass
import concourse.mybir as mybir
from concourse.bass_utils import run_bass_kernel_spmd

F32 = mybir.dt.float32
BF16 = mybir.dt.bfloat16
AF = mybir.ActivationFunctionType
ALU = mybir.AluOpType

ENGS = ("pe", "act", "dve", "pool", "sp")
N_DMA_SEMS = 24

D = 1024
S = 2048
DEPTH = 2
MEM = 256
TT = 512
NTT = S // TT
OFF_A, OFF_Q, OFF_K, OFF_V, OFF_F, OFF_C, OFF_G = 0, 256, 768, 1280, 1792, 1800, 2312
N_IN = 5384
EPS = 1e-6
SLOTW = 384
NSLOT = 3
VPL = 61
NV = VPL * DEPTH + 8
C_ID, C_U, C_INV, C_SEL, C_SWAP, C_NEG, NCST = 0, 128, 256, 288, 1312, 1440, 1568


class Op:
    __slots__ = ("idx", "eng", "fn", "deps", "dma", "need_sig", "sig", "prev_use", "group")

    def __init__(self, idx, eng, fn, deps, dma, group):
        self.idx = idx
        self.eng = eng
        self.fn = fn
        self.deps = deps
        self.dma = dma
        self.need_sig = dma
        self.sig = None
        self.prev_use = None
        self.group = group


class Prog:
    def __init__(self, same_engine_sync=True):
        self.ops = []
        self.last_w = {}
        self.readers = {}
        self.same_engine_sync = same_engine_sync
        self.barrier_deps = {}
        self.last_on = {}
        self.open_dma = []

    def add(self, eng, fn, reads=(), writes=(), dma=False, group=None, persistent=False):
        idx = len(self.ops)
        deps = set()
        for r in reads:
            w = self.last_w.get(r)
            if w is not None:
                deps.update(w[1])
        for r in writes:
            w = self.last_w.get(r)
            if w is not None:
                if not (group is not None and w[0] == group):
                    deps.update(w[1])
            for rd in self.readers.get(r, ()):
                deps.add(rd)
        for r in reads:
            self.readers.setdefault(r, []).append(idx)
        for r in writes:
            w = self.last_w.get(r)
            if group is not None and w is not None and w[0] == group:
                w[1].append(idx)
            else:
                self.last_w[r] = (group, [idx])
                self.readers[r] = []
        if eng in self.barrier_deps:
            deps.update(self.barrier_deps.pop(eng))
        deps.discard(idx)
        self.ops.append(Op(idx, eng, fn, deps, dma, group))
        self.last_on[eng] = idx
        if dma and not persistent:
            self.open_dma.append(idx)
        return idx

    def barrier(self):
        deps = set(self.open_dma)
        for e, i in self.last_on.items():
            if not self.ops[i].dma:
                deps.add(i)
        self.open_dma = []
        for e in ENGS:
            self.barrier_deps.setdefault(e, set()).update(deps)

    def emit(self, nc, final_wait_eng="sp"):
        ops = self.ops
        for op in ops:
            nd = set()
            for d in op.deps:
                dop = ops[d]
                if (not dop.dma) and (not op.dma) and dop.eng == op.eng:
                    if op.eng == "pe" or not self.same_engine_sync:
                        continue
                nd.add(d)
            best = {}
            keep = set()
            for d in nd:
                dop = ops[d]
                if dop.dma:
                    keep.add(d)
                elif best.get(dop.eng, -1) < d:
                    best[dop.eng] = d
            keep.update(best.values())
            op.deps = keep
            for d in keep:
                ops[d].need_sig = True
        cnt = {e: 0 for e in ENGS}
        dma_use = [0] * N_DMA_SEMS
        half = N_DMA_SEMS // 2
        dma_rr_q = {"pool": 0, "sp": 0}
        for op in ops:
            if op.dma:
                qn = "pool" if op.eng == "pool" else "sp"
                s = dma_rr_q[qn] + (half if qn == "pool" else 0)
                dma_rr_q[qn] = (dma_rr_q[qn] + 1) % half
                op.prev_use = dma_use[s]
                dma_use[s] += 1
                op.sig = (("dma", s), 16 * dma_use[s])
            elif op.need_sig:
                cnt[op.eng] += 1
                op.sig = (("eng", op.eng), cnt[op.eng])
        with ExitStack() as es:
            sems = {}
            for e in ENGS:
                sems[("eng", e)] = es.enter_context(nc.semaphore("sem_" + e))
            for i in range(N_DMA_SEMS):
                sems[("dma", i)] = es.enter_context(nc.semaphore("sem_dma%d" % i))
            block = es.enter_context(nc.Block())
            streams = {e: [op for op in ops if op.eng == e] for e in ENGS}
            final = {}
            for op in ops:
                if op.dma:
                    k, v = op.sig
                    final[k] = max(final.get(k, 0), v)

            def run_stream(eng_name, h):
                known = {}
                for op in streams[eng_name]:
                    waits = {}
                    for d in op.deps:
                        k, v = ops[d].sig
                        if waits.get(k, 0) < v:
                            waits[k] = v
                    if op.dma and op.prev_use:
                        k = op.sig[0]
                        v = 16 * op.prev_use
                        if waits.get(k, 0) < v:
                            waits[k] = v
                    for k, v in waits.items():
                        if known.get(k, 0) >= v:
                            continue
                        h.wait_ge(sems[k], v)
                        known[k] = v
                    ins = op.fn(h)
                    if op.sig is not None:
                        k, v = op.sig
                        ins.then_inc(sems[k], 16 if op.dma else 1)
                if eng_name == final_wait_eng:
                    for k, v in final.items():
                        if known.get(k, 0) < v:
                            h.wait_ge(sems[k], v)

            @block.tensor
            def _(h):
                run_stream("pe", h)

            @block.scalar
            def _(h):
                run_stream("act", h)

            @block.vector
            def _(h):
                run_stream("dve", h)

            @block.gpsimd
            def _(h):
                run_stream("pool", h)

            @block.sync
            def _(h):
                run_stream("sp", h)
        return cnt


def build(n_layers=DEPTH, taps=(), same_engine_sync=True):
    nc = bass.Bass("TRN2", target_bir_lowering=False)

    def din(name, shape):
        return nc.dram_tensor(name, list(shape), F32, kind="ExternalInput").ap()

    xT = din("xT", [8, 128, S])
    memT = din("memT", [8, 128, MEM])
    w_in = din("w_in", [DEPTH, D, N_IN])
    pool_wbd = din("pool_wbd", [DEPTH, 128, 2, 128])
    sgu_wT = din("sgu_wT", [DEPTH, 4, 128, 128])
    sgub4 = din("sgub4", [DEPTH, 2, 128, 512])
    w_ba = din("w_branch_a", [DEPTH, 256, D])
    w_bb = din("w_branch_b", [DEPTH, 512, D])
    w_bc = din("w_branch_c", [DEPTH, 256, D])
    w_out = din("w_out", [DEPTH, D, D])
    w_xq = din("w_xq", [DEPTH, D, D])
    w_xkv = din("w_xkv", [DEPTH, D, 2 * D])
    w_xo = din("w_xo", [DEPTH, D, D])
    w_ff1 = din("w_ff1", [DEPTH, D, 4 * D])
    w_ff2 = din("w_ff2", [DEPTH, 4 * D, D])
    vecs_d = din("vecs", [128, NV])
    cst_d = din("cst", [128, NCST])
    outT = nc.dram_tensor("outT", [8, 128, S], F32, kind="ExternalOutput").ap()
    tap_d = {}
    for tname in taps:
        tap_d[tname] = nc.dram_tensor("tap_" + tname, [8, 128, S], F32, kind="ExternalOutput").ap()

    P = Prog(same_engine_sync=same_engine_sync)

    with ExitStack() as outer:
        _uid = [0]

        def sb(es, name, shape, dt):
            _uid[0] += 1
            return es.enter_context(nc.sbuf_tensor("s%d_%s" % (_uid[0], name), list(shape), dt))

        ps = outer.enter_context(nc.psum_tensor("ps", [128, 8, 512], F32))
        xres = sb(outer, "xres", [128, 8, S], F32)
        A = sb(outer, "A", [128, 8, S], BF16)
        wsl = [sb(outer, "wsl%d" % i, [128, 8, SLOTW], BF16) for i in range(NSLOT)]
        vecs = sb(outer, "vecs", [128, NV], F32)
        cst = sb(outer, "cst", [128, NCST], F32)
        halfb = sb(outer, "halfb", [128, DEPTH * 24], F32)
        negbf = sb(outer, "negbf", [128, DEPTH], F32)
        ones_bf = sb(outer, "ones_bf", [128, 128], BF16)
        wf3 = sb(outer, "wf3", [128, 8, 72], BF16)
        ident_bf = sb(outer, "ident_bf", [128, 128], BF16)
        negmask_bf = sb(outer, "negmask_bf", [128, 128], BF16)
        Tp = [sb(outer, "T%d" % i, [128, 512], F32) for i in range(3)]
        Ptp = [sb(outer, "Pt%d" % i, [128, 512], BF16) for i in range(4)]
        rstd = [sb(outer, "rstd%d" % i, [128, 512], F32) for i in range(2)]
        sqp = [sb(outer, "sq%d" % i, [128, 512], BF16) for i in range(2)]

        ident = cst[:, C_ID:C_ID + 128]
        Umat = cst[:, C_U:C_U + 128]

        def act(out, in_, func, reads, writes, **kw):
            P.add("act", lambda h: h.activation(out=out, in_=in_, func=func, **kw), reads, writes)

        def mm(out, lhsT, rhs, start, stop, reads, writes, **kw):
            P.add("pe", lambda h: h.matmul(out, lhsT=lhsT, rhs=rhs, start=start, stop=stop, **kw), reads, writes)

        def stt(out, in0, scalar, in1, op0, op1, reads, writes, **kw):
            P.add("dve", lambda h: h.scalar_tensor_tensor(out=out, in0=in0, scalar=scalar, in1=in1, op0=op0, op1=op1, **kw), reads, writes)

        def tt_(out, in0, in1, op, reads, writes, eng="dve"):
            P.add(eng, lambda h: h.tensor_tensor(out=out, in0=in0, in1=in1, op=op), reads, writes)

        def ts(out, in0, s1, s2, op0, op1, reads, writes, eng="dve"):
            P.add(eng, lambda h: h.tensor_scalar(out=out, in0=in0, scalar1=s1, scalar2=s2, op0=op0, op1=op1), reads, writes)

        def recip(out, in_, reads, writes):
            P.add("dve", lambda h: h.reciprocal(out=out, in_=in_), reads, writes)

        def memset(ap, val, writes, eng="dve"):
            P.add(eng, lambda h: h.memset(ap, val), (), writes)

        def dma(eng, out, in_, reads, writes, group=None, persistent=False):
            P.add(eng, lambda h: h.dma_start(out=out, in_=in_), reads, writes, dma=True, group=group, persistent=persistent)

        class RR:
            def __init__(self, items):
                self.items = list(items)
                self.i = 0

            def next(self):
                v = self.items[self.i % len(self.items)]
                self.i += 1
                return v

        T_rr = RR(range(3))
        Pt_rr = RR(range(4))
        sq_rr = RR(range(2))
        rstd_rr = RR(range(2))

        def wsrc(ap2d):
            return ap2d.rearrange("(kc p) n -> p kc n", p=128)

        wsched = []

        def sched_layer(l):
            wsched.append([(0, 8, 0, 256, w_in[l, :, OFF_A:OFF_A + 256])])
            wsched.append([(0, 8, 0, 256, w_in[l, :, OFF_C:OFF_C + 256])])
            wsched.append([(0, 8, 0, 256, w_in[l, :, OFF_C + 256:OFF_C + 512])])
            for hc in range(4):
                wsched.append([(0, 8, i * 128, (i + 1) * 128, w_in[l, :, off + hc * 128: off + (hc + 1) * 128])
                               for i, off in enumerate((OFF_Q, OFF_K, OFF_V))])
            for j in range(8):
                wsched.append([(0, 8, br * 128, (br + 1) * 128,
                                w_in[l, :, OFF_G + br * 1024 + j * 128: OFF_G + br * 1024 + (j + 1) * 128])
                               for br in range(3)])
                wsched.append([(0, 2, 0, 128, w_ba[l, :, j * 128:(j + 1) * 128]),
                               (2, 6, 0, 128, w_bb[l, :, j * 128:(j + 1) * 128]),
                               (6, 8, 0, 128, w_bc[l, :, j * 128:(j + 1) * 128])])
            for q in range(4):
                wsched.append([(0, 8, 0, 256, w_out[l, :, q * 256:(q + 1) * 256])])
            for q in range(8):
                wsched.append([(0, 8, 0, 256, w_xkv[l, :, q * 256:(q + 1) * 256])])
            for q in range(4):
                wsched.append([(0, 8, 0, 256, w_xq[l, :, q * 256:(q + 1) * 256])])
            for q in range(4):
                wsched.append([(0, 8, 0, 256, w_xo[l, :, q * 256:(q + 1) * 256])])
            for fg in range(4):
                for q in range(4):
                    wsched.append([(0, 8, 0, 256, w_ff1[l, :, fg * 1024 + q * 256: fg * 1024 + (q + 1) * 256])])
                for q in range(4):
                    wsched.append([(0, 8, 0, 256, w_ff2[l, fg * 1024:(fg + 1) * 1024, q * 256:(q + 1) * 256])])

        for l in range(n_layers):
            sched_layer(l)
        wstate = {"issued": 0, "next": 0}

        def w_issue_upto(n):
            while wstate["issued"] < min(n, len(wsched)):
                i = wstate["issued"]
                slot = i % NSLOT
                for (k0, k1, c0, c1, src) in wsched[i]:
                    dma("pool", wsl[slot][:, k0:k1, c0:c1], wsrc(src), [], [("ws", slot)], group=("w", i), persistent=True)
                wstate["issued"] += 1

        def w_acquire(held=0):
            i = wstate["next"]
            wstate["next"] += 1
            w_issue_upto(i - held + NSLOT)
            slot = i % NSLOT
            return wsl[slot], ("ws", slot)

        dma("sp", vecs[:], vecs_d, [], ["vecs"])
        dma("sp", cst[:], cst_d, [], ["cst"])
        for dc in range(8):
            for t in range(NTT):
                dma("sp", xres[:, dc, t * TT:(t + 1) * TT], xT[dc, :, t * TT:(t + 1) * TT], [], [("x", dc, t)])
        memset(ones_bf[:], 1.0, ["ones"])
        memset(wf3[:], 0.0, ["wf3z"])
        P.add("dve", lambda h: h.tensor_copy(ident_bf[:], cst[:, C_ID:C_ID + 128]), ["cst"], ["cbf"])
        P.add("dve", lambda h: h.tensor_copy(negmask_bf[:], cst[:, C_NEG:C_NEG + 128]), ["cst"], ["cbf"])
        for l in range(DEPTH):
            ts(halfb[:, l * 24:(l + 1) * 24], vecs[:, l * VPL + 8:l * VPL + 32], 0.5, None, ALU.mult, ALU.bypass,
               ["vecs"], [("halfb", l)])
            ts(negbf[:, l:l + 1], vecs[:, l * VPL + 60:l * VPL + 61], -1.0, None, ALU.mult, ALU.bypass,
               ["vecs"], [("negbf", l)])
        w_issue_upto(NSLOT)

        def rmsnorm(src, src_res, gcol0, dst, dst_res, ntiles, tw):
            for t in range(ntiles):
                for dc in range(8):
                    qi = sq_rr.next()
                    act(sqp[qi][:, :tw], src(dc, t), AF.Square, [src_res(dc, t)], [("sq", qi)])
                    mm(ps[:, t, :tw], ones_bf[:], sqp[qi][:, :tw], dc == 0, dc == 7,
                       [("sq", qi), "ones"], [("ps", t)])
                act(ps[:, t, :tw], ps[:, t, :tw], AF.Sqrt, [("ps", t)], [("ps", t)], scale=1.0 / D, bias=EPS)
                ri = rstd_rr.next()
                P.add("dve", lambda h, o=rstd[ri][:, :tw], i_=ps[:, t, :tw]: h.reciprocal(out=o, in_=i_),
                      [("ps", t)], [("rstd", ri)])
                for dc in range(8):
                    stt(dst(dc, t), src(dc, t), vecs[:, gcol0 + dc:gcol0 + dc + 1], rstd[ri][:, :tw],
                        ALU.mult, ALU.mult, [src_res(dc, t), ("rstd", ri), "vecs"], [dst_res(dc, t)])

        def xsl(dc, t):
            return xres[:, dc, t * TT:(t + 1) * TT]

        def Asl(dc, t):
            return A[:, dc, t * TT:(t + 1) * TT]

        def xr(dc, t):
            return ("x", dc, t)

        def Ar(dc, t):
            return ("A", dc, t)

        bank_rr = RR(range(8))
        lo_rr = RR(range(4))

        def proj_fm(wt, wres, c0, rhs_fn, rhs_res_fn, nk, t, extra_reads=(), rr=None):
            b = (rr or bank_rr).next()
            for kc in range(nk):
                mm(ps[:, b, :], wt[:, kc, c0:c0 + 128], rhs_fn(kc, t), kc == 0, kc == nk - 1,
                   [wres, rhs_res_fn(kc, t)] + list(extra_reads), [("ps", b)])
            return b

        def gelu_from_psum(src_ap, src_res, dst_ap, dst_res, width):
            t1 = T_rr.next()
            t2 = T_rr.next()
            T1 = Tp[t1][:, :width]
            T2 = Tp[t2][:, :width]
            act(T1, src_ap, AF.Identity, [src_res], [("T", t1)], scale=0.5)
            act(T2, src_ap, AF.Square, [src_res], [("T", t2)])
            ts(T2, T2, 0.044715, 1.0, ALU.mult, ALU.add, [("T", t2)], [("T", t2)])
            tt_(T2, T2, T1, ALU.mult, [("T", t1), ("T", t2)], [("T", t2)])
            act(T2, T2, AF.Tanh, [("T", t2)], [("T", t2)], scale=1.5957691216057308)
            stt(dst_ap, T2, 1.0, T1, ALU.add, ALU.mult, [("T", t1), ("T", t2)], [dst_res])

        def tap(name):
            if name in tap_d:
                for dc in range(8):
                    dma("sp", tap_d[name][dc], xres[:, dc, :], [("x", dc, t) for t in range(NTT)], [])

        for l in range(n_layers):
            vb = l * VPL
            rmsnorm(xsl, xr, vb + 0, Asl, Ar, NTT, TT)

            with ExitStack() as mix:
                pa = sb(mix, "pa", [128, 2, S], BF16)
                ao = sb(mix, "ao", [128, 4, S], BF16)
                sc = sb(mix, "sc", [128, 2, S], BF16)

                with ExitStack() as sub:
                    aT = sb(sub, "aT", [128, 16 + S], F32)
                    sA = sb(sub, "sA", [128, 16 + S], F32)
                    sB = sb(sub, "sB", [128, 16 + S], F32)
                    dT = sb(sub, "dT", [128, S], BF16)
                    wbd = sb(sub, "wbd", [128, 2, 128], BF16)
                    t16 = sb(sub, "t16", [128, 16], F32)
                    dma("pool", wbd[:], pool_wbd[l], [], ["wbd"])
                    for bufn, buf in (("aT", aT), ("sA", sA), ("sB", sB)):
                        memset(buf[:, 0:16], 0.0, [(bufn, "pad")])
                    wa, wa_res = w_acquire()
                    for fc in range(2):
                        for t in range(NTT):
                            b = proj_fm(wa, wa_res, fc * 128, Asl, Ar, 8, t)
                            act(aT[:, 16 + t * TT:16 + (t + 1) * TT], ps[:, b, :], AF.Identity,
                                [("ps", b)], [("aT", t)])
                        allr = [("aT", t) for t in range(NTT)] + [("aT", "pad")]
                        a_ = aT[:, 16:16 + S]
                        tt_(sA[:, 16:], a_, aT[:, 15:15 + S], ALU.add, allr + [("sA", "pad")], ["sA"])
                        if fc == 0:
                            lo, lo_res, lo_w = sA, "sA", 2
                            tt_(sB[:, 16:], sA[:, 16:], sA[:, 14:14 + S], ALU.add, ["sA", ("sA", "pad"), ("sB", "pad")], ["sB"])
                            hi, hi_res, hi_w = sB, "sB", 4
                        else:
                            tt_(sB[:, 16:], sA[:, 16:], sA[:, 14:14 + S], ALU.add, ["sA", ("sA", "pad"), ("sB", "pad")], ["sB"])
                            tt_(sA[:, 16:], sB[:, 16:], sB[:, 12:12 + S], ALU.add, ["sB", ("sB", "pad"), ("sA", "pad")], ["sA"])
                            lo, lo_res, lo_w = sA, "sA", 8
                            tt_(sB[:, 16:], sA[:, 16:], sA[:, 8:8 + S], ALU.add, ["sA", ("sA", "pad"), ("sB", "pad")], ["sB"])
                            hi, hi_res, hi_w = sB, "sB", 16
                        for (p0, buf, bres, win, hidx) in ((0, lo, lo_res, lo_w, 0), (64, hi, hi_res, hi_w, 1)):
                            stt(dT[p0:p0 + 64, :], buf[p0:p0 + 64, 16:], 1.0 / win, aT[p0:p0 + 64, 16:],
                                ALU.mult, ALU.subtract, allr + [bres], [("dT", hidx)])
                            tt_(t16[p0:p0 + 64, :], buf[p0:p0 + 64, 16:32],
                                cst[p0:p0 + 64, C_INV + fc * 16:C_INV + (fc + 1) * 16], ALU.mult,
                                [bres, "cst"], [("t16", hidx)])
                            tt_(dT[p0:p0 + 64, 0:16], t16[p0:p0 + 64, :], aT[p0:p0 + 64, 16:32], ALU.subtract,
                                allr + [("t16", hidx), ("dT", hidx)], [("dT", hidx)])
                        for t in range(NTT):
                            b = bank_rr.next()
                            mm(ps[:, b, :], wbd[:, fc, :], dT[:, t * TT:(t + 1) * TT], True, True,
                               [("dT", 0), ("dT", 1), "wbd"], [("ps", b)])
                            act(pa[:, fc, t * TT:(t + 1) * TT], ps[:, b, :], AF.Identity, [("ps", b), "vecs"],
                                [("pa", fc, t)], scale=vecs[:, vb + 32 + fc:vb + 33 + fc])
                P.barrier()

                with ExitStack() as sub:
                    uT = sb(sub, "uT", [128, 2, S], BF16)
                    gvb = sb(sub, "gvb", [128, 16, 256], BF16)
                    ss = sb(sub, "ss", [128, 16], F32)
                    rs16 = sb(sub, "rs16", [128, 16], F32)
                    gv = sb(sub, "gv", [128, 256], F32)
                    junk = sb(sub, "junk", [128, 256], F32)
                    wTs = sb(sub, "wTs", [128, 4, 128], F32)
                    wcm = sb(sub, "wcm", [128, 4, 128], BF16)
                    bT4 = sb(sub, "bT4", [128, 2, 512], F32)
                    for g in range(4):
                        dma("sp", wTs[:, g, :], sgu_wT[l, g], [], [("wTs", g)])
                        tt_(wcm[:, g, :], wTs[:, g, :], Umat, ALU.mult, [("wTs", g), "cst"], [("wcm", g)])
                    for fc in range(2):
                        dma("sp", bT4[:, fc, :], sgub4[l, fc], [], [("bT4", fc)])
                    wcu, wcu_res = w_acquire()
                    for fc in range(2):
                        for t in range(NTT):
                            b = proj_fm(wcu, wcu_res, fc * 128, Asl, Ar, 8, t)
                            gelu_from_psum(ps[:, b, :], ("ps", b), uT[:, fc, t * TT:(t + 1) * TT], ("uT", fc, t), 512)
                    wcv, wcv_res = w_acquire()
                    for c in range(16):
                        b = bank_rr.next()
                        for dc in range(8):
                            mm(ps[:, b, 0:256], A[:, dc, c * 128:(c + 1) * 128], wcv[:, dc, 0:256], dc == 0, dc == 7,
                               [wcv_res, Ar(dc, c // 4)], [("ps", b)])
                        gelu_from_psum(ps[:, b, 0:256], ("ps", b), gv[:], "gv", 256)
                        stt(junk[:], gv[:], 1.0, gv[:], ALU.mult, ALU.mult, ["gv"], ["junk", ("ss", c)],
                            accum_out=ss[:, c:c + 1])
                        act(gvb[:, c, :], gv[:], AF.Identity, ["gv"], [("gvb", c)])
                    ssr = [("ss", c) for c in range(16)]
                    act(rs16[:], ss[:], AF.Sqrt, ssr, ["rs16"], scale=1.0 / 256, bias=EPS)
                    recip(rs16[:], rs16[:], ["rs16"], ["rs16"])
                    for c in range(16):
                        ts(gvb[:, c, :], gvb[:, c, :], rs16[:, c:c + 1], None, ALU.mult, ALU.bypass,
                           [("gvb", c), "rs16"], [("gvb", c)])
                    for fc in range(2):
                        for t in range(NTT):
                            b = bank_rr.next()
                            for cc in range(4):
                                c = t * 4 + cc
                                for gi in range(2):
                                    g = 2 * fc + gi
                                    mm(ps[gi * 64:(gi + 1) * 64, b, cc * 128:(cc + 1) * 128],
                                       gvb[:, c, g * 64:(g + 1) * 64], wcm[:, g, :], True, True,
                                       [("gvb", c), ("wcm", g)], [("ps", b)], tile_position=(0, gi * 64))
                            ti = T_rr.next()
                            stt(Tp[ti][:], ps[:, b, :], vecs[:, vb + 34 + fc:vb + 35 + fc], bT4[:, fc, :],
                                ALU.mult, ALU.add, [("ps", b), "vecs", ("bT4", fc)], [("T", ti)])
                            tt_(sc[:, fc, t * TT:(t + 1) * TT], Tp[ti][:], uT[:, fc, t * TT:(t + 1) * TT], ALU.mult,
                                [("T", ti), ("uT", fc, t)], [("sc", fc, t)])
                P.barrier()

                with ExitStack() as sub:
                    P3 = sb(sub, "P3", [72, S], BF16)
                    qTz = [sb(sub, "qTz%d" % i, [128, S], BF16) for i in range(2)]
                    kTz = [sb(sub, "kTz%d" % i, [128, S], BF16) for i in range(2)]
                    for hi in range(2):
                        z0 = 64 if hi == 0 else 0
                        memset(kTz[hi][z0:z0 + 64, :], 0.0, [("kTz", hi, "pad")])
                        memset(qTz[hi][z0:z0 + 64, :], 0.0, [("qTz", hi, "pad")])
                        memset(kTz[hi][z0:z0 + 3, :], -1.0, [("kTz", hi, "pad")])
                        memset(qTz[hi][z0:z0 + 6, :], 1.0, [("qTz", hi, "pad")])
                    with ExitStack() as fsub:
                        E2 = sb(fsub, "E2", [72, S], F32)
                        tmpb = sb(fsub, "tmpb", [72, S], BF16)
                        for r in range(3):
                            dma("pool", wf3[:, :, 32 * r:32 * r + 8], wsrc(w_in[l, :, OFF_F:OFF_F + 8]), ["wf3z"], [("wf3", r)])
                        for t in range(NTT):
                            for dc in range(8):
                                mm(ps[0:72, t, :], wf3[:, dc, :], Asl(dc, t), dc == 0, dc == 7,
                                   ["wf3z", ("wf3", 0), ("wf3", 1), ("wf3", 2), Ar(dc, t)], [("ps", t)])
                        psr4 = [("ps", t) for t in range(4)]
                        act(ps[0:72, 0:4, :], ps[0:72, 0:4, :], AF.Exp, psr4 + [("negbf", l)], psr4,
                            scale=-1.0, bias=negbf[0:72, l:l + 1])
                        act(E2[:].rearrange("p (a b) -> p a b", b=512), ps[0:72, 0:4, :], AF.Ln, psr4, ["E2"], bias=1.0)
                        P.add("dve", lambda h, o=E2[:]: h.tensor_tensor_scan(out=o, data0=o, data1=o, initial=0.0,
                                                                              op0=ALU.add, op1=ALU.max), ["E2"], ["E2"])

                        def cp(o, i_, rd, wr):
                            P.add("dve", lambda h: h.tensor_copy(o, i_), rd, wr)

                        cp(P3[0:8, :], E2[0:8, :], ["E2"], [("P3", 0)])
                        cp(tmpb[32:40, :], E2[32:40, :], ["E2"], [("tmpb", 1)])
                        tt_(E2[32:40, :], E2[32:40, :], tmpb[32:40, :], ALU.subtract, ["E2", ("tmpb", 1)], [("E2m", 1)])
                        cp(P3[32:40, :], E2[32:40, :], [("E2m", 1)], [("P3", 1)])
                        cp(tmpb[64:72, :], E2[64:72, :], ["E2"], [("tmpb", 2)])
                        tt_(E2[64:72, :], E2[64:72, :], tmpb[64:72, :], ALU.subtract, ["E2", ("tmpb", 2)], [("E2m", 2)])
                        cp(tmpb[64:72, :], E2[64:72, :], [("E2m", 2)], [("tmpb", 2)])
                        tt_(E2[64:72, :], E2[64:72, :], tmpb[64:72, :], ALU.subtract, [("E2m", 2), ("tmpb", 2)], [("E2m", 2)])
                        cp(P3[64:72, :], E2[64:72, :], [("E2m", 2)], [("P3", 2)])
                    P.barrier()
                    vz = [sb(sub, "vz%d" % i, [128, 16, 128], BF16) for i in range(2)]
                    memset(vz[0][:, :, 64:128], 1.0, [("vz0", "pad")])
                    memset(vz[1][:, :, 0:64], 1.0, [("vz1", "pad")])
                    p3r = [("P3", i) for i in range(3)]

                    for hc in range(4):
                        wq, wq_res = w_acquire()
                        for hi in range(2):
                            hh = 2 * hc + hi
                            z0 = 64 if hi == 0 else 0
                            for r in range(3):
                                dma("sp", qTz[hi][z0 + r:z0 + r + 1, :], P3[32 * r + hh:32 * r + hh + 1, :], p3r + [("qTz", hi, "pad")],
                                    [("qTz", hi, "F", r)])
                                dma("sp", kTz[hi][z0 + 3 + r:z0 + 4 + r, :], P3[32 * r + hh:32 * r + hh + 1, :], p3r + [("kTz", hi, "pad")],
                                    [("kTz", hi, "F", r)])
                        for t in range(NTT):
                            b = proj_fm(wq, wq_res, 0, Asl, Ar, 8, t, rr=lo_rr)
                            act(qTz[0][0:64, t * TT:(t + 1) * TT], ps[0:64, b, :], AF.Identity, [("ps", b)], [("qT", 0, t)], scale=0.125)
                            act(qTz[1][64:128, t * TT:(t + 1) * TT], ps[64:128, b, :], AF.Identity, [("ps", b)], [("qT", 1, t)], scale=0.125)
                            b = proj_fm(wq, wq_res, 128, Asl, Ar, 8, t, rr=lo_rr)
                            act(kTz[0][0:64, t * TT:(t + 1) * TT], ps[0:64, b, :], AF.Identity, [("ps", b)], [("kT", 0, t)])
                            act(kTz[1][64:128, t * TT:(t + 1) * TT], ps[64:128, b, :], AF.Identity, [("ps", b)], [("kT", 1, t)])
                        for t in range(NTT):
                            b = lo_rr.next()
                            for cc in range(4):
                                c = 4 * t + cc
                                for dc in range(8):
                                    mm(ps[:, b, cc * 128:(cc + 1) * 128], A[:, dc, c * 128:(c + 1) * 128],
                                       wq[:, dc, 256:384], dc == 0, dc == 7, [wq_res, Ar(dc, t)], [("ps", b)])
                            psv = ps[:, b, :].rearrange("p (a c) -> p a c", c=128)
                            P.add("dve", lambda h, o=vz[0][:, 4 * t:4 * t + 4, 0:64], i_=psv[:, :, 0:64]: h.tensor_copy(o, i_),
                                  [("ps", b)], [("vz", 0, t)])
                            P.add("dve", lambda h, o=vz[1][:, 4 * t:4 * t + 4, 64:128], i_=psv[:, :, 64:128]: h.tensor_copy(o, i_),
                                  [("ps", b)], [("vz", 1, t)])
                        for Qi in range(NTT):
                            xb = [4 + (Qi % 2) * 2, 5 + (Qi % 2) * 2]
                            nk = 4 * Qi + 4
                            steps = [(kj, hi) for kj in range(nk) for hi in range(2)]
                            LA = 3
                            pend = []
                            for i in range(len(steps) + LA):
                                if i < len(steps):
                                    kj, hi = steps[i]
                                    n0 = max(0, kj - 4 * Qi) * 128
                                    N = TT - n0
                                    sbk = lo_rr.next()
                                    q0 = Qi * TT + n0
                                    diag = kj >= 4 * Qi
                                    frs = [(nm, hi, "F", r) for nm in ("qTz", "kTz") for r in range(3)]
                                    mm(ps[:, sbk, 0:N], kTz[hi][:, kj * 128:(kj + 1) * 128], qTz[hi][:, q0:q0 + N], True, not diag,
                                       [("kT", hi, kj // 4), ("kTz", hi, "pad"), ("qTz", hi, "pad"), ("qT", hi, Qi)] + frs, [("ps", sbk)])
                                    if diag:
                                        mm(ps[:, sbk, 0:128], ident_bf[:], negmask_bf[:], False, True, ["cbf"], [("ps", sbk)])
                                    pi = Pt_rr.next()
                                    act(Ptp[pi][:, 0:N], ps[:, sbk, 0:N], AF.Exp, [("ps", sbk)], [("Pt", pi)])
                                    pend.append((kj, hi, n0, N, pi))
                                if i >= LA:
                                    kj, hi, n0, N, pi = pend[i - LA]
                                    mm(ps[:, xb[hi], n0:TT], vz[hi][:, kj, :], Ptp[pi][:, 0:N], kj == 0, kj == nk - 1,
                                       [("vz", hi, kj // 4), ("vz%d" % hi, "pad"), ("Pt", pi)], [("ps", xb[hi])])
                            ri = rstd_rr.next()
                            P.add("dve", lambda h, o=rstd[ri][0:64, :], i_=ps[0:64, xb[1], :]: h.reciprocal(out=o, in_=i_),
                                  [("ps", xb[1])], [("rstd", ri)])
                            P.add("dve", lambda h, o=rstd[ri][64:128, :], i_=ps[64:128, xb[0], :]: h.reciprocal(out=o, in_=i_),
                                  [("ps", xb[0])], [("rstd", ri)])
                            bs = lo_rr.next()
                            mm(ps[:, bs, :], cst[:, C_SWAP:C_SWAP + 128], rstd[ri][:], True, True,
                               [("rstd", ri), "cst"], [("ps", bs)])
                            ti = T_rr.next()
                            act(Tp[ti][:], ps[:, bs, :], AF.Identity, [("ps", bs)], [("T", ti)])
                            tt_(ao[0:64, hc, Qi * TT:(Qi + 1) * TT], ps[0:64, xb[0], :], Tp[ti][0:64, :],
                                ALU.mult, [("ps", xb[0]), ("T", ti)], [("ao", hc, Qi, 0)])
                            tt_(ao[64:128, hc, Qi * TT:(Qi + 1) * TT], ps[64:128, xb[1], :], Tp[ti][64:128, :],
                                ALU.mult, [("ps", xb[1]), ("T", ti)], [("ao", hc, Qi, 1)])
                P.barrier()
                for nm, buf, nch in (("pa", pa, 2), ("ao", ao, 4), ("sc", sc, 2)):
                    if l == 0 and nm in tap_d:
                        for kc in range(nch):
                            dma("pool", tap_d[nm][kc], buf[:, kc, :], [], [])
                P.barrier()

                with ExitStack() as sub:
                    mg = sb(sub, "mg", [128, 8, S], BF16)
                    for j in range(8):
                        wg, wg_res = w_acquire()
                        wb, wb_res = w_acquire(held=1)
                        for t in range(NTT):
                            gb = []
                            for br in range(3):
                                b = proj_fm(wg, wg_res, br * 128, Asl, Ar, 8, t)
                                gb.append(b)
                            tl = []
                            for br in range(3):
                                ti = T_rr.next()
                                tl.append(ti)
                                act(Tp[ti][:], ps[:, gb[br], :], AF.Tanh, [("ps", gb[br]), ("halfb", l)], [("T", ti)],
                                    scale=0.5, bias=halfb[:, l * 24 + br * 8 + j:l * 24 + br * 8 + j + 1])
                            srcs = [(pa, 0, 2, lambda kc, t_: ("pa", kc, t_)),
                                    (ao, 2, 4, None),
                                    (sc, 6, 2, lambda kc, t_: ("sc", kc, t_))]
                            for br, (buf, k0, nk, rf) in enumerate(srcs):
                                b = bank_rr.next()
                                for kc in range(nk):
                                    rr_ = [("ao", kc, t, 0), ("ao", kc, t, 1)] if rf is None else [rf(kc, t)]
                                    mm(ps[:, b, :], wb[:, k0 + kc, 0:128], buf[:, kc, t * TT:(t + 1) * TT], kc == 0, kc == nk - 1,
                                       [wb_res] + rr_, [("ps", b)])
                                ti = tl[br]
                                stt(Tp[ti][:], Tp[ti][:], 1.0, ps[:, b, :], ALU.add, ALU.mult, [("T", ti), ("ps", b)], [("T", ti)])
                            tt_(Tp[tl[0]][:], Tp[tl[0]][:], Tp[tl[1]][:], ALU.add, [("T", tl[0]), ("T", tl[1])], [("T", tl[0])])
                            tt_(mg[:, j, t * TT:(t + 1) * TT], Tp[tl[0]][:], Tp[tl[2]][:], ALU.add,
                                [("T", tl[0]), ("T", tl[2])], [("mg", j, t)])
                    for q in range(4):
                        wo, wo_res = w_acquire()
                        for jj in range(2):
                            j = 2 * q + jj
                            for t in range(NTT):
                                b = proj_fm(wo, wo_res, jj * 128, lambda kc, t_: mg[:, kc, t_ * TT:(t_ + 1) * TT],
                                            lambda kc, t_: ("mg", kc, t_), 8, t)
                                stt(xsl(j, t), ps[:, b, :], 0.5, xsl(j, t), ALU.mult, ALU.add, [("ps", b), xr(j, t)], [xr(j, t)])
            P.barrier()
            if l == 0:
                tap("mix")

            rmsnorm(xsl, xr, vb + 36, Asl, Ar, NTT, TT)
            with ExitStack() as xa:
                xq = sb(xa, "xq", [128, 8, S], BF16)
                hmT = sb(xa, "hmT", [128, 8, MEM], BF16)
                xkT = sb(xa, "xkT", [128, 8, MEM], BF16)
                xv = sb(xa, "xv", [128, 2, D], BF16)
                with ExitStack() as sub:
                    mT = sb(sub, "mT", [128, 8, MEM], F32)
                    for dc in range(8):
                        dma("sp", mT[:, dc, :], memT[dc], [], [("mT", dc)])
                    rmsnorm(lambda dc, t: mT[:, dc, :], lambda dc, t: ("mT", dc), vb + 44,
                            lambda dc, t: hmT[:, dc, :], lambda dc, t: ("hmT", dc), 1, MEM)
                    P.barrier()
                for q in range(4):
                    wk, wk_res = w_acquire()
                    for jj in range(2):
                        j = 2 * q + jj
                        b = bank_rr.next()
                        for dc in range(8):
                            mm(ps[:, b, 0:MEM], wk[:, dc, jj * 128:(jj + 1) * 128], hmT[:, dc, :], dc == 0, dc == 7,
                               [wk_res, ("hmT", dc)], [("ps", b)])
                        act(xkT[:, j, :], ps[:, b, 0:MEM], AF.Identity, [("ps", b)], [("xkT", j)])
                for q in range(4):
                    wv_, wv_res = w_acquire()
                    for mc in range(2):
                        b = bank_rr.next()
                        for dc in range(8):
                            mm(ps[:, b, 0:256], hmT[:, dc, mc * 128:(mc + 1) * 128], wv_[:, dc, 0:256], dc == 0, dc == 7,
                               [wv_res, ("hmT", dc)], [("ps", b)])
                        act(xv[:, mc, q * 256:(q + 1) * 256], ps[:, b, 0:256], AF.Identity, [("ps", b)], [("xv", mc, q)])
                for q in range(4):
                    wq_, wq_res = w_acquire()
                    for jj in range(2):
                        j = 2 * q + jj
                        for t in range(NTT):
                            b = proj_fm(wq_, wq_res, jj * 128, Asl, Ar, 8, t)
                            act(xq[:, j, t * TT:(t + 1) * TT], ps[:, b, :], AF.Identity, [("ps", b)], [("xq", j, t)],
                                scale=1.0 / 16)
                hi_rr = RR(range(4, 8))
                items = [(t, hh) for t in range(NTT) for hh in range(4)]
                xpend = []

                def xa_scores(t, hh):
                    pts = []
                    for mc in range(2):
                        b = lo_rr.next()
                        for dk in range(2):
                            mm(ps[:, b, :], xkT[:, 2 * hh + dk, mc * 128:(mc + 1) * 128],
                               xq[:, 2 * hh + dk, t * TT:(t + 1) * TT], dk == 0, dk == 1,
                               [("xkT", 2 * hh + dk), ("xq", 2 * hh + dk, t)], [("ps", b)])
                        pi = Pt_rr.next()
                        act(Ptp[pi][:], ps[:, b, :], AF.Exp, [("ps", b)], [("Pt", pi)])
                        pts.append(pi)
                    return pts

                def xa_pv(t, hh, pts):
                    bd = hi_rr.next()
                    for mc in range(2):
                        mm(ps[:, bd, :], ones_bf[:], Ptp[pts[mc]][:], mc == 0, mc == 1,
                           ["ones", ("Pt", pts[mc])], [("ps", bd)])
                    ri = rstd_rr.next()
                    P.add("dve", lambda h, o=rstd[ri][:], i_=ps[:, bd, :]: h.reciprocal(out=o, in_=i_),
                          [("ps", bd)], [("rstd", ri)])
                    for dch in range(2):
                        b = hi_rr.next()
                        for mc in range(2):
                            mm(ps[:, b, :], xv[:, mc, hh * 256 + dch * 128: hh * 256 + (dch + 1) * 128], Ptp[pts[mc]][:],
                               mc == 0, mc == 1, [("xv", mc, hh), ("Pt", pts[mc])], [("ps", b)])
                        tt_(xq[:, 2 * hh + dch, t * TT:(t + 1) * TT], ps[:, b, :], rstd[ri][:], ALU.mult,
                            [("ps", b), ("rstd", ri)], [("xq", 2 * hh + dch, t)])

                for i in range(len(items) + 1):
                    if i < len(items):
                        xpend.append(xa_scores(*items[i]))
                    if i >= 1:
                        xa_pv(items[i - 1][0], items[i - 1][1], xpend[i - 1])
                for q in range(4):
                    wo, wo_res = w_acquire()
                    for jj in range(2):
                        j = 2 * q + jj
                        for t in range(NTT):
                            b = proj_fm(wo, wo_res, jj * 128, lambda kc, t_: xq[:, kc, t_ * TT:(t_ + 1) * TT],
                                        lambda kc, t_: ("xq", kc, t_), 8, t)
                            tt_(xsl(j, t), ps[:, b, :], xsl(j, t), ALU.add, [("ps", b), xr(j, t)], [xr(j, t)])
            P.barrier()
            if l == 0:
                tap("xat")

            rmsnorm(xsl, xr, vb + 52, Asl, Ar, NTT, TT)
            with ExitStack() as ff:
                hid = sb(ff, "hid", [128, 8, S], BF16)
                for fg in range(4):
                    for q in range(4):
                        w1, w1_res = w_acquire()
                        for jj in range(2):
                            fcl = 2 * q + jj
                            for t in range(NTT):
                                b = proj_fm(w1, w1_res, jj * 128, Asl, Ar, 8, t)
                                ti = T_rr.next()
                                act(Tp[ti][:], ps[:, b, :], AF.Relu, [("ps", b)], [("T", ti)])
                                tt_(hid[:, fcl, t * TT:(t + 1) * TT], Tp[ti][:], Tp[ti][:], ALU.mult,
                                    [("T", ti)], [("hid", fcl, t)])
                    for q in range(4):
                        w2, w2_res = w_acquire()
                        for jj in range(2):
                            j = 2 * q + jj
                            for t in range(NTT):
                                b = proj_fm(w2, w2_res, jj * 128, lambda kc, t_: hid[:, kc, t_ * TT:(t_ + 1) * TT],
                                            lambda kc, t_: ("hid", kc, t_), 8, t)
                                tt_(xsl(j, t), ps[:, b, :], xsl(j, t), ALU.add, [("ps", b), xr(j, t)], [xr(j, t)])
            P.barrier()
            if l == 0:
                tap("ffn")

        rmsnorm(xsl, xr, VPL * DEPTH, xsl, xr, NTT, TT)
        for dc in range(8):
            dma("sp", outT[dc], xres[:, dc, :], [xr(dc, t) for t in range(NTT)], [])
        cnt = P.emit(nc)
        nc._cnt = cnt
        nc._nops = len(P.ops)
    return nc


def host_consts():
    cst = np.zeros((128, NCST), np.float32)
    cst[:, C_ID:C_ID + 128] = np.eye(128, dtype=np.float32)
    k = np.arange(128)
    cst[:, C_U:C_U + 128] = (k[:, None] <= k[None, :]).astype(np.float32)
    wins = {(0, 0): 2, (0, 1): 4, (1, 0): 8, (1, 1): 16}
    t = np.arange(16)
    for fc in range(2):
        for half in range(2):
            w = wins[(fc, half)]
            cst[half * 64:(half + 1) * 64, C_INV + fc * 16:C_INV + (fc + 1) * 16] = \
                (1.0 / np.minimum(t + 1, w)).astype(np.float32)[None, :]
    for h in range(8):
        cst[h, C_SEL + h * 128:C_SEL + (h + 1) * 128] = -1.0
    for kk in range(128):
        cst[kk, C_SWAP + (kk + 64) % 128] = 1.0
    cst[:, C_NEG:C_NEG + 128] = np.where(k[:, None] > k[None, :], -30000.0, 0.0).astype(np.float32)
    return cst


def pack_vecs(inp):
    v = np.zeros((128, NV), np.float32)

    def fm(a):
        a = np.asarray(a, np.float32)
        return a.reshape(-1, 128).T

    for l in range(DEPTH):
        b = l * VPL
        v[:, b + 0:b + 8] = fm(inp["norm_mix_g"][l])
        v[:, b + 8:b + 32] = fm(inp["b_gate"][l])
        v[:, b + 32:b + 34] = fm(inp["pool_scale"][l])
        v[:, b + 34:b + 36] = fm(inp["sgu_norm_g"][l])
        v[:, b + 36:b + 44] = fm(inp["norm_xattn_g"][l])
        v[:, b + 44:b + 52] = fm(inp["norm_mem_g"][l])
        v[:, b + 52:b + 60] = fm(inp["norm_ffn_g"][l])
        for r in range(3):
            v[32 * r:32 * r + 8, b + 60] = np.asarray(inp["b_forget"][l], np.float32)
    v[:, VPL * DEPTH:VPL * DEPTH + 8] = fm(inp["final_norm_g"])
    return v


def shared_inputs(inp):
    f = lambda k: np.ascontiguousarray(np.asarray(inp[k], np.float32))
    sgu_b = np.asarray(inp["sgu_b"], np.float32)
    sgub4 = np.zeros((DEPTH, 2, 128, 512), np.float32)
    for fc in range(2):
        for gi in range(2):
            sgub4[:, fc, gi * 64:(gi + 1) * 64, :] = np.tile(sgu_b[:, 2 * fc + gi, :], (1, 4))[:, None, :]
    pw = np.asarray(inp["pool_w"], np.float32)
    pool_wbd = np.zeros((DEPTH, 128, 2, 128), np.float32)
    for g in range(4):
        gi = g % 2
        pool_wbd[:, gi * 64:(gi + 1) * 64, g // 2, gi * 64:(gi + 1) * 64] = pw[:, g]
    return {
        "w_in": f("w_in"), "pool_wbd": pool_wbd,
        "sgu_wT": np.ascontiguousarray(np.asarray(inp["sgu_w"], np.float32).transpose(0, 1, 3, 2)),
        "sgub4": sgub4,
        "w_branch_a": f("w_branch_a"), "w_branch_b": f("w_branch_b"), "w_branch_c": f("w_branch_c"),
        "w_out": f("w_out"), "w_xq": f("w_xq"), "w_xkv": f("w_xkv"), "w_xo": f("w_xo"),
        "w_ff1": f("w_ff1"), "w_ff2": f("w_ff2"),
        "vecs": pack_vecs(inp), "cst": host_consts(),
    }


def core_inputs(inp, b):
    x = np.asarray(inp["x"], np.float32)[b]
    m = np.asarray(inp["mem"], np.float32)[b]
    return {
        "xT": np.ascontiguousarray(x.T).reshape(8, 128, S),
        "memT": np.ascontiguousarray(m.T).reshape(8, 128, MEM),
    }


_NC_CACHE = {}


def kernel(**inputs):
    if "nc" not in _NC_CACHE:
        _NC_CACHE["nc"] = build()
    nc = _NC_CACHE["nc"]
    shared = shared_inputs(inputs)
    in_maps = []
    for b in range(8):
        m = dict(shared)
        m.update(core_inputs(inputs, b))
        in_maps.append(m)
    res = run_bass_kernel_spmd(nc, in_maps, core_ids=list(range(8)))
    out = np.empty((8, S, D), np.float32)
    for b in range(8):
        out[b] = res.results[b]["outT"].reshape(D, S).T
    return out
```

```python
import numpy as np
from contextlib import ExitStack
import concourse.bass as bass
import concourse.mybir as mybir
from concourse.bass_utils import run_bass_kernel_spmd

F32 = mybir.dt.float32
BF16 = mybir.dt.bfloat16
AF = mybir.ActivationFunctionType
ALU = mybir.AluOpType

ENGS = ("pe", "act", "dve", "pool", "sp")
N_DMA_SEMS = 24

D = 1024
S = 2048
DEPTH = 2
MEM = 256
TT = 512
NTT = S // TT
OFF_A, OFF_Q, OFF_K, OFF_V, OFF_F, OFF_C, OFF_G = 0, 256, 768, 1280, 1792, 1800, 2312
N_IN = 5384
EPS = 1e-6
SLOTW = 384
NSLOT = 3
VPL = 61
NV = VPL * DEPTH + 8
C_ID, C_U, C_INV, C_SEL, C_SWAP, C_NEG, NCST = 0, 128, 256, 288, 1312, 1440, 1568


class Op:
    __slots__ = ("idx", "eng", "fn", "deps", "dma", "need_sig", "sig", "prev_use", "group")

    def __init__(self, idx, eng, fn, deps, dma, group):
        self.idx = idx
        self.eng = eng
        self.fn = fn
        self.deps = deps
        self.dma = dma
        self.need_sig = dma
        self.sig = None
        self.prev_use = None
        self.group = group


class Prog:
    def __init__(self, same_engine_sync=True):
        self.ops = []
        self.last_w = {}
        self.readers = {}
        self.same_engine_sync = same_engine_sync
        self.barrier_deps = {}
        self.last_on = {}
        self.open_dma = []

    def add(self, eng, fn, reads=(), writes=(), dma=False, group=None, persistent=False):
        idx = len(self.ops)
        deps = set()
        for r in reads:
            w = self.last_w.get(r)
            if w is not None:
                deps.update(w[1])
        for r in writes:
            w = self.last_w.get(r)
            if w is not None:
                if not (group is not None and w[0] == group):
                    deps.update(w[1])
            for rd in self.readers.get(r, ()):
                deps.add(rd)
        for r in reads:
            self.readers.setdefault(r, []).append(idx)
        for r in writes:
            w = self.last_w.get(r)
            if group is not None and w is not None and w[0] == group:
                w[1].append(idx)
            else:
                self.last_w[r] = (group, [idx])
                self.readers[r] = []
        if eng in self.barrier_deps:
            deps.update(self.barrier_deps.pop(eng))
        deps.discard(idx)
        self.ops.append(Op(idx, eng, fn, deps, dma, group))
        self.last_on[eng] = idx
        if dma and not persistent:
            self.open_dma.append(idx)
        return idx

    def barrier(self):
        deps = set(self.open_dma)
        for e, i in self.last_on.items():
            if not self.ops[i].dma:
                deps.add(i)
        self.open_dma = []
        for e in ENGS:
            self.barrier_deps.setdefault(e, set()).update(deps)

    def emit(self, nc, final_wait_eng="sp"):
        ops = self.ops
        for op in ops:
            nd = set()
            for d in op.deps:
                dop = ops[d]
                if (not dop.dma) and (not op.dma) and dop.eng == op.eng:
                    if op.eng == "pe" or not self.same_engine_sync:
                        continue
                nd.add(d)
            best = {}
            keep = set()
            for d in nd:
                dop = ops[d]
                if dop.dma:
                    keep.add(d)
                elif best.get(dop.eng, -1) < d:
                    best[dop.eng] = d
            keep.update(best.values())
            op.deps = keep
            for d in keep:
                ops[d].need_sig = True
        cnt = {e: 0 for e in ENGS}
        dma_use = [0] * N_DMA_SEMS
        half = N_DMA_SEMS // 2
        dma_rr_q = {"pool": 0, "sp": 0}
        for op in ops:
            if op.dma:
                qn = "pool" if op.eng == "pool" else "sp"
                s = dma_rr_q[qn] + (half if qn == "pool" else 0)
                dma_rr_q[qn] = (dma_rr_q[qn] + 1) % half
                op.prev_use = dma_use[s]
                dma_use[s] += 1
                op.sig = (("dma", s), 16 * dma_use[s])
            elif op.need_sig:
                cnt[op.eng] += 1
                op.sig = (("eng", op.eng), cnt[op.eng])
        with ExitStack() as es:
            sems = {}
            for e in ENGS:
                sems[("eng", e)] = es.enter_context(nc.semaphore("sem_" + e))
            for i in range(N_DMA_SEMS):
                sems[("dma", i)] = es.enter_context(nc.semaphore("sem_dma%d" % i))
            block = es.enter_context(nc.Block())
            streams = {e: [op for op in ops if op.eng == e] for e in ENGS}
            final = {}
            for op in ops:
                if op.dma:
                    k, v = op.sig
                    final[k] = max(final.get(k, 0), v)

            def run_stream(eng_name, h):
                known = {}
                for op in streams[eng_name]:
                    waits = {}
                    for d in op.deps:
                        k, v = ops[d].sig
                        if waits.get(k, 0) < v:
                            waits[k] = v
                    if op.dma and op.prev_use:
                        k = op.sig[0]
                        v = 16 * op.prev_use
                        if waits.get(k, 0) < v:
                            waits[k] = v
                    for k, v in waits.items():
                        if known.get(k, 0) >= v:
                            continue
                        h.wait_ge(sems[k], v)
                        known[k] = v
                    ins = op.fn(h)
                    if op.sig is not None:
                        k, v = op.sig
                        ins.then_inc(sems[k], 16 if op.dma else 1)
                if eng_name == final_wait_eng:
                    for k, v in final.items():
                        if known.get(k, 0) < v:
                            h.wait_ge(sems[k], v)

            @block.tensor
            def _(h):
                run_stream("pe", h)

            @block.scalar
            def _(h):
                run_stream("act", h)

            @block.vector
            def _(h):
                run_stream("dve", h)

            @block.gpsimd
            def _(h):
                run_stream("pool", h)

            @block.sync
            def _(h):
                run_stream("sp", h)
        return cnt


def build(n_layers=DEPTH, taps=(), same_engine_sync=True):
    nc = bass.Bass("TRN2", target_bir_lowering=False)

    def din(name, shape):
        return nc.dram_tensor(name, list(shape), F32, kind="ExternalInput").ap()

    xT = din("xT", [8, 128, S])
    memT = din("memT", [8, 128, MEM])
    w_in = din("w_in", [DEPTH, D, N_IN])
    pool_wbd = din("pool_wbd", [DEPTH, 128, 2, 128])
    sgu_wT = din("sgu_wT", [DEPTH, 4, 128, 128])
    sgub4 = din("sgub4", [DEPTH, 2, 128, 512])
    w_ba = din("w_branch_a", [DEPTH, 256, D])
    w_bb = din("w_branch_b", [DEPTH, 512, D])
    w_bc = din("w_branch_c", [DEPTH, 256, D])
    w_out = din("w_out", [DEPTH, D, D])
    w_xq = din("w_xq", [DEPTH, D, D])
    w_xkv = din("w_xkv", [DEPTH, D, 2 * D])
    w_xo = din("w_xo", [DEPTH, D, D])
    w_ff1 = din("w_ff1", [DEPTH, D, 4 * D])
    w_ff2 = din("w_ff2", [DEPTH, 4 * D, D])
    vecs_d = din("vecs", [128, NV])
    cst_d = din("cst", [128, NCST])
    outT = nc.dram_tensor("outT", [8, 128, S], F32, kind="ExternalOutput").ap()
    tap_d = {}
    for tname in taps:
        tap_d[tname] = nc.dram_tensor("tap_" + tname, [8, 128, S], F32, kind="ExternalOutput").ap()

    P = Prog(same_engine_sync=same_engine_sync)

    with ExitStack() as outer:
        _uid = [0]

        def sb(es, name, shape, dt):
            _uid[0] += 1
            return es.enter_context(nc.sbuf_tensor("s%d_%s" % (_uid[0], name), list(shape), dt))

        ps = outer.enter_context(nc.psum_tensor("ps", [128, 8, 512], F32))
        xres = sb(outer, "xres", [128, 8, S], F32)
        A = sb(outer, "A", [128, 8, S], BF16)
        wsl = [sb(outer, "wsl%d" % i, [128, 8, SLOTW], BF16) for i in range(NSLOT)]
        vecs = sb(outer, "vecs", [128, NV], F32)
        cst = sb(outer, "cst", [128, NCST], F32)
        halfb = sb(outer, "halfb", [128, DEPTH * 24], F32)
        negbf = sb(outer, "negbf", [128, DEPTH], F32)
        ones_bf = sb(outer, "ones_bf", [128, 128], BF16)
        wf3 = sb(outer, "wf3", [128, 8, 72], BF16)
        ident_bf = sb(outer, "ident_bf", [128, 128], BF16)
        negmask_bf = sb(outer, "negmask_bf", [128, 128], BF16)
        Tp = [sb(outer, "T%d" % i, [128, 512], F32) for i in range(3)]
        Ptp = [sb(outer, "Pt%d" % i, [128, 512], BF16) for i in range(4)]
        rstd = [sb(outer, "rstd%d" % i, [128, 512], F32) for i in range(2)]
        sqp = [sb(outer, "sq%d" % i, [128, 512], BF16) for i in range(2)]

        ident = cst[:, C_ID:C_ID + 128]
        Umat = cst[:, C_U:C_U + 128]

        def act(out, in_, func, reads, writes, **kw):
            P.add("act", lambda h: h.activation(out=out, in_=in_, func=func, **kw), reads, writes)

        def mm(out, lhsT, rhs, start, stop, reads, writes, **kw):
            P.add("pe", lambda h: h.matmul(out, lhsT=lhsT, rhs=rhs, start=start, stop=stop, **kw), reads, writes)

        def stt(out, in0, scalar, in1, op0, op1, reads, writes, **kw):
            P.add("dve", lambda h: h.scalar_tensor_tensor(out=out, in0=in0, scalar=scalar, in1=in1, op0=op0, op1=op1, **kw), reads, writes)

        def tt_(out, in0, in1, op, reads, writes, eng="dve"):
            P.add(eng, lambda h: h.tensor_tensor(out=out, in0=in0, in1=in1, op=op), reads, writes)

        def ts(out, in0, s1, s2, op0, op1, reads, writes, eng="dve"):
            P.add(eng, lambda h: h.tensor_scalar(out=out, in0=in0, scalar1=s1, scalar2=s2, op0=op0, op1=op1), reads, writes)

        def recip(out, in_, reads, writes):
            P.add("dve", lambda h: h.reciprocal(out=out, in_=in_), reads, writes)

        def memset(ap, val, writes, eng="dve"):
            P.add(eng, lambda h: h.memset(ap, val), (), writes)

        def dma(eng, out, in_, reads, writes, group=None, persistent=False):
            P.add(eng, lambda h: h.dma_start(out=out, in_=in_), reads, writes, dma=True, group=group, persistent=persistent)

        class RR:
            def __init__(self, items):
                self.items = list(items)
                self.i = 0

            def next(self):
                v = self.items[self.i % len(self.items)]
                self.i += 1
                return v

        T_rr = RR(range(3))
        Pt_rr = RR(range(4))
        sq_rr = RR(range(2))
        rstd_rr = RR(range(2))

        def wsrc(ap2d):
            return ap2d.rearrange("(kc p) n -> p kc n", p=128)

        wsched = []

        def sched_layer(l):
            for hc in range(4):
                wsched.append([(0, 8, i * 128, (i + 1) * 128, w_in[l, :, off + hc * 128: off + (hc + 1) * 128])
                               for i, off in enumerate((OFF_Q, OFF_K, OFF_V))])
            wsched.append([(0, 8, 0, 256, w_in[l, :, OFF_A:OFF_A + 256])])
            wsched.append([(0, 8, 0, 256, w_in[l, :, OFF_C:OFF_C + 256])])
            wsched.append([(0, 8, 0, 256, w_in[l, :, OFF_C + 256:OFF_C + 512])])
            for j in range(8):
                wsched.append([(0, 8, br * 128, (br + 1) * 128,
                                w_in[l, :, OFF_G + br * 1024 + j * 128: OFF_G + br * 1024 + (j + 1) * 128])
                               for br in range(3)])
                wsched.append([(0, 2, 0, 128, w_ba[l, :, j * 128:(j + 1) * 128]),
                               (2, 6, 0, 128, w_bb[l, :, j * 128:(j + 1) * 128]),
                               (6, 8, 0, 128, w_bc[l, :, j * 128:(j + 1) * 128])])
            for q in range(4):
                wsched.append([(0, 8, 0, 256, w_out[l, :, q * 256:(q + 1) * 256])])
            for q in range(8):
                wsched.append([(0, 8, 0, 256, w_xkv[l, :, q * 256:(q + 1) * 256])])
            for q in range(4):
                wsched.append([(0, 8, 0, 256, w_xq[l, :, q * 256:(q + 1) * 256])])
            for q in range(4):
                wsched.append([(0, 8, 0, 256, w_xo[l, :, q * 256:(q + 1) * 256])])
            for fg in range(4):
                for q in range(4):
                    wsched.append([(0, 8, 0, 256, w_ff1[l, :, fg * 1024 + q * 256: fg * 1024 + (q + 1) * 256])])
                for q in range(4):
                    wsched.append([(0, 8, 0, 256, w_ff2[l, fg * 1024:(fg + 1) * 1024, q * 256:(q + 1) * 256])])

        for l in range(n_layers):
            sched_layer(l)
        wstate = {"issued": 0, "next": 0}

        def w_issue_upto(n):
            while wstate["issued"] < min(n, len(wsched)):
                i = wstate["issued"]
                slot = i % NSLOT
                for (k0, k1, c0, c1, src) in wsched[i]:
                    dma("pool", wsl[slot][:, k0:k1, c0:c1], wsrc(src), [], [("ws", slot)], group=("w", i), persistent=True)
                wstate["issued"] += 1

        def w_acquire(held=0):
            i = wstate["next"]
            wstate["next"] += 1
            w_issue_upto(i - held + NSLOT)
            slot = i % NSLOT
            return wsl[slot], ("ws", slot)

        dma("sp", vecs[:], vecs_d, [], ["vecs"])
        dma("sp", cst[:], cst_d, [], ["cst"])
        for t in range(NTT):
            for dc in range(8):
                dma("sp", xres[:, dc, t * TT:(t + 1) * TT], xT[dc, :, t * TT:(t + 1) * TT], [], [("x", dc, t)])
        memset(ones_bf[:], 1.0, ["ones"])
        memset(wf3[:], 0.0, ["wf3z"])
        P.add("dve", lambda h: h.tensor_copy(ident_bf[:], cst[:, C_ID:C_ID + 128]), ["cst"], ["cbf"])
        P.add("dve", lambda h: h.tensor_copy(negmask_bf[:], cst[:, C_NEG:C_NEG + 128]), ["cst"], ["cbf"])
        for l in range(DEPTH):
            ts(halfb[:, l * 24:(l + 1) * 24], vecs[:, l * VPL + 8:l * VPL + 32], 0.5, None, ALU.mult, ALU.bypass,
               ["vecs"], [("halfb", l)])
            ts(negbf[:, l:l + 1], vecs[:, l * VPL + 60:l * VPL + 61], -1.0, None, ALU.mult, ALU.bypass,
               ["vecs"], [("negbf", l)])
        w_issue_upto(NSLOT)

        def rmsnorm(src, src_res, gcol0, dst, dst_res, ntiles, tw):
            for t in range(ntiles):
                for dc in range(8):
                    qi = sq_rr.next()
                    act(sqp[qi][:, :tw], src(dc, t), AF.Square, [src_res(dc, t)], [("sq", qi)])
                    mm(ps[:, t, :tw], ones_bf[:], sqp[qi][:, :tw], dc == 0, dc == 7,
                       [("sq", qi), "ones"], [("ps", t)])
                act(ps[:, t, :tw], ps[:, t, :tw], AF.Sqrt, [("ps", t)], [("ps", t)], scale=1.0 / D, bias=EPS)
                ri = rstd_rr.next()
                P.add("dve", lambda h, o=rstd[ri][:, :tw], i_=ps[:, t, :tw]: h.reciprocal(out=o, in_=i_),
                      [("ps", t)], [("rstd", ri)])
                for dc in range(8):
                    stt(dst(dc, t), src(dc, t), vecs[:, gcol0 + dc:gcol0 + dc + 1], rstd[ri][:, :tw],
                        ALU.mult, ALU.mult, [src_res(dc, t), ("rstd", ri), "vecs"], [dst_res(dc, t)])

        def xsl(dc, t):
            return xres[:, dc, t * TT:(t + 1) * TT]

        def Asl(dc, t):
            return A[:, dc, t * TT:(t + 1) * TT]

        def xr(dc, t):
            return ("x", dc, t)

        def Ar(dc, t):
            return ("A", dc, t)

        bank_rr = RR(range(8))
        lo_rr = RR(range(4))

        def proj_fm(wt, wres, c0, rhs_fn, rhs_res_fn, nk, t, extra_reads=(), rr=None):
            b = (rr or bank_rr).next()
            for kc in range(nk):
                mm(ps[:, b, :], wt[:, kc, c0:c0 + 128], rhs_fn(kc, t), kc == 0, kc == nk - 1,
                   [wres, rhs_res_fn(kc, t)] + list(extra_reads), [("ps", b)])
            return b

        def gelu_from_psum(src_ap, src_res, dst_ap, dst_res, width):
            t1 = T_rr.next()
            t2 = T_rr.next()
            T1 = Tp[t1][:, :width]
            T2 = Tp[t2][:, :width]
            act(T1, src_ap, AF.Identity, [src_res], [("T", t1)], scale=0.5)
            act(T2, src_ap, AF.Square, [src_res], [("T", t2)])
            ts(T2, T2, 0.044715, 1.0, ALU.mult, ALU.add, [("T", t2)], [("T", t2)])
            tt_(T2, T2, T1, ALU.mult, [("T", t1), ("T", t2)], [("T", t2)])
            act(T2, T2, AF.Tanh, [("T", t2)], [("T", t2)], scale=1.5957691216057308)
            stt(dst_ap, T2, 1.0, T1, ALU.add, ALU.mult, [("T", t1), ("T", t2)], [dst_res])

        def tap(name):
            if name in tap_d:
                for dc in range(8):
                    dma("sp", tap_d[name][dc], xres[:, dc, :], [("x", dc, t) for t in range(NTT)], [])

        for l in range(n_layers):
            vb = l * VPL
            rmsnorm(xsl, xr, vb + 0, Asl, Ar, NTT, TT)

            with ExitStack() as mix:
                ao = sb(mix, "ao", [128, 4, S], BF16)

                with ExitStack() as sub:
                    P3 = sb(sub, "P3", [72, S], BF16)
                    qTzs = [[sb(sub, "qTz%d_%d" % (s_, i), [128, S], BF16) for i in range(2)] for s_ in range(2)]
                    kTzs = [[sb(sub, "kTz%d_%d" % (s_, i), [128, S], BF16) for i in range(2)] for s_ in range(2)]
                    for s_ in range(2):
                        for hi in range(2):
                            z0 = 64 if hi == 0 else 0
                            me = "dve" if s_ == 0 else "pool"
                            memset(kTzs[s_][hi][z0:z0 + 64, :], 0.0, [("kTz", s_, hi, "pad")], eng=me)
                            memset(qTzs[s_][hi][z0:z0 + 64, :], 0.0, [("qTz", s_, hi, "pad")], eng=me)
                            memset(kTzs[s_][hi][z0:z0 + 3, :], -1.0, [("kTz", s_, hi, "pad")], eng=me)
                            memset(qTzs[s_][hi][z0:z0 + 6, :], 1.0, [("qTz", s_, hi, "pad")], eng=me)
                    with ExitStack() as fsub:
                        E2 = sb(fsub, "E2", [72, S], F32)
                        tmpb = sb(fsub, "tmpb", [72, S], BF16)
                        for r in range(3):
                            dma("pool", wf3[:, :, 32 * r:32 * r + 8], wsrc(w_in[l, :, OFF_F:OFF_F + 8]), ["wf3z"], [("wf3", r)])
                        for t in range(NTT):
                            for dc in range(8):
                                mm(ps[0:72, t, :], wf3[:, dc, :], Asl(dc, t), dc == 0, dc == 7,
                                   ["wf3z", ("wf3", 0), ("wf3", 1), ("wf3", 2), Ar(dc, t)], [("ps", t)])
                        psr4 = [("ps", t) for t in range(4)]
                        act(ps[0:72, 0:4, :], ps[0:72, 0:4, :], AF.Exp, psr4 + [("negbf", l)], psr4,
                            scale=-1.0, bias=negbf[0:72, l:l + 1])
                        act(E2[:].rearrange("p (a b) -> p a b", b=512), ps[0:72, 0:4, :], AF.Ln, psr4, ["E2"], bias=1.0)
                        P.add("dve", lambda h, o=E2[:]: h.tensor_tensor_scan(out=o, data0=o, data1=o, initial=0.0,
                                                                              op0=ALU.add, op1=ALU.max), ["E2"], ["E2"])

                        def cp(o, i_, rd, wr):
                            P.add("dve", lambda h: h.tensor_copy(o, i_), rd, wr)

                        cp(P3[0:8, :], E2[0:8, :], ["E2"], [("P3", 0)])
                        cp(tmpb[32:40, :], E2[32:40, :], ["E2"], [("tmpb", 1)])
                        tt_(E2[32:40, :], E2[32:40, :], tmpb[32:40, :], ALU.subtract, ["E2", ("tmpb", 1)], [("E2m", 1)])
                        cp(P3[32:40, :], E2[32:40, :], [("E2m", 1)], [("P3", 1)])
                        cp(tmpb[64:72, :], E2[64:72, :], ["E2"], [("tmpb", 2)])
                        tt_(E2[64:72, :], E2[64:72, :], tmpb[64:72, :], ALU.subtract, ["E2", ("tmpb", 2)], [("E2m", 2)])
                        cp(tmpb[64:72, :], E2[64:72, :], [("E2m", 2)], [("tmpb", 2)])
                        tt_(E2[64:72, :], E2[64:72, :], tmpb[64:72, :], ALU.subtract, [("E2m", 2), ("tmpb", 2)], [("E2m", 2)])
                        cp(P3[64:72, :], E2[64:72, :], [("E2m", 2)], [("P3", 2)])
                    P.barrier()
                    vzs = [[sb(sub, "vz%d_%d" % (s_, i), [128, 16, 128], BF16) for i in range(2)] for s_ in range(2)]
                    for s_ in range(2):
                        me = "dve" if s_ == 0 else "pool"
                        memset(vzs[s_][0][:, :, 64:128], 1.0, [("vzp", s_, 0)], eng=me)
                        memset(vzs[s_][1][:, :, 0:64], 1.0, [("vzp", s_, 1)], eng=me)
                    p3r = [("P3", i) for i in range(3)]
                    wq_cur = {}

                    def attn_prep(hc):
                        s_ = hc % 2
                        wq_cur[hc] = w_acquire()
                        for hi in range(2):
                            hh = 2 * hc + hi
                            z0 = 64 if hi == 0 else 0
                            for r in range(3):
                                dma("sp", qTzs[s_][hi][z0 + r:z0 + r + 1, :], P3[32 * r + hh:32 * r + hh + 1, :],
                                    p3r + [("qTz", s_, hi, "pad")], [("qTz", s_, hi, "F", r)])
                                dma("sp", kTzs[s_][hi][z0 + 3 + r:z0 + 4 + r, :], P3[32 * r + hh:32 * r + hh + 1, :],
                                    p3r + [("kTz", s_, hi, "pad")], [("kTz", s_, hi, "F", r)])

                    def attn_proj(hc, t):
                        s_ = hc % 2
                        wq, wq_res = wq_cur[hc]
                        b = proj_fm(wq, wq_res, 0, Asl, Ar, 8, t, rr=lo_rr)
                        ts(qTzs[s_][0][0:64, t * TT:(t + 1) * TT], ps[0:64, b, :], 0.125, None, ALU.mult, ALU.bypass,
                           [("ps", b)], [("qT", s_, 0, t)])
                        ts(qTzs[s_][1][64:128, t * TT:(t + 1) * TT], ps[64:128, b, :], 0.125, None, ALU.mult, ALU.bypass,
                           [("ps", b)], [("qT", s_, 1, t)])
                        b = proj_fm(wq, wq_res, 128, Asl, Ar, 8, t, rr=lo_rr)
                        P.add("dve", lambda h, o=kTzs[s_][0][0:64, t * TT:(t + 1) * TT], i_=ps[0:64, b, :]: h.tensor_copy(o, i_),
                              [("ps", b)], [("kT", s_, 0, t)])
                        P.add("dve", lambda h, o=kTzs[s_][1][64:128, t * TT:(t + 1) * TT], i_=ps[64:128, b, :]: h.tensor_copy(o, i_),
                              [("ps", b)], [("kT", s_, 1, t)])
                        b = lo_rr.next()
                        for cc in range(4):
                            c = 4 * t + cc
                            for dc in range(8):
                                mm(ps[:, b, cc * 128:(cc + 1) * 128], A[:, dc, c * 128:(c + 1) * 128],
                                   wq[:, dc, 256:384], dc == 0, dc == 7, [wq_res, Ar(dc, t)], [("ps", b)])
                        psv = ps[:, b, :].rearrange("p (a c) -> p a c", c=128)
                        P.add("dve", lambda h, o=vzs[s_][0][:, 4 * t:4 * t + 4, 0:64], i_=psv[:, :, 0:64]: h.tensor_copy(o, i_),
                              [("ps", b)], [("vz", s_, 0, t)])
                        P.add("dve", lambda h, o=vzs[s_][1][:, 4 * t:4 * t + 4, 64:128], i_=psv[:, :, 64:128]: h.tensor_copy(o, i_),
                              [("ps", b)], [("vz", s_, 1, t)])

                    pending_norm = []
                    attn_prep(0)
                    for t in range(NTT):
                        attn_proj(0, t)
                    for hc in range(4):
                        s_ = hc % 2
                        qTz, kTz, vz = qTzs[s_], kTzs[s_], vzs[s_]
                        if hc + 1 < 4:
                            attn_prep(hc + 1)
                        for Qi in range(NTT):
                            xb = [4 + (Qi % 2) * 2, 5 + (Qi % 2) * 2]
                            nk = 4 * Qi + 4
                            steps = [(kj, hi) for kj in range(nk) for hi in range(2)]
                            LA = 3
                            pend = []
                            for i in range(len(steps) + LA):
                                if i == min(22, len(steps) + LA - 1) and pending_norm:
                                    pending_norm.pop(0)()
                                if i < len(steps):
                                    kj, hi = steps[i]
                                    n0 = max(0, kj - 4 * Qi) * 128
                                    N = TT - n0
                                    sbk = lo_rr.next()
                                    q0 = Qi * TT + n0
                                    diag = kj >= 4 * Qi
                                    frs = [(nm, s_, hi, "F", r) for nm in ("qTz", "kTz") for r in range(3)]
                                    mm(ps[:, sbk, 0:N], kTz[hi][:, kj * 128:(kj + 1) * 128], qTz[hi][:, q0:q0 + N], True, not diag,
                                       [("kT", s_, hi, kj // 4), ("kTz", s_, hi, "pad"), ("qTz", s_, hi, "pad"), ("qT", s_, hi, Qi)] + frs,
                                       [("ps", sbk)])
                                    if diag:
                                        mm(ps[:, sbk, 0:128], ident_bf[:], negmask_bf[:], False, True, ["cbf"], [("ps", sbk)])
                                    pi = Pt_rr.next()
                                    act(Ptp[pi][:, 0:N], ps[:, sbk, 0:N], AF.Exp, [("ps", sbk)], [("Pt", pi)])
                                    pend.append((kj, hi, n0, N, pi))
                                if i >= LA:
                                    kj, hi, n0, N, pi = pend[i - LA]
                                    mm(ps[:, xb[hi], n0:TT], vz[hi][:, kj, :], Ptp[pi][:, 0:N], kj == 0, kj == nk - 1,
                                       [("vz", s_, hi, kj // 4), ("vzp", s_, hi), ("Pt", pi)], [("ps", xb[hi])])
                            if hc + 1 < 4:
                                attn_proj(hc + 1, Qi)
                            ri = rstd_rr.next()
                            P.add("dve", lambda h, o=rstd[ri][0:64, :], i_=ps[0:64, xb[1], :]: h.reciprocal(out=o, in_=i_),
                                  [("ps", xb[1])], [("rstd", ri)])
                            P.add("dve", lambda h, o=rstd[ri][64:128, :], i_=ps[64:128, xb[0], :]: h.reciprocal(out=o, in_=i_),
                                  [("ps", xb[0])], [("rstd", ri)])

                            def norm_tail(ri=ri, xb=xb, hc=hc, Qi=Qi):
                                bs = lo_rr.next()
                                mm(ps[:, bs, :], cst[:, C_SWAP:C_SWAP + 128], rstd[ri][:], True, True,
                                   [("rstd", ri), "cst"], [("ps", bs)])
                                ti = T_rr.next()
                                act(Tp[ti][:], ps[:, bs, :], AF.Identity, [("ps", bs)], [("T", ti)])
                                tt_(ao[0:64, hc, Qi * TT:(Qi + 1) * TT], ps[0:64, xb[0], :], Tp[ti][0:64, :],
                                    ALU.mult, [("ps", xb[0]), ("T", ti)], [("ao", hc, Qi, 0)])
                                tt_(ao[64:128, hc, Qi * TT:(Qi + 1) * TT], ps[64:128, xb[1], :], Tp[ti][64:128, :],
                                    ALU.mult, [("ps", xb[1]), ("T", ti)], [("ao", hc, Qi, 1)])
                            pending_norm.append(norm_tail)
                    while pending_norm:
                        pending_norm.pop(0)()
                P.barrier()
                pa = sb(mix, "pa", [128, 2, S], BF16)
                sc = sb(mix, "sc", [128, 2, S], BF16)

                with ExitStack() as sub:
                    aT = sb(sub, "aT", [128, 16 + S], F32)
                    sA = sb(sub, "sA", [128, 16 + S], F32)
                    sB = sb(sub, "sB", [128, 16 + S], F32)
                    dT = sb(sub, "dT", [128, S], BF16)
                    wbd = sb(sub, "wbd", [128, 2, 128], BF16)
                    t16 = sb(sub, "t16", [128, 16], F32)
                    dma("pool", wbd[:], pool_wbd[l], [], ["wbd"])
                    for bufn, buf in (("aT", aT), ("sA", sA), ("sB", sB)):
                        memset(buf[:, 0:16], 0.0, [(bufn, "pad")])
                    wa, wa_res = w_acquire()
                    for fc in range(2):
                        for t in range(NTT):
                            b = proj_fm(wa, wa_res, fc * 128, Asl, Ar, 8, t)
                            act(aT[:, 16 + t * TT:16 + (t + 1) * TT], ps[:, b, :], AF.Identity,
                                [("ps", b)], [("aT", t)])
                        allr = [("aT", t) for t in range(NTT)] + [("aT", "pad")]
                        a_ = aT[:, 16:16 + S]
                        tt_(sA[:, 16:], a_, aT[:, 15:15 + S], ALU.add, allr + [("sA", "pad")], ["sA"])
                        if fc == 0:
                            lo, lo_res, lo_w = sA, "sA", 2
                            tt_(sB[:, 16:], sA[:, 16:], sA[:, 14:14 + S], ALU.add, ["sA", ("sA", "pad"), ("sB", "pad")], ["sB"])
                            hi, hi_res, hi_w = sB, "sB", 4
                        else:
                            tt_(sB[:, 16:], sA[:, 16:], sA[:, 14:14 + S], ALU.add, ["sA", ("sA", "pad"), ("sB", "pad")], ["sB"])
                            tt_(sA[:, 16:], sB[:, 16:], sB[:, 12:12 + S], ALU.add, ["sB", ("sB", "pad"), ("sA", "pad")], ["sA"])
                            lo, lo_res, lo_w = sA, "sA", 8
                            tt_(sB[:, 16:], sA[:, 16:], sA[:, 8:8 + S], ALU.add, ["sA", ("sA", "pad"), ("sB", "pad")], ["sB"])
                            hi, hi_res, hi_w = sB, "sB", 16
                        for (p0, buf, bres, win, hidx) in ((0, lo, lo_res, lo_w, 0), (64, hi, hi_res, hi_w, 1)):
                            stt(dT[p0:p0 + 64, :], buf[p0:p0 + 64, 16:], 1.0 / win, aT[p0:p0 + 64, 16:],
                                ALU.mult, ALU.subtract, allr + [bres], [("dT", hidx)])
                            tt_(t16[p0:p0 + 64, :], buf[p0:p0 + 64, 16:32],
                                cst[p0:p0 + 64, C_INV + fc * 16:C_INV + (fc + 1) * 16], ALU.mult,
                                [bres, "cst"], [("t16", hidx)])
                            tt_(dT[p0:p0 + 64, 0:16], t16[p0:p0 + 64, :], aT[p0:p0 + 64, 16:32], ALU.subtract,
                                allr + [("t16", hidx), ("dT", hidx)], [("dT", hidx)])
                        for t in range(NTT):
                            b = bank_rr.next()
                            mm(ps[:, b, :], wbd[:, fc, :], dT[:, t * TT:(t + 1) * TT], True, True,
                               [("dT", 0), ("dT", 1), "wbd"], [("ps", b)])
                            act(pa[:, fc, t * TT:(t + 1) * TT], ps[:, b, :], AF.Identity, [("ps", b), "vecs"],
                                [("pa", fc, t)], scale=vecs[:, vb + 32 + fc:vb + 33 + fc])
                P.barrier()

                with ExitStack() as sub:
                    uT = sb(sub, "uT", [128, 2, S], BF16)
                    gvb = sb(sub, "gvb", [128, 16, 256], BF16)
                    ss = sb(sub, "ss", [128, 16], F32)
                    rs16 = sb(sub, "rs16", [128, 16], F32)
                    gv = sb(sub, "gv", [128, 256], F32)
                    junk = sb(sub, "junk", [128, 256], F32)
                    wTs = sb(sub, "wTs", [128, 4, 128], F32)
                    wcm = sb(sub, "wcm", [128, 4, 128], BF16)
                    bT4 = sb(sub, "bT4", [128, 2, 512], F32)
                    for g in range(4):
                        dma("sp", wTs[:, g, :], sgu_wT[l, g], [], [("wTs", g)])
                        tt_(wcm[:, g, :], wTs[:, g, :], Umat, ALU.mult, [("wTs", g), "cst"], [("wcm", g)])
                    for fc in range(2):
                        dma("sp", bT4[:, fc, :], sgub4[l, fc], [], [("bT4", fc)])
                    wcu, wcu_res = w_acquire()
                    for fc in range(2):
                        for t in range(NTT):
                            b = proj_fm(wcu, wcu_res, fc * 128, Asl, Ar, 8, t)
                            gelu_from_psum(ps[:, b, :], ("ps", b), uT[:, fc, t * TT:(t + 1) * TT], ("uT", fc, t), 512)
                    wcv, wcv_res = w_acquire()
                    for c in range(16):
                        b = bank_rr.next()
                        for dc in range(8):
                            mm(ps[:, b, 0:256], A[:, dc, c * 128:(c + 1) * 128], wcv[:, dc, 0:256], dc == 0, dc == 7,
                               [wcv_res, Ar(dc, c // 4)], [("ps", b)])
                        gelu_from_psum(ps[:, b, 0:256], ("ps", b), gv[:], "gv", 256)
                        stt(junk[:], gv[:], 1.0, gv[:], ALU.mult, ALU.mult, ["gv"], ["junk", ("ss", c)],
                            accum_out=ss[:, c:c + 1])
                        act(gvb[:, c, :], gv[:], AF.Identity, ["gv"], [("gvb", c)])
                    ssr = [("ss", c) for c in range(16)]
                    act(rs16[:], ss[:], AF.Sqrt, ssr, ["rs16"], scale=1.0 / 256, bias=EPS)
                    recip(rs16[:], rs16[:], ["rs16"], ["rs16"])
                    for c in range(16):
                        ts(gvb[:, c, :], gvb[:, c, :], rs16[:, c:c + 1], None, ALU.mult, ALU.bypass,
                           [("gvb", c), "rs16"], [("gvb", c)])
                    for fc in range(2):
                        for t in range(NTT):
                            b = bank_rr.next()
                            for cc in range(4):
                                c = t * 4 + cc
                                for gi in range(2):
                                    g = 2 * fc + gi
                                    mm(ps[gi * 64:(gi + 1) * 64, b, cc * 128:(cc + 1) * 128],
                                       gvb[:, c, g * 64:(g + 1) * 64], wcm[:, g, :], True, True,
                                       [("gvb", c), ("wcm", g)], [("ps", b)], tile_position=(0, gi * 64))
                            ti = T_rr.next()
                            stt(Tp[ti][:], ps[:, b, :], vecs[:, vb + 34 + fc:vb + 35 + fc], bT4[:, fc, :],
                                ALU.mult, ALU.add, [("ps", b), "vecs", ("bT4", fc)], [("T", ti)])
                            tt_(sc[:, fc, t * TT:(t + 1) * TT], Tp[ti][:], uT[:, fc, t * TT:(t + 1) * TT], ALU.mult,
                                [("T", ti), ("uT", fc, t)], [("sc", fc, t)])
                P.barrier()

                P.barrier()
                for nm, buf, nch in (("pa", pa, 2), ("ao", ao, 4), ("sc", sc, 2)):
                    if l == 0 and nm in tap_d:
                        for kc in range(nch):
                            dma("pool", tap_d[nm][kc], buf[:, kc, :], [], [])
                P.barrier()

                with ExitStack() as sub:
                    mg = sb(sub, "mg", [128, 8, S], BF16)
                    for j in range(8):
                        wg, wg_res = w_acquire()
                        wb, wb_res = w_acquire(held=1)
                        for t in range(NTT):
                            gb = []
                            for br in range(3):
                                b = proj_fm(wg, wg_res, br * 128, Asl, Ar, 8, t)
                                gb.append(b)
                            tl = []
                            for br in range(3):
                                ti = T_rr.next()
                                tl.append(ti)
                                act(Tp[ti][:], ps[:, gb[br], :], AF.Tanh, [("ps", gb[br]), ("halfb", l)], [("T", ti)],
                                    scale=0.5, bias=halfb[:, l * 24 + br * 8 + j:l * 24 + br * 8 + j + 1])
                            srcs = [(pa, 0, 2, lambda kc, t_: ("pa", kc, t_)),
                                    (ao, 2, 4, None),
                                    (sc, 6, 2, lambda kc, t_: ("sc", kc, t_))]
                            for br, (buf, k0, nk, rf) in enumerate(srcs):
                                b = bank_rr.next()
                                for kc in range(nk):
                                    rr_ = [("ao", kc, t, 0), ("ao", kc, t, 1)] if rf is None else [rf(kc, t)]
                                    mm(ps[:, b, :], wb[:, k0 + kc, 0:128], buf[:, kc, t * TT:(t + 1) * TT], kc == 0, kc == nk - 1,
                                       [wb_res] + rr_, [("ps", b)])
                                ti = tl[br]
                                stt(Tp[ti][:], Tp[ti][:], 1.0, ps[:, b, :], ALU.add, ALU.mult, [("T", ti), ("ps", b)], [("T", ti)])
                            tt_(Tp[tl[0]][:], Tp[tl[0]][:], Tp[tl[1]][:], ALU.add, [("T", tl[0]), ("T", tl[1])], [("T", tl[0])])
                            tt_(mg[:, j, t * TT:(t + 1) * TT], Tp[tl[0]][:], Tp[tl[2]][:], ALU.add,
                                [("T", tl[0]), ("T", tl[2])], [("mg", j, t)])
                    for q in range(4):
                        wo, wo_res = w_acquire()
                        for jj in range(2):
                            j = 2 * q + jj
                            for t in range(NTT):
                                b = proj_fm(wo, wo_res, jj * 128, lambda kc, t_: mg[:, kc, t_ * TT:(t_ + 1) * TT],
                                            lambda kc, t_: ("mg", kc, t_), 8, t)
                                stt(xsl(j, t), ps[:, b, :], 0.5, xsl(j, t), ALU.mult, ALU.add, [("ps", b), xr(j, t)], [xr(j, t)])
            P.barrier()
            if l == 0:
                tap("mix")

            rmsnorm(xsl, xr, vb + 36, Asl, Ar, NTT, TT)
            with ExitStack() as xa:
                xq = sb(xa, "xq", [128, 8, S], BF16)
                hmT = sb(xa, "hmT", [128, 8, MEM], BF16)
                xkT = sb(xa, "xkT", [128, 8, MEM], BF16)
                xv = sb(xa, "xv", [128, 2, D], BF16)
                with ExitStack() as sub:
                    mT = sb(sub, "mT", [128, 8, MEM], F32)
                    for dc in range(8):
                        dma("sp", mT[:, dc, :], memT[dc], [], [("mT", dc)])
                    rmsnorm(lambda dc, t: mT[:, dc, :], lambda dc, t: ("mT", dc), vb + 44,
                            lambda dc, t: hmT[:, dc, :], lambda dc, t: ("hmT", dc), 1, MEM)
                    P.barrier()
                for q in range(4):
                    wk, wk_res = w_acquire()
                    for jj in range(2):
                        j = 2 * q + jj
                        b = bank_rr.next()
                        for dc in range(8):
                            mm(ps[:, b, 0:MEM], wk[:, dc, jj * 128:(jj + 1) * 128], hmT[:, dc, :], dc == 0, dc == 7,
                               [wk_res, ("hmT", dc)], [("ps", b)])
                        act(xkT[:, j, :], ps[:, b, 0:MEM], AF.Identity, [("ps", b)], [("xkT", j)])
                for q in range(4):
                    wv_, wv_res = w_acquire()
                    for mc in range(2):
                        b = bank_rr.next()
                        for dc in range(8):
                            mm(ps[:, b, 0:256], hmT[:, dc, mc * 128:(mc + 1) * 128], wv_[:, dc, 0:256], dc == 0, dc == 7,
                               [wv_res, ("hmT", dc)], [("ps", b)])
                        act(xv[:, mc, q * 256:(q + 1) * 256], ps[:, b, 0:256], AF.Identity, [("ps", b)], [("xv", mc, q)])
                for q in range(4):
                    wq_, wq_res = w_acquire()
                    for jj in range(2):
                        j = 2 * q + jj
                        for t in range(NTT):
                            b = proj_fm(wq_, wq_res, jj * 128, Asl, Ar, 8, t)
                            act(xq[:, j, t * TT:(t + 1) * TT], ps[:, b, :], AF.Identity, [("ps", b)], [("xq", j, t)],
                                scale=1.0 / 16)
                hi_rr = RR(range(4, 8))
                items = [(t, hh) for t in range(NTT) for hh in range(4)]
                xpend = []

                def xa_scores(t, hh):
                    pts = []
                    for mc in range(2):
                        b = lo_rr.next()
                        for dk in range(2):
                            mm(ps[:, b, :], xkT[:, 2 * hh + dk, mc * 128:(mc + 1) * 128],
                               xq[:, 2 * hh + dk, t * TT:(t + 1) * TT], dk == 0, dk == 1,
                               [("xkT", 2 * hh + dk), ("xq", 2 * hh + dk, t)], [("ps", b)])
                        pi = Pt_rr.next()
                        act(Ptp[pi][:], ps[:, b, :], AF.Exp, [("ps", b)], [("Pt", pi)])
                        pts.append(pi)
                    return pts

                def xa_pv(t, hh, pts):
                    bd = hi_rr.next()
                    for mc in range(2):
                        mm(ps[:, bd, :], ones_bf[:], Ptp[pts[mc]][:], mc == 0, mc == 1,
                           ["ones", ("Pt", pts[mc])], [("ps", bd)])
                    ri = rstd_rr.next()
                    P.add("dve", lambda h, o=rstd[ri][:], i_=ps[:, bd, :]: h.reciprocal(out=o, in_=i_),
                          [("ps", bd)], [("rstd", ri)])
                    for dch in range(2):
                        b = hi_rr.next()
                        for mc in range(2):
                            mm(ps[:, b, :], xv[:, mc, hh * 256 + dch * 128: hh * 256 + (dch + 1) * 128], Ptp[pts[mc]][:],
                               mc == 0, mc == 1, [("xv", mc, hh), ("Pt", pts[mc])], [("ps", b)])
                        tt_(xq[:, 2 * hh + dch, t * TT:(t + 1) * TT], ps[:, b, :], rstd[ri][:], ALU.mult,
                            [("ps", b), ("rstd", ri)], [("xq", 2 * hh + dch, t)])

                for i in range(len(items) + 1):
                    if i < len(items):
                        xpend.append(xa_scores(*items[i]))
                    if i >= 1:
                        xa_pv(items[i - 1][0], items[i - 1][1], xpend[i - 1])
                for q in range(4):
                    wo, wo_res = w_acquire()
                    for jj in range(2):
                        j = 2 * q + jj
                        for t in range(NTT):
                            b = proj_fm(wo, wo_res, jj * 128, lambda kc, t_: xq[:, kc, t_ * TT:(t_ + 1) * TT],
                                        lambda kc, t_: ("xq", kc, t_), 8, t)
                            tt_(xsl(j, t), ps[:, b, :], xsl(j, t), ALU.add, [("ps", b), xr(j, t)], [xr(j, t)])
            P.barrier()
            if l == 0:
                tap("xat")

            rmsnorm(xsl, xr, vb + 52, Asl, Ar, NTT, TT)
            with ExitStack() as ff:
                hid = sb(ff, "hid", [128, 8, S], BF16)
                for fg in range(4):
                    for q in range(4):
                        w1, w1_res = w_acquire()
                        for jj in range(2):
                            fcl = 2 * q + jj
                            for t in range(NTT):
                                b = proj_fm(w1, w1_res, jj * 128, Asl, Ar, 8, t)
                                ti = T_rr.next()
                                act(Tp[ti][:], ps[:, b, :], AF.Relu, [("ps", b)], [("T", ti)])
                                tt_(hid[:, fcl, t * TT:(t + 1) * TT], Tp[ti][:], Tp[ti][:], ALU.mult,
                                    [("T", ti)], [("hid", fcl, t)])
                    for q in range(4):
                        w2, w2_res = w_acquire()
                        for jj in range(2):
                            j = 2 * q + jj
                            for t in range(NTT):
                                b = proj_fm(w2, w2_res, jj * 128, lambda kc, t_: hid[:, kc, t_ * TT:(t_ + 1) * TT],
                                            lambda kc, t_: ("hid", kc, t_), 8, t)
                                tt_(xsl(j, t), ps[:, b, :], xsl(j, t), ALU.add, [("ps", b), xr(j, t)], [xr(j, t)])
            P.barrier()
            if l == 0:
                tap("ffn")

        rmsnorm(xsl, xr, VPL * DEPTH, xsl, xr, NTT, TT)
        for dc in range(8):
            dma("sp", outT[dc], xres[:, dc, :], [xr(dc, t) for t in range(NTT)], [])
        cnt = P.emit(nc)
        nc._cnt = cnt
        nc._nops = len(P.ops)
    return nc


def host_consts():
    cst = np.zeros((128, NCST), np.float32)
    cst[:, C_ID:C_ID + 128] = np.eye(128, dtype=np.float32)
    k = np.arange(128)
    cst[:, C_U:C_U + 128] = (k[:, None] <= k[None, :]).astype(np.float32)
    wins = {(0, 0): 2, (0, 1): 4, (1, 0): 8, (1, 1): 16}
    t = np.arange(16)
    for fc in range(2):
        for half in range(2):
            w = wins[(fc, half)]
            cst[half * 64:(half + 1) * 64, C_INV + fc * 16:C_INV + (fc + 1) * 16] = \
                (1.0 / np.minimum(t + 1, w)).astype(np.float32)[None, :]
    for h in range(8):
        cst[h, C_SEL + h * 128:C_SEL + (h + 1) * 128] = -1.0
    for kk in range(128):
        cst[kk, C_SWAP + (kk + 64) % 128] = 1.0
    cst[:, C_NEG:C_NEG + 128] = np.where(k[:, None] > k[None, :], -30000.0, 0.0).astype(np.float32)
    return cst


def pack_vecs(inp):
    v = np.zeros((128, NV), np.float32)

    def fm(a):
        a = np.asarray(a, np.float32)
        return a.reshape(-1, 128).T

    for l in range(DEPTH):
        b = l * VPL
        v[:, b + 0:b + 8] = fm(inp["norm_mix_g"][l])
        v[:, b + 8:b + 32] = fm(inp["b_gate"][l])
        v[:, b + 32:b + 34] = fm(inp["pool_scale"][l])
        v[:, b + 34:b + 36] = fm(inp["sgu_norm_g"][l])
        v[:, b + 36:b + 44] = fm(inp["norm_xattn_g"][l])
        v[:, b + 44:b + 52] = fm(inp["norm_mem_g"][l])
        v[:, b + 52:b + 60] = fm(inp["norm_ffn_g"][l])
        for r in range(3):
            v[32 * r:32 * r + 8, b + 60] = np.asarray(inp["b_forget"][l], np.float32)
    v[:, VPL * DEPTH:VPL * DEPTH + 8] = fm(inp["final_norm_g"])
    return v


def shared_inputs(inp):
    f = lambda k: np.ascontiguousarray(np.asarray(inp[k], np.float32))
    sgu_b = np.asarray(inp["sgu_b"], np.float32)
    sgub4 = np.zeros((DEPTH, 2, 128, 512), np.float32)
    for fc in range(2):
        for gi in range(2):
            sgub4[:, fc, gi * 64:(gi + 1) * 64, :] = np.tile(sgu_b[:, 2 * fc + gi, :], (1, 4))[:, None, :]
    pw = np.asarray(inp["pool_w"], np.float32)
    pool_wbd = np.zeros((DEPTH, 128, 2, 128), np.float32)
    for g in range(4):
        gi = g % 2
        pool_wbd[:, gi * 64:(gi + 1) * 64, g // 2, gi * 64:(gi + 1) * 64] = pw[:, g]
    return {
        "w_in": f("w_in"), "pool_wbd": pool_wbd,
        "sgu_wT": np.ascontiguousarray(np.asarray(inp["sgu_w"], np.float32).transpose(0, 1, 3, 2)),
        "sgub4": sgub4,
        "w_branch_a": f("w_branch_a"), "w_branch_b": f("w_branch_b"), "w_branch_c": f("w_branch_c"),
        "w_out": f("w_out"), "w_xq": f("w_xq"), "w_xkv": f("w_xkv"), "w_xo": f("w_xo"),
        "w_ff1": f("w_ff1"), "w_ff2": f("w_ff2"),
        "vecs": pack_vecs(inp), "cst": host_consts(),
    }


def core_inputs(inp, b):
    x = np.asarray(inp["x"], np.float32)[b]
    m = np.asarray(inp["mem"], np.float32)[b]
    return {
        "xT": np.ascontiguousarray(x.T).reshape(8, 128, S),
        "memT": np.ascontiguousarray(m.T).reshape(8, 128, MEM),
    }


_NC_CACHE = {}


def kernel(**inputs):
    if "nc" not in _NC_CACHE:
        _NC_CACHE["nc"] = build()
    nc = _NC_CACHE["nc"]
    shared = shared_inputs(inputs)
    in_maps = []
    for b in range(8):
        m = dict(shared)
        m.update(core_inputs(inputs, b))
        in_maps.append(m)
    res = run_bass_kernel_spmd(nc, in_maps, core_ids=list(range(8)))
    out = np.empty((8, S, D), np.float32)
    for b in range(8):
        out[b] = res.results[b]["outT"].reshape(D, S).T
    return out
```

```python
import numpy as np
from contextlib import ExitStack
import concourse.bass as bass
import concourse.mybir as mybir
from concourse.bass_utils import run_bass_kernel_spmd

F32 = mybir.dt.float32
BF16 = mybir.dt.bfloat16
AF = mybir.ActivationFunctionType
ALU = mybir.AluOpType

ENGS = ("pe", "act", "dve", "pool", "sp")
N_DMA_SEMS = 24

D = 1024
S = 2048
DEPTH = 2
MEM = 256
TT = 512
NTT = S // TT
OFF_A, OFF_Q, OFF_K, OFF_V, OFF_F, OFF_C, OFF_G = 0, 256, 768, 1280, 1792, 1800, 2312
N_IN = 5384
EPS = 1e-6
SLOTW = 384
NSLOT = 4
VPL = 61
NV = VPL * DEPTH + 8
C_ID, C_U, C_INV, C_SWAP, C_NEG, NCST = 0, 128, 256, 288, 416, 544


class Op:
    __slots__ = ("idx", "eng", "fn", "deps", "dma", "need_sig", "sig", "prev_use", "group")

    def __init__(self, idx, eng, fn, deps, dma, group):
        self.idx = idx
        self.eng = eng
        self.fn = fn
        self.deps = deps
        self.dma = dma
        self.need_sig = dma
        self.sig = None
        self.prev_use = None
        self.group = group


class Prog:
    def __init__(self, same_engine_sync=True):
        self.ops = []
        self.last_w = {}
        self.readers = {}
        self.same_engine_sync = same_engine_sync
        self.barrier_deps = {}
        self.last_on = {}
        self.open_dma = []

    def add(self, eng, fn, reads=(), writes=(), dma=False, group=None, persistent=False):
        idx = len(self.ops)
        deps = set()
        for r in reads:
            w = self.last_w.get(r)
            if w is not None:
                deps.update(w[1])
        for r in writes:
            w = self.last_w.get(r)
            if w is not None:
                if not (group is not None and w[0] == group):
                    deps.update(w[1])
            for rd in self.readers.get(r, ()):
                deps.add(rd)
        for r in reads:
            self.readers.setdefault(r, []).append(idx)
        for r in writes:
            w = self.last_w.get(r)
            if group is not None and w is not None and w[0] == group:
                w[1].append(idx)
            else:
                self.last_w[r] = (group, [idx])
                self.readers[r] = []
        if eng in self.barrier_deps:
            deps.update(self.barrier_deps.pop(eng))
        deps.discard(idx)
        self.ops.append(Op(idx, eng, fn, deps, dma, group))
        self.last_on[eng] = idx
        if dma and not persistent:
            self.open_dma.append(idx)
        return idx

    def barrier(self):
        deps = set(self.open_dma)
        for e, i in self.last_on.items():
            if not self.ops[i].dma:
                deps.add(i)
        self.open_dma = []
        for e in ENGS:
            self.barrier_deps.setdefault(e, set()).update(deps)

    def emit(self, nc, final_wait_eng="sp"):
        ops = self.ops
        for op in ops:
            nd = set()
            for d in op.deps:
                dop = ops[d]
                if (not dop.dma) and (not op.dma) and dop.eng == op.eng:
                    if op.eng == "pe" or not self.same_engine_sync:
                        continue
                nd.add(d)
            best = {}
            keep = set()
            for d in nd:
                dop = ops[d]
                if dop.dma:
                    keep.add(d)
                elif best.get(dop.eng, -1) < d:
                    best[dop.eng] = d
            keep.update(best.values())
            op.deps = keep
            for d in keep:
                ops[d].need_sig = True
        cnt = {e: 0 for e in ENGS}
        dma_use = [0] * N_DMA_SEMS
        half = N_DMA_SEMS // 2
        dma_rr_q = {"pool": 0, "sp": 0}
        for op in ops:
            if op.dma:
                qn = "pool" if op.eng == "pool" else "sp"
                s = dma_rr_q[qn] + (half if qn == "pool" else 0)
                dma_rr_q[qn] = (dma_rr_q[qn] + 1) % half
                op.prev_use = dma_use[s]
                dma_use[s] += 1
                op.sig = (("dma", s), 16 * dma_use[s])
            elif op.need_sig:
                cnt[op.eng] += 1
                op.sig = (("eng", op.eng), cnt[op.eng])
        with ExitStack() as es:
            sems = {}
            for e in ENGS:
                sems[("eng", e)] = es.enter_context(nc.semaphore("sem_" + e))
            for i in range(N_DMA_SEMS):
                sems[("dma", i)] = es.enter_context(nc.semaphore("sem_dma%d" % i))
            block = es.enter_context(nc.Block())
            streams = {e: [op for op in ops if op.eng == e] for e in ENGS}
            final = {}
            for op in ops:
                if op.dma:
                    k, v = op.sig
                    final[k] = max(final.get(k, 0), v)

            def run_stream(eng_name, h):
                known = {}
                for op in streams[eng_name]:
                    waits = {}
                    for d in op.deps:
                        k, v = ops[d].sig
                        if waits.get(k, 0) < v:
                            waits[k] = v
                    if op.dma and op.prev_use:
                        k = op.sig[0]
                        v = 16 * op.prev_use
                        if waits.get(k, 0) < v:
                            waits[k] = v
                    for k, v in waits.items():
                        if known.get(k, 0) >= v:
                            continue
                        h.wait_ge(sems[k], v)
                        known[k] = v
                    ins = op.fn(h)
                    if op.sig is not None:
                        k, v = op.sig
                        ins.then_inc(sems[k], 16 if op.dma else 1)
                if eng_name == final_wait_eng:
                    for k, v in final.items():
                        if known.get(k, 0) < v:
                            h.wait_ge(sems[k], v)

            @block.tensor
            def _(h):
                run_stream("pe", h)

            @block.scalar
            def _(h):
                run_stream("act", h)

            @block.vector
            def _(h):
                run_stream("dve", h)

            @block.gpsimd
            def _(h):
                run_stream("pool", h)

            @block.sync
            def _(h):
                run_stream("sp", h)
        return cnt


def build(n_layers=DEPTH, taps=(), same_engine_sync=True):
    nc = bass.Bass("TRN2", target_bir_lowering=False)

    def din(name, shape):
        return nc.dram_tensor(name, list(shape), F32, kind="ExternalInput").ap()

    xT = din("xT", [8, 128, S])
    memT = din("memT", [8, 128, MEM])
    w_in = din("w_in", [DEPTH, D, N_IN])
    pool_wbd = din("pool_wbd", [DEPTH, 128, 2, 128])
    sgu_wT = din("sgu_wT", [DEPTH, 4, 128, 128])
    sgub4 = din("sgub4", [DEPTH, 2, 128, 512])
    w_ba = din("w_branch_a", [DEPTH, 256, D])
    w_bb = din("w_branch_b", [DEPTH, 512, D])
    w_bc = din("w_branch_c", [DEPTH, 256, D])
    w_out = din("w_out", [DEPTH, D, D])
    w_xq = din("w_xq", [DEPTH, D, D])
    w_xkv = din("w_xkv", [DEPTH, D, 2 * D])
    w_xo = din("w_xo", [DEPTH, D, D])
    w_ff1 = din("w_ff1", [DEPTH, D, 4 * D])
    w_ff2 = din("w_ff2", [DEPTH, 4 * D, D])
    vecs_d = din("vecs", [128, NV])
    cst_d = din("cst", [128, NCST])
    outT = nc.dram_tensor("outT", [8, 128, S], F32, kind="ExternalOutput").ap()
    tap_d = {}
    for tname in taps:
        tap_d[tname] = nc.dram_tensor("tap_" + tname, [8, 128, S], F32, kind="ExternalOutput").ap()

    P = Prog(same_engine_sync=same_engine_sync)

    with ExitStack() as outer:
        _uid = [0]

        def sb(es, name, shape, dt):
            _uid[0] += 1
            return es.enter_context(nc.sbuf_tensor("s%d_%s" % (_uid[0], name), list(shape), dt))

        ps = outer.enter_context(nc.psum_tensor("ps", [128, 8, 512], F32))
        xres = sb(outer, "xres", [128, 8, S], F32)
        A = sb(outer, "A", [128, 8, S], BF16)
        wsl = [sb(outer, "wsl%d" % i, [128, 8, SLOTW], BF16) for i in range(NSLOT)]
        vecs = sb(outer, "vecs", [128, NV], F32)
        cst = sb(outer, "cst", [128, NCST], F32)
        halfb = sb(outer, "halfb", [128, DEPTH * 24], F32)
        negbf = sb(outer, "negbf", [128, DEPTH], F32)
        ones_bf = sb(outer, "ones_bf", [128, 128], BF16)
        wf3 = sb(outer, "wf3", [128, 8, 72], BF16)
        ident_bf = sb(outer, "ident_bf", [128, 128], BF16)
        negmask_bf = sb(outer, "negmask_bf", [128, 128], BF16)
        Tp = [sb(outer, "T%d" % i, [128, 512], F32) for i in range(3)]
        Ptp = [sb(outer, "Pt%d" % i, [128, 512], BF16) for i in range(4)]
        rstd = [sb(outer, "rstd%d" % i, [128, 512], F32) for i in range(2)]
        sqp = {2: Ptp[2], 3: Ptp[3]}

        ident = cst[:, C_ID:C_ID + 128]
        Umat = cst[:, C_U:C_U + 128]

        def act(out, in_, func, reads, writes, **kw):
            P.add("act", lambda h: h.activation(out=out, in_=in_, func=func, **kw), reads, writes)

        def mm(out, lhsT, rhs, start, stop, reads, writes, **kw):
            P.add("pe", lambda h: h.matmul(out, lhsT=lhsT, rhs=rhs, start=start, stop=stop, **kw), reads, writes)

        def stt(out, in0, scalar, in1, op0, op1, reads, writes, **kw):
            P.add("dve", lambda h: h.scalar_tensor_tensor(out=out, in0=in0, scalar=scalar, in1=in1, op0=op0, op1=op1, **kw), reads, writes)

        def tt_(out, in0, in1, op, reads, writes, eng="dve"):
            P.add(eng, lambda h: h.tensor_tensor(out=out, in0=in0, in1=in1, op=op), reads, writes)

        def ts(out, in0, s1, s2, op0, op1, reads, writes, eng="dve"):
            P.add(eng, lambda h: h.tensor_scalar(out=out, in0=in0, scalar1=s1, scalar2=s2, op0=op0, op1=op1), reads, writes)

        def recip(out, in_, reads, writes):
            P.add("dve", lambda h: h.reciprocal(out=out, in_=in_), reads, writes)

        def memset(ap, val, writes, eng="dve"):
            P.add(eng, lambda h: h.memset(ap, val), (), writes)

        def dma(eng, out, in_, reads, writes, group=None, persistent=False):
            P.add(eng, lambda h: h.dma_start(out=out, in_=in_), reads, writes, dma=True, group=group, persistent=persistent)

        class RR:
            def __init__(self, items):
                self.items = list(items)
                self.i = 0

            def next(self):
                v = self.items[self.i % len(self.items)]
                self.i += 1
                return v

        T_rr = RR(range(3))
        Pt_rr = RR(range(4))
        sq_rr = RR((2, 3))
        rstd_rr = RR(range(2))

        def wsrc(ap2d):
            return ap2d.rearrange("(kc p) n -> p kc n", p=128)

        W3 = ((0, 384), (384, 768), (768, 1024))
        wsched = []

        def sched_layer(l):
            for hc in range(4):
                wsched.append([(0, 8, i * 128, (i + 1) * 128, w_in[l, :, off + hc * 128: off + (hc + 1) * 128])
                               for i, off in enumerate((OFF_Q, OFF_K, OFF_V))])
            wsched.append([(0, 8, 0, 256, w_in[l, :, OFF_A:OFF_A + 256])])
            wsched.append([(0, 8, 0, 256, w_in[l, :, OFF_C:OFF_C + 256])])
            wsched.append([(0, 8, 0, 256, w_in[l, :, OFF_C + 256:OFF_C + 512])])
            for j in range(8):
                wsched.append([(0, 8, br * 128, (br + 1) * 128,
                                w_in[l, :, OFF_G + br * 1024 + j * 128: OFF_G + br * 1024 + (j + 1) * 128])
                               for br in range(3)])
                wsched.append([(0, 2, 0, 128, w_ba[l, :, j * 128:(j + 1) * 128]),
                               (2, 6, 0, 128, w_bb[l, :, j * 128:(j + 1) * 128]),
                               (6, 8, 0, 128, w_bc[l, :, j * 128:(j + 1) * 128])])
            for (c0, c1) in W3:
                wsched.append([(0, 8, 0, c1 - c0, w_out[l, :, c0:c1])])
            for q in range(4):
                wsched.append([(0, 8, 0, 256, w_xq[l, :, q * 256:(q + 1) * 256])])
                wsched.append([(0, 8, 0, 256, w_xkv[l, :, q * 256:(q + 1) * 256])])
                wsched.append([(0, 8, 0, 256, w_xkv[l, :, (4 + q) * 256:(5 + q) * 256])])
            for (c0, c1) in W3:
                wsched.append([(0, 8, 0, c1 - c0, w_xo[l, :, c0:c1])])
            for fg in range(4):
                for q in range(4):
                    wsched.append([(0, 8, 0, 256, w_ff1[l, :, fg * 1024 + q * 256: fg * 1024 + (q + 1) * 256])])
                for (c0, c1) in W3:
                    wsched.append([(0, 8, 0, c1 - c0, w_ff2[l, fg * 1024:(fg + 1) * 1024, c0:c1])])

        for l in range(n_layers):
            sched_layer(l)
        wstate = {"issued": 0, "next": 0}

        def w_issue_upto(n):
            while wstate["issued"] < min(n, len(wsched)):
                i = wstate["issued"]
                slot = i % NSLOT
                for (k0, k1, c0, c1, src) in wsched[i]:
                    dma("pool", wsl[slot][:, k0:k1, c0:c1], wsrc(src), [], [("ws", slot)], group=("w", i), persistent=True)
                wstate["issued"] += 1

        def w_acquire(held=0):
            i = wstate["next"]
            wstate["next"] += 1
            w_issue_upto(i - held + NSLOT)
            slot = i % NSLOT
            return wsl[slot], ("ws", slot)

        dma("sp", vecs[:], vecs_d, [], ["vecs"])
        dma("sp", cst[:], cst_d, [], ["cst"])
        for t in range(NTT):
            for dc in range(8):
                dma("sp", xres[:, dc, t * TT:(t + 1) * TT], xT[dc, :, t * TT:(t + 1) * TT], [], [("x", dc, t)])
        memset(ones_bf[:], 1.0, ["ones"])
        memset(wf3[:], 0.0, ["wf3z"])
        P.add("dve", lambda h: h.tensor_copy(ident_bf[:], cst[:, C_ID:C_ID + 128]), ["cst"], ["cbf"])
        P.add("dve", lambda h: h.tensor_copy(negmask_bf[:], cst[:, C_NEG:C_NEG + 128]), ["cst"], ["cbf"])
        for l in range(DEPTH):
            ts(halfb[:, l * 24:(l + 1) * 24], vecs[:, l * VPL + 8:l * VPL + 32], 0.5, None, ALU.mult, ALU.bypass,
               ["vecs"], [("halfb", l)])
            ts(negbf[:, l:l + 1], vecs[:, l * VPL + 60:l * VPL + 61], -1.0, None, ALU.mult, ALU.bypass,
               ["vecs"], [("negbf", l)])
        w_issue_upto(NSLOT)

        def norm_tile(src, src_res, gcol0, dst, dst_res, t, tw):
            b = bank_rr.next()
            for dc in range(8):
                qi = sq_rr.next()
                act(sqp[qi][:, :tw], src(dc, t), AF.Square, [src_res(dc, t)], [("Pt", qi)])
                mm(ps[:, b, :tw], ones_bf[:], sqp[qi][:, :tw], dc == 0, dc == 7,
                   [("Pt", qi), "ones"], [("ps", b)])
            act(ps[:, b, :tw], ps[:, b, :tw], AF.Sqrt, [("ps", b)], [("ps", b)], scale=1.0 / D, bias=EPS)
            ri = rstd_rr.next()
            P.add("dve", lambda h, o=rstd[ri][:, :tw], i_=ps[:, b, :tw]: h.reciprocal(out=o, in_=i_),
                  [("ps", b)], [("rstd", ri)])
            for dc in range(8):
                stt(dst(dc, t), src(dc, t), vecs[:, gcol0 + dc:gcol0 + dc + 1], rstd[ri][:, :tw],
                    ALU.mult, ALU.mult, [src_res(dc, t), ("rstd", ri), "vecs"], [dst_res(dc, t)])

        def rmsnorm(src, src_res, gcol0, dst, dst_res, ntiles, tw):
            for t in range(ntiles):
                norm_tile(src, src_res, gcol0, dst, dst_res, t, tw)

        def acquire3():
            return [w_acquire(held=i) for i in range(3)]

        def chunk_w(ws, j):
            p_ = (j * 128) // 384
            return ws[p_][0], ws[p_][1], j * 128 - p_ * 384

        def resid_proj_then_norm(src_fn, src_res_fn, evac, norm_args):
            ws = acquire3()
            for t in range(NTT + 1):
                if t < NTT:
                    for j in range(8):
                        wt, wres, c0 = chunk_w(ws, j)
                        b = proj_fm(wt, wres, c0, src_fn, src_res_fn, 8, t)
                        evac(j, t, b)
                if t >= 1 and norm_args is not None:
                    norm_tile(xsl, xr, norm_args[0], norm_args[1], norm_args[2], t - 1, TT)

        def xsl(dc, t):
            return xres[:, dc, t * TT:(t + 1) * TT]

        def Asl(dc, t):
            return A[:, dc, t * TT:(t + 1) * TT]

        def xr(dc, t):
            return ("x", dc, t)

        def Ar(dc, t):
            return ("A", dc, t)

        bank_rr = RR(range(8))
        lo_rr = RR(range(4))

        def proj_fm(wt, wres, c0, rhs_fn, rhs_res_fn, nk, t, extra_reads=(), rr=None):
            b = (rr or bank_rr).next()
            for kc in range(nk):
                mm(ps[:, b, :], wt[:, kc, c0:c0 + 128], rhs_fn(kc, t), kc == 0, kc == nk - 1,
                   [wres, rhs_res_fn(kc, t)] + list(extra_reads), [("ps", b)])
            return b

        def gelu_from_psum(src_ap, src_res, dst_ap, dst_res, width):
            t1 = T_rr.next()
            t2 = T_rr.next()
            T1 = Tp[t1][:, :width]
            T2 = Tp[t2][:, :width]
            act(T1, src_ap, AF.Identity, [src_res], [("T", t1)], scale=0.5)
            act(T2, src_ap, AF.Square, [src_res], [("T", t2)])
            ts(T2, T2, 0.044715, 1.0, ALU.mult, ALU.add, [("T", t2)], [("T", t2)])
            tt_(T2, T2, T1, ALU.mult, [("T", t1), ("T", t2)], [("T", t2)])
            act(T2, T2, AF.Tanh, [("T", t2)], [("T", t2)], scale=1.5957691216057308)
            stt(dst_ap, T2, 1.0, T1, ALU.add, ALU.mult, [("T", t1), ("T", t2)], [dst_res])

        def tap(name):
            if name in tap_d:
                for dc in range(8):
                    dma("sp", tap_d[name][dc], xres[:, dc, :], [("x", dc, t) for t in range(NTT)], [])

        for l in range(n_layers):
            vb = l * VPL
            if l == 0:
                rmsnorm(xsl, xr, vb + 0, Asl, Ar, NTT, TT)

            with ExitStack() as mix:
                ao = sb(mix, "ao", [128, 4, S], BF16)

                with ExitStack() as sub:
                    P3 = sb(sub, "P3", [72, S], BF16)
                    qTzs = [[sb(sub, "qTz%d_%d" % (s_, i), [128, S], BF16) for i in range(2)] for s_ in range(2)]
                    kTzs = [[sb(sub, "kTz%d_%d" % (s_, i), [128, S], BF16) for i in range(2)] for s_ in range(2)]
                    for s_ in range(2):
                        for hi in range(2):
                            z0 = 64 if hi == 0 else 0
                            me = "dve" if s_ == 0 else "pool"
                            memset(kTzs[s_][hi][z0:z0 + 64, :], 0.0, [("kTz", s_, hi, "pad")], eng=me)
                            memset(qTzs[s_][hi][z0:z0 + 64, :], 0.0, [("qTz", s_, hi, "pad")], eng=me)
                            memset(kTzs[s_][hi][z0:z0 + 3, :], -1.0, [("kTz", s_, hi, "pad")], eng=me)
                            memset(qTzs[s_][hi][z0:z0 + 6, :], 1.0, [("qTz", s_, hi, "pad")], eng=me)
                    with ExitStack() as fsub:
                        E2 = sb(fsub, "E2", [72, S], F32)
                        tmpb = sb(fsub, "tmpb", [72, S], BF16)
                        for r in range(3):
                            dma("pool", wf3[:, :, 32 * r:32 * r + 8], wsrc(w_in[l, :, OFF_F:OFF_F + 8]), ["wf3z"], [("wf3", r)])
                        for t in range(NTT):
                            for dc in range(8):
                                mm(ps[0:72, t, :], wf3[:, dc, :], Asl(dc, t), dc == 0, dc == 7,
                                   ["wf3z", ("wf3", 0), ("wf3", 1), ("wf3", 2), Ar(dc, t)], [("ps", t)])
                        psr4 = [("ps", t) for t in range(4)]
                        act(ps[0:72, 0:4, :], ps[0:72, 0:4, :], AF.Exp, psr4 + [("negbf", l)], psr4,
                            scale=-1.0, bias=negbf[0:72, l:l + 1])
                        act(E2[:].rearrange("p (a b) -> p a b", b=512), ps[0:72, 0:4, :], AF.Ln, psr4, ["E2"], bias=1.0)
                        P.add("dve", lambda h, o=E2[:]: h.tensor_tensor_scan(out=o, data0=o, data1=o, initial=0.0,
                                                                              op0=ALU.add, op1=ALU.max), ["E2"], ["E2"])

                        def cp(o, i_, rd, wr):
                            P.add("dve", lambda h: h.tensor_copy(o, i_), rd, wr)

                        cp(P3[0:8, :], E2[0:8, :], ["E2"], [("P3", 0)])
                        cp(tmpb[32:40, :], E2[32:40, :], ["E2"], [("tmpb", 1)])
                        tt_(E2[32:40, :], E2[32:40, :], tmpb[32:40, :], ALU.subtract, ["E2", ("tmpb", 1)], [("E2m", 1)])
                        cp(P3[32:40, :], E2[32:40, :], [("E2m", 1)], [("P3", 1)])
                        cp(tmpb[64:72, :], E2[64:72, :], ["E2"], [("tmpb", 2)])
                        tt_(E2[64:72, :], E2[64:72, :], tmpb[64:72, :], ALU.subtract, ["E2", ("tmpb", 2)], [("E2m", 2)])
                        cp(tmpb[64:72, :], E2[64:72, :], [("E2m", 2)], [("tmpb", 2)])
                        tt_(E2[64:72, :], E2[64:72, :], tmpb[64:72, :], ALU.subtract, [("E2m", 2), ("tmpb", 2)], [("E2m", 2)])
                        cp(P3[64:72, :], E2[64:72, :], [("E2m", 2)], [("P3", 2)])
                    P.barrier()
                    vzs = [[sb(sub, "vz%d_%d" % (s_, i), [128, 16, 128], BF16) for i in range(2)] for s_ in range(2)]
                    for s_ in range(2):
                        me = "dve" if s_ == 0 else "pool"
                        memset(vzs[s_][0][:, :, 64:128], 1.0, [("vzp", s_, 0)], eng=me)
                        memset(vzs[s_][1][:, :, 0:64], 1.0, [("vzp", s_, 1)], eng=me)
                    p3r = [("P3", i) for i in range(3)]
                    wq_cur = {}

                    def attn_prep(hc):
                        s_ = hc % 2
                        wq_cur[hc] = w_acquire()
                        for hi in range(2):
                            hh = 2 * hc + hi
                            z0 = 64 if hi == 0 else 0
                            for r in range(3):
                                dma("sp", qTzs[s_][hi][z0 + r:z0 + r + 1, :], P3[32 * r + hh:32 * r + hh + 1, :],
                                    p3r + [("qTz", s_, hi, "pad")], [("qTz", s_, hi, "F", r)])
                                dma("sp", kTzs[s_][hi][z0 + 3 + r:z0 + 4 + r, :], P3[32 * r + hh:32 * r + hh + 1, :],
                                    p3r + [("kTz", s_, hi, "pad")], [("kTz", s_, hi, "F", r)])

                    def attn_proj(hc, t):
                        s_ = hc % 2
                        wq, wq_res = wq_cur[hc]
                        b = proj_fm(wq, wq_res, 0, Asl, Ar, 8, t, rr=lo_rr)
                        ts(qTzs[s_][0][0:64, t * TT:(t + 1) * TT], ps[0:64, b, :], 0.125, None, ALU.mult, ALU.bypass,
                           [("ps", b)], [("qT", s_, 0, t)])
                        ts(qTzs[s_][1][64:128, t * TT:(t + 1) * TT], ps[64:128, b, :], 0.125, None, ALU.mult, ALU.bypass,
                           [("ps", b)], [("qT", s_, 1, t)])
                        b = proj_fm(wq, wq_res, 128, Asl, Ar, 8, t, rr=lo_rr)
                        P.add("dve", lambda h, o=kTzs[s_][0][0:64, t * TT:(t + 1) * TT], i_=ps[0:64, b, :]: h.tensor_copy(o, i_),
                              [("ps", b)], [("kT", s_, 0, t)])
                        P.add("dve", lambda h, o=kTzs[s_][1][64:128, t * TT:(t + 1) * TT], i_=ps[64:128, b, :]: h.tensor_copy(o, i_),
                              [("ps", b)], [("kT", s_, 1, t)])
                        b = lo_rr.next()
                        for cc in range(4):
                            c = 4 * t + cc
                            for dc in range(8):
                                mm(ps[:, b, cc * 128:(cc + 1) * 128], A[:, dc, c * 128:(c + 1) * 128],
                                   wq[:, dc, 256:384], dc == 0, dc == 7, [wq_res, Ar(dc, t)], [("ps", b)])
                        psv = ps[:, b, :].rearrange("p (a c) -> p a c", c=128)
                        P.add("dve", lambda h, o=vzs[s_][0][:, 4 * t:4 * t + 4, 0:64], i_=psv[:, :, 0:64]: h.tensor_copy(o, i_),
                              [("ps", b)], [("vz", s_, 0, t)])
                        P.add("dve", lambda h, o=vzs[s_][1][:, 4 * t:4 * t + 4, 64:128], i_=psv[:, :, 64:128]: h.tensor_copy(o, i_),
                              [("ps", b)], [("vz", s_, 1, t)])

                    pending_norm = []
                    attn_prep(0)
                    for t in range(NTT):
                        attn_proj(0, t)
                    for hc in range(4):
                        s_ = hc % 2
                        qTz, kTz, vz = qTzs[s_], kTzs[s_], vzs[s_]
                        if hc + 1 < 4:
                            attn_prep(hc + 1)
                        for Qi in range(NTT):
                            xb = [4 + (Qi % 2) * 2, 5 + (Qi % 2) * 2]
                            nk = 4 * Qi + 4
                            steps = [(kj, hi) for kj in range(nk) for hi in range(2)]
                            LA = 3
                            pend = []
                            for i in range(len(steps) + LA):
                                if i == min(22, len(steps) + LA - 1) and pending_norm:
                                    pending_norm.pop(0)()
                                if i < len(steps):
                                    kj, hi = steps[i]
                                    n0 = max(0, kj - 4 * Qi) * 128
                                    N = TT - n0
                                    sbk = lo_rr.next()
                                    q0 = Qi * TT + n0
                                    diag = kj >= 4 * Qi
                                    frs = [(nm, s_, hi, "F", r) for nm in ("qTz", "kTz") for r in range(3)]
                                    mm(ps[:, sbk, 0:N], kTz[hi][:, kj * 128:(kj + 1) * 128], qTz[hi][:, q0:q0 + N], True, not diag,
                                       [("kT", s_, hi, kj // 4), ("kTz", s_, hi, "pad"), ("qTz", s_, hi, "pad"), ("qT", s_, hi, Qi)] + frs,
                                       [("ps", sbk)])
                                    if diag:
                                        mm(ps[:, sbk, 0:128], ident_bf[:], negmask_bf[:], False, True, ["cbf"], [("ps", sbk)])
                                    pi = Pt_rr.next()
                                    act(Ptp[pi][:, 0:N], ps[:, sbk, 0:N], AF.Exp, [("ps", sbk)], [("Pt", pi)])
                                    pend.append((kj, hi, n0, N, pi))
                                if i >= LA:
                                    kj, hi, n0, N, pi = pend[i - LA]
                                    mm(ps[:, xb[hi], n0:TT], vz[hi][:, kj, :], Ptp[pi][:, 0:N], kj == 0, kj == nk - 1,
                                       [("vz", s_, hi, kj // 4), ("vzp", s_, hi), ("Pt", pi)], [("ps", xb[hi])])
                            if hc + 1 < 4:
                                attn_proj(hc + 1, Qi)
                            ri = rstd_rr.next()
                            P.add("dve", lambda h, o=rstd[ri][0:64, :], i_=ps[0:64, xb[1], :]: h.reciprocal(out=o, in_=i_),
                                  [("ps", xb[1])], [("rstd", ri)])
                            P.add("dve", lambda h, o=rstd[ri][64:128, :], i_=ps[64:128, xb[0], :]: h.reciprocal(out=o, in_=i_),
                                  [("ps", xb[0])], [("rstd", ri)])

                            def norm_tail(ri=ri, xb=xb, hc=hc, Qi=Qi):
                                bs = lo_rr.next()
                                mm(ps[:, bs, :], cst[:, C_SWAP:C_SWAP + 128], rstd[ri][:], True, True,
                                   [("rstd", ri), "cst"], [("ps", bs)])
                                ti = T_rr.next()
                                act(Tp[ti][:], ps[:, bs, :], AF.Identity, [("ps", bs)], [("T", ti)])
                                tt_(ao[0:64, hc, Qi * TT:(Qi + 1) * TT], ps[0:64, xb[0], :], Tp[ti][0:64, :],
                                    ALU.mult, [("ps", xb[0]), ("T", ti)], [("ao", hc, Qi, 0)])
                                tt_(ao[64:128, hc, Qi * TT:(Qi + 1) * TT], ps[64:128, xb[1], :], Tp[ti][64:128, :],
                                    ALU.mult, [("ps", xb[1]), ("T", ti)], [("ao", hc, Qi, 1)])
                            pending_norm.append(norm_tail)
                    while pending_norm:
                        pending_norm.pop(0)()
                P.barrier()
                pa = sb(mix, "pa", [128, 2, S], BF16)
                sc = sb(mix, "sc", [128, 2, S], BF16)

                with ExitStack() as sub:
                    aT = sb(sub, "aT", [128, 16 + S], F32)
                    sA = sb(sub, "sA", [128, 16 + S], F32)
                    sB = sb(sub, "sB", [128, 16 + S], F32)
                    dT = sb(sub, "dT", [128, S], BF16)
                    wbd = sb(sub, "wbd", [128, 2, 128], BF16)
                    t16 = sb(sub, "t16", [128, 16], F32)
                    dma("pool", wbd[:], pool_wbd[l], [], ["wbd"])
                    for bufn, buf in (("aT", aT), ("sA", sA), ("sB", sB)):
                        memset(buf[:, 0:16], 0.0, [(bufn, "pad")])
                    wa, wa_res = w_acquire()
                    for fc in range(2):
                        for t in range(NTT):
                            b = proj_fm(wa, wa_res, fc * 128, Asl, Ar, 8, t)
                            act(aT[:, 16 + t * TT:16 + (t + 1) * TT], ps[:, b, :], AF.Identity,
                                [("ps", b)], [("aT", t)])
                        allr = [("aT", t) for t in range(NTT)] + [("aT", "pad")]
                        a_ = aT[:, 16:16 + S]
                        tt_(sA[:, 16:], a_, aT[:, 15:15 + S], ALU.add, allr + [("sA", "pad")], ["sA"])
                        if fc == 0:
                            lo, lo_res, lo_w = sA, "sA", 2
                            tt_(sB[:, 16:], sA[:, 16:], sA[:, 14:14 + S], ALU.add, ["sA", ("sA", "pad"), ("sB", "pad")], ["sB"])
                            hi, hi_res, hi_w = sB, "sB", 4
                        else:
                            tt_(sB[:, 16:], sA[:, 16:], sA[:, 14:14 + S], ALU.add, ["sA", ("sA", "pad"), ("sB", "pad")], ["sB"])
                            tt_(sA[:, 16:], sB[:, 16:], sB[:, 12:12 + S], ALU.add, ["sB", ("sB", "pad"), ("sA", "pad")], ["sA"])
                            lo, lo_res, lo_w = sA, "sA", 8
                            tt_(sB[:, 16:], sA[:, 16:], sA[:, 8:8 + S], ALU.add, ["sA", ("sA", "pad"), ("sB", "pad")], ["sB"])
                            hi, hi_res, hi_w = sB, "sB", 16
                        for (p0, buf, bres, win, hidx) in ((0, lo, lo_res, lo_w, 0), (64, hi, hi_res, hi_w, 1)):
                            stt(dT[p0:p0 + 64, :], buf[p0:p0 + 64, 16:], 1.0 / win, aT[p0:p0 + 64, 16:],
                                ALU.mult, ALU.subtract, allr + [bres], [("dT", hidx)])
                            tt_(t16[p0:p0 + 64, :], buf[p0:p0 + 64, 16:32],
                                cst[p0:p0 + 64, C_INV + fc * 16:C_INV + (fc + 1) * 16], ALU.mult,
                                [bres, "cst"], [("t16", hidx)])
                            tt_(dT[p0:p0 + 64, 0:16], t16[p0:p0 + 64, :], aT[p0:p0 + 64, 16:32], ALU.subtract,
                                allr + [("t16", hidx), ("dT", hidx)], [("dT", hidx)])
                        for t in range(NTT):
                            b = bank_rr.next()
                            mm(ps[:, b, :], wbd[:, fc, :], dT[:, t * TT:(t + 1) * TT], True, True,
                               [("dT", 0), ("dT", 1), "wbd"], [("ps", b)])
                            act(pa[:, fc, t * TT:(t + 1) * TT], ps[:, b, :], AF.Identity, [("ps", b), "vecs"],
                                [("pa", fc, t)], scale=vecs[:, vb + 32 + fc:vb + 33 + fc])
                P.barrier()

                with ExitStack() as sub:
                    uT = sb(sub, "uT", [128, 2, S], BF16)
                    gvb = sb(sub, "gvb", [128, 16, 256], BF16)
                    ss = sb(sub, "ss", [128, 16], F32)
                    rs16 = sb(sub, "rs16", [128, 16], F32)
                    gv = sb(sub, "gv", [128, 256], F32)
                    junk = sb(sub, "junk", [128, 256], F32)
                    wTs = sb(sub, "wTs", [128, 4, 128], F32)
                    wcm = sb(sub, "wcm", [128, 4, 128], BF16)
                    bT4 = sb(sub, "bT4", [128, 2, 512], F32)
                    for g in range(4):
                        dma("sp", wTs[:, g, :], sgu_wT[l, g], [], [("wTs", g)])
                        tt_(wcm[:, g, :], wTs[:, g, :], Umat, ALU.mult, [("wTs", g), "cst"], [("wcm", g)])
                    for fc in range(2):
                        dma("sp", bT4[:, fc, :], sgub4[l, fc], [], [("bT4", fc)])
                    wcu, wcu_res = w_acquire()
                    for fc in range(2):
                        for t in range(NTT):
                            b = proj_fm(wcu, wcu_res, fc * 128, Asl, Ar, 8, t)
                            gelu_from_psum(ps[:, b, :], ("ps", b), uT[:, fc, t * TT:(t + 1) * TT], ("uT", fc, t), 512)
                    wcv, wcv_res = w_acquire()
                    for c in range(16):
                        b = bank_rr.next()
                        for dc in range(8):
                            mm(ps[:, b, 0:256], A[:, dc, c * 128:(c + 1) * 128], wcv[:, dc, 0:256], dc == 0, dc == 7,
                               [wcv_res, Ar(dc, c // 4)], [("ps", b)])
                        gelu_from_psum(ps[:, b, 0:256], ("ps", b), gv[:], "gv", 256)
                        stt(junk[:], gv[:], 1.0, gv[:], ALU.mult, ALU.mult, ["gv"], ["junk", ("ss", c)],
                            accum_out=ss[:, c:c + 1])
                        act(gvb[:, c, :], gv[:], AF.Identity, ["gv"], [("gvb", c)])
                    ssr = [("ss", c) for c in range(16)]
                    act(rs16[:], ss[:], AF.Sqrt, ssr, ["rs16"], scale=1.0 / 256, bias=EPS)
                    recip(rs16[:], rs16[:], ["rs16"], ["rs16"])
                    for c in range(16):
                        ts(gvb[:, c, :], gvb[:, c, :], rs16[:, c:c + 1], None, ALU.mult, ALU.bypass,
                           [("gvb", c), "rs16"], [("gvb", c)])
                    for fc in range(2):
                        for t in range(NTT):
                            b = bank_rr.next()
                            for cc in range(4):
                                c = t * 4 + cc
                                for gi in range(2):
                                    g = 2 * fc + gi
                                    mm(ps[gi * 64:(gi + 1) * 64, b, cc * 128:(cc + 1) * 128],
                                       gvb[:, c, g * 64:(g + 1) * 64], wcm[:, g, :], True, True,
                                       [("gvb", c), ("wcm", g)], [("ps", b)], tile_position=(0, gi * 64))
                            ti = T_rr.next()
                            stt(Tp[ti][:], ps[:, b, :], vecs[:, vb + 34 + fc:vb + 35 + fc], bT4[:, fc, :],
                                ALU.mult, ALU.add, [("ps", b), "vecs", ("bT4", fc)], [("T", ti)])
                            tt_(sc[:, fc, t * TT:(t + 1) * TT], Tp[ti][:], uT[:, fc, t * TT:(t + 1) * TT], ALU.mult,
                                [("T", ti), ("uT", fc, t)], [("sc", fc, t)])
                P.barrier()

                P.barrier()
                for nm, buf, nch in (("pa", pa, 2), ("ao", ao, 4), ("sc", sc, 2)):
                    if l == 0 and nm in tap_d:
                        for kc in range(nch):
                            dma("pool", tap_d[nm][kc], buf[:, kc, :], [], [])
                P.barrier()

                with ExitStack() as sub:
                    mg = sb(sub, "mg", [128, 8, S], BF16)
                    for j in range(8):
                        wg, wg_res = w_acquire()
                        wb, wb_res = w_acquire(held=1)
                        for t in range(NTT):
                            gb = []
                            for br in range(3):
                                b = proj_fm(wg, wg_res, br * 128, Asl, Ar, 8, t)
                                gb.append(b)
                            tl = []
                            for br in range(3):
                                ti = T_rr.next()
                                tl.append(ti)
                                act(Tp[ti][:], ps[:, gb[br], :], AF.Tanh, [("ps", gb[br]), ("halfb", l)], [("T", ti)],
                                    scale=0.5, bias=halfb[:, l * 24 + br * 8 + j:l * 24 + br * 8 + j + 1])
                            srcs = [(pa, 0, 2, lambda kc, t_: ("pa", kc, t_)),
                                    (ao, 2, 4, None),
                                    (sc, 6, 2, lambda kc, t_: ("sc", kc, t_))]
                            for br, (buf, k0, nk, rf) in enumerate(srcs):
                                b = bank_rr.next()
                                for kc in range(nk):
                                    rr_ = [("ao", kc, t, 0), ("ao", kc, t, 1)] if rf is None else [rf(kc, t)]
                                    mm(ps[:, b, :], wb[:, k0 + kc, 0:128], buf[:, kc, t * TT:(t + 1) * TT], kc == 0, kc == nk - 1,
                                       [wb_res] + rr_, [("ps", b)])
                                ti = tl[br]
                                stt(Tp[ti][:], Tp[ti][:], 1.0, ps[:, b, :], ALU.add, ALU.mult, [("T", ti), ("ps", b)], [("T", ti)])
                            tt_(Tp[tl[0]][:], Tp[tl[0]][:], Tp[tl[1]][:], ALU.add, [("T", tl[0]), ("T", tl[1])], [("T", tl[0])])
                            tt_(mg[:, j, t * TT:(t + 1) * TT], Tp[tl[0]][:], Tp[tl[2]][:], ALU.add,
                                [("T", tl[0]), ("T", tl[2])], [("mg", j, t)])
                    resid_proj_then_norm(
                        lambda kc, t_: mg[:, kc, t_ * TT:(t_ + 1) * TT], lambda kc, t_: ("mg", kc, t_),
                        lambda j, t, b: stt(xsl(j, t), ps[:, b, :], 0.5, xsl(j, t), ALU.mult, ALU.add,
                                            [("ps", b), xr(j, t)], [xr(j, t)]),
                        (vb + 36, Asl, Ar))
            P.barrier()
            if l == 0:
                tap("mix")

            with ExitStack() as xa:
                xq = sb(xa, "xq", [128, 8, S], BF16)
                hmT = sb(xa, "hmT", [128, 8, MEM], BF16)
                xkT = sb(xa, "xkT", [128, 8, MEM], BF16)
                xv = sb(xa, "xv", [128, 2, D], BF16)
                with ExitStack() as sub:
                    mT = sb(sub, "mT", [128, 8, MEM], F32)
                    for dc in range(8):
                        dma("sp", mT[:, dc, :], memT[dc], [], [("mT", dc)])
                    rmsnorm(lambda dc, t: mT[:, dc, :], lambda dc, t: ("mT", dc), vb + 44,
                            lambda dc, t: hmT[:, dc, :], lambda dc, t: ("hmT", dc), 1, MEM)
                def xa_kproj(q):
                    wk, wk_res = w_acquire()
                    for jj in range(2):
                        j = 2 * q + jj
                        b = bank_rr.next()
                        for dc in range(8):
                            mm(ps[:, b, 0:MEM], wk[:, dc, jj * 128:(jj + 1) * 128], hmT[:, dc, :], dc == 0, dc == 7,
                               [wk_res, ("hmT", dc)], [("ps", b)])
                        act(xkT[:, j, :], ps[:, b, 0:MEM], AF.Identity, [("ps", b)], [("xkT", j)])

                def xa_vproj(q):
                    wv_, wv_res = w_acquire()
                    for mc in range(2):
                        b = bank_rr.next()
                        for dc in range(8):
                            mm(ps[:, b, 0:256], hmT[:, dc, mc * 128:(mc + 1) * 128], wv_[:, dc, 0:256], dc == 0, dc == 7,
                               [wv_res, ("hmT", dc)], [("ps", b)])
                        act(xv[:, mc, q * 256:(q + 1) * 256], ps[:, b, 0:256], AF.Identity, [("ps", b)], [("xv", mc, q)])

                def xa_qproj(q):
                    wq_, wq_res = w_acquire()
                    for jj in range(2):
                        j = 2 * q + jj
                        for t in range(NTT):
                            b = proj_fm(wq_, wq_res, jj * 128, Asl, Ar, 8, t)
                            act(xq[:, j, t * TT:(t + 1) * TT], ps[:, b, :], AF.Identity, [("ps", b)], [("xq", j, t)],
                                scale=1.0 / 16)

                for q in range(4):
                    xa_qproj(q)
                    xa_kproj(q)
                    xa_vproj(q)
                hi_rr = RR(range(4, 8))
                items = [(t, hh) for t in range(NTT) for hh in range(4)]
                xpend = []

                def xa_scores(t, hh):
                    pts = []
                    for mc in range(2):
                        b = lo_rr.next()
                        for dk in range(2):
                            mm(ps[:, b, :], xkT[:, 2 * hh + dk, mc * 128:(mc + 1) * 128],
                               xq[:, 2 * hh + dk, t * TT:(t + 1) * TT], dk == 0, dk == 1,
                               [("xkT", 2 * hh + dk), ("xq", 2 * hh + dk, t)], [("ps", b)])
                        pi = Pt_rr.next()
                        act(Ptp[pi][:], ps[:, b, :], AF.Exp, [("ps", b)], [("Pt", pi)])
                        pts.append(pi)
                    return pts

                def xa_pv(t, hh, pts):
                    bd = hi_rr.next()
                    for mc in range(2):
                        mm(ps[:, bd, :], ones_bf[:], Ptp[pts[mc]][:], mc == 0, mc == 1,
                           ["ones", ("Pt", pts[mc])], [("ps", bd)])
                    ri = rstd_rr.next()
                    P.add("dve", lambda h, o=rstd[ri][:], i_=ps[:, bd, :]: h.reciprocal(out=o, in_=i_),
                          [("ps", bd)], [("rstd", ri)])
                    for dch in range(2):
                        b = hi_rr.next()
                        for mc in range(2):
                            mm(ps[:, b, :], xv[:, mc, hh * 256 + dch * 128: hh * 256 + (dch + 1) * 128], Ptp[pts[mc]][:],
                               mc == 0, mc == 1, [("xv", mc, hh), ("Pt", pts[mc])], [("ps", b)])
                        tt_(xq[:, 2 * hh + dch, t * TT:(t + 1) * TT], ps[:, b, :], rstd[ri][:], ALU.mult,
                            [("ps", b), ("rstd", ri)], [("xq", 2 * hh + dch, t)])

                for i in range(len(items) + 1):
                    if i < len(items):
                        xpend.append(xa_scores(*items[i]))
                    if i >= 1:
                        xa_pv(items[i - 1][0], items[i - 1][1], xpend[i - 1])
                resid_proj_then_norm(
                    lambda kc, t_: xq[:, kc, t_ * TT:(t_ + 1) * TT], lambda kc, t_: ("xq", kc, t_),
                    lambda j, t, b: tt_(xsl(j, t), ps[:, b, :], xsl(j, t), ALU.add, [("ps", b), xr(j, t)], [xr(j, t)]),
                    (vb + 52, Asl, Ar))
            P.barrier()
            if l == 0:
                tap("xat")

            with ExitStack() as ff:
                hid = sb(ff, "hid", [128, 8, S], BF16)
                for fg in range(4):
                    for q in range(4):
                        w1, w1_res = w_acquire()
                        for jj in range(2):
                            fcl = 2 * q + jj
                            for t in range(NTT):
                                b = proj_fm(w1, w1_res, jj * 128, Asl, Ar, 8, t)
                                ti = T_rr.next()
                                act(Tp[ti][:], ps[:, b, :], AF.Relu, [("ps", b)], [("T", ti)])
                                tt_(hid[:, fcl, t * TT:(t + 1) * TT], Tp[ti][:], Tp[ti][:], ALU.mult,
                                    [("T", ti)], [("hid", fcl, t)])
                    hid_fn = lambda kc, t_: hid[:, kc, t_ * TT:(t_ + 1) * TT]
                    hid_res = lambda kc, t_: ("hid", kc, t_)
                    ev2 = lambda j, t, b: tt_(xsl(j, t), ps[:, b, :], xsl(j, t), ALU.add, [("ps", b), xr(j, t)], [xr(j, t)])
                    if fg < 3:
                        for p_ in range(3):
                            w2, w2_res = w_acquire()
                            for j in range(3 * p_, min(3 * p_ + 3, 8)):
                                for t in range(NTT):
                                    b = proj_fm(w2, w2_res, (j - 3 * p_) * 128, hid_fn, hid_res, 8, t)
                                    ev2(j, t, b)
                    else:
                        if l + 1 < n_layers:
                            nargs = ((l + 1) * VPL + 0, Asl, Ar)
                        else:
                            nargs = (VPL * DEPTH, xsl, xr)
                        resid_proj_then_norm(hid_fn, hid_res, ev2, nargs)
            P.barrier()
            if l == 0:
                tap("ffn")

        for dc in range(8):
            dma("sp", outT[dc], xres[:, dc, :], [xr(dc, t) for t in range(NTT)], [])
        cnt = P.emit(nc)
        nc._cnt = cnt
        nc._nops = len(P.ops)
    return nc


def host_consts():
    cst = np.zeros((128, NCST), np.float32)
    cst[:, C_ID:C_ID + 128] = np.eye(128, dtype=np.float32)
    k = np.arange(128)
    cst[:, C_U:C_U + 128] = (k[:, None] <= k[None, :]).astype(np.float32)
    wins = {(0, 0): 2, (0, 1): 4, (1, 0): 8, (1, 1): 16}
    t = np.arange(16)
    for fc in range(2):
        for half in range(2):
            w = wins[(fc, half)]
            cst[half * 64:(half + 1) * 64, C_INV + fc * 16:C_INV + (fc + 1) * 16] = \
                (1.0 / np.minimum(t + 1, w)).astype(np.float32)[None, :]
    for kk in range(128):
        cst[kk, C_SWAP + (kk + 64) % 128] = 1.0
    cst[:, C_NEG:C_NEG + 128] = np.where(k[:, None] > k[None, :], -30000.0, 0.0).astype(np.float32)
    return cst


def pack_vecs(inp):
    v = np.zeros((128, NV), np.float32)

    def fm(a):
        a = np.asarray(a, np.float32)
        return a.reshape(-1, 128).T

    for l in range(DEPTH):
        b = l * VPL
        v[:, b + 0:b + 8] = fm(inp["norm_mix_g"][l])
        v[:, b + 8:b + 32] = fm(inp["b_gate"][l])
        v[:, b + 32:b + 34] = fm(inp["pool_scale"][l])
        v[:, b + 34:b + 36] = fm(inp["sgu_norm_g"][l])
        v[:, b + 36:b + 44] = fm(inp["norm_xattn_g"][l])
        v[:, b + 44:b + 52] = fm(inp["norm_mem_g"][l])
        v[:, b + 52:b + 60] = fm(inp["norm_ffn_g"][l])
        for r in range(3):
            v[32 * r:32 * r + 8, b + 60] = np.asarray(inp["b_forget"][l], np.float32)
    v[:, VPL * DEPTH:VPL * DEPTH + 8] = fm(inp["final_norm_g"])
    return v


def shared_inputs(inp):
    f = lambda k: np.ascontiguousarray(np.asarray(inp[k], np.float32))
    sgu_b = np.asarray(inp["sgu_b"], np.float32)
    sgub4 = np.zeros((DEPTH, 2, 128, 512), np.float32)
    for fc in range(2):
        for gi in range(2):
            sgub4[:, fc, gi * 64:(gi + 1) * 64, :] = np.tile(sgu_b[:, 2 * fc + gi, :], (1, 4))[:, None, :]
    pw = np.asarray(inp["pool_w"], np.float32)
    pool_wbd = np.zeros((DEPTH, 128, 2, 128), np.float32)
    for g in range(4):
        gi = g % 2
        pool_wbd[:, gi * 64:(gi + 1) * 64, g // 2, gi * 64:(gi + 1) * 64] = pw[:, g]
    return {
        "w_in": f("w_in"), "pool_wbd": pool_wbd,
        "sgu_wT": np.ascontiguousarray(np.asarray(inp["sgu_w"], np.float32).transpose(0, 1, 3, 2)),
        "sgub4": sgub4,
        "w_branch_a": f("w_branch_a"), "w_branch_b": f("w_branch_b"), "w_branch_c": f("w_branch_c"),
        "w_out": f("w_out"), "w_xq": f("w_xq"), "w_xkv": f("w_xkv"), "w_xo": f("w_xo"),
        "w_ff1": f("w_ff1"), "w_ff2": f("w_ff2"),
        "vecs": pack_vecs(inp), "cst": host_consts(),
    }


def core_inputs(inp, b):
    x = np.asarray(inp["x"], np.float32)[b]
    m = np.asarray(inp["mem"], np.float32)[b]
    return {
        "xT": np.ascontiguousarray(x.T).reshape(8, 128, S),
        "memT": np.ascontiguousarray(m.T).reshape(8, 128, MEM),
    }


_NC_CACHE = {}


def kernel(**inputs):
    if "nc" not in _NC_CACHE:
        _NC_CACHE["nc"] = build()
    nc = _NC_CACHE["nc"]
    shared = shared_inputs(inputs)
    in_maps = []
    for b in range(8):
        m = dict(shared)
        m.update(core_inputs(inputs, b))
        in_maps.append(m)
    res = run_bass_kernel_spmd(nc, in_maps, core_ids=list(range(8)))
    out = np.empty((8, S, D), np.float32)
    for b in range(8):
        out[b] = res.results[b]["outT"].reshape(D, S).T
    return out
```

```python
import numpy as np
from contextlib import ExitStack
import concourse.bass as bass
import concourse.mybir as mybir
from concourse.bass_utils import run_bass_kernel_spmd

F32 = mybir.dt.float32
BF16 = mybir.dt.bfloat16
AF = mybir.ActivationFunctionType
ALU = mybir.AluOpType

ENGS = ("pe", "act", "dve", "pool", "sp")
N_DMA_SEMS = 24

D = 1024
S = 2048
DEPTH = 2
MEM = 256
TT = 512
NTT = S // TT
OFF_A, OFF_Q, OFF_K, OFF_V, OFF_F, OFF_C, OFF_G = 0, 256, 768, 1280, 1792, 1800, 2312
N_IN = 5384
EPS = 1e-6
SLOTW = 384
NSLOT = 4
VPL = 61
NV = VPL * DEPTH + 8
C_ID, C_U, C_INV, C_SWAP, C_NEG, NCST = 0, 128, 256, 288, 416, 544


class Op:
    __slots__ = ("idx", "eng", "fn", "deps", "dma", "need_sig", "sig", "prev_use", "group")

    def __init__(self, idx, eng, fn, deps, dma, group):
        self.idx = idx
        self.eng = eng
        self.fn = fn
        self.deps = deps
        self.dma = dma
        self.need_sig = dma
        self.sig = None
        self.prev_use = None
        self.group = group


class Prog:
    def __init__(self, same_engine_sync=True):
        self.ops = []
        self.last_w = {}
        self.readers = {}
        self.same_engine_sync = same_engine_sync
        self.barrier_deps = {}
        self.last_on = {}
        self.open_dma = []

    def add(self, eng, fn, reads=(), writes=(), dma=False, group=None, persistent=False):
        idx = len(self.ops)
        deps = set()
        for r in reads:
            w = self.last_w.get(r)
            if w is not None:
                deps.update(w[1])
        for r in writes:
            w = self.last_w.get(r)
            if w is not None:
                if not (group is not None and w[0] == group):
                    deps.update(w[1])
            for rd in self.readers.get(r, ()):
                deps.add(rd)
        for r in reads:
            self.readers.setdefault(r, []).append(idx)
        for r in writes:
            w = self.last_w.get(r)
            if group is not None and w is not None and w[0] == group:
                w[1].append(idx)
            else:
                self.last_w[r] = (group, [idx])
                self.readers[r] = []
        if eng in self.barrier_deps:
            deps.update(self.barrier_deps.pop(eng))
        deps.discard(idx)
        self.ops.append(Op(idx, eng, fn, deps, dma, group))
        self.last_on[eng] = idx
        if dma and not persistent:
            self.open_dma.append(idx)
        return idx

    def barrier(self):
        deps = set(self.open_dma)
        for e, i in self.last_on.items():
            if not self.ops[i].dma:
                deps.add(i)
        self.open_dma = []
        for e in ENGS:
            self.barrier_deps.setdefault(e, set()).update(deps)

    def emit(self, nc, final_wait_eng="sp"):
        ops = self.ops
        for op in ops:
            nd = set()
            for d in op.deps:
                dop = ops[d]
                if (not dop.dma) and (not op.dma) and dop.eng == op.eng:
                    if op.eng == "pe" or not self.same_engine_sync:
                        continue
                nd.add(d)
            best = {}
            keep = set()
            for d in nd:
                dop = ops[d]
                if dop.dma:
                    keep.add(d)
                elif best.get(dop.eng, -1) < d:
                    best[dop.eng] = d
            keep.update(best.values())
            op.deps = keep
            for d in keep:
                ops[d].need_sig = True
        cnt = {e: 0 for e in ENGS}
        dma_use = [0] * N_DMA_SEMS
        half = N_DMA_SEMS // 2
        dma_rr_q = {"pool": 0, "sp": 0}
        for op in ops:
            if op.dma:
                qn = "pool" if op.eng == "pool" else "sp"
                s = dma_rr_q[qn] + (half if qn == "pool" else 0)
                dma_rr_q[qn] = (dma_rr_q[qn] + 1) % half
                op.prev_use = dma_use[s]
                dma_use[s] += 1
                op.sig = (("dma", s), 16 * dma_use[s])
            elif op.need_sig:
                cnt[op.eng] += 1
                op.sig = (("eng", op.eng), cnt[op.eng])
        with ExitStack() as es:
            sems = {}
            for e in ENGS:
                sems[("eng", e)] = es.enter_context(nc.semaphore("sem_" + e))
            for i in range(N_DMA_SEMS):
                sems[("dma", i)] = es.enter_context(nc.semaphore("sem_dma%d" % i))
            block = es.enter_context(nc.Block())
            streams = {e: [op for op in ops if op.eng == e] for e in ENGS}
            final = {}
            for op in ops:
                if op.dma:
                    k, v = op.sig
                    final[k] = max(final.get(k, 0), v)

            def run_stream(eng_name, h):
                known = {}
                for op in streams[eng_name]:
                    waits = {}
                    for d in op.deps:
                        k, v = ops[d].sig
                        if waits.get(k, 0) < v:
                            waits[k] = v
                    if op.dma and op.prev_use:
                        k = op.sig[0]
                        v = 16 * op.prev_use
                        if waits.get(k, 0) < v:
                            waits[k] = v
                    for k, v in waits.items():
                        if known.get(k, 0) >= v:
                            continue
                        h.wait_ge(sems[k], v)
                        known[k] = v
                    ins = op.fn(h)
                    if op.sig is not None:
                        k, v = op.sig
                        ins.then_inc(sems[k], 16 if op.dma else 1)
                if eng_name == final_wait_eng:
                    for k, v in final.items():
                        if known.get(k, 0) < v:
                            h.wait_ge(sems[k], v)

            @block.tensor
            def _(h):
                run_stream("pe", h)

            @block.scalar
            def _(h):
                run_stream("act", h)

            @block.vector
            def _(h):
                run_stream("dve", h)

            @block.gpsimd
            def _(h):
                run_stream("pool", h)

            @block.sync
            def _(h):
                run_stream("sp", h)
        return cnt


def build(n_layers=DEPTH, taps=(), same_engine_sync=True):
    nc = bass.Bass("TRN2", target_bir_lowering=False)

    def din(name, shape):
        return nc.dram_tensor(name, list(shape), F32, kind="ExternalInput").ap()

    xT = din("xT", [8, 128, S])
    memT = din("memT", [8, 128, MEM])
    w_in = din("w_in", [DEPTH, D, N_IN])
    pool_wbd = din("pool_wbd", [DEPTH, 128, 2, 128])
    sgu_wT = din("sgu_wT", [DEPTH, 4, 128, 128])
    sgub4 = din("sgub4", [DEPTH, 2, 128, 512])
    w_ba = din("w_branch_a", [DEPTH, 256, D])
    w_bb = din("w_branch_b", [DEPTH, 512, D])
    w_bc = din("w_branch_c", [DEPTH, 256, D])
    w_out = din("w_out", [DEPTH, D, D])
    w_xq = din("w_xq", [DEPTH, D, D])
    w_xkv = din("w_xkv", [DEPTH, D, 2 * D])
    w_xo = din("w_xo", [DEPTH, D, D])
    w_ff1 = din("w_ff1", [DEPTH, D, 4 * D])
    w_ff2 = din("w_ff2", [DEPTH, 4 * D, D])
    vecs_d = din("vecs", [128, NV])
    cst_d = din("cst", [128, NCST])
    outT = nc.dram_tensor("outT", [8, 128, S], F32, kind="ExternalOutput").ap()
    tap_d = {}
    for tname in taps:
        tap_d[tname] = nc.dram_tensor("tap_" + tname, [8, 128, S], F32, kind="ExternalOutput").ap()

    P = Prog(same_engine_sync=same_engine_sync)

    with ExitStack() as outer:
        _uid = [0]

        def sb(es, name, shape, dt):
            _uid[0] += 1
            return es.enter_context(nc.sbuf_tensor("s%d_%s" % (_uid[0], name), list(shape), dt))

        ps = outer.enter_context(nc.psum_tensor("ps", [128, 8, 512], F32))
        xres = sb(outer, "xres", [128, 8, S], F32)
        A = sb(outer, "A", [128, 8, S], BF16)
        wsl = [sb(outer, "wsl%d" % i, [128, 8, SLOTW], BF16) for i in range(NSLOT)]
        vecs = sb(outer, "vecs", [128, NV], F32)
        cst = sb(outer, "cst", [128, NCST], F32)
        halfb = sb(outer, "halfb", [128, DEPTH * 24], F32)
        negbf = sb(outer, "negbf", [128, DEPTH], F32)
        ones_bf = sb(outer, "ones_bf", [128, 128], BF16)
        wf3 = sb(outer, "wf3", [128, 8, 72], BF16)
        ident_bf = sb(outer, "ident_bf", [128, 128], BF16)
        negmask_bf = sb(outer, "negmask_bf", [128, 128], BF16)
        Tp = [sb(outer, "T%d" % i, [128, 512], F32) for i in range(3)]
        Ptp = [sb(outer, "Pt%d" % i, [128, 512], BF16) for i in range(4)]
        rstd = [sb(outer, "rstd%d" % i, [128, 512], F32) for i in range(2)]
        sqp = {2: Ptp[2], 3: Ptp[3]}

        ident = cst[:, C_ID:C_ID + 128]
        Umat = cst[:, C_U:C_U + 128]

        def act(out, in_, func, reads, writes, **kw):
            P.add("act", lambda h: h.activation(out=out, in_=in_, func=func, **kw), reads, writes)

        def mm(out, lhsT, rhs, start, stop, reads, writes, **kw):
            P.add("pe", lambda h: h.matmul(out, lhsT=lhsT, rhs=rhs, start=start, stop=stop, **kw), reads, writes)

        def stt(out, in0, scalar, in1, op0, op1, reads, writes, **kw):
            P.add("dve", lambda h: h.scalar_tensor_tensor(out=out, in0=in0, scalar=scalar, in1=in1, op0=op0, op1=op1, **kw), reads, writes)

        def tt_(out, in0, in1, op, reads, writes, eng="dve"):
            P.add(eng, lambda h: h.tensor_tensor(out=out, in0=in0, in1=in1, op=op), reads, writes)

        def ts(out, in0, s1, s2, op0, op1, reads, writes, eng="dve"):
            P.add(eng, lambda h: h.tensor_scalar(out=out, in0=in0, scalar1=s1, scalar2=s2, op0=op0, op1=op1), reads, writes)

        def recip(out, in_, reads, writes):
            P.add("dve", lambda h: h.reciprocal(out=out, in_=in_), reads, writes)

        def memset(ap, val, writes, eng="dve"):
            P.add(eng, lambda h: h.memset(ap, val), (), writes)

        def dma(eng, out, in_, reads, writes, group=None, persistent=False):
            P.add(eng, lambda h: h.dma_start(out=out, in_=in_), reads, writes, dma=True, group=group, persistent=persistent)

        class RR:
            def __init__(self, items):
                self.items = list(items)
                self.i = 0

            def next(self):
                v = self.items[self.i % len(self.items)]
                self.i += 1
                return v

        T_rr = RR(range(3))
        Pt_rr = RR(range(4))
        sq_rr = RR((2, 3))
        rstd_rr = RR(range(2))

        def wsrc(ap2d):
            return ap2d.rearrange("(kc p) n -> p kc n", p=128)

        W3 = ((0, 384), (384, 768), (768, 1024))
        wsched = []

        def sched_layer(l):
            for hc in range(4):
                wsched.append([(0, 8, i * 128, (i + 1) * 128, w_in[l, :, off + hc * 128: off + (hc + 1) * 128])
                               for i, off in enumerate((OFF_Q, OFF_K, OFF_V))])
            wsched.append([(0, 8, 0, 256, w_in[l, :, OFF_A:OFF_A + 256])])
            wsched.append([(0, 8, 0, 256, w_in[l, :, OFF_C:OFF_C + 256])])
            wsched.append([(0, 8, 0, 256, w_in[l, :, OFF_C + 256:OFF_C + 512])])
            for j in range(8):
                wsched.append([(0, 8, br * 128, (br + 1) * 128,
                                w_in[l, :, OFF_G + br * 1024 + j * 128: OFF_G + br * 1024 + (j + 1) * 128])
                               for br in range(3)])
                wsched.append([(0, 2, 0, 128, w_ba[l, :, j * 128:(j + 1) * 128]),
                               (2, 6, 0, 128, w_bb[l, :, j * 128:(j + 1) * 128]),
                               (6, 8, 0, 128, w_bc[l, :, j * 128:(j + 1) * 128])])
            for (c0, c1) in W3:
                wsched.append([(0, 8, 0, c1 - c0, w_out[l, :, c0:c1])])
            for q in range(4):
                wsched.append([(0, 8, 0, 256, w_xq[l, :, q * 256:(q + 1) * 256])])
                wsched.append([(0, 8, 0, 256, w_xkv[l, :, q * 256:(q + 1) * 256])])
                wsched.append([(0, 8, 0, 256, w_xkv[l, :, (4 + q) * 256:(5 + q) * 256])])
            for (c0, c1) in W3:
                wsched.append([(0, 8, 0, c1 - c0, w_xo[l, :, c0:c1])])
            for fg in range(4):
                for q in range(4):
                    wsched.append([(0, 8, 0, 256, w_ff1[l, :, fg * 1024 + q * 256: fg * 1024 + (q + 1) * 256])])
                for (c0, c1) in W3:
                    wsched.append([(0, 8, 0, c1 - c0, w_ff2[l, fg * 1024:(fg + 1) * 1024, c0:c1])])

        for l in range(n_layers):
            sched_layer(l)
        wstate = {"issued": 0, "next": 0}

        def w_issue_upto(n):
            while wstate["issued"] < min(n, len(wsched)):
                i = wstate["issued"]
                slot = i % NSLOT
                for (k0, k1, c0, c1, src) in wsched[i]:
                    dma("pool", wsl[slot][:, k0:k1, c0:c1], wsrc(src), [], [("ws", slot)], group=("w", i), persistent=True)
                wstate["issued"] += 1

        def w_acquire(held=0):
            i = wstate["next"]
            wstate["next"] += 1
            w_issue_upto(i - held + NSLOT)
            slot = i % NSLOT
            return wsl[slot], ("ws", slot)

        dma("sp", vecs[:], vecs_d, [], ["vecs"])
        dma("sp", cst[:], cst_d, [], ["cst"])
        for t in range(NTT):
            for dc in range(8):
                dma("sp", xres[:, dc, t * TT:(t + 1) * TT], xT[dc, :, t * TT:(t + 1) * TT], [], [("x", dc, t)])
        memset(ones_bf[:], 1.0, ["ones"])
        memset(wf3[:], 0.0, ["wf3z"])
        P.add("dve", lambda h: h.tensor_copy(ident_bf[:], cst[:, C_ID:C_ID + 128]), ["cst"], ["cbf"])
        P.add("dve", lambda h: h.tensor_copy(negmask_bf[:], cst[:, C_NEG:C_NEG + 128]), ["cst"], ["cbf"])
        for l in range(DEPTH):
            ts(halfb[:, l * 24:(l + 1) * 24], vecs[:, l * VPL + 8:l * VPL + 32], 0.5, None, ALU.mult, ALU.bypass,
               ["vecs"], [("halfb", l)])
            ts(negbf[:, l:l + 1], vecs[:, l * VPL + 60:l * VPL + 61], -1.0, None, ALU.mult, ALU.bypass,
               ["vecs"], [("negbf", l)])
        def load_wf3(l_):
            for r in range(3):
                dma("pool", wf3[:, :, 32 * r:32 * r + 8], wsrc(w_in[l_, :, OFF_F:OFF_F + 8]), ["wf3z"], [("wf3", r)],
                    persistent=True)

        load_wf3(0)
        w_issue_upto(NSLOT)

        def norm_tile(src, src_res, gcol0, dst, dst_res, t, tw):
            b = bank_rr.next()
            for dc in range(8):
                qi = sq_rr.next()
                act(sqp[qi][:, :tw], src(dc, t), AF.Square, [src_res(dc, t)], [("Pt", qi)])
                mm(ps[:, b, :tw], ones_bf[:], sqp[qi][:, :tw], dc == 0, dc == 7,
                   [("Pt", qi), "ones"], [("ps", b)])
            act(ps[:, b, :tw], ps[:, b, :tw], AF.Sqrt, [("ps", b)], [("ps", b)], scale=1.0 / D, bias=EPS)
            ri = rstd_rr.next()
            P.add("dve", lambda h, o=rstd[ri][:, :tw], i_=ps[:, b, :tw]: h.reciprocal(out=o, in_=i_),
                  [("ps", b)], [("rstd", ri)])
            for dc in range(8):
                stt(dst(dc, t), src(dc, t), vecs[:, gcol0 + dc:gcol0 + dc + 1], rstd[ri][:, :tw],
                    ALU.mult, ALU.mult, [src_res(dc, t), ("rstd", ri), "vecs"], [dst_res(dc, t)])

        def rmsnorm(src, src_res, gcol0, dst, dst_res, ntiles, tw):
            for t in range(ntiles):
                norm_tile(src, src_res, gcol0, dst, dst_res, t, tw)

        def acquire3():
            return [w_acquire(held=i) for i in range(3)]

        def chunk_w(ws, j):
            p_ = (j * 128) // 384
            return ws[p_][0], ws[p_][1], j * 128 - p_ * 384

        def resid_proj_then_norm(src_fn, src_res_fn, evac, norm_args, post_tile=None):
            ws = acquire3()
            for t in range(NTT + 1):
                if t < NTT:
                    for j in range(8):
                        wt, wres, c0 = chunk_w(ws, j)
                        b = proj_fm(wt, wres, c0, src_fn, src_res_fn, 8, t)
                        evac(j, t, b)
                if t >= 1 and norm_args is not None:
                    norm_tile(xsl, xr, norm_args[0], norm_args[1], norm_args[2], t - 1, TT)
                    if post_tile is not None:
                        post_tile(t - 1)

        def xsl(dc, t):
            return xres[:, dc, t * TT:(t + 1) * TT]

        def Asl(dc, t):
            return A[:, dc, t * TT:(t + 1) * TT]

        def xr(dc, t):
            return ("x", dc, t)

        def Ar(dc, t):
            return ("A", dc, t)

        bank_rr = RR(range(8))
        lo_rr = RR(range(4))

        def proj_fm(wt, wres, c0, rhs_fn, rhs_res_fn, nk, t, extra_reads=(), rr=None):
            b = (rr or bank_rr).next()
            for kc in range(nk):
                mm(ps[:, b, :], wt[:, kc, c0:c0 + 128], rhs_fn(kc, t), kc == 0, kc == nk - 1,
                   [wres, rhs_res_fn(kc, t)] + list(extra_reads), [("ps", b)])
            return b

        def gelu_from_psum(src_ap, src_res, dst_ap, dst_res, width, pool=None, rr=None):
            pool = pool or Tp
            rr = rr or T_rr
            t1 = rr.next()
            t2 = rr.next()
            T1 = pool[t1][:, :width]
            T2 = pool[t2][:, :width]
            act(T1, src_ap, AF.Identity, [src_res], [("T", t1)], scale=0.5)
            act(T2, src_ap, AF.Square, [src_res], [("T", t2)])
            ts(T2, T2, 0.044715, 1.0, ALU.mult, ALU.add, [("T", t2)], [("T", t2)])
            tt_(T2, T2, T1, ALU.mult, [("T", t1), ("T", t2)], [("T", t2)])
            act(T2, T2, AF.Tanh, [("T", t2)], [("T", t2)], scale=1.5957691216057308)
            stt(dst_ap, T2, 1.0, T1, ALU.add, ALU.mult, [("T", t1), ("T", t2)], [dst_res])

        def tap(name):
            if name in tap_d:
                for dc in range(8):
                    dma("sp", tap_d[name][dc], xres[:, dc, :], [("x", dc, t) for t in range(NTT)], [])

        for l in range(n_layers):
            vb = l * VPL
            if l == 0:
                rmsnorm(xsl, xr, vb + 0, Asl, Ar, NTT, TT)

            with ExitStack() as mix:
                ao = sb(mix, "ao", [128, 4, S], BF16)

                with ExitStack() as sub:
                    P3 = sb(sub, "P3", [72, S], BF16)
                    qTzs = [[sb(sub, "qTz%d_%d" % (s_, i), [128, S], BF16) for i in range(2)] for s_ in range(2)]
                    kTzs = [[sb(sub, "kTz%d_%d" % (s_, i), [128, S], BF16) for i in range(2)] for s_ in range(2)]
                    with ExitStack() as fsub:
                        E2 = sb(fsub, "E2", [72, S], F32)
                        tmpb = sb(fsub, "tmpb", [72, S], BF16)
                        for s_ in range(2):
                            for hi in range(2):
                                z0 = 64 if hi == 0 else 0
                                me = "pool"
                                memset(kTzs[s_][hi][z0:z0 + 64, :], 0.0, [("kTz", s_, hi, "pad")], eng=me)
                                memset(qTzs[s_][hi][z0:z0 + 64, :], 0.0, [("qTz", s_, hi, "pad")], eng=me)
                                memset(kTzs[s_][hi][z0:z0 + 3, :], -1.0, [("kTz", s_, hi, "pad")], eng=me)
                                memset(qTzs[s_][hi][z0:z0 + 6, :], 1.0, [("qTz", s_, hi, "pad")], eng=me)
                        for t in range(NTT):
                            for dc in range(8):
                                mm(ps[0:72, t, :], wf3[:, dc, :], Asl(dc, t), dc == 0, dc == 7,
                                   ["wf3z", ("wf3", 0), ("wf3", 1), ("wf3", 2), Ar(dc, t)], [("ps", t)])
                        if l + 1 < n_layers:
                            load_wf3(l + 1)
                        psr4 = [("ps", t) for t in range(4)]
                        act(ps[0:72, 0:4, :], ps[0:72, 0:4, :], AF.Exp, psr4 + [("negbf", l)], psr4,
                            scale=-1.0, bias=negbf[0:72, l:l + 1])
                        act(E2[:].rearrange("p (a b) -> p a b", b=512), ps[0:72, 0:4, :], AF.Ln, psr4, ["E2"], bias=1.0)
                        P.add("dve", lambda h, o=E2[:]: h.tensor_tensor_scan(out=o, data0=o, data1=o, initial=0.0,
                                                                              op0=ALU.add, op1=ALU.max), ["E2"], ["E2"])

                        def cp(o, i_, rd, wr):
                            act(o, i_, AF.Identity, rd, wr)

                        cp(P3[0:8, :], E2[0:8, :], ["E2"], [("P3", 0)])
                        cp(tmpb[32:40, :], E2[32:40, :], ["E2"], [("tmpb", 1)])
                        tt_(E2[32:40, :], E2[32:40, :], tmpb[32:40, :], ALU.subtract, ["E2", ("tmpb", 1)], [("E2m", 1)])
                        cp(P3[32:40, :], E2[32:40, :], [("E2m", 1)], [("P3", 1)])
                        cp(tmpb[64:72, :], E2[64:72, :], ["E2"], [("tmpb", 2)])
                        tt_(E2[64:72, :], E2[64:72, :], tmpb[64:72, :], ALU.subtract, ["E2", ("tmpb", 2)], [("E2m", 2)])
                        cp(tmpb[64:72, :], E2[64:72, :], [("E2m", 2)], [("tmpb", 2)])
                        tt_(E2[64:72, :], E2[64:72, :], tmpb[64:72, :], ALU.subtract, [("E2m", 2), ("tmpb", 2)], [("E2m", 2)])
                        cp(P3[64:72, :], E2[64:72, :], [("E2m", 2)], [("P3", 2)])
                    P.barrier()
                    vzs = [[sb(sub, "vz%d_%d" % (s_, i), [128, 16, 128], BF16) for i in range(2)] for s_ in range(2)]
                    for s_ in range(2):
                        me = "pool"
                        memset(vzs[s_][0][:, :, 64:128], 1.0, [("vzp", s_, 0)], eng=me)
                        memset(vzs[s_][1][:, :, 0:64], 1.0, [("vzp", s_, 1)], eng=me)
                    p3r = [("P3", i) for i in range(3)]
                    wq_cur = {}

                    def attn_prep(hc):
                        s_ = hc % 2
                        wq_cur[hc] = w_acquire()
                        for hi in range(2):
                            hh = 2 * hc + hi
                            z0 = 64 if hi == 0 else 0
                            for r in range(3):
                                dma("sp", qTzs[s_][hi][z0 + r:z0 + r + 1, :], P3[32 * r + hh:32 * r + hh + 1, :],
                                    p3r + [("qTz", s_, hi, "pad")], [("qTz", s_, hi, "F", r)])
                                dma("sp", kTzs[s_][hi][z0 + 3 + r:z0 + 4 + r, :], P3[32 * r + hh:32 * r + hh + 1, :],
                                    p3r + [("kTz", s_, hi, "pad")], [("kTz", s_, hi, "F", r)])

                    def attn_proj(hc, t):
                        s_ = hc % 2
                        wq, wq_res = wq_cur[hc]
                        b = proj_fm(wq, wq_res, 0, Asl, Ar, 8, t, rr=lo_rr)
                        ts(qTzs[s_][0][0:64, t * TT:(t + 1) * TT], ps[0:64, b, :], 0.125, None, ALU.mult, ALU.bypass,
                           [("ps", b)], [("qT", s_, 0, t)])
                        ts(qTzs[s_][1][64:128, t * TT:(t + 1) * TT], ps[64:128, b, :], 0.125, None, ALU.mult, ALU.bypass,
                           [("ps", b)], [("qT", s_, 1, t)])
                        b = proj_fm(wq, wq_res, 128, Asl, Ar, 8, t, rr=lo_rr)
                        P.add("dve", lambda h, o=kTzs[s_][0][0:64, t * TT:(t + 1) * TT], i_=ps[0:64, b, :]: h.tensor_copy(o, i_),
                              [("ps", b)], [("kT", s_, 0, t)])
                        P.add("dve", lambda h, o=kTzs[s_][1][64:128, t * TT:(t + 1) * TT], i_=ps[64:128, b, :]: h.tensor_copy(o, i_),
                              [("ps", b)], [("kT", s_, 1, t)])
                        b = lo_rr.next()
                        for cc in range(4):
                            c = 4 * t + cc
                            for dc in range(8):
                                mm(ps[:, b, cc * 128:(cc + 1) * 128], A[:, dc, c * 128:(c + 1) * 128],
                                   wq[:, dc, 256:384], dc == 0, dc == 7, [wq_res, Ar(dc, t)], [("ps", b)])
                        psv = ps[:, b, :].rearrange("p (a c) -> p a c", c=128)
                        P.add("dve", lambda h, o=vzs[s_][0][:, 4 * t:4 * t + 4, 0:64], i_=psv[:, :, 0:64]: h.tensor_copy(o, i_),
                              [("ps", b)], [("vz", s_, 0, t)])
                        P.add("dve", lambda h, o=vzs[s_][1][:, 4 * t:4 * t + 4, 64:128], i_=psv[:, :, 64:128]: h.tensor_copy(o, i_),
                              [("ps", b)], [("vz", s_, 1, t)])

                    pending_norm = []
                    attn_prep(0)
                    for t in range(NTT):
                        attn_proj(0, t)
                    for hc in range(4):
                        s_ = hc % 2
                        qTz, kTz, vz = qTzs[s_], kTzs[s_], vzs[s_]
                        if hc + 1 < 4:
                            attn_prep(hc + 1)
                        for Qi in range(NTT):
                            xb = [4 + (Qi % 2) * 2, 5 + (Qi % 2) * 2]
                            nk = 4 * Qi + 4
                            steps = [(kj, hi) for kj in range(nk) for hi in range(2)]
                            LA = 3
                            pend = []
                            for i in range(len(steps) + LA):
                                if i == min(22, len(steps) + LA - 1) and pending_norm:
                                    pending_norm.pop(0)()
                                if i < len(steps):
                                    kj, hi = steps[i]
                                    n0 = max(0, kj - 4 * Qi) * 128
                                    N = TT - n0
                                    sbk = lo_rr.next()
                                    q0 = Qi * TT + n0
                                    diag = kj >= 4 * Qi
                                    frs = [(nm, s_, hi, "F", r) for nm in ("qTz", "kTz") for r in range(3)]
                                    mm(ps[:, sbk, 0:N], kTz[hi][:, kj * 128:(kj + 1) * 128], qTz[hi][:, q0:q0 + N], True, not diag,
                                       [("kT", s_, hi, kj // 4), ("kTz", s_, hi, "pad"), ("qTz", s_, hi, "pad"), ("qT", s_, hi, Qi)] + frs,
                                       [("ps", sbk)])
                                    if diag:
                                        mm(ps[:, sbk, 0:128], ident_bf[:], negmask_bf[:], False, True, ["cbf"], [("ps", sbk)])
                                    pi = Pt_rr.next()
                                    act(Ptp[pi][:, 0:N], ps[:, sbk, 0:N], AF.Exp, [("ps", sbk)], [("Pt", pi)])
                                    pend.append((kj, hi, n0, N, pi))
                                if i >= LA:
                                    kj, hi, n0, N, pi = pend[i - LA]
                                    mm(ps[:, xb[hi], n0:TT], vz[hi][:, kj, :], Ptp[pi][:, 0:N], kj == 0, kj == nk - 1,
                                       [("vz", s_, hi, kj // 4), ("vzp", s_, hi), ("Pt", pi)], [("ps", xb[hi])])
                            if hc + 1 < 4:
                                attn_proj(hc + 1, Qi)
                            ri = rstd_rr.next()
                            P.add("dve", lambda h, o=rstd[ri][0:64, :], i_=ps[0:64, xb[1], :]: h.reciprocal(out=o, in_=i_),
                                  [("ps", xb[1])], [("rstd", ri)])
                            P.add("dve", lambda h, o=rstd[ri][64:128, :], i_=ps[64:128, xb[0], :]: h.reciprocal(out=o, in_=i_),
                                  [("ps", xb[0])], [("rstd", ri)])

                            def norm_tail(ri=ri, xb=xb, hc=hc, Qi=Qi):
                                bs = lo_rr.next()
                                mm(ps[:, bs, :], cst[:, C_SWAP:C_SWAP + 128], rstd[ri][:], True, True,
                                   [("rstd", ri), "cst"], [("ps", bs)])
                                ti = T_rr.next()
                                act(Tp[ti][:], ps[:, bs, :], AF.Identity, [("ps", bs)], [("T", ti)])
                                tt_(ao[0:64, hc, Qi * TT:(Qi + 1) * TT], ps[0:64, xb[0], :], Tp[ti][0:64, :],
                                    ALU.mult, [("ps", xb[0]), ("T", ti)], [("ao", hc, Qi, 0)])
                                tt_(ao[64:128, hc, Qi * TT:(Qi + 1) * TT], ps[64:128, xb[1], :], Tp[ti][64:128, :],
                                    ALU.mult, [("ps", xb[1]), ("T", ti)], [("ao", hc, Qi, 1)])
                            pending_norm.append(norm_tail)
                    while pending_norm:
                        pending_norm.pop(0)()
                P.barrier()
                pa = sb(mix, "pa", [128, 2, S], BF16)
                sc = sb(mix, "sc", [128, 2, S], BF16)

                with ExitStack() as sub:
                    aTs = [sb(sub, "aT%d" % i, [128, 16 + S], F32) for i in range(2)]
                    sA = sb(sub, "sA", [128, 16 + S], F32)
                    sB = sb(sub, "sB", [128, 16 + S], F32)
                    dT = sb(sub, "dT", [128, S], BF16)
                    wbd = sb(sub, "wbd", [128, 2, 128], BF16)
                    t16 = sb(sub, "t16", [128, 16], F32)
                    dma("pool", wbd[:], pool_wbd[l], [], ["wbd"])
                    for bufn, buf in (("aT0", aTs[0]), ("aT1", aTs[1]), ("sA", sA), ("sB", sB)):
                        memset(buf[:, 0:16], 0.0, [(bufn, "pad")])
                    wa, wa_res = w_acquire()
                    for fc in range(2):
                        for t in range(NTT):
                            b = proj_fm(wa, wa_res, fc * 128, Asl, Ar, 8, t)
                            act(aTs[fc][:, 16 + t * TT:16 + (t + 1) * TT], ps[:, b, :], AF.Identity,
                                [("ps", b)], [("aT", fc, t)])
                    for fc in range(2):
                        aT = aTs[fc]
                        allr = [("aT", fc, t) for t in range(NTT)] + [("aT%d" % fc, "pad")]
                        a_ = aT[:, 16:16 + S]
                        tt_(sA[:, 16:], a_, aT[:, 15:15 + S], ALU.add, allr + [("sA", "pad")], ["sA"])
                        if fc == 0:
                            lo, lo_res, lo_w = sA, "sA", 2
                            tt_(sB[:, 16:], sA[:, 16:], sA[:, 14:14 + S], ALU.add, ["sA", ("sA", "pad"), ("sB", "pad")], ["sB"])
                            hi, hi_res, hi_w = sB, "sB", 4
                        else:
                            tt_(sB[:, 16:], sA[:, 16:], sA[:, 14:14 + S], ALU.add, ["sA", ("sA", "pad"), ("sB", "pad")], ["sB"])
                            tt_(sA[:, 16:], sB[:, 16:], sB[:, 12:12 + S], ALU.add, ["sB", ("sB", "pad"), ("sA", "pad")], ["sA"])
                            lo, lo_res, lo_w = sA, "sA", 8
                            tt_(sB[:, 16:], sA[:, 16:], sA[:, 8:8 + S], ALU.add, ["sA", ("sA", "pad"), ("sB", "pad")], ["sB"])
                            hi, hi_res, hi_w = sB, "sB", 16
                        for (p0, buf, bres, win, hidx) in ((0, lo, lo_res, lo_w, 0), (64, hi, hi_res, hi_w, 1)):
                            stt(dT[p0:p0 + 64, :], buf[p0:p0 + 64, 16:], 1.0 / win, aT[p0:p0 + 64, 16:],
                                ALU.mult, ALU.subtract, allr + [bres], [("dT", hidx)])
                            tt_(t16[p0:p0 + 64, :], buf[p0:p0 + 64, 16:32],
                                cst[p0:p0 + 64, C_INV + fc * 16:C_INV + (fc + 1) * 16], ALU.mult,
                                [bres, "cst"], [("t16", hidx)])
                            tt_(dT[p0:p0 + 64, 0:16], t16[p0:p0 + 64, :], aT[p0:p0 + 64, 16:32], ALU.subtract,
                                allr + [("t16", hidx), ("dT", hidx)], [("dT", hidx)])
                        for t in range(NTT):
                            b = bank_rr.next()
                            mm(ps[:, b, :], wbd[:, fc, :], dT[:, t * TT:(t + 1) * TT], True, True,
                               [("dT", 0), ("dT", 1), "wbd"], [("ps", b)])
                            act(pa[:, fc, t * TT:(t + 1) * TT], ps[:, b, :], AF.Identity, [("ps", b), "vecs"],
                                [("pa", fc, t)], scale=vecs[:, vb + 32 + fc:vb + 33 + fc])
                P.barrier()

                with ExitStack() as sub:
                    uT = sb(sub, "uT", [128, 2, S], BF16)
                    gvb = sb(sub, "gvb", [128, 16, 256], BF16)
                    ss = sb(sub, "ss", [128, 16], F32)
                    rs16 = sb(sub, "rs16", [128, 16], F32)
                    gv = sb(sub, "gv", [128, 512], F32)
                    junk = sb(sub, "junk", [128, 256], F32)
                    wTs = sb(sub, "wTs", [128, 4, 128], F32)
                    wcm = sb(sub, "wcm", [128, 4, 128], BF16)
                    bT4 = sb(sub, "bT4", [128, 2, 512], F32)
                    Tg = {10 + i: sb(sub, "Tg%d" % i, [128, 512], F32) for i in range(4)}
                    Tg_rr = RR(sorted(Tg))
                    for g in range(4):
                        dma("sp", wTs[:, g, :], sgu_wT[l, g], [], [("wTs", g)])
                        tt_(wcm[:, g, :], wTs[:, g, :], Umat, ALU.mult, [("wTs", g), "cst"], [("wcm", g)])
                    for fc in range(2):
                        dma("sp", bT4[:, fc, :], sgub4[l, fc], [], [("bT4", fc)])
                    wcu, wcu_res = w_acquire()
                    for fc in range(2):
                        for t in range(NTT):
                            b = proj_fm(wcu, wcu_res, fc * 128, Asl, Ar, 8, t)
                            gelu_from_psum(ps[:, b, :], ("ps", b), uT[:, fc, t * TT:(t + 1) * TT], ("uT", fc, t), 512, Tg, Tg_rr)
                    wcv, wcv_res = w_acquire()
                    for cp in range(8):
                        b = bank_rr.next()
                        for ci in range(2):
                            c = 2 * cp + ci
                            for dc in range(8):
                                mm(ps[:, b, ci * 256:(ci + 1) * 256], A[:, dc, c * 128:(c + 1) * 128], wcv[:, dc, 0:256],
                                   dc == 0, dc == 7, [wcv_res, Ar(dc, c // 4)], [("ps", b)])
                        gelu_from_psum(ps[:, b, :], ("ps", b), gv[:], "gv", 512, Tg, Tg_rr)
                        for ci in range(2):
                            c = 2 * cp + ci
                            stt(junk[:], gv[:, ci * 256:(ci + 1) * 256], 1.0, gv[:, ci * 256:(ci + 1) * 256], ALU.mult, ALU.mult,
                                ["gv"], ["junk", ("ss", c)], accum_out=ss[:, c:c + 1])
                        act(gvb[:, 2 * cp:2 * cp + 2, :], gv[:].rearrange("p (a c) -> p a c", c=256), AF.Identity,
                            ["gv"], [("gvb", 2 * cp), ("gvb", 2 * cp + 1)])
                    ssr = [("ss", c) for c in range(16)]
                    act(rs16[:], ss[:], AF.Sqrt, ssr, ["rs16"], scale=1.0 / 256, bias=EPS)
                    recip(rs16[:], rs16[:], ["rs16"], ["rs16"])
                    for c in range(16):
                        ts(gvb[:, c, :], gvb[:, c, :], rs16[:, c:c + 1], None, ALU.mult, ALU.bypass,
                           [("gvb", c), "rs16"], [("gvb", c)])
                    for fc in range(2):
                        for t in range(NTT):
                            b = bank_rr.next()
                            for cc in range(4):
                                c = t * 4 + cc
                                for gi in range(2):
                                    g = 2 * fc + gi
                                    mm(ps[gi * 64:(gi + 1) * 64, b, cc * 128:(cc + 1) * 128],
                                       gvb[:, c, g * 64:(g + 1) * 64], wcm[:, g, :], True, True,
                                       [("gvb", c), ("wcm", g)], [("ps", b)], tile_position=(0, gi * 64))
                            ti = T_rr.next()
                            stt(Tp[ti][:], ps[:, b, :], vecs[:, vb + 34 + fc:vb + 35 + fc], bT4[:, fc, :],
                                ALU.mult, ALU.add, [("ps", b), "vecs", ("bT4", fc)], [("T", ti)])
                            tt_(sc[:, fc, t * TT:(t + 1) * TT], Tp[ti][:], uT[:, fc, t * TT:(t + 1) * TT], ALU.mult,
                                [("T", ti), ("uT", fc, t)], [("sc", fc, t)])
                P.barrier()

                P.barrier()
                for nm, buf, nch in (("pa", pa, 2), ("ao", ao, 4), ("sc", sc, 2)):
                    if l == 0 and nm in tap_d:
                        for kc in range(nch):
                            dma("pool", tap_d[nm][kc], buf[:, kc, :], [], [])
                P.barrier()

                with ExitStack() as sub:
                    mg = sb(sub, "mg", [128, 8, S], BF16)
                    for j in range(8):
                        wg, wg_res = w_acquire()
                        wb, wb_res = w_acquire(held=1)
                        for t in range(NTT):
                            gb = []
                            for br in range(3):
                                b = proj_fm(wg, wg_res, br * 128, Asl, Ar, 8, t)
                                gb.append(b)
                            tl = []
                            for br in range(3):
                                ti = T_rr.next()
                                tl.append(ti)
                                act(Tp[ti][:], ps[:, gb[br], :], AF.Tanh, [("ps", gb[br]), ("halfb", l)], [("T", ti)],
                                    scale=0.5, bias=halfb[:, l * 24 + br * 8 + j:l * 24 + br * 8 + j + 1])
                            srcs = [(pa, 0, 2, lambda kc, t_: ("pa", kc, t_)),
                                    (ao, 2, 4, None),
                                    (sc, 6, 2, lambda kc, t_: ("sc", kc, t_))]
                            for br, (buf, k0, nk, rf) in enumerate(srcs):
                                b = bank_rr.next()
                                for kc in range(nk):
                                    rr_ = [("ao", kc, t, 0), ("ao", kc, t, 1)] if rf is None else [rf(kc, t)]
                                    mm(ps[:, b, :], wb[:, k0 + kc, 0:128], buf[:, kc, t * TT:(t + 1) * TT], kc == 0, kc == nk - 1,
                                       [wb_res] + rr_, [("ps", b)])
                                ti = tl[br]
                                stt(Tp[ti][:], Tp[ti][:], 1.0, ps[:, b, :], ALU.add, ALU.mult, [("T", ti), ("ps", b)], [("T", ti)])
                            tt_(Tp[tl[0]][:], Tp[tl[0]][:], Tp[tl[1]][:], ALU.add, [("T", tl[0]), ("T", tl[1])], [("T", tl[0])])
                            tt_(mg[:, j, t * TT:(t + 1) * TT], Tp[tl[0]][:], Tp[tl[2]][:], ALU.add,
                                [("T", tl[0]), ("T", tl[2])], [("mg", j, t)])
                    resid_proj_then_norm(
                        lambda kc, t_: mg[:, kc, t_ * TT:(t_ + 1) * TT], lambda kc, t_: ("mg", kc, t_),
                        lambda j, t, b: stt(xsl(j, t), ps[:, b, :], 0.5, xsl(j, t), ALU.mult, ALU.add,
                                            [("ps", b), xr(j, t)], [xr(j, t)]),
                        (vb + 36, Asl, Ar))
            P.barrier()
            if l == 0:
                tap("mix")

            with ExitStack() as xa:
                xq = sb(xa, "xq", [128, 8, S], BF16)
                hmT = sb(xa, "hmT", [128, 8, MEM], BF16)
                xkT = sb(xa, "xkT", [128, 8, MEM], BF16)
                xv = sb(xa, "xv", [128, 2, D], BF16)
                with ExitStack() as sub:
                    mT = sb(sub, "mT", [128, 8, MEM], F32)
                    for dc in range(8):
                        dma("sp", mT[:, dc, :], memT[dc], [], [("mT", dc)])
                    rmsnorm(lambda dc, t: mT[:, dc, :], lambda dc, t: ("mT", dc), vb + 44,
                            lambda dc, t: hmT[:, dc, :], lambda dc, t: ("hmT", dc), 1, MEM)
                def xa_kproj(q):
                    wk, wk_res = w_acquire()
                    for jj in range(2):
                        j = 2 * q + jj
                        b = bank_rr.next()
                        for dc in range(8):
                            mm(ps[:, b, 0:MEM], wk[:, dc, jj * 128:(jj + 1) * 128], hmT[:, dc, :], dc == 0, dc == 7,
                               [wk_res, ("hmT", dc)], [("ps", b)])
                        act(xkT[:, j, :], ps[:, b, 0:MEM], AF.Identity, [("ps", b)], [("xkT", j)])

                def xa_vproj(q):
                    wv_, wv_res = w_acquire()
                    for mc in range(2):
                        b = bank_rr.next()
                        for dc in range(8):
                            mm(ps[:, b, 0:256], hmT[:, dc, mc * 128:(mc + 1) * 128], wv_[:, dc, 0:256], dc == 0, dc == 7,
                               [wv_res, ("hmT", dc)], [("ps", b)])
                        act(xv[:, mc, q * 256:(q + 1) * 256], ps[:, b, 0:256], AF.Identity, [("ps", b)], [("xv", mc, q)])

                def xa_qproj(q):
                    wq_, wq_res = w_acquire()
                    for jj in range(2):
                        j = 2 * q + jj
                        for t in range(NTT):
                            b = proj_fm(wq_, wq_res, jj * 128, Asl, Ar, 8, t)
                            act(xq[:, j, t * TT:(t + 1) * TT], ps[:, b, :], AF.Identity, [("ps", b)], [("xq", j, t)],
                                scale=1.0 / 16)

                for q in range(4):
                    xa_qproj(q)
                    xa_kproj(q)
                    xa_vproj(q)
                hi_rr = RR(range(4, 8))
                items = [(t, hh) for t in range(NTT) for hh in range(4)]
                xpend = []

                def xa_scores(t, hh):
                    pts = []
                    for mc in range(2):
                        b = lo_rr.next()
                        for dk in range(2):
                            mm(ps[:, b, :], xkT[:, 2 * hh + dk, mc * 128:(mc + 1) * 128],
                               xq[:, 2 * hh + dk, t * TT:(t + 1) * TT], dk == 0, dk == 1,
                               [("xkT", 2 * hh + dk), ("xq", 2 * hh + dk, t)], [("ps", b)])
                        pi = Pt_rr.next()
                        act(Ptp[pi][:], ps[:, b, :], AF.Exp, [("ps", b)], [("Pt", pi)])
                        pts.append(pi)
                    return pts

                def xa_pv(t, hh, pts):
                    bd = hi_rr.next()
                    for mc in range(2):
                        mm(ps[:, bd, :], ones_bf[:], Ptp[pts[mc]][:], mc == 0, mc == 1,
                           ["ones", ("Pt", pts[mc])], [("ps", bd)])
                    ri = rstd_rr.next()
                    P.add("dve", lambda h, o=rstd[ri][:], i_=ps[:, bd, :]: h.reciprocal(out=o, in_=i_),
                          [("ps", bd)], [("rstd", ri)])
                    for dch in range(2):
                        b = hi_rr.next()
                        for mc in range(2):
                            mm(ps[:, b, :], xv[:, mc, hh * 256 + dch * 128: hh * 256 + (dch + 1) * 128], Ptp[pts[mc]][:],
                               mc == 0, mc == 1, [("xv", mc, hh), ("Pt", pts[mc])], [("ps", b)])
                        tt_(xq[:, 2 * hh + dch, t * TT:(t + 1) * TT], ps[:, b, :], rstd[ri][:], ALU.mult,
                            [("ps", b), ("rstd", ri)], [("xq", 2 * hh + dch, t)])

                for i in range(len(items) + 1):
                    if i < len(items):
                        xpend.append(xa_scores(*items[i]))
                    if i >= 1:
                        xa_pv(items[i - 1][0], items[i - 1][1], xpend[i - 1])
                resid_proj_then_norm(
                    lambda kc, t_: xq[:, kc, t_ * TT:(t_ + 1) * TT], lambda kc, t_: ("xq", kc, t_),
                    lambda j, t, b: tt_(xsl(j, t), ps[:, b, :], xsl(j, t), ALU.add, [("ps", b), xr(j, t)], [xr(j, t)]),
                    (vb + 52, Asl, Ar))
            P.barrier()
            if l == 0:
                tap("xat")

            with ExitStack() as ff:
                hid = sb(ff, "hid", [128, 8, S], BF16)
                for fg in range(4):
                    for q in range(4):
                        w1, w1_res = w_acquire()
                        for jj in range(2):
                            fcl = 2 * q + jj
                            for t in range(NTT):
                                b = proj_fm(w1, w1_res, jj * 128, Asl, Ar, 8, t)
                                ti = T_rr.next()
                                act(Tp[ti][:], ps[:, b, :], AF.Relu, [("ps", b)], [("T", ti)])
                                tt_(hid[:, fcl, t * TT:(t + 1) * TT], Tp[ti][:], Tp[ti][:], ALU.mult,
                                    [("T", ti)], [("hid", fcl, t)])
                    hid_fn = lambda kc, t_: hid[:, kc, t_ * TT:(t_ + 1) * TT]
                    hid_res = lambda kc, t_: ("hid", kc, t_)
                    ev2 = lambda j, t, b: tt_(xsl(j, t), ps[:, b, :], xsl(j, t), ALU.add, [("ps", b), xr(j, t)], [xr(j, t)])
                    if fg < 3:
                        for p_ in range(3):
                            w2, w2_res = w_acquire()
                            for j in range(3 * p_, min(3 * p_ + 3, 8)):
                                for t in range(NTT):
                                    b = proj_fm(w2, w2_res, (j - 3 * p_) * 128, hid_fn, hid_res, 8, t)
                                    ev2(j, t, b)
                    else:
                        if l + 1 < n_layers:
                            nargs = ((l + 1) * VPL + 0, Asl, Ar)
                            post = None
                        else:
                            nargs = (VPL * DEPTH, xsl, xr)

                            def post(t):
                                for dc in range(8):
                                    dma("sp", outT[dc, :, t * TT:(t + 1) * TT], xres[:, dc, t * TT:(t + 1) * TT], [xr(dc, t)], [])
                        resid_proj_then_norm(hid_fn, hid_res, ev2, nargs, post)
            P.barrier()
            if l == 0:
                tap("ffn")

        cnt = P.emit(nc)
        nc._cnt = cnt
        nc._nops = len(P.ops)
    return nc


def host_consts():
    cst = np.zeros((128, NCST), np.float32)
    cst[:, C_ID:C_ID + 128] = np.eye(128, dtype=np.float32)
    k = np.arange(128)
    cst[:, C_U:C_U + 128] = (k[:, None] <= k[None, :]).astype(np.float32)
    wins = {(0, 0): 2, (0, 1): 4, (1, 0): 8, (1, 1): 16}
    t = np.arange(16)
    for fc in range(2):
        for half in range(2):
            w = wins[(fc, half)]
            cst[half * 64:(half + 1) * 64, C_INV + fc * 16:C_INV + (fc + 1) * 16] = \
                (1.0 / np.minimum(t + 1, w)).astype(np.float32)[None, :]
    for kk in range(128):
        cst[kk, C_SWAP + (kk + 64) % 128] = 1.0
    cst[:, C_NEG:C_NEG + 128] = np.where(k[:, None] > k[None, :], -30000.0, 0.0).astype(np.float32)
    return cst


def pack_vecs(inp):
    v = np.zeros((128, NV), np.float32)

    def fm(a):
        a = np.asarray(a, np.float32)
        return a.reshape(-1, 128).T

    for l in range(DEPTH):
        b = l * VPL
        v[:, b + 0:b + 8] = fm(inp["norm_mix_g"][l])
        v[:, b + 8:b + 32] = fm(inp["b_gate"][l])
        v[:, b + 32:b + 34] = fm(inp["pool_scale"][l])
        v[:, b + 34:b + 36] = fm(inp["sgu_norm_g"][l])
        v[:, b + 36:b + 44] = fm(inp["norm_xattn_g"][l])
        v[:, b + 44:b + 52] = fm(inp["norm_mem_g"][l])
        v[:, b + 52:b + 60] = fm(inp["norm_ffn_g"][l])
        for r in range(3):
            v[32 * r:32 * r + 8, b + 60] = np.asarray(inp["b_forget"][l], np.float32)
    v[:, VPL * DEPTH:VPL * DEPTH + 8] = fm(inp["final_norm_g"])
    return v


def shared_inputs(inp):
    f = lambda k: np.ascontiguousarray(np.asarray(inp[k], np.float32))
    sgu_b = np.asarray(inp["sgu_b"], np.float32)
    sgub4 = np.zeros((DEPTH, 2, 128, 512), np.float32)
    for fc in range(2):
        for gi in range(2):
            sgub4[:, fc, gi * 64:(gi + 1) * 64, :] = np.tile(sgu_b[:, 2 * fc + gi, :], (1, 4))[:, None, :]
    pw = np.asarray(inp["pool_w"], np.float32)
    pool_wbd = np.zeros((DEPTH, 128, 2, 128), np.float32)
    for g in range(4):
        gi = g % 2
        pool_wbd[:, gi * 64:(gi + 1) * 64, g // 2, gi * 64:(gi + 1) * 64] = pw[:, g]
    return {
        "w_in": f("w_in"), "pool_wbd": pool_wbd,
        "sgu_wT": np.ascontiguousarray(np.asarray(inp["sgu_w"], np.float32).transpose(0, 1, 3, 2)),
        "sgub4": sgub4,
        "w_branch_a": f("w_branch_a"), "w_branch_b": f("w_branch_b"), "w_branch_c": f("w_branch_c"),
        "w_out": f("w_out"), "w_xq": f("w_xq"), "w_xkv": f("w_xkv"), "w_xo": f("w_xo"),
        "w_ff1": f("w_ff1"), "w_ff2": f("w_ff2"),
        "vecs": pack_vecs(inp), "cst": host_consts(),
    }


def core_inputs(inp, b):
    x = np.asarray(inp["x"], np.float32)[b]
    m = np.asarray(inp["mem"], np.float32)[b]
    return {
        "xT": np.ascontiguousarray(x.T).reshape(8, 128, S),
        "memT": np.ascontiguousarray(m.T).reshape(8, 128, MEM),
    }


_NC_CACHE = {}


def kernel(**inputs):
    if "nc" not in _NC_CACHE:
        _NC_CACHE["nc"] = build()
    nc = _NC_CACHE["nc"]
    shared = shared_inputs(inputs)
    in_maps = []
    for b in range(8):
        m = dict(shared)
        m.update(core_inputs(inputs, b))
        in_maps.append(m)
    res = run_bass_kernel_spmd(nc, in_maps, core_ids=list(range(8)))
    out = np.empty((8, S, D), np.float32)
    for b in range(8):
        out[b] = res.results[b]["outT"].reshape(D, S).T
    return out
```

```python
import numpy as np
from contextlib import ExitStack
import concourse.bass as bass
import concourse.mybir as mybir
from concourse.bass_utils import run_bass_kernel_spmd

F32 = mybir.dt.float32
BF16 = mybir.dt.bfloat16
AF = mybir.ActivationFunctionType
ALU = mybir.AluOpType

ENGS = ("pe", "act", "dve", "pool", "sp")
N_DMA_SEMS = 24

D = 1024
S = 2048
DEPTH = 2
MEM = 256
TT = 512
NTT = S // TT
OFF_A, OFF_Q, OFF_K, OFF_V, OFF_F, OFF_C, OFF_G = 0, 256, 768, 1280, 1792, 1800, 2312
N_IN = 5384
EPS = 1e-6
SLOTW = 384
NSLOT = 4
VPL = 61
NV = VPL * DEPTH + 8
C_ID, C_U, C_INV, C_SWAP, C_NEG, NCST = 0, 128, 256, 288, 416, 544


class Op:
    __slots__ = ("idx", "eng", "fn", "deps", "dma", "need_sig", "sig", "prev_use", "group")

    def __init__(self, idx, eng, fn, deps, dma, group):
        self.idx = idx
        self.eng = eng
        self.fn = fn
        self.deps = deps
        self.dma = dma
        self.need_sig = dma
        self.sig = None
        self.prev_use = None
        self.group = group


class Prog:
    def __init__(self, same_engine_sync=True):
        self.ops = []
        self.last_w = {}
        self.readers = {}
        self.same_engine_sync = same_engine_sync
        self.barrier_deps = {}
        self.last_on = {}
        self.open_dma = []

    def add(self, eng, fn, reads=(), writes=(), dma=False, group=None, persistent=False):
        idx = len(self.ops)
        deps = set()
        for r in reads:
            w = self.last_w.get(r)
            if w is not None:
                deps.update(w[1])
        for r in writes:
            w = self.last_w.get(r)
            if w is not None:
                if not (group is not None and w[0] == group):
                    deps.update(w[1])
            for rd in self.readers.get(r, ()):
                deps.add(rd)
        for r in reads:
            self.readers.setdefault(r, []).append(idx)
        for r in writes:
            w = self.last_w.get(r)
            if group is not None and w is not None and w[0] == group:
                w[1].append(idx)
            else:
                self.last_w[r] = (group, [idx])
                self.readers[r] = []
        if eng in self.barrier_deps:
            deps.update(self.barrier_deps.pop(eng))
        deps.discard(idx)
        self.ops.append(Op(idx, eng, fn, deps, dma, group))
        self.last_on[eng] = idx
        if dma and not persistent:
            self.open_dma.append(idx)
        return idx

    def barrier(self):
        deps = set(self.open_dma)
        for e, i in self.last_on.items():
            if not self.ops[i].dma:
                deps.add(i)
        self.open_dma = []
        for e in ENGS:
            self.barrier_deps.setdefault(e, set()).update(deps)

    def emit(self, nc, final_wait_eng="sp"):
        ops = self.ops
        for op in ops:
            nd = set()
            for d in op.deps:
                dop = ops[d]
                if (not dop.dma) and (not op.dma) and dop.eng == op.eng:
                    if op.eng == "pe" or not self.same_engine_sync:
                        continue
                nd.add(d)
            best = {}
            keep = set()
            for d in nd:
                dop = ops[d]
                if dop.dma:
                    keep.add(d)
                elif best.get(dop.eng, -1) < d:
                    best[dop.eng] = d
            keep.update(best.values())
            op.deps = keep
            for d in keep:
                ops[d].need_sig = True
        cnt = {e: 0 for e in ENGS}
        dma_use = [0] * N_DMA_SEMS
        half = N_DMA_SEMS // 2
        dma_rr_q = {"pool": 0, "sp": 0}
        for op in ops:
            if op.dma:
                qn = "pool" if op.eng == "pool" else "sp"
                s = dma_rr_q[qn] + (half if qn == "pool" else 0)
                dma_rr_q[qn] = (dma_rr_q[qn] + 1) % half
                op.prev_use = dma_use[s]
                dma_use[s] += 1
                op.sig = (("dma", s), 16 * dma_use[s])
            elif op.need_sig:
                cnt[op.eng] += 1
                op.sig = (("eng", op.eng), cnt[op.eng])
        with ExitStack() as es:
            sems = {}
            for e in ENGS:
                sems[("eng", e)] = es.enter_context(nc.semaphore("sem_" + e))
            for i in range(N_DMA_SEMS):
                sems[("dma", i)] = es.enter_context(nc.semaphore("sem_dma%d" % i))
            block = es.enter_context(nc.Block())
            streams = {e: [op for op in ops if op.eng == e] for e in ENGS}
            final = {}
            for op in ops:
                if op.dma:
                    k, v = op.sig
                    final[k] = max(final.get(k, 0), v)

            def run_stream(eng_name, h):
                known = {}
                for op in streams[eng_name]:
                    waits = {}
                    for d in op.deps:
                        k, v = ops[d].sig
                        if waits.get(k, 0) < v:
                            waits[k] = v
                    if op.dma and op.prev_use:
                        k = op.sig[0]
                        v = 16 * op.prev_use
                        if waits.get(k, 0) < v:
                            waits[k] = v
                    for k, v in waits.items():
                        if known.get(k, 0) >= v:
                            continue
                        h.wait_ge(sems[k], v)
                        known[k] = v
                    ins = op.fn(h)
                    if op.sig is not None:
                        k, v = op.sig
                        ins.then_inc(sems[k], 16 if op.dma else 1)
                if eng_name == final_wait_eng:
                    for k, v in final.items():
                        if known.get(k, 0) < v:
                            h.wait_ge(sems[k], v)

            @block.tensor
            def _(h):
                run_stream("pe", h)

            @block.scalar
            def _(h):
                run_stream("act", h)

            @block.vector
            def _(h):
                run_stream("dve", h)

            @block.gpsimd
            def _(h):
                run_stream("pool", h)

            @block.sync
            def _(h):
                run_stream("sp", h)
        return cnt


def build(n_layers=DEPTH, taps=(), same_engine_sync=True):
    nc = bass.Bass("TRN2", target_bir_lowering=False)

    def din(name, shape):
        return nc.dram_tensor(name, list(shape), F32, kind="ExternalInput").ap()

    xT = din("xT", [8, 128, S])
    memT = din("memT", [8, 128, MEM])
    w_in = din("w_in", [DEPTH, D, N_IN])
    pool_wbd = din("pool_wbd", [DEPTH, 128, 2, 128])
    sgu_wT = din("sgu_wT", [DEPTH, 4, 128, 128])
    sgub4 = din("sgub4", [DEPTH, 2, 128, 512])
    w_ba = din("w_branch_a", [DEPTH, 256, D])
    w_bb = din("w_branch_b", [DEPTH, 512, D])
    w_bc = din("w_branch_c", [DEPTH, 256, D])
    w_out = din("w_out", [DEPTH, D, D])
    w_xq = din("w_xq", [DEPTH, D, D])
    w_xkv = din("w_xkv", [DEPTH, D, 2 * D])
    w_xo = din("w_xo", [DEPTH, D, D])
    w_ff1 = din("w_ff1", [DEPTH, D, 4 * D])
    w_ff2 = din("w_ff2", [DEPTH, 4 * D, D])
    vecs_d = din("vecs", [128, NV])
    cst_d = din("cst", [128, NCST])
    outT = nc.dram_tensor("outT", [8, 128, S], F32, kind="ExternalOutput").ap()
    tap_d = {}
    for tname in taps:
        tap_d[tname] = nc.dram_tensor("tap_" + tname, [8, 128, S], F32, kind="ExternalOutput").ap()

    P = Prog(same_engine_sync=same_engine_sync)

    with ExitStack() as outer:
        _uid = [0]

        def sb(es, name, shape, dt):
            _uid[0] += 1
            return es.enter_context(nc.sbuf_tensor("s%d_%s" % (_uid[0], name), list(shape), dt))

        ps = outer.enter_context(nc.psum_tensor("ps", [128, 8, 512], F32))
        xres = sb(outer, "xres", [128, 8, S], F32)
        A = sb(outer, "A", [128, 8, S], BF16)
        wsl = [sb(outer, "wsl%d" % i, [128, 8, SLOTW], BF16) for i in range(NSLOT)]
        vecs = sb(outer, "vecs", [128, NV], F32)
        cst = sb(outer, "cst", [128, NCST], F32)
        halfb = sb(outer, "halfb", [128, DEPTH * 24], F32)
        negbf = sb(outer, "negbf", [128, DEPTH], F32)
        ones_bf = sb(outer, "ones_bf", [128, 128], BF16)
        wf3 = sb(outer, "wf3", [128, 8, 72], BF16)
        ident_bf = sb(outer, "ident_bf", [128, 128], BF16)
        negmask_bf = sb(outer, "negmask_bf", [128, 128], BF16)
        Tp = [sb(outer, "T%d" % i, [128, 512], F32) for i in range(3)]
        Ptp = [sb(outer, "Pt%d" % i, [128, 512], BF16) for i in range(4)]
        rstd = [sb(outer, "rstd%d" % i, [128, 512], F32) for i in range(2)]
        sqp = {2: Ptp[2], 3: Ptp[3]}

        ident = cst[:, C_ID:C_ID + 128]
        Umat = cst[:, C_U:C_U + 128]

        def act(out, in_, func, reads, writes, **kw):
            P.add("act", lambda h: h.activation(out=out, in_=in_, func=func, **kw), reads, writes)

        def mm(out, lhsT, rhs, start, stop, reads, writes, **kw):
            P.add("pe", lambda h: h.matmul(out, lhsT=lhsT, rhs=rhs, start=start, stop=stop, **kw), reads, writes)

        def stt(out, in0, scalar, in1, op0, op1, reads, writes, **kw):
            P.add("dve", lambda h: h.scalar_tensor_tensor(out=out, in0=in0, scalar=scalar, in1=in1, op0=op0, op1=op1, **kw), reads, writes)

        def tt_(out, in0, in1, op, reads, writes, eng="dve"):
            P.add(eng, lambda h: h.tensor_tensor(out=out, in0=in0, in1=in1, op=op), reads, writes)

        def ts(out, in0, s1, s2, op0, op1, reads, writes, eng="dve"):
            P.add(eng, lambda h: h.tensor_scalar(out=out, in0=in0, scalar1=s1, scalar2=s2, op0=op0, op1=op1), reads, writes)

        def recip(out, in_, reads, writes):
            P.add("dve", lambda h: h.reciprocal(out=out, in_=in_), reads, writes)

        def memset(ap, val, writes, eng="dve"):
            P.add(eng, lambda h: h.memset(ap, val), (), writes)

        def dma(eng, out, in_, reads, writes, group=None, persistent=False):
            P.add(eng, lambda h: h.dma_start(out=out, in_=in_), reads, writes, dma=True, group=group, persistent=persistent)

        class RR:
            def __init__(self, items):
                self.items = list(items)
                self.i = 0

            def next(self):
                v = self.items[self.i % len(self.items)]
                self.i += 1
                return v

        T_rr = RR(range(3))
        Pt_rr = RR(range(4))
        sq_rr = RR((2, 3))
        rstd_rr = RR(range(2))

        def wsrc(ap2d):
            return ap2d.rearrange("(kc p) n -> p kc n", p=128)

        W3 = ((0, 384), (384, 768), (768, 1024))
        TORD = [(jj, t) for t in range(NTT - 1) for jj in range(2)] + [(jj, NTT - 1) for jj in range(2)]
        wsched = []

        def sched_layer(l):
            for hc in range(4):
                wsched.append([(0, 8, i * 128, (i + 1) * 128, w_in[l, :, off + hc * 128: off + (hc + 1) * 128])
                               for i, off in enumerate((OFF_Q, OFF_K, OFF_V))])
            wsched.append([(0, 8, 0, 256, w_in[l, :, OFF_A:OFF_A + 256])])
            wsched.append([(0, 8, 0, 256, w_in[l, :, OFF_C:OFF_C + 256])])
            wsched.append([(0, 8, 0, 256, w_in[l, :, OFF_C + 256:OFF_C + 512])])
            for j in range(8):
                wsched.append([(0, 8, br * 128, (br + 1) * 128,
                                w_in[l, :, OFF_G + br * 1024 + j * 128: OFF_G + br * 1024 + (j + 1) * 128])
                               for br in range(3)])
                wsched.append([(0, 2, 0, 128, w_ba[l, :, j * 128:(j + 1) * 128]),
                               (2, 6, 0, 128, w_bb[l, :, j * 128:(j + 1) * 128]),
                               (6, 8, 0, 128, w_bc[l, :, j * 128:(j + 1) * 128])])
            for (c0, c1) in W3:
                wsched.append([(0, 8, 0, c1 - c0, w_out[l, :, c0:c1])])
            for q in range(4):
                wsched.append([(0, 8, 0, 256, w_xq[l, :, q * 256:(q + 1) * 256])])
                wsched.append([(0, 8, 0, 256, w_xkv[l, :, q * 256:(q + 1) * 256])])
                wsched.append([(0, 8, 0, 256, w_xkv[l, :, (4 + q) * 256:(5 + q) * 256])])
            for (c0, c1) in W3:
                wsched.append([(0, 8, 0, c1 - c0, w_xo[l, :, c0:c1])])
            for fg in range(4):
                for q in range(4):
                    wsched.append([(0, 8, 0, 256, w_ff1[l, :, fg * 1024 + q * 256: fg * 1024 + (q + 1) * 256])])
                for (c0, c1) in W3:
                    wsched.append([(0, 8, 0, c1 - c0, w_ff2[l, fg * 1024:(fg + 1) * 1024, c0:c1])])

        for l in range(n_layers):
            sched_layer(l)
        wstate = {"issued": 0, "next": 0}

        def w_issue_upto(n):
            while wstate["issued"] < min(n, len(wsched)):
                i = wstate["issued"]
                slot = i % NSLOT
                for (k0, k1, c0, c1, src) in wsched[i]:
                    dma("pool", wsl[slot][:, k0:k1, c0:c1], wsrc(src), [], [("ws", slot)], group=("w", i), persistent=True)
                wstate["issued"] += 1

        def w_acquire(held=0):
            i = wstate["next"]
            wstate["next"] += 1
            w_issue_upto(i - held + NSLOT)
            slot = i % NSLOT
            return wsl[slot], ("ws", slot)

        dma("sp", vecs[:], vecs_d, [], ["vecs"])
        dma("sp", cst[:], cst_d, [], ["cst"])
        for t in range(NTT):
            for dc in range(8):
                dma("sp", xres[:, dc, t * TT:(t + 1) * TT], xT[dc, :, t * TT:(t + 1) * TT], [], [("x", dc, t)])
        memset(ones_bf[:], 1.0, ["ones"])
        memset(wf3[:], 0.0, ["wf3z"])
        P.add("dve", lambda h: h.tensor_copy(ident_bf[:], cst[:, C_ID:C_ID + 128]), ["cst"], ["cbf"])
        P.add("dve", lambda h: h.tensor_copy(negmask_bf[:], cst[:, C_NEG:C_NEG + 128]), ["cst"], ["cbf"])
        for l in range(DEPTH):
            ts(halfb[:, l * 24:(l + 1) * 24], vecs[:, l * VPL + 8:l * VPL + 32], 0.5, None, ALU.mult, ALU.bypass,
               ["vecs"], [("halfb", l)])
            ts(negbf[:, l:l + 1], vecs[:, l * VPL + 60:l * VPL + 61], -1.0, None, ALU.mult, ALU.bypass,
               ["vecs"], [("negbf", l)])
        def load_wf3(l_):
            for r in range(3):
                dma("pool", wf3[:, :, 32 * r:32 * r + 8], wsrc(w_in[l_, :, OFF_F:OFF_F + 8]), ["wf3z"], [("wf3", r)],
                    persistent=True)

        load_wf3(0)
        w_issue_upto(NSLOT)

        def norm_tile(src, src_res, gcol0, dst, dst_res, t, tw):
            b = bank_rr.next()
            for dc in range(8):
                qi = sq_rr.next()
                act(sqp[qi][:, :tw], src(dc, t), AF.Square, [src_res(dc, t)], [("Pt", qi)])
                mm(ps[:, b, :tw], ones_bf[:], sqp[qi][:, :tw], dc == 0, dc == 7,
                   [("Pt", qi), "ones"], [("ps", b)])
            act(ps[:, b, :tw], ps[:, b, :tw], AF.Sqrt, [("ps", b)], [("ps", b)], scale=1.0 / D, bias=EPS)
            ri = rstd_rr.next()
            P.add("dve", lambda h, o=rstd[ri][:, :tw], i_=ps[:, b, :tw]: h.reciprocal(out=o, in_=i_),
                  [("ps", b)], [("rstd", ri)])
            for dc in range(8):
                stt(dst(dc, t), src(dc, t), vecs[:, gcol0 + dc:gcol0 + dc + 1], rstd[ri][:, :tw],
                    ALU.mult, ALU.mult, [src_res(dc, t), ("rstd", ri), "vecs"], [dst_res(dc, t)])

        def rmsnorm(src, src_res, gcol0, dst, dst_res, ntiles, tw):
            for t in range(ntiles):
                norm_tile(src, src_res, gcol0, dst, dst_res, t, tw)

        def acquire3():
            return [w_acquire(held=i) for i in range(3)]

        def chunk_w(ws, j):
            p_ = (j * 128) // 384
            return ws[p_][0], ws[p_][1], j * 128 - p_ * 384

        def resid_proj_then_norm(src_fn, src_res_fn, evac, norm_args, post_tile=None):
            ws = acquire3()

            def norm_part(t):
                if norm_args is not None:
                    norm_tile(xsl, xr, norm_args[0], norm_args[1], norm_args[2], t, TT)
                    if post_tile is not None:
                        post_tile(t)

            for t in range(NTT):
                for j in range(8):
                    wt, wres, c0 = chunk_w(ws, j)
                    b = proj_fm(wt, wres, c0, src_fn, src_res_fn, 8, t)
                    evac(j, t, b)
                if t >= 1:
                    norm_part(t - 1)
            return lambda: norm_part(NTT - 1)

        def xsl(dc, t):
            return xres[:, dc, t * TT:(t + 1) * TT]

        def Asl(dc, t):
            return A[:, dc, t * TT:(t + 1) * TT]

        def xr(dc, t):
            return ("x", dc, t)

        def Ar(dc, t):
            return ("A", dc, t)

        bank_rr = RR(range(8))
        lo_rr = RR(range(4))

        def proj_fm(wt, wres, c0, rhs_fn, rhs_res_fn, nk, t, extra_reads=(), rr=None):
            b = (rr or bank_rr).next()
            for kc in range(nk):
                mm(ps[:, b, :], wt[:, kc, c0:c0 + 128], rhs_fn(kc, t), kc == 0, kc == nk - 1,
                   [wres, rhs_res_fn(kc, t)] + list(extra_reads), [("ps", b)])
            return b

        def gelu_from_psum(src_ap, src_res, dst_ap, dst_res, width, pool=None, rr=None):
            pool = pool or Tp
            rr = rr or T_rr
            t1 = rr.next()
            t2 = rr.next()
            T1 = pool[t1][:, :width]
            T2 = pool[t2][:, :width]
            act(T1, src_ap, AF.Identity, [src_res], [("T", t1)], scale=0.5)
            act(T2, src_ap, AF.Square, [src_res], [("T", t2)])
            ts(T2, T2, 0.044715, 1.0, ALU.mult, ALU.add, [("T", t2)], [("T", t2)])
            tt_(T2, T2, T1, ALU.mult, [("T", t1), ("T", t2)], [("T", t2)])
            act(T2, T2, AF.Tanh, [("T", t2)], [("T", t2)], scale=1.5957691216057308)
            stt(dst_ap, T2, 1.0, T1, ALU.add, ALU.mult, [("T", t1), ("T", t2)], [dst_res])

        def tap(name):
            if name in tap_d:
                for dc in range(8):
                    dma("sp", tap_d[name][dc], xres[:, dc, :], [("x", dc, t) for t in range(NTT)], [])

        for l in range(n_layers):
            vb = l * VPL
            if l == 0:
                rmsnorm(xsl, xr, vb + 0, Asl, Ar, NTT, TT)

            with ExitStack() as mix:
                ao = sb(mix, "ao", [128, 4, S], BF16)

                with ExitStack() as sub:
                    P3 = sb(sub, "P3", [72, S], BF16)
                    qTzs = [[sb(sub, "qTz%d_%d" % (s_, i), [128, S], BF16) for i in range(2)] for s_ in range(2)]
                    kTzs = [[sb(sub, "kTz%d_%d" % (s_, i), [128, S], BF16) for i in range(2)] for s_ in range(2)]
                    with ExitStack() as fsub:
                        E2 = sb(fsub, "E2", [72, S], F32)
                        tmpb = sb(fsub, "tmpb", [72, S], BF16)
                        for s_ in range(2):
                            for hi in range(2):
                                z0 = 64 if hi == 0 else 0
                                me = "pool"
                                memset(kTzs[s_][hi][z0:z0 + 64, :], 0.0, [("kTz", s_, hi, "pad")], eng=me)
                                memset(qTzs[s_][hi][z0:z0 + 64, :], 0.0, [("qTz", s_, hi, "pad")], eng=me)
                                memset(kTzs[s_][hi][z0:z0 + 3, :], -1.0, [("kTz", s_, hi, "pad")], eng=me)
                                memset(qTzs[s_][hi][z0:z0 + 6, :], 1.0, [("qTz", s_, hi, "pad")], eng=me)
                        for t in range(NTT):
                            for dc in range(8):
                                mm(ps[0:72, t, :], wf3[:, dc, :], Asl(dc, t), dc == 0, dc == 7,
                                   ["wf3z", ("wf3", 0), ("wf3", 1), ("wf3", 2), Ar(dc, t)], [("ps", t)])
                        if l + 1 < n_layers:
                            load_wf3(l + 1)
                        psr4 = [("ps", t) for t in range(4)]
                        act(ps[0:72, 0:4, :], ps[0:72, 0:4, :], AF.Exp, psr4 + [("negbf", l)], psr4,
                            scale=-1.0, bias=negbf[0:72, l:l + 1])
                        act(E2[:].rearrange("p (a b) -> p a b", b=512), ps[0:72, 0:4, :], AF.Ln, psr4, ["E2"], bias=1.0)
                        P.add("dve", lambda h, o=E2[:]: h.tensor_tensor_scan(out=o, data0=o, data1=o, initial=0.0,
                                                                              op0=ALU.add, op1=ALU.max), ["E2"], ["E2"])

                        def cp(o, i_, rd, wr):
                            act(o, i_, AF.Identity, rd, wr)

                        cp(P3[0:8, :], E2[0:8, :], ["E2"], [("P3", 0)])
                        cp(tmpb[32:40, :], E2[32:40, :], ["E2"], [("tmpb", 1)])
                        tt_(E2[32:40, :], E2[32:40, :], tmpb[32:40, :], ALU.subtract, ["E2", ("tmpb", 1)], [("E2m", 1)])
                        cp(P3[32:40, :], E2[32:40, :], [("E2m", 1)], [("P3", 1)])
                        cp(tmpb[64:72, :], E2[64:72, :], ["E2"], [("tmpb", 2)])
                        tt_(E2[64:72, :], E2[64:72, :], tmpb[64:72, :], ALU.subtract, ["E2", ("tmpb", 2)], [("E2m", 2)])
                        cp(tmpb[64:72, :], E2[64:72, :], [("E2m", 2)], [("tmpb", 2)])
                        tt_(E2[64:72, :], E2[64:72, :], tmpb[64:72, :], ALU.subtract, [("E2m", 2), ("tmpb", 2)], [("E2m", 2)])
                        cp(P3[64:72, :], E2[64:72, :], [("E2m", 2)], [("P3", 2)])
                    P.barrier()
                    vzs = [[sb(sub, "vz%d_%d" % (s_, i), [128, 16, 128], BF16) for i in range(2)] for s_ in range(2)]
                    for s_ in range(2):
                        me = "pool"
                        memset(vzs[s_][0][:, :, 64:128], 1.0, [("vzp", s_, 0)], eng=me)
                        memset(vzs[s_][1][:, :, 0:64], 1.0, [("vzp", s_, 1)], eng=me)
                    p3r = [("P3", i) for i in range(3)]
                    wq_cur = {}

                    def attn_prep(hc):
                        s_ = hc % 2
                        wq_cur[hc] = w_acquire()
                        for hi in range(2):
                            hh = 2 * hc + hi
                            z0 = 64 if hi == 0 else 0
                            for r in range(3):
                                dma("sp", qTzs[s_][hi][z0 + r:z0 + r + 1, :], P3[32 * r + hh:32 * r + hh + 1, :],
                                    p3r + [("qTz", s_, hi, "pad")], [("qTz", s_, hi, "F", r)])
                                dma("sp", kTzs[s_][hi][z0 + 3 + r:z0 + 4 + r, :], P3[32 * r + hh:32 * r + hh + 1, :],
                                    p3r + [("kTz", s_, hi, "pad")], [("kTz", s_, hi, "F", r)])

                    def attn_proj(hc, t):
                        s_ = hc % 2
                        wq, wq_res = wq_cur[hc]
                        b = proj_fm(wq, wq_res, 0, Asl, Ar, 8, t, rr=lo_rr)
                        ts(qTzs[s_][0][0:64, t * TT:(t + 1) * TT], ps[0:64, b, :], 0.125, None, ALU.mult, ALU.bypass,
                           [("ps", b)], [("qT", s_, 0, t)])
                        ts(qTzs[s_][1][64:128, t * TT:(t + 1) * TT], ps[64:128, b, :], 0.125, None, ALU.mult, ALU.bypass,
                           [("ps", b)], [("qT", s_, 1, t)])
                        b = proj_fm(wq, wq_res, 128, Asl, Ar, 8, t, rr=lo_rr)
                        P.add("dve", lambda h, o=kTzs[s_][0][0:64, t * TT:(t + 1) * TT], i_=ps[0:64, b, :]: h.tensor_copy(o, i_),
                              [("ps", b)], [("kT", s_, 0, t)])
                        P.add("dve", lambda h, o=kTzs[s_][1][64:128, t * TT:(t + 1) * TT], i_=ps[64:128, b, :]: h.tensor_copy(o, i_),
                              [("ps", b)], [("kT", s_, 1, t)])
                        b = lo_rr.next()
                        for cc in range(4):
                            c = 4 * t + cc
                            for dc in range(8):
                                mm(ps[:, b, cc * 128:(cc + 1) * 128], A[:, dc, c * 128:(c + 1) * 128],
                                   wq[:, dc, 256:384], dc == 0, dc == 7, [wq_res, Ar(dc, t)], [("ps", b)])
                        psv = ps[:, b, :].rearrange("p (a c) -> p a c", c=128)
                        P.add("dve", lambda h, o=vzs[s_][0][:, 4 * t:4 * t + 4, 0:64], i_=psv[:, :, 0:64]: h.tensor_copy(o, i_),
                              [("ps", b)], [("vz", s_, 0, t)])
                        P.add("dve", lambda h, o=vzs[s_][1][:, 4 * t:4 * t + 4, 64:128], i_=psv[:, :, 64:128]: h.tensor_copy(o, i_),
                              [("ps", b)], [("vz", s_, 1, t)])

                    pending_norm = []
                    attn_prep(0)
                    for t in range(NTT):
                        attn_proj(0, t)
                    for hc in range(4):
                        s_ = hc % 2
                        qTz, kTz, vz = qTzs[s_], kTzs[s_], vzs[s_]
                        if hc + 1 < 4:
                            attn_prep(hc + 1)
                        for Qi in range(NTT):
                            xb = [4 + (Qi % 2) * 2, 5 + (Qi % 2) * 2]
                            nk = 4 * Qi + 4
                            steps = [(kj, hi) for kj in range(nk) for hi in range(2)]
                            LA = 3
                            pend = []
                            for i in range(len(steps) + LA):
                                if i == min(22, len(steps) + LA - 1) and pending_norm:
                                    pending_norm.pop(0)()
                                if i < len(steps):
                                    kj, hi = steps[i]
                                    n0 = max(0, kj - 4 * Qi) * 128
                                    N = TT - n0
                                    sbk = lo_rr.next()
                                    q0 = Qi * TT + n0
                                    diag = kj >= 4 * Qi
                                    frs = [(nm, s_, hi, "F", r) for nm in ("qTz", "kTz") for r in range(3)]
                                    mm(ps[:, sbk, 0:N], kTz[hi][:, kj * 128:(kj + 1) * 128], qTz[hi][:, q0:q0 + N], True, not diag,
                                       [("kT", s_, hi, kj // 4), ("kTz", s_, hi, "pad"), ("qTz", s_, hi, "pad"), ("qT", s_, hi, Qi)] + frs,
                                       [("ps", sbk)])
                                    if diag:
                                        mm(ps[:, sbk, 0:128], ident_bf[:], negmask_bf[:], False, True, ["cbf"], [("ps", sbk)])
                                    pi = Pt_rr.next()
                                    act(Ptp[pi][:, 0:N], ps[:, sbk, 0:N], AF.Exp, [("ps", sbk)], [("Pt", pi)])
                                    pend.append((kj, hi, n0, N, pi))
                                if i >= LA:
                                    kj, hi, n0, N, pi = pend[i - LA]
                                    mm(ps[:, xb[hi], n0:TT], vz[hi][:, kj, :], Ptp[pi][:, 0:N], kj == 0, kj == nk - 1,
                                       [("vz", s_, hi, kj // 4), ("vzp", s_, hi), ("Pt", pi)], [("ps", xb[hi])])
                            if hc + 1 < 4:
                                attn_proj(hc + 1, Qi)
                            ri = rstd_rr.next()
                            P.add("dve", lambda h, o=rstd[ri][0:64, :], i_=ps[0:64, xb[1], :]: h.reciprocal(out=o, in_=i_),
                                  [("ps", xb[1])], [("rstd", ri)])
                            P.add("dve", lambda h, o=rstd[ri][64:128, :], i_=ps[64:128, xb[0], :]: h.reciprocal(out=o, in_=i_),
                                  [("ps", xb[0])], [("rstd", ri)])

                            def norm_tail(ri=ri, xb=xb, hc=hc, Qi=Qi):
                                bs = lo_rr.next()
                                mm(ps[:, bs, :], cst[:, C_SWAP:C_SWAP + 128], rstd[ri][:], True, True,
                                   [("rstd", ri), "cst"], [("ps", bs)])
                                ti = T_rr.next()
                                act(Tp[ti][:], ps[:, bs, :], AF.Identity, [("ps", bs)], [("T", ti)])
                                tt_(ao[0:64, hc, Qi * TT:(Qi + 1) * TT], ps[0:64, xb[0], :], Tp[ti][0:64, :],
                                    ALU.mult, [("ps", xb[0]), ("T", ti)], [("ao", hc, Qi, 0)])
                                tt_(ao[64:128, hc, Qi * TT:(Qi + 1) * TT], ps[64:128, xb[1], :], Tp[ti][64:128, :],
                                    ALU.mult, [("ps", xb[1]), ("T", ti)], [("ao", hc, Qi, 1)])
                            pending_norm.append(norm_tail)
                    while pending_norm:
                        pending_norm.pop(0)()
                P.barrier()
                pa = sb(mix, "pa", [128, 2, S], BF16)
                sc = sb(mix, "sc", [128, 2, S], BF16)

                with ExitStack() as sub:
                    aTs = [sb(sub, "aT%d" % i, [128, 16 + S], F32) for i in range(2)]
                    sA = sb(sub, "sA", [128, 16 + S], F32)
                    sB = sb(sub, "sB", [128, 16 + S], F32)
                    dT = sb(sub, "dT", [128, S], BF16)
                    wbd = sb(sub, "wbd", [128, 2, 128], BF16)
                    t16 = sb(sub, "t16", [128, 16], F32)
                    dma("pool", wbd[:], pool_wbd[l], [], ["wbd"])
                    for bufn, buf in (("aT0", aTs[0]), ("aT1", aTs[1]), ("sA", sA), ("sB", sB)):
                        memset(buf[:, 0:16], 0.0, [(bufn, "pad")])
                    wa, wa_res = w_acquire()
                    for fc in range(2):
                        for t in range(NTT):
                            b = proj_fm(wa, wa_res, fc * 128, Asl, Ar, 8, t)
                            act(aTs[fc][:, 16 + t * TT:16 + (t + 1) * TT], ps[:, b, :], AF.Identity,
                                [("ps", b)], [("aT", fc, t)])
                    for fc in range(2):
                        aT = aTs[fc]
                        allr = [("aT", fc, t) for t in range(NTT)] + [("aT%d" % fc, "pad")]
                        a_ = aT[:, 16:16 + S]
                        tt_(sA[:, 16:], a_, aT[:, 15:15 + S], ALU.add, allr + [("sA", "pad")], ["sA"])
                        if fc == 0:
                            lo, lo_res, lo_w = sA, "sA", 2
                            tt_(sB[:, 16:], sA[:, 16:], sA[:, 14:14 + S], ALU.add, ["sA", ("sA", "pad"), ("sB", "pad")], ["sB"])
                            hi, hi_res, hi_w = sB, "sB", 4
                        else:
                            tt_(sB[:, 16:], sA[:, 16:], sA[:, 14:14 + S], ALU.add, ["sA", ("sA", "pad"), ("sB", "pad")], ["sB"])
                            tt_(sA[:, 16:], sB[:, 16:], sB[:, 12:12 + S], ALU.add, ["sB", ("sB", "pad"), ("sA", "pad")], ["sA"])
                            lo, lo_res, lo_w = sA, "sA", 8
                            tt_(sB[:, 16:], sA[:, 16:], sA[:, 8:8 + S], ALU.add, ["sA", ("sA", "pad"), ("sB", "pad")], ["sB"])
                            hi, hi_res, hi_w = sB, "sB", 16
                        for (p0, buf, bres, win, hidx) in ((0, lo, lo_res, lo_w, 0), (64, hi, hi_res, hi_w, 1)):
                            stt(dT[p0:p0 + 64, :], buf[p0:p0 + 64, 16:], 1.0 / win, aT[p0:p0 + 64, 16:],
                                ALU.mult, ALU.subtract, allr + [bres], [("dT", hidx)])
                            tt_(t16[p0:p0 + 64, :], buf[p0:p0 + 64, 16:32],
                                cst[p0:p0 + 64, C_INV + fc * 16:C_INV + (fc + 1) * 16], ALU.mult,
                                [bres, "cst"], [("t16", hidx)])
                            tt_(dT[p0:p0 + 64, 0:16], t16[p0:p0 + 64, :], aT[p0:p0 + 64, 16:32], ALU.subtract,
                                allr + [("t16", hidx), ("dT", hidx)], [("dT", hidx)])
                        for t in range(NTT):
                            b = bank_rr.next()
                            mm(ps[:, b, :], wbd[:, fc, :], dT[:, t * TT:(t + 1) * TT], True, True,
                               [("dT", 0), ("dT", 1), "wbd"], [("ps", b)])
                            act(pa[:, fc, t * TT:(t + 1) * TT], ps[:, b, :], AF.Identity, [("ps", b), "vecs"],
                                [("pa", fc, t)], scale=vecs[:, vb + 32 + fc:vb + 33 + fc])
                P.barrier()

                with ExitStack() as sub:
                    uT = sb(sub, "uT", [128, 2, S], BF16)
                    gvb = sb(sub, "gvb", [128, 16, 256], BF16)
                    ss = sb(sub, "ss", [128, 16], F32)
                    rs16 = sb(sub, "rs16", [128, 16], F32)
                    gv = sb(sub, "gv", [128, 512], F32)
                    junk = sb(sub, "junk", [128, 256], F32)
                    wTs = sb(sub, "wTs", [128, 4, 128], F32)
                    wcm = sb(sub, "wcm", [128, 4, 128], BF16)
                    bT4 = sb(sub, "bT4", [128, 2, 512], F32)
                    Tg = {10 + i: sb(sub, "Tg%d" % i, [128, 512], F32) for i in range(4)}
                    Tg_rr = RR(sorted(Tg))
                    for g in range(4):
                        dma("sp", wTs[:, g, :], sgu_wT[l, g], [], [("wTs", g)])
                        tt_(wcm[:, g, :], wTs[:, g, :], Umat, ALU.mult, [("wTs", g), "cst"], [("wcm", g)])
                    for fc in range(2):
                        dma("sp", bT4[:, fc, :], sgub4[l, fc], [], [("bT4", fc)])
                    wcu, wcu_res = w_acquire()
                    for fc in range(2):
                        for t in range(NTT):
                            b = proj_fm(wcu, wcu_res, fc * 128, Asl, Ar, 8, t)
                            gelu_from_psum(ps[:, b, :], ("ps", b), uT[:, fc, t * TT:(t + 1) * TT], ("uT", fc, t), 512, Tg, Tg_rr)
                    wcv, wcv_res = w_acquire()
                    for cp in range(8):
                        b = bank_rr.next()
                        for ci in range(2):
                            c = 2 * cp + ci
                            for dc in range(8):
                                mm(ps[:, b, ci * 256:(ci + 1) * 256], A[:, dc, c * 128:(c + 1) * 128], wcv[:, dc, 0:256],
                                   dc == 0, dc == 7, [wcv_res, Ar(dc, c // 4)], [("ps", b)])
                        gelu_from_psum(ps[:, b, :], ("ps", b), gv[:], "gv", 512, Tg, Tg_rr)
                        for ci in range(2):
                            c = 2 * cp + ci
                            stt(junk[:], gv[:, ci * 256:(ci + 1) * 256], 1.0, gv[:, ci * 256:(ci + 1) * 256], ALU.mult, ALU.mult,
                                ["gv"], ["junk", ("ss", c)], accum_out=ss[:, c:c + 1])
                        act(gvb[:, 2 * cp:2 * cp + 2, :], gv[:].rearrange("p (a c) -> p a c", c=256), AF.Identity,
                            ["gv"], [("gvb", 2 * cp), ("gvb", 2 * cp + 1)])
                    ssr = [("ss", c) for c in range(16)]
                    act(rs16[:], ss[:], AF.Sqrt, ssr, ["rs16"], scale=1.0 / 256, bias=EPS)
                    recip(rs16[:], rs16[:], ["rs16"], ["rs16"])
                    for c in range(16):
                        ts(gvb[:, c, :], gvb[:, c, :], rs16[:, c:c + 1], None, ALU.mult, ALU.bypass,
                           [("gvb", c), "rs16"], [("gvb", c)])
                    for fc in range(2):
                        for t in range(NTT):
                            b = bank_rr.next()
                            for cc in range(4):
                                c = t * 4 + cc
                                for gi in range(2):
                                    g = 2 * fc + gi
                                    mm(ps[gi * 64:(gi + 1) * 64, b, cc * 128:(cc + 1) * 128],
                                       gvb[:, c, g * 64:(g + 1) * 64], wcm[:, g, :], True, True,
                                       [("gvb", c), ("wcm", g)], [("ps", b)], tile_position=(0, gi * 64))
                            ti = T_rr.next()
                            stt(Tp[ti][:], ps[:, b, :], vecs[:, vb + 34 + fc:vb + 35 + fc], bT4[:, fc, :],
                                ALU.mult, ALU.add, [("ps", b), "vecs", ("bT4", fc)], [("T", ti)])
                            tt_(sc[:, fc, t * TT:(t + 1) * TT], Tp[ti][:], uT[:, fc, t * TT:(t + 1) * TT], ALU.mult,
                                [("T", ti), ("uT", fc, t)], [("sc", fc, t)])
                P.barrier()

                P.barrier()
                for nm, buf, nch in (("pa", pa, 2), ("ao", ao, 4), ("sc", sc, 2)):
                    if l == 0 and nm in tap_d:
                        for kc in range(nch):
                            dma("pool", tap_d[nm][kc], buf[:, kc, :], [], [])
                P.barrier()

                with ExitStack() as sub:
                    mg = sb(sub, "mg", [128, 8, S], BF16)
                    for j in range(8):
                        wg, wg_res = w_acquire()
                        wb, wb_res = w_acquire(held=1)
                        for t in range(NTT):
                            gb = []
                            for br in range(3):
                                b = proj_fm(wg, wg_res, br * 128, Asl, Ar, 8, t)
                                gb.append(b)
                            tl = []
                            for br in range(3):
                                ti = T_rr.next()
                                tl.append(ti)
                                act(Tp[ti][:], ps[:, gb[br], :], AF.Tanh, [("ps", gb[br]), ("halfb", l)], [("T", ti)],
                                    scale=0.5, bias=halfb[:, l * 24 + br * 8 + j:l * 24 + br * 8 + j + 1])
                            srcs = [(pa, 0, 2, lambda kc, t_: ("pa", kc, t_)),
                                    (ao, 2, 4, None),
                                    (sc, 6, 2, lambda kc, t_: ("sc", kc, t_))]
                            for br, (buf, k0, nk, rf) in enumerate(srcs):
                                b = bank_rr.next()
                                for kc in range(nk):
                                    rr_ = [("ao", kc, t, 0), ("ao", kc, t, 1)] if rf is None else [rf(kc, t)]
                                    mm(ps[:, b, :], wb[:, k0 + kc, 0:128], buf[:, kc, t * TT:(t + 1) * TT], kc == 0, kc == nk - 1,
                                       [wb_res] + rr_, [("ps", b)])
                                ti = tl[br]
                                stt(Tp[ti][:], Tp[ti][:], 1.0, ps[:, b, :], ALU.add, ALU.mult, [("T", ti), ("ps", b)], [("T", ti)])
                            tt_(Tp[tl[0]][:], Tp[tl[0]][:], Tp[tl[1]][:], ALU.add, [("T", tl[0]), ("T", tl[1])], [("T", tl[0])])
                            tt_(mg[:, j, t * TT:(t + 1) * TT], Tp[tl[0]][:], Tp[tl[2]][:], ALU.add,
                                [("T", tl[0]), ("T", tl[2])], [("mg", j, t)])
                    last_norm = resid_proj_then_norm(
                        lambda kc, t_: mg[:, kc, t_ * TT:(t_ + 1) * TT], lambda kc, t_: ("mg", kc, t_),
                        lambda j, t, b: stt(xsl(j, t), ps[:, b, :], 0.5, xsl(j, t), ALU.mult, ALU.add,
                                            [("ps", b), xr(j, t)], [xr(j, t)]),
                        (vb + 36, Asl, Ar))
            P.barrier()
            last_norm()
            if l == 0:
                tap("mix")

            with ExitStack() as xa:
                xq = sb(xa, "xq", [128, 8, S], BF16)
                hmT = sb(xa, "hmT", [128, 8, MEM], BF16)
                xkT = sb(xa, "xkT", [128, 8, MEM], BF16)
                xv = sb(xa, "xv", [128, 2, D], BF16)
                with ExitStack() as sub:
                    mT = sb(sub, "mT", [128, 8, MEM], F32)
                    for dc in range(8):
                        dma("sp", mT[:, dc, :], memT[dc], [], [("mT", dc)])
                    rmsnorm(lambda dc, t: mT[:, dc, :], lambda dc, t: ("mT", dc), vb + 44,
                            lambda dc, t: hmT[:, dc, :], lambda dc, t: ("hmT", dc), 1, MEM)
                def xa_kproj(q):
                    wk, wk_res = w_acquire()
                    for jj in range(2):
                        j = 2 * q + jj
                        b = bank_rr.next()
                        for dc in range(8):
                            mm(ps[:, b, 0:MEM], wk[:, dc, jj * 128:(jj + 1) * 128], hmT[:, dc, :], dc == 0, dc == 7,
                               [wk_res, ("hmT", dc)], [("ps", b)])
                        act(xkT[:, j, :], ps[:, b, 0:MEM], AF.Identity, [("ps", b)], [("xkT", j)])

                def xa_vproj(q):
                    wv_, wv_res = w_acquire()
                    for mc in range(2):
                        b = bank_rr.next()
                        for dc in range(8):
                            mm(ps[:, b, 0:256], hmT[:, dc, mc * 128:(mc + 1) * 128], wv_[:, dc, 0:256], dc == 0, dc == 7,
                               [wv_res, ("hmT", dc)], [("ps", b)])
                        act(xv[:, mc, q * 256:(q + 1) * 256], ps[:, b, 0:256], AF.Identity, [("ps", b)], [("xv", mc, q)])

                def xa_qproj(q):
                    wq_, wq_res = w_acquire()
                    for (jj, t) in TORD:
                        j = 2 * q + jj
                        b = proj_fm(wq_, wq_res, jj * 128, Asl, Ar, 8, t)
                        act(xq[:, j, t * TT:(t + 1) * TT], ps[:, b, :], AF.Identity, [("ps", b)], [("xq", j, t)],
                            scale=1.0 / 16)

                for q in range(4):
                    xa_qproj(q)
                    xa_kproj(q)
                    xa_vproj(q)
                hi_rr = RR(range(4, 8))
                items = [(t, hh) for t in range(NTT) for hh in range(4)]
                xpend = []

                def xa_scores(t, hh):
                    pts = []
                    for mc in range(2):
                        b = lo_rr.next()
                        for dk in range(2):
                            mm(ps[:, b, :], xkT[:, 2 * hh + dk, mc * 128:(mc + 1) * 128],
                               xq[:, 2 * hh + dk, t * TT:(t + 1) * TT], dk == 0, dk == 1,
                               [("xkT", 2 * hh + dk), ("xq", 2 * hh + dk, t)], [("ps", b)])
                        pi = Pt_rr.next()
                        act(Ptp[pi][:], ps[:, b, :], AF.Exp, [("ps", b)], [("Pt", pi)])
                        pts.append(pi)
                    return pts

                def xa_pv(t, hh, pts):
                    bd = hi_rr.next()
                    for mc in range(2):
                        mm(ps[:, bd, :], ones_bf[:], Ptp[pts[mc]][:], mc == 0, mc == 1,
                           ["ones", ("Pt", pts[mc])], [("ps", bd)])
                    ri = rstd_rr.next()
                    P.add("dve", lambda h, o=rstd[ri][:], i_=ps[:, bd, :]: h.reciprocal(out=o, in_=i_),
                          [("ps", bd)], [("rstd", ri)])
                    for dch in range(2):
                        b = hi_rr.next()
                        for mc in range(2):
                            mm(ps[:, b, :], xv[:, mc, hh * 256 + dch * 128: hh * 256 + (dch + 1) * 128], Ptp[pts[mc]][:],
                               mc == 0, mc == 1, [("xv", mc, hh), ("Pt", pts[mc])], [("ps", b)])
                        ti = T_rr.next()
                        act(Tp[ti][:], ps[:, b, :], AF.Identity, [("ps", b)], [("T", ti)])
                        tt_(xq[:, 2 * hh + dch, t * TT:(t + 1) * TT], Tp[ti][:], rstd[ri][:], ALU.mult,
                            [("T", ti), ("rstd", ri)], [("xq", 2 * hh + dch, t)], eng="pool")

                for i in range(len(items) + 1):
                    if i < len(items):
                        xpend.append(xa_scores(*items[i]))
                    if i >= 1:
                        xa_pv(items[i - 1][0], items[i - 1][1], xpend[i - 1])
                last_norm = resid_proj_then_norm(
                    lambda kc, t_: xq[:, kc, t_ * TT:(t_ + 1) * TT], lambda kc, t_: ("xq", kc, t_),
                    lambda j, t, b: tt_(xsl(j, t), ps[:, b, :], xsl(j, t), ALU.add, [("ps", b), xr(j, t)], [xr(j, t)]),
                    (vb + 52, Asl, Ar))
            P.barrier()
            last_norm()
            if l == 0:
                tap("xat")

            with ExitStack() as ff:
                hid = sb(ff, "hid", [128, 8, S], BF16)
                for fg in range(4):
                    for q in range(4):
                        w1, w1_res = w_acquire()
                        for (jj, t) in TORD:
                            fcl = 2 * q + jj
                            b = proj_fm(w1, w1_res, jj * 128, Asl, Ar, 8, t)
                            ti = T_rr.next()
                            act(Tp[ti][:], ps[:, b, :], AF.Relu, [("ps", b)], [("T", ti)])
                            tt_(hid[:, fcl, t * TT:(t + 1) * TT], Tp[ti][:], Tp[ti][:], ALU.mult,
                                [("T", ti)], [("hid", fcl, t)])
                    hid_fn = lambda kc, t_: hid[:, kc, t_ * TT:(t_ + 1) * TT]
                    hid_res = lambda kc, t_: ("hid", kc, t_)
                    ev2 = lambda j, t, b: tt_(xsl(j, t), ps[:, b, :], xsl(j, t), ALU.add, [("ps", b), xr(j, t)], [xr(j, t)])
                    if fg < 3:
                        for p_ in range(3):
                            w2, w2_res = w_acquire()
                            for j in range(3 * p_, min(3 * p_ + 3, 8)):
                                for t in range(NTT):
                                    b = proj_fm(w2, w2_res, (j - 3 * p_) * 128, hid_fn, hid_res, 8, t)
                                    ev2(j, t, b)
                    else:
                        if l + 1 < n_layers:
                            nargs = ((l + 1) * VPL + 0, Asl, Ar)
                            post = None
                        else:
                            nargs = (VPL * DEPTH, xsl, xr)

                            def post(t):
                                for dc in range(8):
                                    dma("sp", outT[dc, :, t * TT:(t + 1) * TT], xres[:, dc, t * TT:(t + 1) * TT], [xr(dc, t)], [])
                        last_norm = resid_proj_then_norm(hid_fn, hid_res, ev2, nargs, post)
            P.barrier()
            last_norm()
            if l == 0:
                tap("ffn")

        cnt = P.emit(nc)
        nc._cnt = cnt
        nc._nops = len(P.ops)
    return nc


def host_consts():
    cst = np.zeros((128, NCST), np.float32)
    cst[:, C_ID:C_ID + 128] = np.eye(128, dtype=np.float32)
    k = np.arange(128)
    cst[:, C_U:C_U + 128] = (k[:, None] <= k[None, :]).astype(np.float32)
    wins = {(0, 0): 2, (0, 1): 4, (1, 0): 8, (1, 1): 16}
    t = np.arange(16)
    for fc in range(2):
        for half in range(2):
            w = wins[(fc, half)]
            cst[half * 64:(half + 1) * 64, C_INV + fc * 16:C_INV + (fc + 1) * 16] = \
                (1.0 / np.minimum(t + 1, w)).astype(np.float32)[None, :]
    for kk in range(128):
        cst[kk, C_SWAP + (kk + 64) % 128] = 1.0
    cst[:, C_NEG:C_NEG + 128] = np.where(k[:, None] > k[None, :], -30000.0, 0.0).astype(np.float32)
    return cst


def pack_vecs(inp):
    v = np.zeros((128, NV), np.float32)

    def fm(a):
        a = np.asarray(a, np.float32)
        return a.reshape(-1, 128).T

    for l in range(DEPTH):
        b = l * VPL
        v[:, b + 0:b + 8] = fm(inp["norm_mix_g"][l])
        v[:, b + 8:b + 32] = fm(inp["b_gate"][l])
        v[:, b + 32:b + 34] = fm(inp["pool_scale"][l])
        v[:, b + 34:b + 36] = fm(inp["sgu_norm_g"][l])
        v[:, b + 36:b + 44] = fm(inp["norm_xattn_g"][l])
        v[:, b + 44:b + 52] = fm(inp["norm_mem_g"][l])
        v[:, b + 52:b + 60] = fm(inp["norm_ffn_g"][l])
        for r in range(3):
            v[32 * r:32 * r + 8, b + 60] = np.asarray(inp["b_forget"][l], np.float32)
    v[:, VPL * DEPTH:VPL * DEPTH + 8] = fm(inp["final_norm_g"])
    return v


def shared_inputs(inp):
    f = lambda k: np.ascontiguousarray(np.asarray(inp[k], np.float32))
    sgu_b = np.asarray(inp["sgu_b"], np.float32)
    sgub4 = np.zeros((DEPTH, 2, 128, 512), np.float32)
    for fc in range(2):
        for gi in range(2):
            sgub4[:, fc, gi * 64:(gi + 1) * 64, :] = np.tile(sgu_b[:, 2 * fc + gi, :], (1, 4))[:, None, :]
    pw = np.asarray(inp["pool_w"], np.float32)
    pool_wbd = np.zeros((DEPTH, 128, 2, 128), np.float32)
    for g in range(4):
        gi = g % 2
        pool_wbd[:, gi * 64:(gi + 1) * 64, g // 2, gi * 64:(gi + 1) * 64] = pw[:, g]
    return {
        "w_in": f("w_in"), "pool_wbd": pool_wbd,
        "sgu_wT": np.ascontiguousarray(np.asarray(inp["sgu_w"], np.float32).transpose(0, 1, 3, 2)),
        "sgub4": sgub4,
        "w_branch_a": f("w_branch_a"), "w_branch_b": f("w_branch_b"), "w_branch_c": f("w_branch_c"),
        "w_out": f("w_out"), "w_xq": f("w_xq"), "w_xkv": f("w_xkv"), "w_xo": f("w_xo"),
        "w_ff1": f("w_ff1"), "w_ff2": f("w_ff2"),
        "vecs": pack_vecs(inp), "cst": host_consts(),
    }


def core_inputs(inp, b):
    x = np.asarray(inp["x"], np.float32)[b]
    m = np.asarray(inp["mem"], np.float32)[b]
    return {
        "xT": np.ascontiguousarray(x.T).reshape(8, 128, S),
        "memT": np.ascontiguousarray(m.T).reshape(8, 128, MEM),
    }


_NC_CACHE = {}


def kernel(**inputs):
    if "nc" not in _NC_CACHE:
        _NC_CACHE["nc"] = build()
    nc = _NC_CACHE["nc"]
    shared = shared_inputs(inputs)
    in_maps = []
    for b in range(8):
        m = dict(shared)
        m.update(core_inputs(inputs, b))
        in_maps.append(m)
    res = run_bass_kernel_spmd(nc, in_maps, core_ids=list(range(8)))
    out = np.empty((8, S, D), np.float32)
    for b in range(8):
        out[b] = res.results[b]["outT"].reshape(D, S).T
    return out
```

```python
import numpy as np
from contextlib import ExitStack
import concourse.bass as bass
import concourse.mybir as mybir
from concourse.bass_utils import run_bass_kernel_spmd

F32 = mybir.dt.float32
BF16 = mybir.dt.bfloat16
AF = mybir.ActivationFunctionType
ALU = mybir.AluOpType

ENGS = ("pe", "act", "dve", "pool", "sp")
N_DMA_SEMS = 24

D = 1024
S = 2048
DEPTH = 2
MEM = 256
TT = 512
NTT = S // TT
OFF_A, OFF_Q, OFF_K, OFF_V, OFF_F, OFF_C, OFF_G = 0, 256, 768, 1280, 1792, 1800, 2312
N_IN = 5384
EPS = 1e-6
SLOTW = 384
NSLOT = 4
VPL = 61
NV = VPL * DEPTH + 8
C_ID, C_U, C_INV, C_SWAP, C_NEG, NCST = 0, 128, 256, 288, 416, 544


class Op:
    __slots__ = ("idx", "eng", "fn", "deps", "dma", "need_sig", "sig", "prev_use", "group")

    def __init__(self, idx, eng, fn, deps, dma, group):
        self.idx = idx
        self.eng = eng
        self.fn = fn
        self.deps = deps
        self.dma = dma
        self.need_sig = dma
        self.sig = None
        self.prev_use = None
        self.group = group


class Prog:
    def __init__(self, same_engine_sync=True):
        self.ops = []
        self.last_w = {}
        self.readers = {}
        self.same_engine_sync = same_engine_sync
        self.barrier_deps = {}
        self.last_on = {}
        self.open_dma = []

    def add(self, eng, fn, reads=(), writes=(), dma=False, group=None, persistent=False):
        idx = len(self.ops)
        deps = set()
        for r in reads:
            w = self.last_w.get(r)
            if w is not None:
                deps.update(w[1])
        for r in writes:
            w = self.last_w.get(r)
            if w is not None:
                if not (group is not None and w[0] == group):
                    deps.update(w[1])
            for rd in self.readers.get(r, ()):
                deps.add(rd)
        for r in reads:
            self.readers.setdefault(r, []).append(idx)
        for r in writes:
            w = self.last_w.get(r)
            if group is not None and w is not None and w[0] == group:
                w[1].append(idx)
            else:
                self.last_w[r] = (group, [idx])
                self.readers[r] = []
        if eng in self.barrier_deps:
            deps.update(self.barrier_deps.pop(eng))
        deps.discard(idx)
        self.ops.append(Op(idx, eng, fn, deps, dma, group))
        self.last_on[eng] = idx
        if dma and not persistent:
            self.open_dma.append(idx)
        return idx

    def barrier(self):
        deps = set(self.open_dma)
        for e, i in self.last_on.items():
            if not self.ops[i].dma:
                deps.add(i)
        self.open_dma = []
        for e in ENGS:
            self.barrier_deps.setdefault(e, set()).update(deps)

    def emit(self, nc, final_wait_eng="sp"):
        ops = self.ops
        for op in ops:
            nd = set()
            for d in op.deps:
                dop = ops[d]
                if (not dop.dma) and (not op.dma) and dop.eng == op.eng:
                    if op.eng == "pe" or not self.same_engine_sync:
                        continue
                nd.add(d)
            best = {}
            keep = set()
            for d in nd:
                dop = ops[d]
                if dop.dma:
                    keep.add(d)
                elif best.get(dop.eng, -1) < d:
                    best[dop.eng] = d
            keep.update(best.values())
            op.deps = keep
            for d in keep:
                ops[d].need_sig = True
        cnt = {e: 0 for e in ENGS}
        dma_use = [0] * N_DMA_SEMS
        half = N_DMA_SEMS // 2
        dma_rr_q = {"pool": 0, "sp": 0}
        for op in ops:
            if op.dma:
                qn = "pool" if op.eng == "pool" else "sp"
                s = dma_rr_q[qn] + (half if qn == "pool" else 0)
                dma_rr_q[qn] = (dma_rr_q[qn] + 1) % half
                op.prev_use = dma_use[s]
                dma_use[s] += 1
                op.sig = (("dma", s), 16 * dma_use[s])
            elif op.need_sig:
                cnt[op.eng] += 1
                op.sig = (("eng", op.eng), cnt[op.eng])
        with ExitStack() as es:
            sems = {}
            for e in ENGS:
                sems[("eng", e)] = es.enter_context(nc.semaphore("sem_" + e))
            for i in range(N_DMA_SEMS):
                sems[("dma", i)] = es.enter_context(nc.semaphore("sem_dma%d" % i))
            block = es.enter_context(nc.Block())
            streams = {e: [op for op in ops if op.eng == e] for e in ENGS}
            final = {}
            for op in ops:
                if op.dma:
                    k, v = op.sig
                    final[k] = max(final.get(k, 0), v)

            def run_stream(eng_name, h):
                known = {}
                for op in streams[eng_name]:
                    waits = {}
                    for d in op.deps:
                        k, v = ops[d].sig
                        if waits.get(k, 0) < v:
                            waits[k] = v
                    if op.dma and op.prev_use:
                        k = op.sig[0]
                        v = 16 * op.prev_use
                        if waits.get(k, 0) < v:
                            waits[k] = v
                    for k, v in waits.items():
                        if known.get(k, 0) >= v:
                            continue
                        h.wait_ge(sems[k], v)
                        known[k] = v
                    ins = op.fn(h)
                    if op.sig is not None:
                        k, v = op.sig
                        ins.then_inc(sems[k], 16 if op.dma else 1)
                if eng_name == final_wait_eng:
                    for k, v in final.items():
                        if known.get(k, 0) < v:
                            h.wait_ge(sems[k], v)

            @block.tensor
            def _(h):
                run_stream("pe", h)

            @block.scalar
            def _(h):
                run_stream("act", h)

            @block.vector
            def _(h):
                run_stream("dve", h)

            @block.gpsimd
            def _(h):
                run_stream("pool", h)

            @block.sync
            def _(h):
                run_stream("sp", h)
        return cnt


def build(n_layers=DEPTH, taps=(), same_engine_sync=True):
    nc = bass.Bass("TRN2", target_bir_lowering=False)

    def din(name, shape):
        return nc.dram_tensor(name, list(shape), F32, kind="ExternalInput").ap()

    xT = din("xT", [8, 128, S])
    memT = din("memT", [8, 128, MEM])
    w_in = din("w_in", [DEPTH, D, N_IN])
    pool_wbd = din("pool_wbd", [DEPTH, 128, 2, 128])
    sgu_wT = din("sgu_wT", [DEPTH, 4, 128, 128])
    sgub4 = din("sgub4", [DEPTH, 2, 128, 512])
    w_ba = din("w_branch_a", [DEPTH, 256, D])
    w_bb = din("w_branch_b", [DEPTH, 512, D])
    w_bc = din("w_branch_c", [DEPTH, 256, D])
    w_out = din("w_out", [DEPTH, D, D])
    w_xq = din("w_xq", [DEPTH, D, D])
    w_xkv = din("w_xkv", [DEPTH, D, 2 * D])
    w_xo = din("w_xo", [DEPTH, D, D])
    w_ff1 = din("w_ff1", [DEPTH, D, 4 * D])
    w_ff2 = din("w_ff2", [DEPTH, 4 * D, D])
    vecs_d = din("vecs", [128, NV])
    cst_d = din("cst", [128, NCST])
    outT = nc.dram_tensor("outT", [8, 128, S], F32, kind="ExternalOutput").ap()
    tap_d = {}
    for tname in taps:
        tap_d[tname] = nc.dram_tensor("tap_" + tname, [8, 128, S], F32, kind="ExternalOutput").ap()

    P = Prog(same_engine_sync=same_engine_sync)

    with ExitStack() as outer:
        _uid = [0]

        def sb(es, name, shape, dt):
            _uid[0] += 1
            return es.enter_context(nc.sbuf_tensor("s%d_%s" % (_uid[0], name), list(shape), dt))

        ps = outer.enter_context(nc.psum_tensor("ps", [128, 8, 512], F32))
        xres = sb(outer, "xres", [128, 8, S], F32)
        A = sb(outer, "A", [128, 8, S], BF16)
        wsl = [sb(outer, "wsl%d" % i, [128, 8, SLOTW], BF16) for i in range(NSLOT)]
        vecs = sb(outer, "vecs", [128, NV], F32)
        cst = sb(outer, "cst", [128, NCST], F32)
        halfb = sb(outer, "halfb", [128, DEPTH * 24], F32)
        negbf = sb(outer, "negbf", [128, DEPTH], F32)
        ones_bf = sb(outer, "ones_bf", [128, 128], BF16)
        wf3 = sb(outer, "wf3", [128, 8, 72], BF16)
        ident_bf = sb(outer, "ident_bf", [128, 128], BF16)
        negmask_bf = sb(outer, "negmask_bf", [128, 128], BF16)
        Tp = [sb(outer, "T%d" % i, [128, 512], F32) for i in range(3)]
        Ptp = [sb(outer, "Pt%d" % i, [128, 512], BF16) for i in range(4)]
        rstd = [sb(outer, "rstd%d" % i, [128, 512], F32) for i in range(2)]
        sqp = {2: Ptp[2], 3: Ptp[3]}

        ident = cst[:, C_ID:C_ID + 128]
        Umat = cst[:, C_U:C_U + 128]

        def act(out, in_, func, reads, writes, **kw):
            P.add("act", lambda h: h.activation(out=out, in_=in_, func=func, **kw), reads, writes)

        def mm(out, lhsT, rhs, start, stop, reads, writes, **kw):
            P.add("pe", lambda h: h.matmul(out, lhsT=lhsT, rhs=rhs, start=start, stop=stop, **kw), reads, writes)

        def stt(out, in0, scalar, in1, op0, op1, reads, writes, **kw):
            P.add("dve", lambda h: h.scalar_tensor_tensor(out=out, in0=in0, scalar=scalar, in1=in1, op0=op0, op1=op1, **kw), reads, writes)

        def tt_(out, in0, in1, op, reads, writes, eng="dve"):
            P.add(eng, lambda h: h.tensor_tensor(out=out, in0=in0, in1=in1, op=op), reads, writes)

        def ts(out, in0, s1, s2, op0, op1, reads, writes, eng="dve"):
            P.add(eng, lambda h: h.tensor_scalar(out=out, in0=in0, scalar1=s1, scalar2=s2, op0=op0, op1=op1), reads, writes)

        def recip(out, in_, reads, writes):
            P.add("dve", lambda h: h.reciprocal(out=out, in_=in_), reads, writes)

        def memset(ap, val, writes, eng="dve"):
            P.add(eng, lambda h: h.memset(ap, val), (), writes)

        def dma(eng, out, in_, reads, writes, group=None, persistent=False):
            P.add(eng, lambda h: h.dma_start(out=out, in_=in_), reads, writes, dma=True, group=group, persistent=persistent)

        class RR:
            def __init__(self, items):
                self.items = list(items)
                self.i = 0

            def next(self):
                v = self.items[self.i % len(self.items)]
                self.i += 1
                return v

        T_rr = RR(range(3))
        Pt_rr = RR(range(4))
        sq_rr = RR((2, 3))
        rstd_rr = RR(range(2))

        def wsrc(ap2d):
            return ap2d.rearrange("(kc p) n -> p kc n", p=128)

        W3 = ((0, 384), (384, 768), (768, 1024))
        TORD = [(jj, t) for t in range(NTT - 1) for jj in range(2)] + [(jj, NTT - 1) for jj in range(2)]
        wsched = []

        def sched_layer(l):
            for hc in range(4):
                wsched.append([(0, 8, i * 128, (i + 1) * 128, w_in[l, :, off + hc * 128: off + (hc + 1) * 128])
                               for i, off in enumerate((OFF_Q, OFF_K, OFF_V))])
            wsched.append([(0, 8, 0, 256, w_in[l, :, OFF_A:OFF_A + 256])])
            wsched.append([(0, 8, 0, 256, w_in[l, :, OFF_C:OFF_C + 256])])
            wsched.append([(0, 8, 0, 256, w_in[l, :, OFF_C + 256:OFF_C + 512])])
            for j in range(8):
                wsched.append([(0, 8, br * 128, (br + 1) * 128,
                                w_in[l, :, OFF_G + br * 1024 + j * 128: OFF_G + br * 1024 + (j + 1) * 128])
                               for br in range(3)])
                wsched.append([(0, 2, 0, 128, w_ba[l, :, j * 128:(j + 1) * 128]),
                               (2, 6, 0, 128, w_bb[l, :, j * 128:(j + 1) * 128]),
                               (6, 8, 0, 128, w_bc[l, :, j * 128:(j + 1) * 128])])
            for (c0, c1) in W3:
                wsched.append([(0, 8, 0, c1 - c0, w_out[l, :, c0:c1])])
            for q in range(4):
                wsched.append([(0, 8, 0, 256, w_xq[l, :, q * 256:(q + 1) * 256])])
                wsched.append([(0, 8, 0, 256, w_xkv[l, :, q * 256:(q + 1) * 256])])
                wsched.append([(0, 8, 0, 256, w_xkv[l, :, (4 + q) * 256:(5 + q) * 256])])
            for (c0, c1) in W3:
                wsched.append([(0, 8, 0, c1 - c0, w_xo[l, :, c0:c1])])
            for fg in range(4):
                for q in range(4):
                    wsched.append([(0, 8, 0, 256, w_ff1[l, :, fg * 1024 + q * 256: fg * 1024 + (q + 1) * 256])])
                for (c0, c1) in W3:
                    wsched.append([(0, 8, 0, c1 - c0, w_ff2[l, fg * 1024:(fg + 1) * 1024, c0:c1])])

        for l in range(n_layers):
            sched_layer(l)
        wstate = {"issued": 0, "next": 0}

        def w_issue_upto(n):
            while wstate["issued"] < min(n, len(wsched)):
                i = wstate["issued"]
                slot = i % NSLOT
                for (k0, k1, c0, c1, src) in wsched[i]:
                    dma("pool", wsl[slot][:, k0:k1, c0:c1], wsrc(src), [], [("ws", slot)], group=("w", i), persistent=True)
                wstate["issued"] += 1

        def w_acquire(held=0):
            i = wstate["next"]
            wstate["next"] += 1
            w_issue_upto(i - held + NSLOT)
            slot = i % NSLOT
            return wsl[slot], ("ws", slot)

        dma("sp", vecs[:], vecs_d, [], ["vecs"])
        dma("sp", cst[:], cst_d, [], ["cst"])
        for t in range(NTT):
            for dc in range(8):
                dma("sp", xres[:, dc, t * TT:(t + 1) * TT], xT[dc, :, t * TT:(t + 1) * TT], [], [("x", dc, t)])
        memset(ones_bf[:], 1.0, ["ones"])
        memset(wf3[:], 0.0, ["wf3z"])
        P.add("dve", lambda h: h.tensor_copy(ident_bf[:], cst[:, C_ID:C_ID + 128]), ["cst"], ["cbf"])
        P.add("dve", lambda h: h.tensor_copy(negmask_bf[:], cst[:, C_NEG:C_NEG + 128]), ["cst"], ["cbf"])
        for l in range(DEPTH):
            ts(halfb[:, l * 24:(l + 1) * 24], vecs[:, l * VPL + 8:l * VPL + 32], 0.5, None, ALU.mult, ALU.bypass,
               ["vecs"], [("halfb", l)])
            ts(negbf[:, l:l + 1], vecs[:, l * VPL + 60:l * VPL + 61], -1.0, None, ALU.mult, ALU.bypass,
               ["vecs"], [("negbf", l)])
        def load_wf3(l_):
            for r in range(3):
                dma("pool", wf3[:, :, 32 * r:32 * r + 8], wsrc(w_in[l_, :, OFF_F:OFF_F + 8]), ["wf3z"], [("wf3", r)],
                    persistent=True)

        load_wf3(0)
        w_issue_upto(NSLOT)

        def norm_tile(src, src_res, gcol0, dst, dst_res, t, tw):
            b = bank_rr.next()
            for dc in range(8):
                qi = sq_rr.next()
                act(sqp[qi][:, :tw], src(dc, t), AF.Square, [src_res(dc, t)], [("Pt", qi)])
                mm(ps[:, b, :tw], ones_bf[:], sqp[qi][:, :tw], dc == 0, dc == 7,
                   [("Pt", qi), "ones"], [("ps", b)])
            act(ps[:, b, :tw], ps[:, b, :tw], AF.Ln, [("ps", b)], [("ps", b)], scale=1.0 / D, bias=EPS)
            ri = rstd_rr.next()
            act(rstd[ri][:, :tw], ps[:, b, :tw], AF.Exp, [("ps", b)], [("rstd", ri)], scale=-0.5)
            for dc in range(8):
                stt(dst(dc, t), src(dc, t), vecs[:, gcol0 + dc:gcol0 + dc + 1], rstd[ri][:, :tw],
                    ALU.mult, ALU.mult, [src_res(dc, t), ("rstd", ri), "vecs"], [dst_res(dc, t)])

        def rmsnorm(src, src_res, gcol0, dst, dst_res, ntiles, tw):
            for t in range(ntiles):
                norm_tile(src, src_res, gcol0, dst, dst_res, t, tw)

        def acquire3():
            return [w_acquire(held=i) for i in range(3)]

        def chunk_w(ws, j):
            p_ = (j * 128) // 384
            return ws[p_][0], ws[p_][1], j * 128 - p_ * 384

        def resid_proj_then_norm(src_fn, src_res_fn, evac, norm_args, post_tile=None):
            ws = acquire3()

            def norm_part(t):
                if norm_args is not None:
                    norm_tile(xsl, xr, norm_args[0], norm_args[1], norm_args[2], t, TT)
                    if post_tile is not None:
                        post_tile(t)

            for t in range(NTT):
                for j in range(8):
                    wt, wres, c0 = chunk_w(ws, j)
                    b = proj_fm(wt, wres, c0, src_fn, src_res_fn, 8, t)
                    evac(j, t, b)
                if t >= 1:
                    norm_part(t - 1)
            return lambda: norm_part(NTT - 1)

        def xsl(dc, t):
            return xres[:, dc, t * TT:(t + 1) * TT]

        def Asl(dc, t):
            return A[:, dc, t * TT:(t + 1) * TT]

        def xr(dc, t):
            return ("x", dc, t)

        def Ar(dc, t):
            return ("A", dc, t)

        bank_rr = RR(range(8))
        lo_rr = RR(range(4))

        def proj_fm(wt, wres, c0, rhs_fn, rhs_res_fn, nk, t, extra_reads=(), rr=None):
            b = (rr or bank_rr).next()
            for kc in range(nk):
                mm(ps[:, b, :], wt[:, kc, c0:c0 + 128], rhs_fn(kc, t), kc == 0, kc == nk - 1,
                   [wres, rhs_res_fn(kc, t)] + list(extra_reads), [("ps", b)])
            return b

        def gelu_from_psum(src_ap, src_res, dst_ap, dst_res, width, pool=None, rr=None):
            pool = pool or Tp
            rr = rr or T_rr
            t1 = rr.next()
            t2 = rr.next()
            T1 = pool[t1][:, :width]
            T2 = pool[t2][:, :width]
            act(T1, src_ap, AF.Identity, [src_res], [("T", t1)], scale=0.5)
            act(T2, src_ap, AF.Square, [src_res], [("T", t2)])
            ts(T2, T2, 0.044715, 1.0, ALU.mult, ALU.add, [("T", t2)], [("T", t2)])
            tt_(T2, T2, T1, ALU.mult, [("T", t1), ("T", t2)], [("T", t2)])
            act(T2, T2, AF.Tanh, [("T", t2)], [("T", t2)], scale=1.5957691216057308)
            stt(dst_ap, T2, 1.0, T1, ALU.add, ALU.mult, [("T", t1), ("T", t2)], [dst_res])

        def tap(name):
            if name in tap_d:
                for dc in range(8):
                    dma("sp", tap_d[name][dc], xres[:, dc, :], [("x", dc, t) for t in range(NTT)], [])

        for l in range(n_layers):
            vb = l * VPL
            if l == 0:
                rmsnorm(xsl, xr, vb + 0, Asl, Ar, NTT, TT)

            with ExitStack() as mix:
                ao = sb(mix, "ao", [128, 4, S], BF16)

                with ExitStack() as sub:
                    P3 = sb(sub, "P3", [72, S], BF16)
                    qTzs = [[sb(sub, "qTz%d_%d" % (s_, i), [128, S], BF16) for i in range(2)] for s_ in range(2)]
                    kTzs = [[sb(sub, "kTz%d_%d" % (s_, i), [128, S], BF16) for i in range(2)] for s_ in range(2)]
                    with ExitStack() as fsub:
                        E2 = sb(fsub, "E2", [72, S], F32)
                        tmpb = sb(fsub, "tmpb", [72, S], BF16)
                        for s_ in range(2):
                            for hi in range(2):
                                z0 = 64 if hi == 0 else 0
                                me = "pool"
                                memset(kTzs[s_][hi][z0:z0 + 64, :], 0.0, [("kTz", s_, hi, "pad")], eng=me)
                                memset(qTzs[s_][hi][z0:z0 + 64, :], 0.0, [("qTz", s_, hi, "pad")], eng=me)
                                memset(kTzs[s_][hi][z0:z0 + 3, :], -1.0, [("kTz", s_, hi, "pad")], eng=me)
                                memset(qTzs[s_][hi][z0:z0 + 6, :], 1.0, [("qTz", s_, hi, "pad")], eng=me)
                        for t in range(NTT):
                            for dc in range(8):
                                mm(ps[0:72, t, :], wf3[:, dc, :], Asl(dc, t), dc == 0, dc == 7,
                                   ["wf3z", ("wf3", 0), ("wf3", 1), ("wf3", 2), Ar(dc, t)], [("ps", t)])
                        if l + 1 < n_layers:
                            load_wf3(l + 1)
                        psr4 = [("ps", t) for t in range(4)]
                        act(ps[0:72, 0:4, :], ps[0:72, 0:4, :], AF.Exp, psr4 + [("negbf", l)], psr4,
                            scale=-1.0, bias=negbf[0:72, l:l + 1])
                        act(E2[:].rearrange("p (a b) -> p a b", b=512), ps[0:72, 0:4, :], AF.Ln, psr4, ["E2"], bias=1.0)
                        P.add("dve", lambda h, o=E2[:]: h.tensor_tensor_scan(out=o, data0=o, data1=o, initial=0.0,
                                                                              op0=ALU.add, op1=ALU.max), ["E2"], ["E2"])

                        def cp(o, i_, rd, wr):
                            act(o, i_, AF.Identity, rd, wr)

                        cp(P3[0:8, :], E2[0:8, :], ["E2"], [("P3", 0)])
                        cp(tmpb[32:40, :], E2[32:40, :], ["E2"], [("tmpb", 1)])
                        tt_(E2[32:40, :], E2[32:40, :], tmpb[32:40, :], ALU.subtract, ["E2", ("tmpb", 1)], [("E2m", 1)])
                        cp(P3[32:40, :], E2[32:40, :], [("E2m", 1)], [("P3", 1)])
                        cp(tmpb[64:72, :], E2[64:72, :], ["E2"], [("tmpb", 2)])
                        tt_(E2[64:72, :], E2[64:72, :], tmpb[64:72, :], ALU.subtract, ["E2", ("tmpb", 2)], [("E2m", 2)])
                        cp(tmpb[64:72, :], E2[64:72, :], [("E2m", 2)], [("tmpb", 2)])
                        tt_(E2[64:72, :], E2[64:72, :], tmpb[64:72, :], ALU.subtract, [("E2m", 2), ("tmpb", 2)], [("E2m", 2)])
                        cp(P3[64:72, :], E2[64:72, :], [("E2m", 2)], [("P3", 2)])
                    P.barrier()
                    vzs = [[sb(sub, "vz%d_%d" % (s_, i), [128, 16, 128], BF16) for i in range(2)] for s_ in range(2)]
                    for s_ in range(2):
                        me = "pool"
                        memset(vzs[s_][0][:, :, 64:128], 1.0, [("vzp", s_, 0)], eng=me)
                        memset(vzs[s_][1][:, :, 0:64], 1.0, [("vzp", s_, 1)], eng=me)
                    p3r = [("P3", i) for i in range(3)]
                    wq_cur = {}

                    def attn_prep(hc):
                        s_ = hc % 2
                        wq_cur[hc] = w_acquire()
                        for hi in range(2):
                            hh = 2 * hc + hi
                            z0 = 64 if hi == 0 else 0
                            for r in range(3):
                                dma("sp", qTzs[s_][hi][z0 + r:z0 + r + 1, :], P3[32 * r + hh:32 * r + hh + 1, :],
                                    p3r + [("qTz", s_, hi, "pad")], [("qTz", s_, hi, "F", r)])
                                dma("sp", kTzs[s_][hi][z0 + 3 + r:z0 + 4 + r, :], P3[32 * r + hh:32 * r + hh + 1, :],
                                    p3r + [("kTz", s_, hi, "pad")], [("kTz", s_, hi, "F", r)])

                    def attn_proj(hc, t):
                        s_ = hc % 2
                        wq, wq_res = wq_cur[hc]
                        b = proj_fm(wq, wq_res, 0, Asl, Ar, 8, t, rr=lo_rr)
                        ts(qTzs[s_][0][0:64, t * TT:(t + 1) * TT], ps[0:64, b, :], 0.125, None, ALU.mult, ALU.bypass,
                           [("ps", b)], [("qT", s_, 0, t)])
                        ts(qTzs[s_][1][64:128, t * TT:(t + 1) * TT], ps[64:128, b, :], 0.125, None, ALU.mult, ALU.bypass,
                           [("ps", b)], [("qT", s_, 1, t)])
                        b = proj_fm(wq, wq_res, 128, Asl, Ar, 8, t, rr=lo_rr)
                        P.add("dve", lambda h, o=kTzs[s_][0][0:64, t * TT:(t + 1) * TT], i_=ps[0:64, b, :]: h.tensor_copy(o, i_),
                              [("ps", b)], [("kT", s_, 0, t)])
                        P.add("dve", lambda h, o=kTzs[s_][1][64:128, t * TT:(t + 1) * TT], i_=ps[64:128, b, :]: h.tensor_copy(o, i_),
                              [("ps", b)], [("kT", s_, 1, t)])
                        b = lo_rr.next()
                        for cc in range(4):
                            c = 4 * t + cc
                            for dc in range(8):
                                mm(ps[:, b, cc * 128:(cc + 1) * 128], A[:, dc, c * 128:(c + 1) * 128],
                                   wq[:, dc, 256:384], dc == 0, dc == 7, [wq_res, Ar(dc, t)], [("ps", b)])
                        psv = ps[:, b, :].rearrange("p (a c) -> p a c", c=128)
                        P.add("dve", lambda h, o=vzs[s_][0][:, 4 * t:4 * t + 4, 0:64], i_=psv[:, :, 0:64]: h.tensor_copy(o, i_),
                              [("ps", b)], [("vz", s_, 0, t)])
                        P.add("dve", lambda h, o=vzs[s_][1][:, 4 * t:4 * t + 4, 64:128], i_=psv[:, :, 64:128]: h.tensor_copy(o, i_),
                              [("ps", b)], [("vz", s_, 1, t)])

                    pending_norm = []
                    attn_prep(0)
                    for t in range(NTT):
                        attn_proj(0, t)
                    for hc in range(4):
                        s_ = hc % 2
                        qTz, kTz, vz = qTzs[s_], kTzs[s_], vzs[s_]
                        if hc + 1 < 4:
                            attn_prep(hc + 1)
                        for Qi in range(NTT):
                            xb = [4 + (Qi % 2) * 2, 5 + (Qi % 2) * 2]
                            nk = 4 * Qi + 4
                            steps = [(kj, hi) for kj in range(nk) for hi in range(2)]
                            LA = 3
                            pend = []
                            for i in range(len(steps) + LA):
                                if i == min(22, len(steps) + LA - 1) and pending_norm:
                                    pending_norm.pop(0)()
                                if i < len(steps):
                                    kj, hi = steps[i]
                                    n0 = max(0, kj - 4 * Qi) * 128
                                    N = TT - n0
                                    sbk = lo_rr.next()
                                    q0 = Qi * TT + n0
                                    diag = kj >= 4 * Qi
                                    frs = [(nm, s_, hi, "F", r) for nm in ("qTz", "kTz") for r in range(3)]
                                    mm(ps[:, sbk, 0:N], kTz[hi][:, kj * 128:(kj + 1) * 128], qTz[hi][:, q0:q0 + N], True, not diag,
                                       [("kT", s_, hi, kj // 4), ("kTz", s_, hi, "pad"), ("qTz", s_, hi, "pad"), ("qT", s_, hi, Qi)] + frs,
                                       [("ps", sbk)])
                                    if diag:
                                        mm(ps[:, sbk, 0:128], ident_bf[:], negmask_bf[:], False, True, ["cbf"], [("ps", sbk)])
                                    pi = Pt_rr.next()
                                    act(Ptp[pi][:, 0:N], ps[:, sbk, 0:N], AF.Exp, [("ps", sbk)], [("Pt", pi)])
                                    pend.append((kj, hi, n0, N, pi))
                                if i >= LA:
                                    kj, hi, n0, N, pi = pend[i - LA]
                                    mm(ps[:, xb[hi], n0:TT], vz[hi][:, kj, :], Ptp[pi][:, 0:N], kj == 0, kj == nk - 1,
                                       [("vz", s_, hi, kj // 4), ("vzp", s_, hi), ("Pt", pi)], [("ps", xb[hi])])
                            if hc + 1 < 4:
                                attn_proj(hc + 1, Qi)
                            ri = rstd_rr.next()
                            P.add("dve", lambda h, o=rstd[ri][0:64, :], i_=ps[0:64, xb[1], :]: h.reciprocal(out=o, in_=i_),
                                  [("ps", xb[1])], [("rstd", ri)])
                            P.add("dve", lambda h, o=rstd[ri][64:128, :], i_=ps[64:128, xb[0], :]: h.reciprocal(out=o, in_=i_),
                                  [("ps", xb[0])], [("rstd", ri)])

                            def norm_tail(ri=ri, xb=xb, hc=hc, Qi=Qi):
                                bs = lo_rr.next()
                                mm(ps[:, bs, :], cst[:, C_SWAP:C_SWAP + 128], rstd[ri][:], True, True,
                                   [("rstd", ri), "cst"], [("ps", bs)])
                                ti = T_rr.next()
                                act(Tp[ti][:], ps[:, bs, :], AF.Identity, [("ps", bs)], [("T", ti)])
                                tt_(ao[0:64, hc, Qi * TT:(Qi + 1) * TT], ps[0:64, xb[0], :], Tp[ti][0:64, :],
                                    ALU.mult, [("ps", xb[0]), ("T", ti)], [("ao", hc, Qi, 0)])
                                tt_(ao[64:128, hc, Qi * TT:(Qi + 1) * TT], ps[64:128, xb[1], :], Tp[ti][64:128, :],
                                    ALU.mult, [("ps", xb[1]), ("T", ti)], [("ao", hc, Qi, 1)])
                            pending_norm.append(norm_tail)
                    while pending_norm:
                        pending_norm.pop(0)()
                P.barrier()
                pa = sb(mix, "pa", [128, 2, S], BF16)
                sc = sb(mix, "sc", [128, 2, S], BF16)

                with ExitStack() as sub:
                    aTs = [sb(sub, "aT%d" % i, [128, 16 + S], F32) for i in range(2)]
                    sA = sb(sub, "sA", [128, 16 + S], F32)
                    sB = sb(sub, "sB", [128, 16 + S], F32)
                    dT = sb(sub, "dT", [128, S], BF16)
                    wbd = sb(sub, "wbd", [128, 2, 128], BF16)
                    t16 = sb(sub, "t16", [128, 16], F32)
                    dma("pool", wbd[:], pool_wbd[l], [], ["wbd"])
                    for bufn, buf in (("aT0", aTs[0]), ("aT1", aTs[1]), ("sA", sA), ("sB", sB)):
                        memset(buf[:, 0:16], 0.0, [(bufn, "pad")])
                    wa, wa_res = w_acquire()
                    for fc in range(2):
                        for t in range(NTT):
                            b = proj_fm(wa, wa_res, fc * 128, Asl, Ar, 8, t)
                            act(aTs[fc][:, 16 + t * TT:16 + (t + 1) * TT], ps[:, b, :], AF.Identity,
                                [("ps", b)], [("aT", fc, t)])
                    for fc in range(2):
                        aT = aTs[fc]
                        allr = [("aT", fc, t) for t in range(NTT)] + [("aT%d" % fc, "pad")]
                        a_ = aT[:, 16:16 + S]
                        tt_(sA[:, 16:], a_, aT[:, 15:15 + S], ALU.add, allr + [("sA", "pad")], ["sA"])
                        if fc == 0:
                            lo, lo_res, lo_w = sA, "sA", 2
                            tt_(sB[:, 16:], sA[:, 16:], sA[:, 14:14 + S], ALU.add, ["sA", ("sA", "pad"), ("sB", "pad")], ["sB"])
                            hi, hi_res, hi_w = sB, "sB", 4
                        else:
                            tt_(sB[:, 16:], sA[:, 16:], sA[:, 14:14 + S], ALU.add, ["sA", ("sA", "pad"), ("sB", "pad")], ["sB"])
                            tt_(sA[:, 16:], sB[:, 16:], sB[:, 12:12 + S], ALU.add, ["sB", ("sB", "pad"), ("sA", "pad")], ["sA"])
                            lo, lo_res, lo_w = sA, "sA", 8
                            tt_(sB[:, 16:], sA[:, 16:], sA[:, 8:8 + S], ALU.add, ["sA", ("sA", "pad"), ("sB", "pad")], ["sB"])
                            hi, hi_res, hi_w = sB, "sB", 16
                        for (p0, buf, bres, win, hidx) in ((0, lo, lo_res, lo_w, 0), (64, hi, hi_res, hi_w, 1)):
                            stt(dT[p0:p0 + 64, :], buf[p0:p0 + 64, 16:], 1.0 / win, aT[p0:p0 + 64, 16:],
                                ALU.mult, ALU.subtract, allr + [bres], [("dT", hidx)])
                            tt_(t16[p0:p0 + 64, :], buf[p0:p0 + 64, 16:32],
                                cst[p0:p0 + 64, C_INV + fc * 16:C_INV + (fc + 1) * 16], ALU.mult,
                                [bres, "cst"], [("t16", hidx)])
                            tt_(dT[p0:p0 + 64, 0:16], t16[p0:p0 + 64, :], aT[p0:p0 + 64, 16:32], ALU.subtract,
                                allr + [("t16", hidx), ("dT", hidx)], [("dT", hidx)])
                        for t in range(NTT):
                            b = bank_rr.next()
                            mm(ps[:, b, :], wbd[:, fc, :], dT[:, t * TT:(t + 1) * TT], True, True,
                               [("dT", 0), ("dT", 1), "wbd"], [("ps", b)])
                            act(pa[:, fc, t * TT:(t + 1) * TT], ps[:, b, :], AF.Identity, [("ps", b), "vecs"],
                                [("pa", fc, t)], scale=vecs[:, vb + 32 + fc:vb + 33 + fc])
                P.barrier()

                with ExitStack() as sub:
                    uT = sb(sub, "uT", [128, 2, S], BF16)
                    gvb = sb(sub, "gvb", [128, 16, 256], BF16)
                    ss = sb(sub, "ss", [128, 16], F32)
                    rs16 = sb(sub, "rs16", [128, 16], F32)
                    gv = sb(sub, "gv", [128, 512], F32)
                    junk = sb(sub, "junk", [128, 256], F32)
                    wTs = sb(sub, "wTs", [128, 4, 128], F32)
                    wcm = sb(sub, "wcm", [128, 4, 128], BF16)
                    bT4 = sb(sub, "bT4", [128, 2, 512], F32)
                    Tg = {10 + i: sb(sub, "Tg%d" % i, [128, 512], F32) for i in range(4)}
                    Tg_rr = RR(sorted(Tg))
                    for g in range(4):
                        dma("sp", wTs[:, g, :], sgu_wT[l, g], [], [("wTs", g)])
                        tt_(wcm[:, g, :], wTs[:, g, :], Umat, ALU.mult, [("wTs", g), "cst"], [("wcm", g)])
                    for fc in range(2):
                        dma("sp", bT4[:, fc, :], sgub4[l, fc], [], [("bT4", fc)])
                    wcu, wcu_res = w_acquire()
                    for fc in range(2):
                        for t in range(NTT):
                            b = proj_fm(wcu, wcu_res, fc * 128, Asl, Ar, 8, t)
                            gelu_from_psum(ps[:, b, :], ("ps", b), uT[:, fc, t * TT:(t + 1) * TT], ("uT", fc, t), 512, Tg, Tg_rr)
                    wcv, wcv_res = w_acquire()
                    for cp in range(8):
                        b = bank_rr.next()
                        for ci in range(2):
                            c = 2 * cp + ci
                            for dc in range(8):
                                mm(ps[:, b, ci * 256:(ci + 1) * 256], A[:, dc, c * 128:(c + 1) * 128], wcv[:, dc, 0:256],
                                   dc == 0, dc == 7, [wcv_res, Ar(dc, c // 4)], [("ps", b)])
                        gelu_from_psum(ps[:, b, :], ("ps", b), gv[:], "gv", 512, Tg, Tg_rr)
                        for ci in range(2):
                            c = 2 * cp + ci
                            stt(junk[:], gv[:, ci * 256:(ci + 1) * 256], 1.0, gv[:, ci * 256:(ci + 1) * 256], ALU.mult, ALU.mult,
                                ["gv"], ["junk", ("ss", c)], accum_out=ss[:, c:c + 1])
                        act(gvb[:, 2 * cp:2 * cp + 2, :], gv[:].rearrange("p (a c) -> p a c", c=256), AF.Identity,
                            ["gv"], [("gvb", 2 * cp), ("gvb", 2 * cp + 1)])
                    ssr = [("ss", c) for c in range(16)]
                    act(rs16[:], ss[:], AF.Sqrt, ssr, ["rs16"], scale=1.0 / 256, bias=EPS)
                    recip(rs16[:], rs16[:], ["rs16"], ["rs16"])
                    for c in range(16):
                        ts(gvb[:, c, :], gvb[:, c, :], rs16[:, c:c + 1], None, ALU.mult, ALU.bypass,
                           [("gvb", c), "rs16"], [("gvb", c)])
                    for fc in range(2):
                        for t in range(NTT):
                            b = bank_rr.next()
                            for cc in range(4):
                                c = t * 4 + cc
                                for gi in range(2):
                                    g = 2 * fc + gi
                                    mm(ps[gi * 64:(gi + 1) * 64, b, cc * 128:(cc + 1) * 128],
                                       gvb[:, c, g * 64:(g + 1) * 64], wcm[:, g, :], True, True,
                                       [("gvb", c), ("wcm", g)], [("ps", b)], tile_position=(0, gi * 64))
                            ti = T_rr.next()
                            stt(Tp[ti][:], ps[:, b, :], vecs[:, vb + 34 + fc:vb + 35 + fc], bT4[:, fc, :],
                                ALU.mult, ALU.add, [("ps", b), "vecs", ("bT4", fc)], [("T", ti)])
                            tt_(sc[:, fc, t * TT:(t + 1) * TT], Tp[ti][:], uT[:, fc, t * TT:(t + 1) * TT], ALU.mult,
                                [("T", ti), ("uT", fc, t)], [("sc", fc, t)])
                P.barrier()

                P.barrier()
                for nm, buf, nch in (("pa", pa, 2), ("ao", ao, 4), ("sc", sc, 2)):
                    if l == 0 and nm in tap_d:
                        for kc in range(nch):
                            dma("pool", tap_d[nm][kc], buf[:, kc, :], [], [])
                P.barrier()

                with ExitStack() as sub:
                    mg = sb(sub, "mg", [128, 8, S], BF16)
                    for j in range(8):
                        wg, wg_res = w_acquire()
                        wb, wb_res = w_acquire(held=1)
                        for t in range(NTT):
                            gb = []
                            for br in range(3):
                                b = proj_fm(wg, wg_res, br * 128, Asl, Ar, 8, t)
                                gb.append(b)
                            tl = []
                            for br in range(3):
                                ti = T_rr.next()
                                tl.append(ti)
                                act(Tp[ti][:], ps[:, gb[br], :], AF.Tanh, [("ps", gb[br]), ("halfb", l)], [("T", ti)],
                                    scale=0.5, bias=halfb[:, l * 24 + br * 8 + j:l * 24 + br * 8 + j + 1])
                            srcs = [(pa, 0, 2, lambda kc, t_: ("pa", kc, t_)),
                                    (ao, 2, 4, None),
                                    (sc, 6, 2, lambda kc, t_: ("sc", kc, t_))]
                            for br, (buf, k0, nk, rf) in enumerate(srcs):
                                b = bank_rr.next()
                                for kc in range(nk):
                                    rr_ = [("ao", kc, t, 0), ("ao", kc, t, 1)] if rf is None else [rf(kc, t)]
                                    mm(ps[:, b, :], wb[:, k0 + kc, 0:128], buf[:, kc, t * TT:(t + 1) * TT], kc == 0, kc == nk - 1,
                                       [wb_res] + rr_, [("ps", b)])
                                ti = tl[br]
                                stt(Tp[ti][:], Tp[ti][:], 1.0, ps[:, b, :], ALU.add, ALU.mult, [("T", ti), ("ps", b)], [("T", ti)])
                            tt_(Tp[tl[0]][:], Tp[tl[0]][:], Tp[tl[1]][:], ALU.add, [("T", tl[0]), ("T", tl[1])], [("T", tl[0])])
                            tt_(mg[:, j, t * TT:(t + 1) * TT], Tp[tl[0]][:], Tp[tl[2]][:], ALU.add,
                                [("T", tl[0]), ("T", tl[2])], [("mg", j, t)])
                    last_norm = resid_proj_then_norm(
                        lambda kc, t_: mg[:, kc, t_ * TT:(t_ + 1) * TT], lambda kc, t_: ("mg", kc, t_),
                        lambda j, t, b: stt(xsl(j, t), ps[:, b, :], 0.5, xsl(j, t), ALU.mult, ALU.add,
                                            [("ps", b), xr(j, t)], [xr(j, t)]),
                        (vb + 36, Asl, Ar))
            P.barrier()
            last_norm()
            if l == 0:
                tap("mix")

            with ExitStack() as xa:
                xq = sb(xa, "xq", [128, 8, S], BF16)
                hmT = sb(xa, "hmT", [128, 8, MEM], BF16)
                xkT = sb(xa, "xkT", [128, 8, MEM], BF16)
                xv = sb(xa, "xv", [128, 2, D], BF16)
                with ExitStack() as sub:
                    mT = sb(sub, "mT", [128, 8, MEM], F32)
                    for dc in range(8):
                        dma("sp", mT[:, dc, :], memT[dc], [], [("mT", dc)])
                    rmsnorm(lambda dc, t: mT[:, dc, :], lambda dc, t: ("mT", dc), vb + 44,
                            lambda dc, t: hmT[:, dc, :], lambda dc, t: ("hmT", dc), 1, MEM)
                def xa_kproj(q):
                    wk, wk_res = w_acquire()
                    for jj in range(2):
                        j = 2 * q + jj
                        b = bank_rr.next()
                        for dc in range(8):
                            mm(ps[:, b, 0:MEM], wk[:, dc, jj * 128:(jj + 1) * 128], hmT[:, dc, :], dc == 0, dc == 7,
                               [wk_res, ("hmT", dc)], [("ps", b)])
                        act(xkT[:, j, :], ps[:, b, 0:MEM], AF.Identity, [("ps", b)], [("xkT", j)])

                def xa_vproj(q):
                    wv_, wv_res = w_acquire()
                    for mc in range(2):
                        b = bank_rr.next()
                        for dc in range(8):
                            mm(ps[:, b, 0:256], hmT[:, dc, mc * 128:(mc + 1) * 128], wv_[:, dc, 0:256], dc == 0, dc == 7,
                               [wv_res, ("hmT", dc)], [("ps", b)])
                        act(xv[:, mc, q * 256:(q + 1) * 256], ps[:, b, 0:256], AF.Identity, [("ps", b)], [("xv", mc, q)])

                def xa_qproj(q):
                    wq_, wq_res = w_acquire()
                    for (jj, t) in TORD:
                        j = 2 * q + jj
                        b = proj_fm(wq_, wq_res, jj * 128, Asl, Ar, 8, t)
                        act(xq[:, j, t * TT:(t + 1) * TT], ps[:, b, :], AF.Identity, [("ps", b)], [("xq", j, t)],
                            scale=1.0 / 16)

                for q in range(4):
                    xa_qproj(q)
                    xa_kproj(q)
                    xa_vproj(q)
                hi_rr = RR(range(4, 8))
                items = [(t, hh) for t in range(NTT) for hh in range(4)]
                xpend = []

                def xa_scores(t, hh):
                    pts = []
                    for mc in range(2):
                        b = lo_rr.next()
                        for dk in range(2):
                            mm(ps[:, b, :], xkT[:, 2 * hh + dk, mc * 128:(mc + 1) * 128],
                               xq[:, 2 * hh + dk, t * TT:(t + 1) * TT], dk == 0, dk == 1,
                               [("xkT", 2 * hh + dk), ("xq", 2 * hh + dk, t)], [("ps", b)])
                        pi = Pt_rr.next()
                        act(Ptp[pi][:], ps[:, b, :], AF.Exp, [("ps", b)], [("Pt", pi)])
                        pts.append(pi)
                    return pts

                def xa_pv(t, hh, pts):
                    bd = hi_rr.next()
                    for mc in range(2):
                        mm(ps[:, bd, :], ones_bf[:], Ptp[pts[mc]][:], mc == 0, mc == 1,
                           ["ones", ("Pt", pts[mc])], [("ps", bd)])
                    ri = rstd_rr.next()
                    P.add("dve", lambda h, o=rstd[ri][:], i_=ps[:, bd, :]: h.reciprocal(out=o, in_=i_),
                          [("ps", bd)], [("rstd", ri)])
                    for dch in range(2):
                        b = hi_rr.next()
                        for mc in range(2):
                            mm(ps[:, b, :], xv[:, mc, hh * 256 + dch * 128: hh * 256 + (dch + 1) * 128], Ptp[pts[mc]][:],
                               mc == 0, mc == 1, [("xv", mc, hh), ("Pt", pts[mc])], [("ps", b)])
                        ti = T_rr.next()
                        act(Tp[ti][:], ps[:, b, :], AF.Identity, [("ps", b)], [("T", ti)])
                        tt_(xq[:, 2 * hh + dch, t * TT:(t + 1) * TT], Tp[ti][:], rstd[ri][:], ALU.mult,
                            [("T", ti), ("rstd", ri)], [("xq", 2 * hh + dch, t)], eng="pool")

                for i in range(len(items) + 1):
                    if i < len(items):
                        xpend.append(xa_scores(*items[i]))
                    if i >= 1:
                        xa_pv(items[i - 1][0], items[i - 1][1], xpend[i - 1])
                last_norm = resid_proj_then_norm(
                    lambda kc, t_: xq[:, kc, t_ * TT:(t_ + 1) * TT], lambda kc, t_: ("xq", kc, t_),
                    lambda j, t, b: tt_(xsl(j, t), ps[:, b, :], xsl(j, t), ALU.add, [("ps", b), xr(j, t)], [xr(j, t)]),
                    (vb + 52, Asl, Ar))
            P.barrier()
            last_norm()
            if l == 0:
                tap("xat")

            with ExitStack() as ff:
                hid = sb(ff, "hid", [128, 8, S], BF16)
                for fg in range(4):
                    for q in range(4):
                        w1, w1_res = w_acquire()
                        for (jj, t) in TORD:
                            fcl = 2 * q + jj
                            b = proj_fm(w1, w1_res, jj * 128, Asl, Ar, 8, t)
                            ti = T_rr.next()
                            act(Tp[ti][:], ps[:, b, :], AF.Relu, [("ps", b)], [("T", ti)])
                            tt_(hid[:, fcl, t * TT:(t + 1) * TT], Tp[ti][:], Tp[ti][:], ALU.mult,
                                [("T", ti)], [("hid", fcl, t)])
                    hid_fn = lambda kc, t_: hid[:, kc, t_ * TT:(t_ + 1) * TT]
                    hid_res = lambda kc, t_: ("hid", kc, t_)
                    ev2 = lambda j, t, b: tt_(xsl(j, t), ps[:, b, :], xsl(j, t), ALU.add, [("ps", b), xr(j, t)], [xr(j, t)])
                    if fg < 3:
                        for p_ in range(3):
                            w2, w2_res = w_acquire()
                            for j in range(3 * p_, min(3 * p_ + 3, 8)):
                                for t in range(NTT):
                                    b = proj_fm(w2, w2_res, (j - 3 * p_) * 128, hid_fn, hid_res, 8, t)
                                    ev2(j, t, b)
                    else:
                        if l + 1 < n_layers:
                            nargs = ((l + 1) * VPL + 0, Asl, Ar)
                            post = None
                        else:
                            nargs = (VPL * DEPTH, xsl, xr)

                            def post(t):
                                for dc in range(8):
                                    dma("sp", outT[dc, :, t * TT:(t + 1) * TT], xres[:, dc, t * TT:(t + 1) * TT], [xr(dc, t)], [])
                        last_norm = resid_proj_then_norm(hid_fn, hid_res, ev2, nargs, post)
            P.barrier()
            last_norm()
            if l == 0:
                tap("ffn")

        cnt = P.emit(nc)
        nc._cnt = cnt
        nc._nops = len(P.ops)
    return nc


def host_consts():
    cst = np.zeros((128, NCST), np.float32)
    cst[:, C_ID:C_ID + 128] = np.eye(128, dtype=np.float32)
    k = np.arange(128)
    cst[:, C_U:C_U + 128] = (k[:, None] <= k[None, :]).astype(np.float32)
    wins = {(0, 0): 2, (0, 1): 4, (1, 0): 8, (1, 1): 16}
    t = np.arange(16)
    for fc in range(2):
        for half in range(2):
            w = wins[(fc, half)]
            cst[half * 64:(half + 1) * 64, C_INV + fc * 16:C_INV + (fc + 1) * 16] = \
                (1.0 / np.minimum(t + 1, w)).astype(np.float32)[None, :]
    for kk in range(128):
        cst[kk, C_SWAP + (kk + 64) % 128] = 1.0
    cst[:, C_NEG:C_NEG + 128] = np.where(k[:, None] > k[None, :], -30000.0, 0.0).astype(np.float32)
    return cst


def pack_vecs(inp):
    v = np.zeros((128, NV), np.float32)

    def fm(a):
        a = np.asarray(a, np.float32)
        return a.reshape(-1, 128).T

    for l in range(DEPTH):
        b = l * VPL
        v[:, b + 0:b + 8] = fm(inp["norm_mix_g"][l])
        v[:, b + 8:b + 32] = fm(inp["b_gate"][l])
        v[:, b + 32:b + 34] = fm(inp["pool_scale"][l])
        v[:, b + 34:b + 36] = fm(inp["sgu_norm_g"][l])
        v[:, b + 36:b + 44] = fm(inp["norm_xattn_g"][l])
        v[:, b + 44:b + 52] = fm(inp["norm_mem_g"][l])
        v[:, b + 52:b + 60] = fm(inp["norm_ffn_g"][l])
        for r in range(3):
            v[32 * r:32 * r + 8, b + 60] = np.asarray(inp["b_forget"][l], np.float32)
    v[:, VPL * DEPTH:VPL * DEPTH + 8] = fm(inp["final_norm_g"])
    return v


def shared_inputs(inp):
    f = lambda k: np.ascontiguousarray(np.asarray(inp[k], np.float32))
    sgu_b = np.asarray(inp["sgu_b"], np.float32)
    sgub4 = np.zeros((DEPTH, 2, 128, 512), np.float32)
    for fc in range(2):
        for gi in range(2):
            sgub4[:, fc, gi * 64:(gi + 1) * 64, :] = np.tile(sgu_b[:, 2 * fc + gi, :], (1, 4))[:, None, :]
    pw = np.asarray(inp["pool_w"], np.float32)
    pool_wbd = np.zeros((DEPTH, 128, 2, 128), np.float32)
    for g in range(4):
        gi = g % 2
        pool_wbd[:, gi * 64:(gi + 1) * 64, g // 2, gi * 64:(gi + 1) * 64] = pw[:, g]
    return {
        "w_in": f("w_in"), "pool_wbd": pool_wbd,
        "sgu_wT": np.ascontiguousarray(np.asarray(inp["sgu_w"], np.float32).transpose(0, 1, 3, 2)),
        "sgub4": sgub4,
        "w_branch_a": f("w_branch_a"), "w_branch_b": f("w_branch_b"), "w_branch_c": f("w_branch_c"),
        "w_out": f("w_out"), "w_xq": f("w_xq"), "w_xkv": f("w_xkv"), "w_xo": f("w_xo"),
        "w_ff1": f("w_ff1"), "w_ff2": f("w_ff2"),
        "vecs": pack_vecs(inp), "cst": host_consts(),
    }


def core_inputs(inp, b):
    x = np.asarray(inp["x"], np.float32)[b]
    m = np.asarray(inp["mem"], np.float32)[b]
    return {
        "xT": np.ascontiguousarray(x.T).reshape(8, 128, S),
        "memT": np.ascontiguousarray(m.T).reshape(8, 128, MEM),
    }


_NC_CACHE = {}


def kernel(**inputs):
    if "nc" not in _NC_CACHE:
        _NC_CACHE["nc"] = build()
    nc = _NC_CACHE["nc"]
    shared = shared_inputs(inputs)
    in_maps = []
    for b in range(8):
        m = dict(shared)
        m.update(core_inputs(inputs, b))
        in_maps.append(m)
    res = run_bass_kernel_spmd(nc, in_maps, core_ids=list(range(8)))
    out = np.empty((8, S, D), np.float32)
    for b in range(8):
        out[b] = res.results[b]["outT"].reshape(D, S).T
    return out
```
